# Optimizing a Trainium2 kernel written in Bass

```python
import math
import jax, jax.numpy as jnp
from jax import lax
import numpy as np

D_MODEL = 1024
BATCH = 4
SEQ = 4096
DEPTH = 1

EPS = 1e-6
MLA_HEADS = 8
QK_NOPE = 64
QK_ROPE = 32
QK_HEAD = QK_NOPE + QK_ROPE
V_HEAD = 64
Q_LORA = 256
KV_LORA = 128
ROPE_THETA = 10000.0
Q_BLOCK = 128
RWKV_HEADS = 8
RWKV_HEAD = 64
RWKV_DIM = RWKV_HEADS * RWKV_HEAD
DECAY_LORA = 64
AAA_LORA = 64
GATE_LORA = 128
GN_EPS = 64e-5
PEER_HEADS = 8
N_KEYS = 128
N_EXPERTS = N_KEYS * N_KEYS
D_KEY = 256
HALF_KEY = D_KEY // 2
TOPK_HALF = 16
TOPK = 16
PEER_BLOCK = 64
N_BRANCH = 2
MLA_IN = Q_LORA + KV_LORA + QK_ROPE
RWKV_IN = 3 * RWKV_DIM + DECAY_LORA + AAA_LORA + GATE_LORA
GATE_IN = N_BRANCH * D_MODEL
IN_DIM = MLA_IN + RWKV_IN + GATE_IN
N_MOD = 6

kernel_name = "hybrid_mla_rwkv7_peer_adaln_block"


def rms_norm(x, g, eps=EPS):
    xf = x.astype(jnp.float32)
    y = xf * lax.rsqrt(jnp.mean(xf * xf, axis=-1, keepdims=True) + eps)
    return (y * g.astype(jnp.float32)).astype(x.dtype)


def modulate(h, shift, scale):
    return h * (1.0 + scale[:, None, :]) + shift[:, None, :]


def rope(x, positions):
    half = x.shape[-1] // 2
    inv_freq = ROPE_THETA ** (-jnp.arange(half, dtype=jnp.float32) / half)
    ang = positions.astype(jnp.float32)[..., None] * inv_freq
    cos = jnp.cos(ang)[:, :, None, :]
    sin = jnp.sin(ang)[:, :, None, :]
    xf = x.astype(jnp.float32)
    x1, x2 = xf[..., :half], xf[..., half:]
    return jnp.concatenate([x1 * cos - x2 * sin, x1 * sin + x2 * cos], axis=-1).astype(x.dtype)


def causal_block_attention(q, k, v):
    B, S, H, Dq = q.shape
    nb = S // Q_BLOCK
    scale = Dq ** -0.5
    qb = q.reshape(B, nb, Q_BLOCK, H, Dq).transpose(1, 0, 2, 3, 4)
    kpos = jnp.arange(S)

    def one_block(args):
        i, q_blk = args
        s = jnp.einsum('bqhd,bkhd->bhqk', q_blk, k).astype(jnp.float32) * scale
        qpos = i * Q_BLOCK + jnp.arange(Q_BLOCK)
        mask = kpos[None, :] <= qpos[:, None]
        s = jnp.where(mask[None, None], s, -jnp.inf)
        p = jax.nn.softmax(s, axis=-1).astype(v.dtype)
        return jnp.einsum('bhqk,bkhd->bqhd', p, v)

    out = lax.map(one_block, (jnp.arange(nb), qb))
    return out.transpose(1, 0, 2, 3, 4).reshape(B, S, H, v.shape[-1])


def mla_branch(q_a, kv_a, k_rope_raw, positions, q_a_norm, w_uq, kv_a_norm, w_ukv, q_norm, k_norm, w_o_mla):
    B, S, _ = q_a.shape
    cq = rms_norm(q_a, q_a_norm)
    q = (cq @ w_uq).reshape(B, S, MLA_HEADS, QK_HEAD)
    ckv = rms_norm(kv_a, kv_a_norm)
    kv = (ckv @ w_ukv).reshape(B, S, MLA_HEADS, QK_NOPE + V_HEAD)
    k_nope, v = kv[..., :QK_NOPE], kv[..., QK_NOPE:]
    q_nope = rms_norm(q[..., :QK_NOPE], q_norm[:QK_NOPE])
    q_pe = rope(rms_norm(q[..., QK_NOPE:], q_norm[QK_NOPE:]), positions)
    k_nope = rms_norm(k_nope, k_norm[:QK_NOPE])
    k_pe = rope(rms_norm(k_rope_raw[:, :, None, :], k_norm[QK_NOPE:]), positions)
    k_pe = jnp.broadcast_to(k_pe, (B, S, MLA_HEADS, QK_ROPE))
    qh = jnp.concatenate([q_nope, q_pe], axis=-1)
    kh = jnp.concatenate([k_nope, k_pe], axis=-1)
    o = causal_block_attention(qh, kh, v)
    return o.reshape(B, S, MLA_HEADS * V_HEAD) @ w_o_mla


def token_shift(p, mu):
    prev = jnp.pad(p, ((0, 0), (1, 0), (0, 0)))[:, :-1]
    return p + (prev - p) * mu


def rwkv7_scan(r, decay, k, v, kk, a):
    B, S, H, N = r.shape
    b = kk * a

    def step(state, inp):
        r_t, w_t, k_t, v_t, kk_t, b_t = inp
        sa = jnp.einsum('bhvk,bhk->bhv', state, -kk_t)
        state = (state * w_t[:, :, None, :] + sa[..., None] * b_t[:, :, None, :]
                 + v_t[..., None] * k_t[:, :, None, :])
        y = jnp.einsum('bhvk,bhk->bhv', state, r_t)
        return state, y

    xs = tuple(t.transpose(1, 0, 2, 3) for t in (r, decay, k, v, kk, b))
    s0 = jnp.zeros((B, H, N, N), jnp.float32)
    _, y = lax.scan(step, s0, xs)
    return y.transpose(1, 0, 2, 3)


def rwkv7_branch(p, mu_shift, w_decay_up, decay_base, w_aaa_up, aaa_base, w_gate_up,
                 k_k, k_a, r_k, ln_x_w, ln_x_b, w_o_rwkv):
    B, S, _ = p.shape
    p = token_shift(p, mu_shift)
    splits = [RWKV_DIM, 2 * RWKV_DIM, 3 * RWKV_DIM, 3 * RWKV_DIM + DECAY_LORA,
              3 * RWKV_DIM + DECAY_LORA + AAA_LORA]
    r, k, v, wd, ad, gd = jnp.split(p, splits, axis=-1)
    w = -jax.nn.softplus(-(decay_base + jnp.tanh(wd) @ w_decay_up)) - 0.5
    decay = jnp.exp(-jnp.exp(w.astype(jnp.float32)))
    a = jax.nn.sigmoid(aaa_base + ad @ w_aaa_up)
    g = jax.nn.sigmoid(gd) @ w_gate_up

    def heads(t):
        return t.reshape(B, S, RWKV_HEADS, RWKV_HEAD).astype(jnp.float32)

    kk = heads(k * k_k)
    kk = kk / jnp.maximum(jnp.linalg.norm(kk, axis=-1, keepdims=True), 1e-12)
    a_h = heads(a)
    k_h = heads(k) * (1.0 + (a_h - 1.0) * k_a.reshape(RWKV_HEADS, RWKV_HEAD).astype(jnp.float32))
    r_h, v_h = heads(r), heads(v)
    y = rwkv7_scan(r_h, heads(decay), k_h, v_h, kk, a_h)
    mu = jnp.mean(y, axis=-1, keepdims=True)
    var = jnp.mean(jnp.square(y - mu), axis=-1, keepdims=True)
    y = (y - mu) * lax.rsqrt(var + GN_EPS)
    y = y * ln_x_w.reshape(RWKV_HEADS, RWKV_HEAD) + ln_x_b.reshape(RWKV_HEADS, RWKV_HEAD)
    y = y + jnp.sum(r_h * k_h * r_k, axis=-1, keepdims=True) * v_h
    y = y.reshape(B, S, RWKV_DIM).astype(p.dtype) * g
    return y @ w_o_rwkv


def token_mixer(h, positions, w_in, mu_shift, q_a_norm, w_uq, kv_a_norm, w_ukv, q_norm, k_norm, w_o_mla,
                w_decay_up, decay_base, w_aaa_up, aaa_base, w_gate_up, k_k, k_a, r_k, ln_x_w, ln_x_b,
                w_o_rwkv, w_out):
    proj = h @ w_in
    q_a, kv_a, k_rope_raw, rwkv_in, gate_logits = jnp.split(
        proj, [Q_LORA, Q_LORA + KV_LORA, MLA_IN, MLA_IN + RWKV_IN], axis=-1)
    o_mla = mla_branch(q_a, kv_a, k_rope_raw, positions, q_a_norm, w_uq, kv_a_norm, w_ukv,
                       q_norm, k_norm, w_o_mla)
    o_rwkv = rwkv7_branch(rwkv_in, mu_shift, w_decay_up, decay_base, w_aaa_up, aaa_base, w_gate_up,
                          k_k, k_a, r_k, ln_x_w, ln_x_b, w_o_rwkv)
    g_mla, g_rwkv = jnp.split(jax.nn.sigmoid(gate_logits), 2, axis=-1)
    return (g_mla * o_mla + g_rwkv * o_rwkv) @ w_out


def peer_ffn(h, w_query, sub_keys1, sub_keys2, expert_down, expert_up):
    B, S, D = h.shape
    q = (h @ w_query).reshape(B, S, PEER_HEADS, 2, HALF_KEY)
    s1 = jnp.einsum('bshd,nd->bshn', q[..., 0, :], sub_keys1).astype(jnp.float32)
    s2 = jnp.einsum('bshd,nd->bshn', q[..., 1, :], sub_keys2).astype(jnp.float32)
    v1, i1 = lax.top_k(s1, TOPK_HALF)
    v2, i2 = lax.top_k(s2, TOPK_HALF)
    cand = (v1[..., :, None] + v2[..., None, :]).reshape(B, S, PEER_HEADS, TOPK_HALF * TOPK_HALF)
    cand_idx = (i1[..., :, None] * N_KEYS + i2[..., None, :]).reshape(B, S, PEER_HEADS, TOPK_HALF * TOPK_HALF)
    top_v, pos = lax.top_k(cand, TOPK)
    expert_idx = jnp.take_along_axis(cand_idx, pos, axis=-1)
    gates = jax.nn.softmax(top_v, axis=-1).astype(h.dtype)
    nb = S // PEER_BLOCK
    hb = h.reshape(B, nb, PEER_BLOCK, D).transpose(1, 0, 2, 3)
    idxb = expert_idx.reshape(B, nb, PEER_BLOCK, PEER_HEADS, TOPK).transpose(1, 0, 2, 3, 4)
    gb = gates.reshape(B, nb, PEER_BLOCK, PEER_HEADS, TOPK).transpose(1, 0, 2, 3, 4)

    def one_block(args):
        h_blk, idx_blk, g_blk = args
        u = jnp.take(expert_down, idx_blk, axis=0)
        act = jax.nn.gelu(jnp.einsum('bpd,bphkd->bphk', h_blk, u))
        vv = jnp.take(expert_up, idx_blk, axis=0)
        return jnp.einsum('bphk,bphkd->bpd', g_blk * act, vv)

    out = lax.map(one_block, (hb, idxb, gb))
    return out.transpose(1, 0, 2, 3).reshape(B, S, D)


def setup_inputs(seed: int = 0) -> dict:
    key = jax.random.key(seed)
    ks = jax.random.split(key, 40)
    f32 = jnp.float32

    def nrm(k, shape, scale):
        return jax.random.normal(k, (DEPTH,) + shape, f32) * scale

    def gain(k, shape):
        return 1.0 + 0.1 * jax.random.normal(k, (DEPTH,) + shape, f32)

    x = jax.random.normal(ks[0], (BATCH, SEQ, D_MODEL), f32)
    c = jax.random.normal(ks[1], (BATCH, D_MODEL), f32)
    offsets = jax.random.randint(ks[2], (BATCH, 1), 0, 1024, dtype=jnp.int32)
    positions = offsets + jnp.arange(SEQ, dtype=jnp.int32)[None, :]
    return {
        "x": x, "c": c, "positions": positions,
        "w_ada": nrm(ks[3], (D_MODEL, N_MOD * D_MODEL), 0.3 * D_MODEL ** -0.5),
        "b_ada": nrm(ks[4], (N_MOD * D_MODEL,), 0.05),
        "norm_mix": gain(ks[5], (D_MODEL,)),
        "w_in": nrm(ks[6], (D_MODEL, IN_DIM), D_MODEL ** -0.5),
        "mu_shift": jax.random.uniform(ks[7], (DEPTH, RWKV_IN), f32),
        "q_a_norm": gain(ks[8], (Q_LORA,)),
        "w_uq": nrm(ks[9], (Q_LORA, MLA_HEADS * QK_HEAD), Q_LORA ** -0.5),
        "kv_a_norm": gain(ks[10], (KV_LORA,)),
        "w_ukv": nrm(ks[11], (KV_LORA, MLA_HEADS * (QK_NOPE + V_HEAD)), KV_LORA ** -0.5),
        "q_norm": gain(ks[12], (QK_HEAD,)),
        "k_norm": gain(ks[13], (QK_HEAD,)),
        "w_o_mla": nrm(ks[14], (MLA_HEADS * V_HEAD, D_MODEL), (MLA_HEADS * V_HEAD) ** -0.5),
        "w_decay_up": nrm(ks[15], (DECAY_LORA, RWKV_DIM), 0.1),
        "decay_base": jax.random.uniform(ks[16], (DEPTH, RWKV_DIM), f32, -6.0, 2.0),
        "w_aaa_up": nrm(ks[17], (AAA_LORA, RWKV_DIM), 0.1),
        "aaa_base": nrm(ks[18], (RWKV_DIM,), 0.5),
        "w_gate_up": nrm(ks[19], (GATE_LORA, RWKV_DIM), GATE_LORA ** -0.5),
        "k_k": 0.85 + nrm(ks[20], (RWKV_DIM,), 0.1),
        "k_a": gain(ks[21], (RWKV_DIM,)),
        "r_k": nrm(ks[22], (RWKV_HEADS, RWKV_HEAD), 0.1),
        "ln_x_w": gain(ks[23], (RWKV_DIM,)),
        "ln_x_b": nrm(ks[24], (RWKV_DIM,), 0.02),
        "w_o_rwkv": nrm(ks[25], (RWKV_DIM, D_MODEL), RWKV_DIM ** -0.5),
        "w_out": nrm(ks[26], (D_MODEL, D_MODEL), D_MODEL ** -0.5),
        "norm_ffn": gain(ks[27], (D_MODEL,)),
        "w_query": nrm(ks[28], (D_MODEL, PEER_HEADS * D_KEY), D_MODEL ** -0.5),
        "sub_keys1": nrm(ks[29], (N_KEYS, HALF_KEY), HALF_KEY ** -0.5),
        "sub_keys2": nrm(ks[30], (N_KEYS, HALF_KEY), HALF_KEY ** -0.5),
        "expert_down": nrm(ks[31], (N_EXPERTS, D_MODEL), D_MODEL ** -0.5),
        "expert_up": nrm(ks[32], (N_EXPERTS, D_MODEL), PEER_HEADS ** -0.5),
    }


def reference(x, c, positions, w_ada, b_ada, norm_mix, w_in, mu_shift, q_a_norm, w_uq, kv_a_norm, w_ukv,
              q_norm, k_norm, w_o_mla, w_decay_up, decay_base, w_aaa_up, aaa_base, w_gate_up, k_k, k_a, r_k,
              ln_x_w, ln_x_b, w_o_rwkv, w_out, norm_ffn, w_query, sub_keys1, sub_keys2, expert_down, expert_up):
    for l in range(DEPTH):
        mod = jax.nn.silu(c) @ w_ada[l] + b_ada[l]
        sh1, sc1, gt1, sh2, sc2, gt2 = jnp.split(mod, N_MOD, axis=-1)
        h = modulate(rms_norm(x, norm_mix[l]), sh1, sc1)
        mix = token_mixer(h, positions, w_in[l], mu_shift[l], q_a_norm[l], w_uq[l], kv_a_norm[l], w_ukv[l],
                          q_norm[l], k_norm[l], w_o_mla[l], w_decay_up[l], decay_base[l], w_aaa_up[l],
                          aaa_base[l], w_gate_up[l], k_k[l], k_a[l], r_k[l], ln_x_w[l], ln_x_b[l],
                          w_o_rwkv[l], w_out[l])
        x = x + gt1[:, None, :] * mix
        h = modulate(rms_norm(x, norm_ffn[l]), sh2, sc2)
        x = x + gt2[:, None, :] * peer_ffn(h, w_query[l], sub_keys1[l], sub_keys2[l], expert_down[l], expert_up[l])
    return x
```

```python
import os
import numpy as np
from contextlib import ExitStack
import concourse.bass as bass
import concourse.mybir as mybir
from concourse.bass_utils import run_bass_kernel_spmd

F32 = mybir.dt.float32
BF16 = mybir.dt.bfloat16
I32 = mybir.dt.int32
U32 = mybir.dt.uint32
F32R = mybir.dt.float32r if os.environ.get("NO_F32R") is None else mybir.dt.float32
AF = mybir.ActivationFunctionType
ALU = mybir.AluOpType
AX = mybir.AxisListType

ENGS = ("pe", "act", "dve", "pool", "sp")
EPOCH = 20000
NDMA = 24
D = 1024
SEQ = 4096
C0 = float(np.exp(-0.5))
EPS = 1e-6
GN_EPS = 64e-5


class Sched:
    def __init__(self, nc, stack):
        self.nc = nc
        self.stack = stack
        self.ops = {e: [] for e in ENGS}
        self.cnt = {e: 0 for e in ENGS}
        self.sems = {}
        self.lastw = {}
        self.readers = {}
        self.seen = {e: {} for e in ENGS}
        self.dma_i = 0
        self.dma_cnt = [0] * NDMA

    def sem(self, key):
        if key not in self.sems:
            self.sems[key] = self.stack.enter_context(self.nc.semaphore("s_%s_%s" % key))
        return self.sems[key]

    def _tok(self, eng):
        c = self.cnt[eng]
        return ((eng, c // EPOCH), c % EPOCH + 1)

    def _deps(self, r, w):
        toks = []
        for b in list(r) + list(w):
            t = self.lastw.get(b)
            if t is not None:
                toks.append(t)
        for b in w:
            toks.extend(self.readers.get(b, ()))
        return toks

    def _filter(self, eng, toks):
        need = {}
        for (k, v) in toks:
            if eng == "pe" and k[0] == "pe":
                continue
            if self.seen[eng].get(k, 0) >= v:
                continue
            if need.get(k, 0) < v:
                need[k] = v
        for k, v in need.items():
            self.seen[eng][k] = v
        return [(self.sem(k), v) for k, v in need.items()]

    def _commit(self, tok, r, w):
        for b in w:
            self.lastw[b] = tok
            self.readers[b] = []
        for b in r:
            self.readers.setdefault(b, []).append(tok)

    def op(self, eng, fn, r=(), w=()):
        ex = [k for k in r if k.startswith("@")]
        if ex:
            w = list(w) + ex
            r = [k for k in r if not k.startswith("@")]
        waits = self._filter(eng, self._deps(r, w))
        tok = self._tok(eng)
        self.cnt[eng] += 1
        self.ops[eng].append((fn, waits, (self.sem(tok[0]), 1)))
        self._commit(tok, r, w)
        return tok

    def dma(self, eng, out, in_, r=(), w=()):
        k = self.dma_i % NDMA
        self.dma_i += 1
        toks = self._deps(r, w)
        if self.dma_cnt[k] > 0:
            toks.append((("dma", k), 16 * self.dma_cnt[k]))
        waits = self._filter(eng, toks)
        self.dma_cnt[k] += 1
        tok = (("dma", k), 16 * self.dma_cnt[k])
        fn = lambda e, out=out, in_=in_: e.dma_start(out=out, in_=in_)
        self.ops[eng].append((fn, waits, (self.sem(tok[0]), 16)))
        self._commit(tok, r, w)
        return tok

    def barrier(self):
        toks = []
        for e in ENGS:
            if self.cnt[e] > 0:
                c = self.cnt[e] - 1
                toks.append(((e, c // EPOCH), c % EPOCH + 1))
        for k in range(NDMA):
            if self.dma_cnt[k] > 0:
                toks.append((("dma", k), 16 * self.dma_cnt[k]))
        for e in ENGS:
            waits = [(self.sem(k), v) for (k, v) in toks if self.seen[e].get(k, 0) < v]
            for (k, v) in toks:
                self.seen[e][k] = max(self.seen[e].get(k, 0), v)
            if waits:
                self.ops[e].append((None, waits, None))
        self.lastw.clear()
        self.readers.clear()

    def emit(self):
        nc = self.nc
        ops = self.ops

        def run(e, lst):
            for (fn, waits, inc) in lst:
                for (s, v) in waits:
                    e.wait_ge(s, v)
                if fn is not None:
                    fn(e).then_inc(inc[0], inc[1])

        with nc.Block() as block:
            @block.sync
            def _(e):
                run(e, ops["sp"])

            @block.tensor
            def _(e):
                run(e, ops["pe"])

            @block.scalar
            def _(e):
                run(e, ops["act"])

            @block.vector
            def _(e):
                run(e, ops["dve"])

            @block.gpsimd
            def _(e):
                run(e, ops["pool"])
        self.ops = {e: [] for e in ENGS}


CO = dict(c=0, bsh1=8, bsc1=16, bsh2=24, bsc2=32, nmix=40, nffn=48, dbase=56, abase=60, kk=64, ka=68,
          rk=72, lnw=76, lnb=80)
NCOL = 84
RB = dict(qan=0, kvan=256, qn=384, kn=480, invf=576, bgt1=592)
NRB = 592 + 1024


class K:
    def __init__(self, nblk=32, dbg=False, phases="ABC"):
        self.nblk = nblk
        self.dbg = dbg
        self.phases = phases
        self.nc = bass.Bass("TRN2", target_bir_lowering=False)
        self.build()

    def dram(self, name, shape, dt, kind="ExternalInput"):
        return self.nc.dram_tensor(name, list(shape), dt, kind=kind).ap()

    def sb(self, name, shape, dt, st=None):
        self._uid = getattr(self, "_uid", 0) + 1
        return (st or self.st).enter_context(self.nc.sbuf_tensor("sb%d_%s" % (self._uid, name), list(shape), dt))

    def mm(self, out, lhsT, rhs, r, w, start=True, stop=True):
        self.S.op("pe", lambda e: e.matmul(out, lhsT=lhsT, rhs=rhs, start=start, stop=stop), r=r, w=w)

    def tr(self, out, in_, ident, r, w):
        self.S.op("pe", lambda e: e.transpose(out, in_, ident), r=r, w=w)

    def act(self, out, in_, func, r, w, bias=None, scale=None, accum=None, eng="act"):
        kw = {}
        if bias is not None:
            kw["bias"] = bias
        if scale is not None:
            kw["scale"] = scale
        if accum is not None:
            kw["accum_out"] = accum
        self.S.op("act", lambda e: e.activation(out=out, in_=in_, func=func, **kw), r=r, w=w)

    def tt(self, eng, out, in0, in1, op, r, w):
        self.S.op(eng, lambda e: e.tensor_tensor(out=out, in0=in0, in1=in1, op=op), r=r, w=w)

    def ts(self, eng, out, in0, s1, s2, op0, op1, r, w, accum=None):
        if s2 is None:
            self.S.op(eng, lambda e: e.tensor_single_scalar(out=out, in_=in0, scalar=s1, op=op0), r=r, w=w)
        elif accum is not None:
            self.S.op(eng, lambda e: e.tensor_scalar(out=out, in0=in0, scalar1=s1, scalar2=s2, op0=op0, op1=op1,
                                                     accum_out=accum), r=r, w=w)
        else:
            self.S.op(eng, lambda e: e.tensor_scalar(out=out, in0=in0, scalar1=s1, scalar2=s2, op0=op0, op1=op1),
                      r=r, w=w)

    def stt(self, eng, out, in0, scalar, in1, op0, op1, r, w, accum=None):
        if accum is None:
            self.S.op(eng, lambda e: e.scalar_tensor_tensor(out=out, in0=in0, scalar=scalar, in1=in1, op0=op0,
                                                            op1=op1), r=r, w=w)
        else:
            self.S.op(eng, lambda e: e.scalar_tensor_tensor(out=out, in0=in0, scalar=scalar, in1=in1, op0=op0,
                                                            op1=op1, accum_out=accum), r=r, w=w)

    def cp(self, eng, out, in_, r, w):
        if eng == "act":
            self.S.op("act", lambda e: e.copy(out=out, in_=in_), r=r, w=w)
        else:
            self.S.op(eng, lambda e: e.tensor_copy(out=out, in_=in_), r=r, w=w)

    def recip(self, out, in_, r, w):
        self.S.op("dve", lambda e: e.reciprocal(out=out, in_=in_), r=r, w=w)

    def memset(self, eng, ap, val, w):
        self.S.op(eng, lambda e: e.memset(ap, val), w=w)

    def ld(self, out, in_, w, r=(), eng="sp"):
        self.S.dma(eng, out, in_, r=r, w=w)

    def ps(self, n=1):
        banks = getattr(self, "banks1", (0, 1, 2, 3, 4))
        if not hasattr(self, "bank_last"):
            self.bank_last = {}
            self.bank_cnt = {}
            self.alloc_ctr = 0
        self.alloc_ctr += 1
        b = min(banks, key=lambda x: self.bank_last.get(x, 0))
        self.bank_last[b] = self.alloc_ctr
        c = self.bank_cnt.get((b, n), 0)
        self.bank_cnt[(b, n)] = c + 1
        off = (c % (4 // n)) * n
        return self.pbank[b][:, off * 128:(off + n) * 128], ["@b%d" % b]

    def psb(self, n):
        u = self.psb_ptr
        if u + n > 8:
            u = 0
        self.psb_ptr = u + n
        return self.ptb[:, u * 128:(u + n) * 128], ["@b5"]

    def build(self):
        nc = self.nc
        NB = self.nblk
        T = NB * 128
        din = {}
        for name, shape, dt in [
            ("xf", [SEQ, D], F32), ("xown", [2048, D], F32), ("poso", [128, 16], I32), ("colp", [128, NCOL], F32), ("rowA", [128, 1792], F32),
            ("rowB", [128, NRB], F32), ("rowC", [128, 1024], F32), ("posc", [128, 32], I32),
            ("ident", [128, 128], F32), ("msl", [128, 128], F32), ("mg1", [128, 256], F32),
            ("mg2", [128, 256], F32), ("bones", [128, 128], F32), ("caus", [128, 128], F32),
            ("iota", [128, 128], F32), ("selm", [128, 2], F32),
            ("w_ada", [D, 6 * D], F32), ("w_in", [D, 4256], F32), ("w_uq", [256, 768], F32),
            ("w_ukv", [128, 1024], F32), ("w_o_mla", [512, D], F32), ("wl", [128, 512], F32),
            ("w_gate_up", [128, 512], F32), ("w_o_rwkv", [512, D], F32), ("w_out", [D, D], F32),
            ("w_query", [D, 2048], F32), ("sk1T", [128, 128], F32), ("sk2T", [128, 128], F32),
            ("edT", [128, D, 128], F32), ("eup", [16384, D], F32),
        ]:
            din[name] = self.dram(name, shape, dt)
        self.din = din
        out = self.dram("out", [2048, D], F32, kind="ExternalOutput")
        zscr = self.dram("zscr", [512, SEQ], BF16, kind="Internal")
        x1scr = self.dram("x1scr", [SEQ, D], F32, kind="Internal")
        gscr = self.dram("gscr", [16, 128, 16 * 8 * 128], BF16, kind="Internal")
        self.dbg_out = {}
        if self.dbg:
            for name, shape in [("d_z", [512, SEQ]), ("d_x1", [SEQ, D]), ("d_misc", [128, 4096])]:
                self.dbg_out[name] = self.dram(name, shape, F32 if name != "d_z" else BF16, kind="ExternalOutput")

        with ExitStack() as st0:
            self.st = st0
            self.S = S = Sched(nc, st0)
            self.pbank = [st0.enter_context(nc.psum_tensor("pb%d" % i, [128, 512], F32)) for i in range(8) if i != 5]
            self.pbank.insert(5, None)
            self.ptb = st0.enter_context(nc.psum_tensor("ptb", [128, 1024], BF16))
            self.ps_ptr = 0
            self.psb_ptr = 0
            sb = self.sb
            ident = sb("ident", [128, 128], F32)
            identb = sb("identb", [128, 128], BF16)
            colp = sb("colp", [128, NCOL], F32)
            modc = sb("modc", [128, 32], F32)
            sc1 = sb("sc1", [128, 8], F32)
            sc2 = sb("sc2", [128, 8], F32)
            gtb = sb("gtb", [128, 2048], F32)
            ones = sb("ones", [128, 128], F32)
            self.scr1 = sb("scr1", [128, 2], F32)
            self.ld(ident[:], din["ident"][:, :], w=["ident"])
            self.ld(colp[:], din["colp"][:, :], w=["colp"])
            self.cp("dve", identb[:], ident[:], r=["ident"], w=["identb"])
            self.memset("pool", ones[:], 1.0, w=["ones"])
            self.ident, self.identb, self.colp, self.ones = ident, identb, colp, ones

            with ExitStack() as st:
                self.st = st
                silc = sb("silc", [128, 8], F32)
                self.act(silc[:], colp[:, 0:8], AF.Silu, r=["colp"], w=["silc"])
                wst = [sb("wst%d" % j, [128, 8, 1024], F32) for j in range(2)]
                gi = 0
                mps, mk = self.ps(1)
                for grp, c0 in enumerate([0, 1024, 3072, 4096]):
                    wt = wst[gi % 2]
                    wk = "wst%d" % (gi % 2)
                    gi += 1
                    for k in range(8):
                        self.ld(wt[:, k, :], din["w_ada"][k * 128:(k + 1) * 128, c0:c0 + 1024], w=[wk + "_%d" % k])
                    for j in range(8):
                        for k in range(8):
                            self.mm(mps[:, grp * 8 + j:grp * 8 + j + 1], wt[:, k, j * 128:(j + 1) * 128],
                                    silc[:, k:k + 1], r=[wk + "_%d" % k, "silc"], w=mk, start=(k == 0), stop=(k == 7))
                self.tt("dve", modc[:], mps[:, 0:32], colp[:, 8:40], ALU.add, r=mk + ["colp"], w=["modc"])
                self.stt("dve", sc1[:], modc[:, 8:16], 1.0, colp[:, CO["nmix"]:CO["nmix"] + 8], ALU.add, ALU.mult,
                         r=["modc", "colp"], w=["sc1"])
                self.stt("dve", sc2[:], modc[:, 24:32], 1.0, colp[:, CO["nffn"]:CO["nffn"] + 8], ALU.add, ALU.mult,
                         r=["modc", "colp"], w=["sc2"])
                grow = sb("grow", [1, 2048], F32)
                for grp, c0 in enumerate([2048, 5120]):
                    wt = wst[gi % 2]
                    wk = "wst%d" % (gi % 2)
                    gi += 1
                    for k in range(8):
                        self.ld(wt[:, k, :], din["w_ada"][k * 128:(k + 1) * 128, c0:c0 + 1024], w=[wk + "_%d" % k])
                    for hf in range(2):
                        rp, rk_ = self.ps(4)
                        for k in range(8):
                            self.mm(rp[0:1, :], silc[:, k:k + 1], wt[:, k, hf * 512:(hf + 1) * 512],
                                    r=[wk + "_%d" % k, "silc"], w=rk_, start=(k == 0), stop=(k == 7))
                        self.cp("act", grow[0:1, grp * 1024 + hf * 512: grp * 1024 + (hf + 1) * 512], rp[0:1, :],
                                r=rk_, w=["grow"])
                for q in range(4):
                    bp, bk = self.ps(4)
                    self.mm(bp[:, :], ones[0:1, :], grow[0:1, q * 512:(q + 1) * 512], r=["ones", "grow"], w=bk)
                    self.cp("act", gtb[:, q * 512:(q + 1) * 512], bp[:, :], r=bk, w=["gtb"])
                self.S.barrier()
                self.S.emit()
            self.modc, self.sc1, self.sc2, self.gtb = modc, sc1, sc2, gtb

            if "A" in self.phases:
                with ExitStack() as st:
                    self.st = st
                    self.phaseA(zscr)
                    self.S.barrier()
                    self.S.emit()
            if "B" in self.phases:
                with ExitStack() as st:
                    self.st = st
                    self._phase_st = st
                    self.phaseB(zscr, x1scr)
                    self.S.barrier()
                    self.S.emit()
            if "C" in self.phases:
                with ExitStack() as st:
                    self.st = st
                    self.phaseC(x1scr, gscr, out)
                    self.S.barrier()
                    self.S.emit()
            if self.dbg:
                with ExitStack() as st:
                    self.st = st
                    if "A" in self.phases:
                        self.S.dma("sp", self.dbg_out["d_z"][:, :], zscr[:, :])
                    if "B" in self.phases:
                        self.S.dma("sp", self.dbg_out["d_x1"][:, :], x1scr[:, :])
                    self.S.barrier()
                    self.S.emit()

    def alloc_h(self, nbuf=2):
        sb = self.sb
        self.xt = [sb("xt%d" % j, [128, D], F32) for j in range(nbuf)]
        if nbuf == 1:
            self.xt = [self.xt[0], self.xt[0]]
        self.xn = sb("xn", [128, D], BF16)
        self.junk = self.xn
        self.ss = sb("ss", [128, 2], F32)
        self.hT = [sb("hT%d" % j, [128, 8, 130], BF16) for j in range(nbuf)]
        if nbuf == 1:
            self.hT = [self.hT[0], self.hT[0]]
        self.nbuf = nbuf
        self.memset("pool", self.hT[0][:, :, 0:2], 0.0, w=["hT0"])
        self.epsc = sb("epsc", [128, 2], F32)
        self.memset("pool", self.epsc[:, 0:1], EPS, w=["epsc"])
        self.memset("pool", self.epsc[:, 1:2], GN_EPS, w=["epsc"])

    def make_h(self, i, src, scale, shift, skeys, carry=True, evac_act=False):
        b = i % self.nbuf
        xt, xk = self.xt[b], "xt%d" % b
        hk = "hT%d" % b
        if src is not None:
            self.ld(xt[:], src, w=[xk])
        self.memset("pool", self.ss[:, 0:1], 0.0, w=["ss"])
        self.act(self.junk[:], xt[:], AF.Square, r=[xk], w=["xn", "ss"], accum=self.ss[:, 0:1])
        self.act(self.ss[:, 1:2], self.ss[:, 0:1], AF.Sqrt, r=["ss", "epsc"], w=["ss"], bias=self.epsc[:, 0:1],
                 scale=1.0 / D)
        self.recip(self.ss[:, 1:2], self.ss[:, 1:2], r=["ss"], w=["ss"])
        self.ts("dve", self.xn[:], xt[:], self.ss[:, 1:2], None, ALU.mult, None, r=[xk, "ss"], w=["xn"])
        HS = int(os.environ.get("HSTOP", "99"))
        if HS <= 1:
            return self.hT[b], hk
        for half in range(2):
            ppb, pk = self.psb(4)
            for j in range(4):
                c = half * 4 + j
                self.tr(ppb[:, j * 128:(j + 1) * 128], self.xn[:, c * 128:(c + 1) * 128], self.identb[:],
                        r=["xn", "identb"], w=pk)
            if HS <= 2:
                continue
            for j in range(4):
                c = half * 4 + j
                if os.environ.get("EV", "") == "cp":
                    self.cp("dve", self.hT[b][:, c, 2:130], ppb[:, j * 128:(j + 1) * 128], r=pk, w=[hk])
                elif os.environ.get("EV", "") == "sb":
                    self.ts("dve", self.hT[b][:, c, 2:130], self.xn[:, 0:128],
                            scale[:, c:c + 1], shift[:, c:c + 1], ALU.mult, ALU.add, r=pk + skeys, w=[hk])
                elif evac_act:
                    self.act(self.hT[b][:, c, 2:130], ppb[:, j * 128:(j + 1) * 128], AF.Identity, r=pk + skeys, w=[hk],
                             bias=shift[:, c:c + 1], scale=scale[:, c:c + 1])
                else:
                    self.ts("dve", self.hT[b][:, c, 2:130], ppb[:, j * 128:(j + 1) * 128],
                            scale[:, c:c + 1], shift[:, c:c + 1], ALU.mult, ALU.add, r=pk + skeys, w=[hk])
        if carry and HS > 3:
            self.cp("pool", self.hT[1 - b][:, :, 1:2], self.hT[b][:, :, 129:130], r=[hk], w=["hT%d" % (1 - b)])
        return self.hT[b], hk

    def load_w_bf16(self, dst, dkey, src_rows, ncols, stg, skeys, kchunks):
        for k in range(kchunks):
            self.S.dma("pool", dst[:, k, :], src_rows(k), w=[dkey + "_k%d" % k])
        self.S.op("pool", lambda e: e.memset(self.scr1[0:1, 0:1], 0.0), r=[dkey + "_k%d" % k for k in range(kchunks)], w=[dkey, "scr1"])

    def phaseA(self, zscr):
        sb, din, S = self.sb, self.din, self.S
        NB = self.nblk
        colp = self.colp
        self.alloc_h()
        self.banks1 = (0, 1, 2, 3, 4, 7)
        wa = sb("wa", [128, 8, 1792], BF16)
        wb = sb("wb", [128, 8, 1792], BF16)
        with ExitStack() as stw:
            mu = sb("mu", [128, 1792], F32, stw)
            self.ld(mu[:], din["rowA"][:, :], w=["mu"])
            stg = [sb("stgA%d" % j, [128, 1792], F32, stw) for j in range(2)]
            tmpb = [sb("tmpbA%d" % j, [128, 1792], F32, stw) for j in range(2)]
            for k in range(8):
                s, sk = stg[k % 2], "stgA%d" % (k % 2)
                t, tk = tmpb[k % 2], "tmpbA%d" % (k % 2)
                self.ld(s[:], din["w_in"][k * 128:(k + 1) * 128, 416:2208], w=[sk])
                self.tt("dve", t[:], s[:], mu[:], ALU.mult, r=[sk, "mu"], w=[tk])
                self.cp("act", wb[:, k, :], t[:], r=[tk], w=["wb"])
                self.tt("dve", wa[:, k, :], s[:], t[:], ALU.subtract, r=[sk, tk], w=["wa"])
            S.barrier()
        wl = sb("wl", [128, 512], F32)
        wg = sb("wg", [128, 512], F32)
        msl = sb("msl", [128, 128], F32)
        mg1 = sb("mg1", [128, 256], F32)
        mg2 = sb("mg2", [128, 256], F32)
        bones = sb("bones", [128, 128], F32)
        for t_, n_ in [(wl, "wl"), (wg, "w_gate_up"), (msl, "msl"), (mg1, "mg1"), (mg2, "mg2"), (bones, "bones")]:
            self.ld(t_[:], din[n_][:, :], w=[n_])
        omka = sb("omka", [128, 4], F32)
        self.ts("dve", omka[:], colp[:, CO["ka"]:CO["ka"] + 4], -1.0, 1.0, ALU.mult, ALU.add, r=["colp"], w=["omka"])
        ident, ones = self.ident, self.ones
        identr = sb("identr", [128, 128], F32R)
        self.cp("dve", identr[:], ident[:], r=["ident"], w=["identr"])
        H = sb("H", [128, 4, 64], F32R)
        self.memset("pool", H[:].bitcast(F32), 0.0, w=["H%d_%d" % (g, a) for g in range(4) for a in range(2)])
        rkv = sb("rkv", [128, 12, 128], F32)
        ta = sb("ta", [128, 128], F32)
        sg = sb("sg", [128, 128], F32)
        G = [sb("G_%d" % j, [128, 4, 128], F32) for j in range(2)]
        BON = [sb("BON_%d" % j, [128, 4, 128], F32) for j in range(2)]
        GC = [sb("GC_%d" % j, [128, 4], F32) for j in range(2)]
        PREP_STEPS = int(os.environ.get("PREP_STEPS", "4"))
        sigG = [sb("sig_g%d" % g_, [128, 128], F32) for g_ in range(2)] * 2
        aaG = [sb("aa_g%d" % g_, [128, 128], F32) for g_ in range(2)] * 2
        kkG = [sb("kk_g%d" % g_, [128, 128], F32) for g_ in range(2)] * 2
        sqG = [sb("sq_g%d" % g_, [128, 128], F32) for g_ in range(2)] * 2
        rnG = [sb("rn_g%d" % g_, [128, 128], F32) for g_ in range(2)] * 2
        kapG = [sb("kap_g%d" % g_, [128, 128], F32) for g_ in range(2)] * 2
        tqG = [sb("tq_g%d" % g_, [128, 128], F32) for g_ in range(2)] * 2
        khG = [sb("kh_g%d" % g_, [128, 128], F32) for g_ in range(2)] * 2
        bbG = [sb("bb_g%d" % g_, [128, 128], F32) for g_ in range(2)] * 2
        rkrG = [sb("rkr_g%d" % g_, [128, 128], F32) for g_ in range(2)] * 2
        LsG = [sb("Ls_g%d" % g_, [128, 128], F32) for g_ in range(2)] * 2
        LxG = [sb("Lx_g%d" % g_, [128, 128], F32) for g_ in range(2)] * 2
        EpG = [sb("Ep_g%d" % g_, [128, 128], F32) for g_ in range(2)] * 2
        EnG = [sb("En_g%d" % g_, [128, 128], F32) for g_ in range(2)] * 2
        ExG = [sb("Ex_g%d" % g_, [128, 128], F32) for g_ in range(2)] * 2
        CM1 = [[sb("CM1_%d_%d" % (j, g), [128, 256], F32R) for g in range(4)] for j in range(2)]
        CM2 = [[sb("CM2_%d_%d" % (j, g), [128, 256], F32R) for g in range(4)] for j in range(2)]
        TM = [[sb("TM%d_%d" % (j, g), [128, 4, 128], F32R) for g in range(4)] for j in range(2)]
        WT = [sb("WT%d" % g, [128, 128], F32R) for g in range(4)]
        PPH = [[sb("PP%d_%d" % (j, h), [128, 256], F32R) for j in range(2)] for h in range(8)]
        PH = [[PPH[h][j][:, 0:128] for j in range(2)] for h in range(8)]
        PTH = [[PPH[h][j][:, 128:256] for j in range(2)] for h in range(8)]
        TTH = [sb("TT_%d" % h, [128, 128], F32R) for h in range(8)]
        AkTH = [sb("AkT_%d" % h, [128, 256], F32R) for h in range(8)]
        nBrH = [sb("nBr_%d" % h, [128, 128], F32R) for h in range(8)]
        X0H = [sb("X0_%d" % h, [128, 64], F32R) for h in range(8)]
        U0H = [sb("U0_%d" % h, [128, 64], F32R) for h in range(8)]
        UsbH = [sb("Usb_%d" % h, [128, 64], F32R) for h in range(8)]
        ysb4 = sb("ysb4", [128, 512], F32)
        cen4 = sb("cen4", [128, 512], F32)
        sq4 = sb("sq4", [128, 512], F32)
        rs4 = sb("rs4", [128, 512], F32)
        zt4 = [sb("zt4_%d" % j, [128, 512], BF16) for j in range(2)]

        for i0_ in range(min(2, NB)):
            self.ld(self.xt[i0_][:], din["xf"][i0_ * 128:(i0_ + 1) * 128, :], w=["xt%d" % i0_])

        def prep_gen(i):
            pb = i % 2
            hT, hk = self.make_h(i, None, self.sc1, self.modc[:, 0:8], ["sc1", "modc"])
            if i + 2 < NB:
                self.ld(self.xt[i % 2][:], din["xf"][(i + 2) * 128:(i + 3) * 128, :], w=["xt%d" % (i % 2)])
            STOP = int(os.environ.get("STOP", "99"))
            if STOP <= 1:
                return
            for c in range(14):
                pp, pk = self.ps(1)
                for k in range(8):
                    self.mm(pp[:, :], wa[:, k, c * 128:(c + 1) * 128], hT[:, k, 2:130], r=["wa", hk], w=pk,
                            start=(k == 0), stop=False)
                    self.mm(pp[:, :], wb[:, k, c * 128:(c + 1) * 128], hT[:, k, 1:129], r=["wb", hk], w=pk,
                            start=False, stop=(k == 7))
                if c < 12:
                    self.cp("act", rkv[:, c, :], pp[:, :], r=pk, w=["rkv%d" % c])
                    yield
                elif c == 12:
                    self.act(ta[0:64, :], pp[0:64, :], AF.Tanh, r=pk, w=["ta"])
                    self.cp("act", ta[64:128, :], pp[64:128, :], r=pk, w=["ta"])
                else:
                    self.act(sg[:, :], pp[:, :], AF.Sigmoid, r=pk, w=["sg"])
                    yield
            if STOP <= 2:
                return
            for g in range(4):
                gp, gk = self.ps(1)
                self.mm(gp[:, :], wg[:, g * 128:(g + 1) * 128], sg[:, :], r=["w_gate_up", "sg"], w=gk)
                self.cp("act", G[pb][:, g, :], gp[:, :], r=gk, w=["G%d_%d" % (pb, g)])
                yield
            if STOP <= 3:
                return
            def prep_g(g):
                rK, kK, vK = "rkv%d" % g, "rkv%d" % (4 + g), "rkv%d" % (8 + g)
                r_, k_, v_ = rkv[:, g, :], rkv[:, 4 + g, :], rkv[:, 8 + g, :]
                up, uk = self.ps(1)
                self.mm(up[:, :], wl[0:64, g * 128:(g + 1) * 128], ta[0:64, :], r=["wl", "ta"], w=uk)
                self.act(sigG[g][:], up[:, :], AF.Sigmoid, r=uk + ["colp"], w=["sig%d" % (g % 2)],
                         bias=colp[:, CO["dbase"] + g:CO["dbase"] + g + 1])
                ap_, ak = self.ps(1)
                self.mm(ap_[:, :], wl[64:128, g * 128:(g + 1) * 128], ta[64:128, :], r=["wl", "ta"], w=ak)
                self.act(aaG[g][:], ap_[:, :], AF.Sigmoid, r=ak + ["colp"], w=["aa%d" % (g % 2)],
                         bias=colp[:, CO["abase"] + g:CO["abase"] + g + 1])
                self.ts("dve", kkG[g][:], k_, colp[:, CO["kk"] + g:CO["kk"] + g + 1], None, ALU.mult, None,
                        r=[kK, "colp"], w=["kk%d" % (g % 2)])
                self.tt("pool", sqG[g][:], kkG[g][:], kkG[g][:], ALU.mult, r=["kk%d" % (g % 2)], w=["sq%d" % (g % 2)])
                yield
                sp_, sk_ = self.ps(1)
                self.mm(sp_[:, :], bones[:], sqG[g][:], r=["bones", "sq%d" % (g % 2)], w=sk_)
                self.act(rnG[g][:], sp_[:, :], AF.Sqrt, r=sk_, w=["rn%d" % (g % 2)])
                yield
                self.ts("dve", rnG[g][:], rnG[g][:], 1e-12, None, ALU.max, None, r=["rn%d" % (g % 2)], w=["rn%d" % (g % 2)])
                self.recip(rnG[g][:], rnG[g][:], r=["rn%d" % (g % 2)], w=["rn%d" % (g % 2)])
                self.tt("dve", kapG[g][:], kkG[g][:], rnG[g][:], ALU.mult, r=["kk%d" % (g % 2), "rn%d" % (g % 2)], w=["kap%d" % (g % 2)])
                yield
                self.ts("pool", tqG[g][:], aaG[g][:], colp[:, CO["ka"] + g:CO["ka"] + g + 1], omka[:, g:g + 1], ALU.mult,
                        ALU.add, r=["aa%d" % (g % 2), "colp", "omka"], w=["tq%d" % (g % 2)])
                self.tt("pool", khG[g][:], k_, tqG[g][:], ALU.mult, r=[kK, "tq%d" % (g % 2)], w=["kh%d" % (g % 2)])
                self.tt("dve", bbG[g][:], kapG[g][:], aaG[g][:], ALU.mult, r=["kap%d" % (g % 2), "aa%d" % (g % 2)], w=["bb%d" % (g % 2)])
                self.stt("dve", rkrG[g][:], r_, colp[:, CO["rk"] + g:CO["rk"] + g + 1], khG[g][:], ALU.mult, ALU.mult,
                         r=[rK, "colp", "kh%d" % (g % 2)], w=["rkr%d" % (g % 2)])
                yield
                bp, bk = self.ps(1)
                self.mm(bp[:, :], bones[:], rkrG[g][:], r=["bones", "rkr%d" % (g % 2)], w=bk)
                self.tt("dve", BON[pb][:, g, :], bp[:, :], v_, ALU.mult, r=bk + [vK], w=["BON%d_%d" % (pb, g)])
                yield
                if STOP <= 4:
                    return
                S.op("dve", lambda e: e.tensor_tensor_scan(out=LsG[g][:], data0=ones[:], data1=sigG[g][:], initial=0.0,
                                                          op0=ALU.mult, op1=ALU.add), r=["ones", "sig%d" % (g % 2)], w=["Ls%d" % (g % 2)])
                yield
                self.tt("pool", LxG[g][:], LsG[g][:], sigG[g][:], ALU.subtract, r=["Ls%d" % (g % 2), "sig%d" % (g % 2)], w=["Lx%d" % (g % 2)])
                self.act(EpG[g][:], LsG[g][:], AF.Exp, r=["Ls%d" % (g % 2)], w=["Ep%d" % (g % 2)], scale=-C0)
                self.act(EnG[g][:], LsG[g][:], AF.Exp, r=["Ls%d" % (g % 2)], w=["En%d" % (g % 2)], scale=C0)
                self.act(ExG[g][:], LxG[g][:], AF.Exp, r=["Lx%d" % (g % 2)], w=["Ex%d" % (g % 2)], scale=-C0)
                yield
                c1, c2 = CM1[pb][g], CM2[pb][g]
                c1k, c2k = "CM1_%d_%d" % (pb, g), "CM2_%d_%d" % (pb, g)
                self.tt("dve", c1[:, 0:128], kapG[g][:], ExG[g][:], ALU.mult, r=["kap%d" % (g % 2), "Ex%d" % (g % 2)], w=[c1k])
                self.tt("pool", c1[:, 128:256], r_, EpG[g][:], ALU.mult, r=[rK, "Ep%d" % (g % 2)], w=[c1k])
                self.tt("dve", c2[:, 0:128], bbG[g][:], EnG[g][:], ALU.mult, r=["bb%d" % (g % 2), "En%d" % (g % 2)], w=[c2k])
                self.tt("pool", c2[:, 128:256], khG[g][:], EnG[g][:], ALU.mult, r=["kh%d" % (g % 2), "En%d" % (g % 2)], w=[c2k])
                self.cp("pool", GC[pb][:, g:g + 1], EpG[g][:, 127:128], r=["Ep%d" % (g % 2)], w=["GC%d_%d" % (pb, g)])
                yield
                if STOP <= 5:
                    return
                tm, tmk = TM[pb][g], "TM%d_%d" % (pb, g)
                yield
                tp, tk_ = self.ps(4)
                self.tr(tp[:, 0:128], c1[:, 0:128].bitcast(F32), ident[:], r=[c1k, "ident"], w=tk_)
                self.tr(tp[:, 128:256], c2[:, 0:128].bitcast(F32), ident[:], r=[c2k, "ident"], w=tk_)
                self.tr(tp[:, 256:384], c2[:, 128:256].bitcast(F32), ident[:], r=[c2k, "ident"], w=tk_)
                self.tr(tp[:, 384:512], v_, ident[:], r=[vK, "ident"], w=tk_)
                self.cp("act", tm[:, 0, :], tp[:, 0:128], r=tk_, w=[tmk])
                self.ts("dve", tm[:, 1, :], tp[:, 128:256], -1.0, None, ALU.mult, None, r=tk_, w=[tmk])
                self.cp("act", tm[:, 2:4, :], tp[:, 256:512].rearrange("p (a b) -> p a b", a=2), r=tk_, w=[tmk])
                yield
                yield

            for g0 in (0, 2):
                subs = [prep_g(g0), prep_g(g0 + 1)]
                while subs:
                    for sg_ in list(subs):
                        try:
                            next(sg_)
                        except StopIteration:
                            subs.remove(sg_)
                    yield
            if STOP <= 6:
                return

            yield

        def head_gen(g, a, i):
            pb = i % 2
            STOP = int(os.environ.get("STOP", "99"))
            hx = 2 * g + a
            sx = "_%d" % hx
            c1, c2, tm = CM1[pb][g], CM2[pb][g], TM[pb][g]
            c1k, c2k, tmk = "CM1_%d_%d" % (pb, g), "CM2_%d_%d" % (pb, g), "TM%d_%d" % (pb, g)
            Pq, PTq, TTq, AkTq, nBrq = PH[hx], PTH[hx], TTH[hx], AkTH[hx], nBrH[hx]
            PPq = PPH[hx]
            X0q, U0q, Usbq = X0H[hx], U0H[hx], UsbH[hx]
            yp, yk = self.pbank[6][:, g * 128:(g + 1) * 128], ["@b6"]
            R = slice(64 * a, 64 * a + 64)
            Cc = slice(64 * a, 64 * a + 64)
            Hk = "H%d_%d" % (g, a)
            BC = (lambda ap_: ap_) if a == 0 else (lambda ap_: ap_.bitcast(F32))
            idm, idk = (identr, "identr") if a == 0 else (ident, "ident")
            a_ps, a_k = self.ps(1)
            self.mm(a_ps[:, :], c1[R, 0:128], c2[R, 0:128], r=[c1k, c2k], w=a_k)
            self.tt("dve", Pq[0][:], a_ps[:, :], msl[:], ALU.mult, r=a_k + ["msl"], w=["PP0" + sx])
            gb, gbk = self.ps(2)
            self.mm(gb[:, :], c2[R, 0:128], c1[R, 0:256], r=[c1k, c2k], w=gbk)
            self.tt("dve", PTq[0][:], gb[:, 0:128], mg1[:, 0:128], ALU.mult, r=gbk + ["mg1"], w=["PP0" + sx])
            self.tt("pool" if False else "dve", nBrq[:], gb[:, 128:256], mg1[:, 128:256], ALU.mult,
                    r=gbk + ["mg1"], w=["nBr" + sx])
            gk2, gkk = self.ps(2)
            self.mm(gk2[:, :], c2[R, 128:256], c1[R, 0:256], r=[c1k, c2k], w=gkk)
            self.tt("dve", AkTq[:], gk2[:, :], mg2[:], ALU.mult, r=gkk + ["mg2"], w=["AkT" + sx])
            self.tt("pool", TTq[:], ident[:], PTq[0][:], ALU.subtract, r=["ident", "PP0" + sx], w=["TT" + sx])
            yield
            if STOP <= 7:
                return
            cur = 0
            for j in range(1, 7):
                nxt = 1 - cur
                if j < 6:
                    pq_ps, pq_k = self.ps(2)
                    self.mm(pq_ps[:, 0:128], PTq[cur][:], Pq[cur][:], r=["PP%d" % cur + sx], w=pq_k)
                    self.mm(pq_ps[:, 128:256], Pq[cur][:], PTq[cur][:], r=["PP%d" % cur + sx], w=pq_k)
                    self.cp("act", PPq[nxt][:], pq_ps[:, :], r=pq_k, w=["PP%d" % nxt + sx])
                else:
                    p_ps, p_k = self.ps(1)
                    self.mm(p_ps[:, :], PTq[cur][:], Pq[cur][:], r=["PP%d" % cur + sx], w=p_k)
                    self.cp("act", Pq[nxt][:], p_ps[:, :], r=p_k, w=["PP%d" % nxt + sx])
                yield
                t_ps, t_k = self.ps(1)
                self.mm(t_ps[:, :], Pq[nxt][:], TTq[:], r=["PP%d" % nxt + sx, "TT" + sx], w=t_k)
                self.tt("dve", TTq[:], TTq[:], t_ps[:, :], ALU.add, r=["TT" + sx] + t_k, w=["TT" + sx])
                yield
                cur = nxt
            if STOP <= 8:
                return
            w_ps, w_k = self.ps(1)
            self.mm(w_ps[R, :], BC(tm[:, 0, Cc]), BC(TTq[:]), r=[tmk, "TT" + sx], w=w_k)
            self.cp("act", WT[g][R, :], w_ps[R, :], r=w_k, w=["WT%d_%d" % (g, a)])
            yield
            x_ps, x_k = self.ps(1)
            self.mm(x_ps[:, 0:64], AkTq[:, 0:128], tm[:, 3, Cc], r=["AkT" + sx, tmk], w=x_k)
            self.cp("act", X0q[:], x_ps[:, 0:64], r=x_k, w=["X0" + sx])
            yield
            u0_ps, u0_k = self.ps(1)
            self.mm(u0_ps[:, 0:64], TTq[:], X0q[:], r=["TT" + sx, "X0" + sx], w=u0_k)
            self.cp("act", U0q[:], u0_ps[:, 0:64], r=u0_k, w=["U0" + sx])
            yield
            if STOP <= 9:
                return
            u_ps, u_k = self.ps(1)
            self.mm(u_ps[:, 0:64], WT[g][R, :], H[R, g, :], r=["WT%d_%d" % (g, a), Hk], w=u_k, start=True, stop=False)
            self.mm(u_ps[:, 0:64], identr[:], U0q[:], r=["identr", "U0" + sx], w=u_k, start=False, stop=True)
            self.cp("act", Usbq[:], u_ps[:, 0:64], r=u_k, w=["Usb" + sx])
            yield
            self.mm(yp[R, :], BC(H[R, g, :]), BC(c1[R, 128:256]), r=[Hk, c1k], w=yk, start=True, stop=False)
            self.mm(yp[R, :], BC(Usbq[:]), BC(nBrq[:]), r=["Usb" + sx, "nBr" + sx], w=yk, start=False, stop=False)
            self.mm(yp[R, :], BC(tm[:, 3, Cc]), BC(AkTq[:, 128:256]), r=[tmk, "AkT" + sx], w=yk, start=False, stop=True)
            h_ps, h_k = self.ps(1)
            self.mm(h_ps[R, 0:64], BC(tm[:, 2, Cc]), BC(tm[:, 3, Cc]), r=[tmk], w=h_k, start=True, stop=False)
            self.mm(h_ps[R, 0:64], idm[R, R], BC(H[R, g, :]), r=[idk, Hk], w=h_k, start=False, stop=False)
            self.mm(h_ps[R, 0:64], BC(tm[:, 1, Cc]), BC(Usbq[:]), r=[tmk, "Usb" + sx], w=h_k, start=False, stop=True)
            self.ts("dve", H[R, g, :], h_ps[R, 0:64], GC[pb][R, g:g + 1], None, ALU.mult, None,
                    r=h_k + ["GC%d_%d" % (pb, g)], w=[Hk])
            yield


        def gn(i):
            pb = i % 2
            yp, yk = self.pbank[6][:, 0:512], ["@b6"]
            gk_all = ["G%d_%d" % (pb, g) for g in range(4)]
            bk_all = ["BON%d_%d" % (pb, g) for g in range(4)]
            lnw = colp[:, CO["lnw"]:CO["lnw"] + 4].unsqueeze(2).to_broadcast([128, 4, 128])
            lnb = colp[:, CO["lnb"]:CO["lnb"] + 4].unsqueeze(2).to_broadcast([128, 4, 128])
            v3 = lambda t: t[:].rearrange("p (g t) -> p g t", g=4)
            self.cp("act", ysb4[:], yp[:, :], r=yk, w=["ysb4"])
            m_ps, m_k = self.ps(4)
            self.mm(m_ps[:, :], bones[:], ysb4[:], r=["bones", "ysb4"], w=m_k)
            self.stt("dve", cen4[:], m_ps[:, :], -1.0 / 64, ysb4[:], ALU.mult, ALU.add, r=m_k + ["ysb4"], w=["cen4"])
            self.tt("pool", sq4[:], cen4[:], cen4[:], ALU.mult, r=["cen4"], w=["sq4"])
            v_ps, v_k = self.ps(4)
            self.mm(v_ps[:, :], bones[:], sq4[:], r=["bones", "sq4"], w=v_k)
            self.act(rs4[:], v_ps[:, :], AF.Sqrt, r=v_k + ["epsc"], w=["rs4"], bias=self.epsc[:, 1:2], scale=1.0 / 64)
            self.recip(rs4[:], rs4[:], r=["rs4"], w=["rs4"])
            self.tt("dve", cen4[:], cen4[:], rs4[:], ALU.mult, r=["cen4", "rs4"], w=["cen4"])
            self.tt("pool", v3(cen4), v3(cen4), lnw, ALU.mult, r=["cen4", "colp"], w=["cen4"])
            self.tt("pool", v3(cen4), v3(cen4), lnb, ALU.add, r=["cen4", "colp"], w=["cen4"])
            self.tt("dve", v3(cen4), v3(cen4), BON[pb][:], ALU.add, r=["cen4"] + bk_all, w=["cen4"])
            zb, zk = zt4[i % 2], "zt4_%d" % (i % 2)
            self.tt("dve", v3(zb), v3(cen4), G[pb][:], ALU.mult, r=["cen4"] + gk_all, w=[zk])
            self.ld(zscr[:, i * 128:(i + 1) * 128].rearrange("(g p) t -> p g t", p=128), v3(zb), r=[zk], w=[])

        def drain(gen):
            for _ in gen:
                pass

        drain(prep_gen(0))
        for i in range(NB):
            if int(os.environ.get("STOP", "99")) <= 6:
                if i + 1 < NB:
                    drain(prep_gen(i + 1))
                continue
            heads = [head_gen(g, a, i) for g in range(4) for a in range(2)]
            nxt = prep_gen(i + 1) if i + 1 < NB else None
            while heads or nxt is not None:
                hl = list(heads)
                marks = set(int((k_ + 1) * max(len(hl), 1) / PREP_STEPS) - 1 for k_ in range(PREP_STEPS))
                for hi_, gen in enumerate(hl):
                    try:
                        next(gen)
                    except StopIteration:
                        heads.remove(gen)
                    if nxt is not None and hi_ in marks:
                        try:
                            next(nxt)
                        except StopIteration:
                            nxt = None
                if not hl and nxt is not None:
                    try:
                        next(nxt)
                    except StopIteration:
                        nxt = None
            gn(i)

    def phaseB(self, zscr, x1scr):
        sb, din, S = self.sb, self.din, self.S
        self.banks1 = (0, 1, 2, 3, 4)
        NB = self.nblk
        ident, identb = self.ident, self.identb
        rowB = sb("rowB", [128, 592], F32)
        self.ld(rowB[:], din["rowB"][:, 0:592], w=["rowB"])
        posc = sb("posc", [128, 32], I32)
        posf = sb("posf", [128, 32], F32)
        self.ld(posc[:], din["posc"][:, :], w=["posc"])
        self.cp("dve", posf[:], posc[:], r=["posc"], w=["posf"])
        caus = sb("caus", [128, 128], F32)
        causb = sb("causb", [128, 128], BF16)
        self.ld(caus[:], din["caus"][:, :], w=["caus"])
        self.cp("dve", causb[:], caus[:], r=["caus"], w=["causb"])
        with ExitStack() as stw:
            bg = sb("bg1", [128, 1024], F32, stw)
            self.ld(bg[:], din["rowB"][:, 592:1616], w=["bg1"])
            self.tt("dve", self.gtb[:, 0:1024], self.gtb[:, 0:1024], bg[:], ALU.add, r=["gtb", "bg1"], w=["gtb"])
            S.barrier()
        KnT = sb("KnT", [128, 4, NB * 128], BF16)
        KpT = sb("KpT", [128, NB * 128], BF16)
        Vx = sb("Vx", [128, NB, 8, 65], BF16)
        self.memset("pool", Vx[:, :, :, 64:65], 1.0, w=["Vx"])
        NS = 4
        with ExitStack() as stp1:
            self.st = stp1
            wkv1 = sb("wkv1", [128, 8, 160], BF16)
            wukv1 = sb("wukv1", [128, 1, 1024], BF16)
            self.load_w_bf16(wkv1, "wkv1", lambda k: din["w_in"][k * 128:(k + 1) * 128, 256:416], 160, None, None, 8)
            self.load_w_bf16(wukv1, "wukv1", lambda k: din["w_ukv"][:, :], 1024, None, None, 1)
            eps1 = sb("eps1", [128, 1], F32)
            self.memset("pool", eps1[:], EPS, w=["eps1"])

            def make_slot(sx):
                K = lambda nm: nm + sx
                xt = sb("p1xt" + sx, [128, D], F32)
                xn = sb("p1xn" + sx, [128, D], BF16)
                ss = sb("p1ss" + sx, [128, 2], F32)
                hT = sb("p1hT" + sx, [128, 8, 130], BF16)
                kva = sb("p1kva" + sx, [128, 160], F32)
                kvs = sb("p1kvs" + sx, [128, 1024], F32)
                st2 = sb("p1st2" + sx, [128, 40], F32)
                ckv = sb("p1ckv" + sx, [128, 128], BF16)
                ckvT = sb("p1ckvT" + sx, [128, 128], BF16)
                sqb = sb("p1sqb" + sx, [128, 768], F32)
                knn = sb("p1knn" + sx, [128, 8, 64], BF16)
                kr = sb("p1kr" + sx, [128, 32], F32)
                kr2 = sb("p1kr2" + sx, [128, 128], BF16)
                cs = sb("p1cs" + sx, [128, 32], F32)
                uu = sb("p1uu" + sx, [128, 32], F32)
                ui = sb("p1ui" + sx, [128, 32], I32)
                uf = sb("p1uf" + sx, [128, 32], F32)
                um = sb("p1um" + sx, [128, 32], F32)
                t16 = [sb("p1t16_%d" % j + sx, [128, 8, 16], F32) for j in range(4)]

                def rms_rows(dst_rstd, src_sq_ap, n, rkeys, extra_scale=1.0):
                    self.act(dst_rstd, src_sq_ap, AF.Sqrt, r=rkeys + ["eps1"], w=[K("st2")], bias=eps1[:, 0:1], scale=1.0 / n)
                    self.recip(dst_rstd, dst_rstd, r=[K("st2")], w=[K("st2")])
                    if extra_scale != 1.0:
                        self.ts("dve", dst_rstd, dst_rstd, extra_scale, None, ALU.mult, None, r=[K("st2")], w=[K("st2")])

                def rope(dst, src, nh, rk, wk):
                    cosb = cs[:, 0:16].unsqueeze(1).to_broadcast([128, nh, 16])
                    sinb = cs[:, 16:32].unsqueeze(1).to_broadcast([128, nh, 16])
                    a, b, c, d = [t[:, 0:nh, :] for t in t16]
                    x1_, x2_ = src[:, :, 0:16], src[:, :, 16:32]
                    self.tt("dve", a, x1_, cosb, ALU.mult, r=rk + [K("cs")], w=[K("t16a")])
                    self.tt("pool", b, x2_, sinb, ALU.mult, r=rk + [K("cs")], w=[K("t16b")])
                    self.tt("dve", c, x1_, sinb, ALU.mult, r=rk + [K("cs")], w=[K("t16c")])
                    self.tt("pool", d, x2_, cosb, ALU.mult, r=rk + [K("cs")], w=[K("t16d")])
                    self.tt("dve", dst[:, :, 0:16], a, b, ALU.subtract, r=[K("t16a"), K("t16b")], w=wk)
                    self.tt("dve", dst[:, :, 16:32], c, d, ALU.add, r=[K("t16c"), K("t16d")], w=wk)

                def cossin(pcol):
                    self.ts("dve", uu[:, 0:16], rowB[:, RB["invf"]:RB["invf"] + 16], pcol, None, ALU.mult, None,
                            r=["rowB", "posf"], w=[K("uu")])
                    self.ts("dve", uu[:, 16:32], uu[:, 0:16], 0.0, None, ALU.add, None, r=[K("uu")], w=[K("uu")])
                    self.ts("dve", uu[:, 0:16], uu[:, 0:16], 0.25, None, ALU.add, None, r=[K("uu")], w=[K("uu")])
                    self.cp("dve", ui[:], uu[:], r=[K("uu")], w=[K("ui")])
                    self.cp("dve", uf[:], ui[:], r=[K("ui")], w=[K("uf")])
                    self.tt("dve", uf[:], uu[:], uf[:], ALU.subtract, r=[K("uu"), K("uf")], w=[K("uf")])
                    self.ts("dve", um[:], uf[:], 0.5, None, ALU.is_gt, None, r=[K("uf")], w=[K("um")])
                    self.tt("dve", uf[:], uf[:], um[:], ALU.subtract, r=[K("uf"), K("um")], w=[K("uf")])
                    self.act(cs[:], uf[:], AF.Sin, r=[K("uf")], w=[K("cs")], scale=float(2 * np.pi))


                def gen(i):
                    hk = K("hT0")
                    self.ld(xt[:], din["xf"][i * 128:(i + 1) * 128, :], w=[K("xt0")])
                    self.memset("pool", ss[:, 0:1], 0.0, w=[K("ss")])
                    self.act(xn[:], xt[:], AF.Square, r=[K("xt0")], w=[K("xn"), K("ss")], accum=ss[:, 0:1])
                    yield
                    self.act(ss[:, 1:2], ss[:, 0:1], AF.Sqrt, r=[K("ss"), "eps1"], w=[K("ss")], bias=eps1[:, 0:1], scale=1.0 / D)
                    self.recip(ss[:, 1:2], ss[:, 1:2], r=[K("ss")], w=[K("ss")])
                    self.ts("dve", xn[:], xt[:], ss[:, 1:2], None, ALU.mult, None, r=[K("xt0"), K("ss")], w=[K("xn")])
                    yield
                    for half in range(2):
                        ppb, pk = self.psb(4)
                        for j in range(4):
                            c = half * 4 + j
                            self.tr(ppb[:, j * 128:(j + 1) * 128], xn[:, c * 128:(c + 1) * 128], identb[:], r=[K("xn"), "identb"], w=pk)
                        for j in range(4):
                            c = half * 4 + j
                            self.ts("dve", hT[:, c, 2:130], ppb[:, j * 128:(j + 1) * 128], self.sc1[:, c:c + 1],
                                    self.modc[:, c:c + 1], ALU.mult, ALU.add, r=pk + ["sc1", "modc"], w=[hk])
                        yield
                    cossin(posf[:, i:i + 1])
                    yield
                    kp, kk_ = self.ps(2)
                    for k in range(8):
                        self.mm(kp[:, 0:160], hT[:, k, 2:130], wkv1[:, k, 0:160], r=[hk, "wkv1"], w=kk_, start=(k == 0), stop=(k == 7))
                    self.cp("act", kva[:], kp[:, 0:160], r=kk_, w=[K("kva")])
                    yield
                    self.memset("pool", st2[:, 0:8], 0.0, w=[K("st2")])
                    self.act(sqb[:, 0:128], kva[:, 0:128], AF.Square, r=[K("kva")], w=[K("sqb"), K("st2")], accum=st2[:, 0:1])
                    self.act(sqb[:, 128:160], kva[:, 128:160], AF.Square, r=[K("kva")], w=[K("sqb"), K("st2")], accum=st2[:, 1:2])
                    rms_rows(st2[:, 2:3], st2[:, 0:1], 128, [K("st2")])
                    rms_rows(st2[:, 3:4], st2[:, 1:2], 32, [K("st2")])
                    yield
                    self.stt("dve", ckv[:], kva[:, 0:128], st2[:, 2:3], rowB[:, RB["kvan"]:RB["kvan"] + 128], ALU.mult, ALU.mult,
                             r=[K("kva"), K("st2"), "rowB"], w=[K("ckv")])
                    self.stt("dve", kr[:], kva[:, 128:160], st2[:, 3:4], rowB[:, RB["kn"] + 64:RB["kn"] + 96], ALU.mult,
                             ALU.mult, r=[K("kva"), K("st2"), "rowB"], w=[K("kr")])
                    rope(kr2[:, 0:32].rearrange("p (h d) -> p h d", h=1), kr[:].rearrange("p (h d) -> p h d", h=1), 1,
                         [K("kr")], [K("kr2")])
                    self.cp("pool", kr2[:, 32:128].rearrange("p (c d) -> p c d", c=3), kr2[:, 0:32].unsqueeze(1).to_broadcast([128, 3, 32]), r=[K("kr2")], w=[K("kr2")])
                    yield
                    tpb, tk_ = self.psb(2)
                    self.tr(tpb[:, 0:128], ckv[:], identb[:], r=[K("ckv"), "identb"], w=tk_)
                    self.tr(tpb[:, 128:256], kr2[:], identb[:], r=[K("kr2"), "identb"], w=tk_)
                    self.cp("act", ckvT[:], tpb[:, 0:128], r=tk_, w=[K("ckvT")])
                    self.cp("act", KpT[:, i * 128:(i + 1) * 128], tpb[:, 128:256], r=tk_, w=["KpT"])
                    yield
                    for hf in range(2):
                        vp, vk = self.ps(4)
                        self.mm(vp[:, :], ckvT[:], wukv1[:, 0, hf * 512:(hf + 1) * 512], r=[K("ckvT"), "wukv1"], w=vk)
                        self.cp("act", kvs[:, hf * 512:(hf + 1) * 512], vp[:, :], r=vk, w=[K("kvs")])
                        yield
                    kv3 = kvs[:].rearrange("p (h d) -> p h d", h=8)
                    self.cp("pool", Vx[:, i, :, 0:64], kv3[:, :, 64:128], r=[K("kvs")], w=["Vx"])
                    yield
                    self.tt("dve", sqb[:, 0:512].rearrange("p (h d) -> p h d", h=8), kv3[:, :, 0:64], kv3[:, :, 0:64], ALU.mult,
                            r=[K("kvs")], w=[K("sqb")])
                    S.op("dve", lambda e: e.tensor_reduce(out=st2[:, 8:16], in_=sqb[:, 0:512].rearrange("p (h d) -> p h d", h=8),
                                                          axis=AX.X, op=ALU.add), r=[K("sqb")], w=[K("st2")])
                    rms_rows(st2[:, 16:24], st2[:, 8:16], 64, [K("st2")])
                    yield
                    self.tt("dve", sqb[:, 0:512].rearrange("p (h d) -> p h d", h=8), kv3[:, :, 0:64],
                            st2[:, 16:24].unsqueeze(2).to_broadcast([128, 8, 64]), ALU.mult, r=[K("kvs"), K("st2")], w=[K("sqb")])
                    self.tt("pool", knn[:], sqb[:, 0:512].rearrange("p (h d) -> p h d", h=8),
                            rowB[:, RB["kn"]:RB["kn"] + 64].unsqueeze(1).to_broadcast([128, 8, 64]), ALU.mult,
                            r=[K("sqb"), "rowB"], w=[K("knn")])
                    tpb, tk_ = self.psb(4)
                    for g in range(4):
                        self.tr(tpb[:, g * 128:(g + 1) * 128], knn[:, 2 * g:2 * g + 2, :].rearrange("p h d -> p (h d)"),
                                identb[:], r=[K("knn"), "identb"], w=tk_)
                    self.cp("act", KnT[:, :, i * 128:(i + 1) * 128], tpb[:, 0:512].rearrange("p (g t) -> p g t", g=4), r=tk_,
                            w=["KnT"])

                    yield
                return gen

            slots = [make_slot("_s%d" % j) for j in range(NS)]
            active = []
            nxt_i = 0
            while nxt_i < NB or active:
                while len(active) < NS and nxt_i < NB:
                    active.append(slots[nxt_i % NS](nxt_i))
                    nxt_i += 1
                    break
                for gen in list(active):
                    try:
                        next(gen)
                    except StopIteration:
                        active.remove(gen)
            S.barrier()
        self.st = self._phase_st
        self.alloc_h(1)
        wi = sb("wi", [128, 8, 2464], BF16)
        wuq = sb("wuq", [128, 2, 768], BF16)
        wukv = sb("wukv", [128, 1, 1024], BF16)
        womla = sb("womla", [128, 4, 1024], BF16)
        worw = sb("worw", [128, 4, 1024], BF16)
        wout = sb("wout", [128, 8, 1024], BF16)
        with ExitStack() as stw:
            stg = [sb("stgB%d" % j, [128, 2048], F32, stw) for j in range(2)]
            sk = ["stgB0", "stgB1"]
            self.load_w_bf16(wi[:, :, 0:416], "wi", lambda k: din["w_in"][k * 128:(k + 1) * 128, 0:416], 416, stg, sk, 8)
            self.load_w_bf16(wi[:, :, 416:2464], "wi", lambda k: din["w_in"][k * 128:(k + 1) * 128, 2208:4256], 2048,
                             stg, sk, 8)
            self.load_w_bf16(wuq, "wuq", lambda k: din["w_uq"][k * 128:(k + 1) * 128, :], 768, stg, sk, 2)
            self.load_w_bf16(wukv, "wukv", lambda k: din["w_ukv"][:, :], 1024, stg, sk, 1)
            self.load_w_bf16(womla, "womla", lambda k: din["w_o_mla"][k * 128:(k + 1) * 128, :], 1024, stg, sk, 4)
            self.load_w_bf16(worw, "worw", lambda k: din["w_o_rwkv"][k * 128:(k + 1) * 128, :], 1024, stg, sk, 4)
            self.load_w_bf16(wout, "wout", lambda k: din["w_out"][k * 128:(k + 1) * 128, :], 1024, stg, sk, 8)
            S.barrier()
        kva = sb("kva", [128, 160], F32)
        kvs = sb("kvs", [128, 1024], F32)
        st2 = sb("st2", [128, 40], F32)
        ckv = sb("ckv", [128, 128], BF16)
        ckvT = sb("ckvT", [128, 128], BF16)
        sqb = sb("sqb", [128, 768], F32)
        knn = sb("knn", [128, 8, 64], BF16)
        kr = sb("kr", [128, 32], F32)
        kr2 = sb("kr2", [128, 128], BF16)
        cs = sb("cs", [128, 32], F32)
        uu = sb("uu", [128, 32], F32)
        ui = sb("ui", [128, 32], I32)
        uf = sb("uf", [128, 32], F32)
        um = sb("um", [128, 32], F32)
        t16 = [sb("t16_%d" % j, [128, 8, 16], F32) for j in range(4)]
        qa = sb("qa", [128, 256], F32)
        cq = sb("cq", [128, 256], BF16)
        cqT = sb("cqT", [128, 2, 128], BF16)
        qf = sb("qf", [128, 768], F32)
        qn = sb("qn", [128, 8, 64], BF16)
        qpf = sb("qpf", [128, 8, 32], F32)
        qp = sb("qp", [128, 8, 32], BF16)
        QnB = sb("QnB", [128, 4, 256], BF16)
        QpB = sb("QpB", [128, 4, 256], BF16)
        self.memset("pool", QnB[:], 0.0, w=["QnB"])
        self.memset("pool", QpB[:], 0.0, w=["QpB"])
        PTt = [sb("PTt%d" % j, [128, 512], BF16) for j in range(3)]
        orc = sb("orc", [128, 8], F32)
        OA = sb("OA", [128, 8, 64], BF16)
        OT = sb("OT", [128, 4, 128], BF16)
        ZT = sb("ZT", [128, 4, 128], BF16)
        gsa = sb("gsa", [128, 128], F32)
        gsb = sb("gsb", [128, 128], F32)
        xtmp = sb("xtmp", [128, 512], F32)
        g1 = sb("g1", [128, 128], F32)
        g2 = sb("g2", [128, 128], F32)
        GTt = sb("GTt", [128, 8, 128], BF16)
        ATT_SCALE = 96.0 ** -0.5
        Ops = self.pbank[6:8]

        def rms_rows(dst_rstd, src_sq_ap, n, rkeys, extra_scale=1.0):
            self.act(dst_rstd, src_sq_ap, AF.Sqrt, r=rkeys + ["epsc"], w=["st2"], bias=self.epsc[:, 0:1], scale=1.0 / n)
            self.recip(dst_rstd, dst_rstd, r=["st2"], w=["st2"])
            if extra_scale != 1.0:
                self.ts("dve", dst_rstd, dst_rstd, extra_scale, None, ALU.mult, None, r=["st2"], w=["st2"])

        def rope(dst, src, nh, rk, wk):
            cosb = cs[:, 0:16].unsqueeze(1).to_broadcast([128, nh, 16])
            sinb = cs[:, 16:32].unsqueeze(1).to_broadcast([128, nh, 16])
            a, b, c, d = [t[:, 0:nh, :] for t in t16]
            x1_, x2_ = src[:, :, 0:16], src[:, :, 16:32]
            self.tt("dve", a, x1_, cosb, ALU.mult, r=rk + ["cs"], w=["t16a"])
            self.tt("pool", b, x2_, sinb, ALU.mult, r=rk + ["cs"], w=["t16b"])
            self.tt("dve", c, x1_, sinb, ALU.mult, r=rk + ["cs"], w=["t16c"])
            self.tt("pool", d, x2_, cosb, ALU.mult, r=rk + ["cs"], w=["t16d"])
            self.tt("dve", dst[:, :, 0:16], a, b, ALU.subtract, r=["t16a", "t16b"], w=wk)
            self.tt("dve", dst[:, :, 16:32], c, d, ALU.add, r=["t16c", "t16d"], w=wk)

        def cossin(pcol):
            self.ts("dve", uu[:, 0:16], rowB[:, RB["invf"]:RB["invf"] + 16], pcol, None, ALU.mult, None,
                    r=["rowB", "posf"], w=["uu"])
            self.ts("dve", uu[:, 16:32], uu[:, 0:16], 0.0, None, ALU.add, None, r=["uu"], w=["uu"])
            self.ts("dve", uu[:, 0:16], uu[:, 0:16], 0.25, None, ALU.add, None, r=["uu"], w=["uu"])
            self.cp("dve", ui[:], uu[:], r=["uu"], w=["ui"])
            self.cp("dve", uf[:], ui[:], r=["ui"], w=["uf"])
            self.tt("dve", uf[:], uu[:], uf[:], ALU.subtract, r=["uu", "uf"], w=["uf"])
            self.ts("dve", um[:], uf[:], 0.5, None, ALU.is_gt, None, r=["uf"], w=["um"])
            self.tt("dve", uf[:], uf[:], um[:], ALU.subtract, r=["uf", "um"], w=["uf"])
            self.act(cs[:], uf[:], AF.Sin, r=["uf"], w=["cs"], scale=float(2 * np.pi))

        NBh = NB // 2
        selm = sb("selmB", [128, 2], F32)
        self.ld(selm[:], din["selm"][:, :], w=["selmB"])
        MA = sb("MA", [128, 128], BF16)
        MB = sb("MB", [128, 128], BF16)
        negb = sb("negb", [128, 1], F32)
        self.ts("dve", MA[:], caus[:], selm[:, 0:1], selm[:, 1:2], ALU.mult, ALU.add, r=["caus", "selmB"], w=["MA"])
        self.ts("dve", MB[:], caus[:], selm[:, 1:2], None, ALU.mult, None, r=["caus", "selmB"], w=["MB"])
        self.ts("dve", negb[:], selm[:, 0:1], -10000.0, None, ALU.mult, None, r=["selmB"], w=["negb"])
        poso = sb("poso", [128, 16], I32)
        posfo = sb("posfo", [128, 16], F32)
        self.ld(poso[:], din["poso"][:, :], w=["poso"])
        self.cp("dve", posfo[:], poso[:], r=["poso"], w=["posfo"])
        ZTa = sb("ZTa", [128, 4, 128], BF16)
        ZTb = sb("ZTb", [128, 4, 128], BF16)

        def kb_groups(j):
            groups = []
            lo = list(range(0, 2 * j))
            for q in range(0, len(lo), 2):
                groups.append((lo[q:q + 2], 0))
            groups.append(([2 * j], 1))
            groups.append(([2 * j + 1], 3))
            return groups

        for i in range(NBh):
            hT, hk = self.make_h(i, din["xown"][i * 128:(i + 1) * 128, :], self.sc1, self.modc[:, 0:8],
                                 ["sc1", "modc"], carry=False)
            self.ld(ZTa[:], zscr[:, (2 * i) * 128:(2 * i + 1) * 128].rearrange("(g p) t -> p g t", p=128), w=["ZTa"])
            self.ld(ZTb[:], zscr[:, (2 * i + 1) * 128:(2 * i + 2) * 128].rearrange("(g p) t -> p g t", p=128), w=["ZTb"])
            xt, xk = self.xt[0], "xt0"
            cossin(posfo[:, i:i + 1])
            qp_, qk_ = self.ps(2)
            for k in range(8):
                self.mm(qp_[:, 0:256], hT[:, k, 2:130], wi[:, k, 0:256], r=[hk, "wi"], w=qk_, start=(k == 0), stop=(k == 7))
            self.cp("act", qa[:], qp_[:, 0:256], r=qk_, w=["qa"])
            self.act(sqb[:, 0:256], qa[:], AF.Square, r=["qa"], w=["sqb", "st2"], accum=st2[:, 4:5])
            rms_rows(st2[:, 5:6], st2[:, 4:5], 256, ["st2"])
            self.stt("dve", cq[:], qa[:], st2[:, 5:6], rowB[:, RB["qan"]:RB["qan"] + 256], ALU.mult, ALU.mult,
                     r=["qa", "st2", "rowB"], w=["cq"])
            tpb, tk_ = self.psb(2)
            for c in range(2):
                self.tr(tpb[:, c * 128:(c + 1) * 128], cq[:, c * 128:(c + 1) * 128], identb[:], r=["cq", "identb"], w=tk_)
            self.cp("act", cqT[:], tpb[:, 0:256].rearrange("p (c t) -> p c t", c=2), r=tk_, w=["cqT"])
            for hf in range(2):
                qq, qqk = self.ps(4)
                for c in range(2):
                    self.mm(qq[:, 0:384], cqT[:, c, :], wuq[:, c, hf * 384:(hf + 1) * 384], r=["cqT", "wuq"], w=qqk,
                            start=(c == 0), stop=(c == 1))
                self.cp("act", qf[:, hf * 384:(hf + 1) * 384], qq[:, 0:384], r=qqk, w=["qf"])
            q3 = qf[:].rearrange("p (h d) -> p h d", h=8)
            self.tt("dve", sqb[:].rearrange("p (h d) -> p h d", h=8), q3, q3, ALU.mult, r=["qf"], w=["sqb"])
            s3 = sqb[:].rearrange("p (h d) -> p h d", h=8)
            S.op("dve", lambda e: e.tensor_reduce(out=st2[:, 24:32], in_=s3[:, :, 0:64], axis=AX.X, op=ALU.add),
                 r=["sqb"], w=["st2"])
            S.op("dve", lambda e: e.tensor_reduce(out=st2[:, 32:40], in_=s3[:, :, 64:96], axis=AX.X, op=ALU.add),
                 r=["sqb"], w=["st2"])
            rms_rows(st2[:, 8:16], st2[:, 24:32], 64, ["st2"], ATT_SCALE)
            rms_rows(st2[:, 16:24], st2[:, 32:40], 32, ["st2"], ATT_SCALE)
            self.tt("dve", s3[:, :, 0:64], q3[:, :, 0:64], st2[:, 8:16].unsqueeze(2).to_broadcast([128, 8, 64]),
                    ALU.mult, r=["qf", "st2"], w=["sqb"])
            self.tt("pool", qn[:], s3[:, :, 0:64], rowB[:, RB["qn"]:RB["qn"] + 64].unsqueeze(1).to_broadcast([128, 8, 64]),
                    ALU.mult, r=["sqb", "rowB"], w=["qn"])
            self.tt("dve", s3[:, :, 64:96], q3[:, :, 64:96], st2[:, 16:24].unsqueeze(2).to_broadcast([128, 8, 32]),
                    ALU.mult, r=["qf", "st2"], w=["sqb"])
            self.tt("pool", qpf[:], s3[:, :, 64:96],
                    rowB[:, RB["qn"] + 64:RB["qn"] + 96].unsqueeze(1).to_broadcast([128, 8, 32]), ALU.mult,
                    r=["sqb", "rowB"], w=["qpf"])
            rope(qp[:], qpf[:], 8, ["qpf"], ["qp"])
            tpb, tk_ = self.psb(4)
            for g in range(4):
                self.tr(tpb[:, g * 128:(g + 1) * 128], qn[:, 2 * g:2 * g + 2, :].rearrange("p h d -> p (h d)"),
                        identb[:], r=["qn", "identb"], w=tk_)
            self.cp("act", QnB[0:64, :, 0:128], tpb[0:64, 0:512].rearrange("p (g t) -> p g t", g=4), r=tk_, w=["QnB"])
            self.cp("act", QnB[64:128, :, 128:256], tpb[64:128, 0:512].rearrange("p (g t) -> p g t", g=4), r=tk_, w=["QnB"])
            tpb, tk_ = self.psb(4)
            for g in range(4):
                self.tr(tpb[0:64, g * 128:(g + 1) * 128], qp[:, 2 * g:2 * g + 2, :].rearrange("p h d -> p (h d)"),
                        identb[:], r=["qp", "identb"], w=tk_)
            self.cp("act", QpB[0:32, :, 0:128], tpb[0:32, 0:512].rearrange("p (g t) -> p g t", g=4), r=tk_, w=["QpB"])
            self.cp("act", QpB[32:64, :, 128:256], tpb[32:64, 0:512].rearrange("p (g t) -> p g t", g=4), r=tk_, w=["QpB"])
            last_kb = 2 * i + 1
            work = [(g, kbs, mode) for g in range(4) for (kbs, mode) in kb_groups(i)]
            firsts = [True] * 8

            def stage1(n):
                g, kbs, mode = work[n]
                nb = len(kbs)
                sp_, sk_ = self.ps(4)
                for j, kb in enumerate(kbs):
                    self.mm(sp_[:, j * 256:(j + 1) * 256], KnT[:, g, kb * 128:(kb + 1) * 128], QnB[:, g, :],
                            r=["KnT", "QnB"], w=sk_, start=True, stop=False)
                    self.mm(sp_[:, j * 256:(j + 1) * 256], KpT[:, kb * 128:(kb + 1) * 128], QpB[:, g, :],
                            r=["KpT", "QpB"], w=sk_, start=False, stop=True)
                pt, ptk = PTt[n % 3], "PTt%d" % (n % 3)
                if mode == 2:
                    self.act(pt[:, 0:nb * 256], sp_[:, 0:nb * 256], AF.Exp, r=sk_ + ["negb"], w=[ptk], bias=negb[:, 0:1])
                else:
                    self.act(pt[:, 0:nb * 256], sp_[:, 0:nb * 256], AF.Exp, r=sk_, w=[ptk])
                if mode in (1, 3):
                    mt, mk_ = (MA, "MA") if mode == 1 else (MB, "MB")
                    self.tt("pool", pt[:, 0:256].rearrange("p (a q) -> p a q", a=2), pt[:, 0:256].rearrange("p (a q) -> p a q", a=2),
                            mt[:].unsqueeze(1).to_broadcast([128, 2, 128]), ALU.mult, r=[ptk, mk_], w=[ptk])

            def stage2(n):
                g, kbs, mode = work[n]
                pt, ptk = PTt[n % 3], "PTt%d" % (n % 3)
                for j, kb in enumerate(kbs):
                    for a in range(2):
                        h = 2 * g + a
                        ob = Ops[a]
                        oc = g * 65
                        self.mm(ob[:, oc:oc + 65], pt[:, j * 256 + a * 128:j * 256 + (a + 1) * 128], Vx[:, kb, h, :],
                                r=[ptk, "Vx"], w=["@b%d" % (6 + a)], start=firsts[h], stop=(kb == last_kb))
                        firsts[h] = False

            for n in range(len(work) + 2):
                if n < len(work):
                    stage1(n)
                if n >= 2:
                    stage2(n - 2)
            OA4 = OA[:].rearrange("p (g a) d -> p g a d", a=2)
            for bk in range(2):
                o3 = Ops[bk][:, 0:260].rearrange("p (h d) -> p h d", h=4)
                self.recip(orc[:, bk * 4:(bk + 1) * 4], o3[:, :, 64], r=["@b%d" % (6 + bk)], w=["orc"])
                self.tt("dve", OA4[:, :, bk, :], o3[:, :, 0:64],
                        orc[:, bk * 4:(bk + 1) * 4].unsqueeze(2).to_broadcast([128, 4, 64]), ALU.mult,
                        r=["@b%d" % (6 + bk), "orc"], w=["OA"])
            tpb, tk_ = self.psb(4)
            for g in range(4):
                self.tr(tpb[:, g * 128:(g + 1) * 128], OA[:, 2 * g:2 * g + 2, :].rearrange("p h d -> p (h d)"),
                        identb[:], r=["OA", "identb"], w=tk_)
            self.cp("act", OT[:], tpb[:, 0:512].rearrange("p (g t) -> p g t", g=4), r=tk_, w=["OT"])
            self.ts("dve", ZT[:], ZTa[:], selm[:, 0:1], None, ALU.mult, None, r=["ZTa", "selmB"], w=["ZT"])
            self.stt("dve", ZT[:], ZTb[:], selm[:, 1:2], ZT[:], ALU.mult, ALU.add, r=["ZTb", "selmB", "ZT"], w=["ZT"])
            for m in range(8):
                for (mm_, gs_, gsk) in ((m, gsa, "gsa"), (8 + m, gsb, "gsb")):
                    gp, gk = self.ps(1)
                    for k in range(8):
                        self.mm(gp[:, :], wi[:, k, 416 + mm_ * 128:416 + (mm_ + 1) * 128], hT[:, k, 2:130], r=["wi", hk],
                                w=gk, start=(k == 0), stop=(k == 7))
                    self.act(gs_[:], gp[:, :], AF.Sigmoid, r=gk, w=[gsk])
                p1, k1 = self.ps(1)
                for k in range(4):
                    self.mm(p1[:, :], womla[:, k, m * 128:(m + 1) * 128], OT[:, k, :], r=["womla", "OT"], w=k1,
                            start=(k == 0), stop=(k == 3))
                self.tt("dve", g1[:], p1[:, :], gsa[:], ALU.mult, r=k1 + ["gsa"], w=["g1"])
                p2, k2 = self.ps(1)
                for k in range(4):
                    self.mm(p2[:, :], worw[:, k, m * 128:(m + 1) * 128], ZT[:, k, :], r=["worw", "ZT"], w=k2,
                            start=(k == 0), stop=(k == 3))
                self.tt("dve", g2[:], p2[:, :], gsb[:], ALU.mult, r=k2 + ["gsb"], w=["g2"])
                self.tt("pool", GTt[:, m, :], g1[:], g2[:], ALU.add, r=["g1", "g2"], w=["GTt"])
            for hf in range(2):
                mp, mk = self.ps(4)
                for m in range(8):
                    self.mm(mp[:, :], GTt[:, m, :], wout[:, m, hf * 512:(hf + 1) * 512], r=["GTt", "wout"], w=mk,
                            start=(m == 0), stop=(m == 7))
                self.tt("dve", xtmp[:], mp[:, :], self.gtb[:, hf * 512:(hf + 1) * 512], ALU.mult, r=mk + ["gtb"],
                        w=["xtmp"])
                self.tt("pool", xt[:, hf * 512:(hf + 1) * 512], xt[:, hf * 512:(hf + 1) * 512], xtmp[:], ALU.add,
                        r=[xk, "xtmp"], w=[xk])
            self.ld(x1scr[i * 128:(i + 1) * 128, :], xt[:], r=[xk], w=[])

    def phaseC(self, x1scr, gscr, out):
        sb, din, S = self.sb, self.din, self.S
        NT = int(os.environ.get("NT", "16"))
        TOK = NT * 128
        from_x = "B" not in self.phases
        src = din["xf"] if from_x else x1scr
        ident = self.ident
        selm = sb("selm", [128, 2], F32)
        self.ld(selm[:], din["selm"][:, :], w=["selm"])
        with ExitStack() as stw:
            rowC = sb("rowC", [128, 1024], F32, stw)
            self.ld(rowC[:], din["rowC"][:, :], w=["rowC"])
            self.tt("dve", self.gtb[:, 1024:2048], self.gtb[:, 1024:2048], rowC[:], ALU.add, r=["gtb", "rowC"], w=["gtb"])
            S.barrier()
        H2 = sb("H2", [128, 8, TOK], BF16)
        xa = None

        def load_sel(tt, dst, dkey):
            self.ld(dst[:], src[tt * 128:(tt + 1) * 128, :], w=[dkey])
            if not from_x:
                return
            self.ld(xa[:], src[2048 + tt * 128:2048 + (tt + 1) * 128, :], w=["xa"])
            self.ts("dve", dst[:], dst[:], selm[:, 0:1], None, ALU.mult, None, r=[dkey, "selm"], w=[dkey])
            self.stt("dve", dst[:], xa[:], selm[:, 1:2], dst[:], ALU.mult, ALU.add, r=["xa", "selm", dkey], w=[dkey])

        self.banks1 = (0, 1, 2, 3, 4, 6, 7)
        with ExitStack() as st1:
            self.st = st1
            self.alloc_h(2)
            if from_x:
                xa = sb("xa", [128, D], F32)
            wq = sb("wq", [128, 8, 2048], BF16)
            skT = sb("skT", [128, 2, 128], BF16)
            iota = sb("iota", [128, 128], F32)
            self.ld(iota[:], din["iota"][:, :], w=["iota"])
            with ExitStack() as stw:
                stg = [sb("stgC%d" % j, [128, 2048], F32, stw) for j in range(2)]
                sk = ["stgC0", "stgC1"]
                self.load_w_bf16(wq, "wq", lambda k: din["w_query"][k * 128:(k + 1) * 128, :], 2048, stg, sk, 8)
                self.ld(stg[0][:, 0:128], din["sk1T"][:, :], w=[sk[0]])
                self.ld(stg[1][:, 0:128], din["sk2T"][:, :], w=[sk[1]])
                self.cp("dve", skT[:, 0, :], stg[0][:, 0:128], r=[sk[0]], w=["skT"])
                self.cp("dve", skT[:, 1, :], stg[1][:, 0:128], r=[sk[1]], w=["skT"])
                S.barrier()
            qTs = [sb("qT%d" % j, [128, 16, 128], BF16) for j in range(2)]
            scs = [sb("sc%d" % j, [128, 16, 128], F32) for j in range(2)]
            scrA = sb("scrA", [128, 16, 128], F32)
            scrB = sb("scrB", [128, 8, 256], F32)
            V16 = sb("V16", [128, 16, 16], F32)
            I16u = sb("I16u", [128, 16, 16], U32)
            I16 = sb("I16", [128, 16, 16], F32)
            Asel = sb("Asel", [128, 128], F32)
            Bsel = sb("Bsel", [128, 128], F32)
            Wsel = sb("Wsel", [128, 128], F32)
            ABW = sb("ABW", [128, 3, 128], F32)
            Aoh = [sb("Aoh%d" % j, [128, 128], BF16) for j in range(2)]
            Boh = [sb("Boh%d" % j, [128, 128], BF16) for j in range(2)]
            GT = sb("GT", [128, 128, 128], BF16)
            cand8 = sb("cand8", [128, 8, 256], F32)
            tv8 = sb("tv8", [128, 8, 16], F32)
            posu8 = sb("posu8", [128, 8, 16], U32)
            ju = sb("ju", [128, 2, 8, 16], U32)
            jf = sb("jf", [128, 2, 8, 16], F32)
            E8 = sb("E8", [128, 2048], F32)
            e8 = sb("e8", [128, 8, 16], F32)
            s8 = sb("s8", [128, 16], F32)
            AohB = [sb("AohB%d" % j, [128, 16, 128], BF16) for j in range(2)]
            BohB = [sb("BohB%d" % j, [128, 16, 128], BF16) for j in range(2)]
            def front(tt):
                b = tt % 2
                hT, hk = self.make_h(tt, None, self.sc2, self.modc[:, 16:24], ["sc2", "modc"], carry=False, evac_act=True)
                if tt + 2 < NT:
                    load_sel(tt + 2, self.xt[b], "xt%d" % b)
                self.cp("pool", H2[:, :, tt * 128:(tt + 1) * 128], hT[:, :, 2:130], r=[hk], w=["H2"])
                qT, qTk = qTs[b], "qT%d" % b
                sc, sck = scs[b], "sc%d" % b
                for m in range(16):
                    qp_, qk_ = self.ps(1)
                    for k in range(8):
                        self.mm(qp_[:, :], wq[:, k, m * 128:(m + 1) * 128], hT[:, k, 2:130], r=["wq", hk], w=qk_,
                                start=(k == 0), stop=(k == 7))
                    self.cp("act", qT[:, m, :], qp_[:, :], r=qk_, w=[qTk])
                for m4 in range(4):
                    sp_, sk_ = self.ps(4)
                    for j in range(4):
                        m = m4 * 4 + j
                        self.mm(sp_[:, j * 128:(j + 1) * 128], qT[:, m, :], skT[:, m % 2, :], r=[qTk, "skT"], w=sk_)
                    self.cp("act", sc[:, m4 * 4:(m4 + 1) * 4, :], sp_[:, :].rearrange("p (a b) -> p a b", a=4), r=sk_,
                            w=[sck])

            load_sel(0, self.xt[0], "xt0")
            if NT > 1:
                load_sel(1, self.xt[1], "xt1")
            front(0)
            for tt in range(NT):
                if tt + 1 < NT:
                    front(tt + 1)
                sc, sck = scs[tt % 2], "sc%d" % (tt % 2)

                def top16_multi(items):
                    for (vo, io, sa, sc_, tg, ks) in items:
                        S.op("dve", lambda e, vo=vo, sa=sa: e.max(out=vo[:, 0:8], in_=sa), r=ks, w=["tkv" + tg])
                    for (vo, io, sa, sc_, tg, ks) in items:
                        S.op("dve", lambda e, vo=vo, io=io, sa=sa: e.max_index(out=io[:, 0:8], in_max=vo[:, 0:8], in_values=sa),
                             r=ks + ["tkv" + tg], w=["tki" + tg])
                    for (vo, io, sa, sc_, tg, ks) in items:
                        S.op("dve", lambda e, vo=vo, sa=sa, sc_=sc_: e.match_replace(out=sc_, in_to_replace=vo[:, 0:8],
                                                                                 in_values=sa, imm_value=-1e30),
                             r=ks + ["tkv" + tg], w=["tks" + tg])
                    for (vo, io, sa, sc_, tg, ks) in items:
                        S.op("dve", lambda e, vo=vo, sc_=sc_: e.max(out=vo[:, 8:16], in_=sc_), r=["tks" + tg], w=["tkv" + tg])
                    for (vo, io, sa, sc_, tg, ks) in items:
                        S.op("dve", lambda e, vo=vo, io=io, sc_=sc_: e.max_index(out=io[:, 8:16], in_max=vo[:, 8:16], in_values=sc_),
                             r=["tks" + tg, "tkv" + tg], w=["tki" + tg])

                top16_multi([(V16[:, m, :], I16u[:, m, :], sc[:, m, :], scrA[:, m, :], "a%d" % m, [sck]) for m in range(16)])
                kA_v = ["tkva%d" % m for m in range(16)]
                kA_i = ["tkia%d" % m for m in range(16)]
                self.cp("pool", I16[:], I16u[:], r=kA_i, w=["I16"])
                V4 = V16[:].rearrange("p (h two) k -> p h two k", two=2)
                I4 = I16[:].rearrange("p (h two) k -> p h two k", two=2)
                self.tt("dve", cand8[:].rearrange("p h (a b) -> p h a b", a=16),
                        V4[:, :, 0, :].unsqueeze(3).to_broadcast([128, 8, 16, 16]),
                        V4[:, :, 1, :].unsqueeze(2).to_broadcast([128, 8, 16, 16]), ALU.add, r=kA_v, w=["cand8"])
                top16_multi([(tv8[:, h, :], posu8[:, h, :], cand8[:, h, :], scrB[:, h, :], "b%d" % h, ["cand8"]) for h in range(8)])
                kB_v = ["tkvb%d" % h for h in range(8)]
                kB_i = ["tkib%d" % h for h in range(8)]
                S.op("dve", lambda e: e.tensor_single_scalar(out=ju[:, 0], in_=posu8[:], scalar=4, op=ALU.logical_shift_right),
                     r=kB_i, w=["ju"])
                S.op("dve", lambda e: e.tensor_single_scalar(out=ju[:, 1], in_=posu8[:], scalar=15, op=ALU.bitwise_and),
                     r=kB_i, w=["ju"])
                self.cp("dve", jf[:], ju[:], r=["ju"], w=["jf"])
                io16 = iota[:, 0:16].unsqueeze(1).unsqueeze(1).to_broadcast([128, 8, 16, 16])
                for w_, dst in ((0, Asel), (1, Bsel)):
                    e4 = E8[:].rearrange("p (h k j) -> p h k j", h=8, k=16)
                    self.tt("dve", e4, io16, jf[:, w_].unsqueeze(3).to_broadcast([128, 8, 16, 16]), ALU.is_equal,
                            r=["iota", "jf"], w=["E8"])
                    self.tt("pool", e4, e4, I4[:, :, w_, :].unsqueeze(2).to_broadcast([128, 8, 16, 16]), ALU.mult,
                            r=["E8", "I16"], w=["E8"])
                    S.op("dve", lambda e, dst=dst, e4=e4: e.tensor_reduce(out=dst[:].rearrange("p (h k) -> p h k", h=8), in_=e4,
                                                                        axis=AX.X, op=ALU.add), r=["E8"], w=[dst.name if False else ("Asel" if dst is Asel else "Bsel")])
                self.tt("dve", e8[:], tv8[:], tv8[:, :, 0:1].to_broadcast([128, 8, 16]), ALU.subtract, r=kB_v, w=["e8"])
                self.act(e8[:], e8[:], AF.Exp, r=["e8"], w=["e8"])
                S.op("dve", lambda e: e.tensor_reduce(out=s8[:, 0:8], in_=e8[:], axis=AX.X, op=ALU.add), r=["e8"], w=["s8"])
                self.recip(s8[:, 8:16], s8[:, 0:8], r=["s8"], w=["s8"])
                self.tt("dve", Wsel[:].rearrange("p (h k) -> p h k", h=8), e8[:], s8[:, 8:16].unsqueeze(2).to_broadcast([128, 8, 16]),
                        ALU.mult, r=["e8", "s8"], w=["Wsel"])
                tp, tk_ = self.ps(4)
                self.tr(tp[:, 0:128], Asel[:], ident[:], r=["Asel", "ident"], w=tk_)
                self.tr(tp[:, 128:256], Bsel[:], ident[:], r=["Bsel", "ident"], w=tk_)
                self.tr(tp[:, 256:384], Wsel[:], ident[:], r=["Wsel", "ident"], w=tk_)
                self.cp("act", ABW[:], tp[:, 0:384].rearrange("p (a b) -> p a b", a=3), r=tk_, w=["ABW"])
                iob = iota[:].unsqueeze(1).to_broadcast([128, 16, 128])
                for tb in range(8):
                    t0 = tb * 16
                    ao, aok = AohB[tb % 2], "AohB%d" % (tb % 2)
                    bo, bok = BohB[tb % 2], "BohB%d" % (tb % 2)
                    self.tt("dve", ao[:], iob, ABW[:, 0, t0:t0 + 16].unsqueeze(2).to_broadcast([128, 16, 128]), ALU.is_equal,
                            r=["iota", "ABW"], w=[aok])
                    self.tt("dve", bo[:], iob, ABW[:, 1, t0:t0 + 16].unsqueeze(2).to_broadcast([128, 16, 128]), ALU.is_equal,
                            r=["iota", "ABW"], w=[bok])
                    self.tt("pool", bo[:], bo[:], ABW[:, 2, t0:t0 + 16].unsqueeze(2).to_broadcast([128, 16, 128]), ALU.mult,
                            r=[bok, "ABW"], w=[bok])
                    for q4 in range(4):
                        gp, gk = self.ps(4)
                        for j in range(4):
                            tl = q4 * 4 + j
                            self.mm(gp[:, j * 128:(j + 1) * 128], ao[:, tl, :], bo[:, tl, :], r=[aok, bok], w=gk)
                        tg = t0 + q4 * 4
                        self.cp("act", GT[:, :, tg:tg + 4], gp[:, :].rearrange("p (t i) -> p i t", t=4), r=gk, w=["GT"])
                self.ld(gscr[tt, :, :], GT[:].rearrange("p a b -> p (a b)"), r=["GT"], w=["gscr"])
            S.barrier()
        with ExitStack() as st2:
            self.st = st2
            if from_x:
                xa = sb("xa2", [128, D], F32)
            ACC = sb("ACC", [128, NT, D], F32)
            xtCs = [sb("xtC%d" % j, [128, D], F32) for j in range(2)]
            DTbs = [sb("DTb%d" % j, [128, 4, 8, 128], BF16) for j in range(2)]
            UPbs = [sb("UPb%d" % j, [128, 4, D], BF16) for j in range(2)]
            GSl = sb("GSl", [128, NT, 512], BF16)
            Wts = [sb("Wt%d" % j, [128, 4, TOK], BF16) for j in range(2)]
            g_us = [sb("g_u%d" % j, [128, 512], F32) for j in range(2)]
            g_ss = [sb("g_s%d" % j, [128, 512], F32) for j in range(2)]
            gctr = [0]
            eupv = din["eup"].rearrange("(a b) d -> a b d", b=128)
            NCH = (TOK + 511) // 512
            chunks = [(l, tc) for tc in range(NCH) for l in range(4)]
            SQ = float(np.sqrt(0.044715))
            bctr = {"d": 0, "u": 0}

            def bank_of(kind):
                banks = (0, 1, 2, 3) if kind == "d" else (4, 6, 7)
                b = banks[bctr[kind] % len(banks)]
                bctr[kind] += 1
                return self.pbank[b][:, :], ["@b%d" % b]

            def pf_load(g, p):
                if p < 4:
                    self.S.dma("pool", DTbs[g % 2][:, p, :, :], din["edT"][g * 4 + p].rearrange("(k p) i -> p k i", p=128),
                               w=["DTb%d_%d" % (g % 2, p)])
                else:
                    u2 = p - 4
                    self.S.dma("pool", UPbs[g % 2][:, u2, :], eupv[:, g * 4 + u2, :], w=["UPb%d_%d" % (g % 2, u2)])

            def pf_cast(g, p):
                pass

            def load_gsl(g):
                for tt in range(NT):
                    self.ld(GSl[:, tt, :], gscr[tt, :, g * 512:(g + 1) * 512], r=["gscr"], w=["GSl%d" % tt])

            pend = []

            def down_chunk(g, ci):
                l, tc = chunks[ci]
                DTb, dk = DTbs[g % 2], "DTb%d" % (g % 2)
                Wt, wk = Wts[g % 2], "Wt%d" % (g % 2)
                n = min(512, TOK - tc * 512)
                ap_, ak = bank_of("d")
                for k in range(8):
                    self.mm(ap_[:, 0:n], DTb[:, l, k, :], H2[:, k, tc * 512:tc * 512 + n], r=[dk + "_%d" % l, "H2"], w=ak,
                            start=(k == 0), stop=(k == 7))
                gb_ = gctr[0] % 2
                gctr[0] += 1
                g_u, g_s, guk, gsk = g_us[gb_], g_ss[gb_], "g_u%d" % gb_, "g_s%d" % gb_
                self.act(g_u[:, 0:n], ap_[:, 0:n], AF.Square, r=ak, w=[guk], scale=SQ)
                self.stt("dve", g_u[:, 0:n], g_u[:, 0:n], 1.0, ap_[:, 0:n], ALU.add, ALU.mult, r=ak + [guk], w=[guk])
                self.act(g_s[:, 0:n], g_u[:, 0:n], AF.Sigmoid, r=[guk], w=[gsk], scale=1.5957691216057308)
                pend.append((l, tc, n, ap_, ak, g_s, gsk, Wt, wk))

            def down_finish():
                if not pend:
                    return
                l, tc, n, ap_, ak, g_s, gsk, Wt, wk = pend.pop(0)
                gsl = GSl[:, tc * 4:tc * 4 + (n // 128), l * 128:(l + 1) * 128]
                self.tt("pool", g_s[:, 0:n].rearrange("p (a b) -> p a b", b=128), g_s[:, 0:n].rearrange("p (a b) -> p a b", b=128),
                        gsl, ALU.mult, r=[gsk] + ["GSl%d" % t_ for t_ in range(tc * 4, tc * 4 + (n // 128))], w=[gsk])
                self.tt("dve", Wt[:, l, tc * 512:tc * 512 + n], ap_[:, 0:n], g_s[:, 0:n], ALU.mult, r=ak + [gsk], w=[wk])

            def up_tile(g, tt):
                UPb, uk_ = UPbs[g % 2], "UPb%d" % (g % 2)
                Wt, wk = Wts[g % 2], "Wt%d" % (g % 2)
                for hf in range(2):
                    op_, ok_ = bank_of("u")
                    for l in range(4):
                        self.mm(op_[:, :], Wt[:, l, tt * 128:(tt + 1) * 128], UPb[:, l, hf * 512:(hf + 1) * 512],
                                r=[wk, uk_ + "_%d" % l], w=ok_, start=(l == 0), stop=(l == 3))
                    if g == 0:
                        self.cp("dve", ACC[:, tt, hf * 512:(hf + 1) * 512], op_[:, :], r=ok_, w=["ACC%d" % tt])
                    else:
                        self.tt("dve", ACC[:, tt, hf * 512:(hf + 1) * 512], ACC[:, tt, hf * 512:(hf + 1) * 512], op_[:, :],
                                ALU.add, r=ok_ + ["ACC%d" % tt], w=["ACC%d" % tt])

            for p in range(8):
                pf_load(0, p)
                pf_cast(0, p)
            for p in range(4):
                pf_load(1, p)
                pf_cast(1, p)
            load_gsl(0)
            for ci in range(len(chunks)):
                down_chunk(0, ci)
                down_finish()
            for g in range(32):
                if g + 1 < 32:
                    load_gsl(g + 1)
                pieces = []
                if g + 1 < 32:
                    pieces += [(g + 1, p) for p in range(4, 8)]
                if g + 2 < 32:
                    pieces += [(g + 2, p) for p in range(4)]
                nsteps = max(len(chunks), NT, 2 * len(pieces))
                for s_ in range(nsteps):
                    if s_ % 2 == 0 and s_ // 2 < len(pieces):
                        pf_load(*pieces[s_ // 2])
                    if s_ % 2 == 1 and s_ // 2 < len(pieces):
                        pf_cast(*pieces[s_ // 2])
                    if g + 1 < 32 and s_ < len(chunks):
                        down_chunk(g + 1, s_)
                    if s_ < NT:
                        up_tile(g, s_)
                    if len(pend) > 1 or s_ == nsteps - 1:
                        down_finish()
                while pend:
                    down_finish()
            for tt in range(NT):
                b = 0
                xt, xk = xtCs[tt % 2], "xtC%d" % (tt % 2)
                load_sel(tt, xt, xk)
                self.tt("pool", ACC[:, tt, :], ACC[:, tt, :], self.gtb[:, 1024:2048], ALU.mult, r=["ACC%d" % tt, "gtb"],
                        w=["ACC%d" % tt])
                self.tt("dve", xt[:], xt[:], ACC[:, tt, :], ALU.add, r=[xk, "ACC%d" % tt], w=[xk])
                self.ld(out[tt * 128:(tt + 1) * 128, :], xt[:], r=[xk], w=[])


def _col(v):
    v = np.asarray(v, np.float32).reshape(-1)
    return np.ascontiguousarray(v.reshape(-1, 128).T)


def _rep(v):
    v = np.asarray(v, np.float32).reshape(-1)
    return np.ascontiguousarray(np.broadcast_to(v[None, :], (128, v.size)))


def prep(inp, nblk=32):
    f = lambda a: np.ascontiguousarray(np.asarray(a, np.float32))
    x = f(inp["x"])
    c = f(inp["c"])
    pos = np.asarray(inp["positions"], np.int32)
    b_ada = f(inp["b_ada"])[0]
    t = np.arange(128)
    msl = (t[None, :] < t[:, None]).astype(np.float32)
    msu = (t[None, :] > t[:, None]).astype(np.float32)
    mui = (t[None, :] >= t[:, None]).astype(np.float32)
    bones = np.kron(np.eye(2, dtype=np.float32), np.ones((64, 64), np.float32))
    half = 16
    invf = (10000.0 ** (-np.arange(half, dtype=np.float32) / half)).astype(np.float32)
    shared = dict(
        rowA=_rep(inp["mu_shift"]),
        ident=np.eye(128, dtype=np.float32), msl=msl, mg1=np.concatenate([msu, -mui], 1),
        mg2=np.concatenate([msu, mui], 1), bones=bones, caus=mui.copy(),
        iota=_rep(np.arange(128, dtype=np.float32)),
        w_ada=f(inp["w_ada"])[0], w_in=f(inp["w_in"])[0], w_uq=f(inp["w_uq"])[0], w_ukv=f(inp["w_ukv"])[0],
        w_o_mla=f(inp["w_o_mla"])[0],
        wl=np.ascontiguousarray(np.concatenate([f(inp["w_decay_up"])[0], f(inp["w_aaa_up"])[0]], 0)),
        w_gate_up=f(inp["w_gate_up"])[0], w_o_rwkv=f(inp["w_o_rwkv"])[0], w_out=f(inp["w_out"])[0],
        w_query=f(inp["w_query"])[0],
        sk1T=np.ascontiguousarray(f(inp["sub_keys1"])[0].T), sk2T=np.ascontiguousarray(f(inp["sub_keys2"])[0].T),
        edT=np.ascontiguousarray(f(inp["expert_down"])[0].reshape(128, 128, D).transpose(1, 2, 0)),
        eup=f(inp["expert_up"])[0],
        rowC=_rep(b_ada[5 * D:6 * D]),
    )
    maps = []
    for core in range(8):
        b, hf = core // 2, core % 2
        colp = np.concatenate([
            _col(c[b]), _col(b_ada[0:D]), _col(b_ada[D:2 * D]), _col(b_ada[3 * D:4 * D]), _col(b_ada[4 * D:5 * D]),
            _col(inp["norm_mix"]), _col(inp["norm_ffn"]), _col(inp["decay_base"]), _col(inp["aaa_base"]),
            _col(inp["k_k"]), _col(inp["k_a"]), _col(inp["r_k"]), _col(inp["ln_x_w"]), _col(inp["ln_x_b"])], 1)
        rowB = np.concatenate([_rep(inp["q_a_norm"]), _rep(inp["kv_a_norm"]), _rep(inp["q_norm"]), _rep(inp["k_norm"]),
                               _rep(invf / np.float32(2 * np.pi)), _rep(b_ada[2 * D:3 * D])], 1)
        selm = np.zeros((128, 2), np.float32)
        selm[:, hf] = 1.0
        m = dict(shared)
        nh = (nblk // 2) * 128
        xo = np.zeros((2048, D), np.float32)
        own = np.concatenate([np.arange((2 * j + hf) * 128, (2 * j + hf + 1) * 128) for j in range(nblk // 2)])
        xo[:nh] = x[b, own]
        po = np.zeros((2048,), np.int32)
        po[:nh] = pos[b, own]
        m.update(xown=xo, poso=np.ascontiguousarray(po.reshape(16, 128).T), xf=x[b], colp=np.ascontiguousarray(colp), rowB=np.ascontiguousarray(rowB),
                 posc=np.ascontiguousarray(pos[b].reshape(32, 128).T), selm=selm)
        maps.append(m)
    return maps


_CACHE = {}


def kernel(**inputs):
    if "k" not in _CACHE:
        _CACHE["k"] = K()
    kb = _CACHE["k"]
    maps = prep(inputs)
    res = run_bass_kernel_spmd(kb.nc, maps, core_ids=list(range(8)))
    outp = np.zeros((4, SEQ, D), np.float32)
    for core in range(8):
        b, hf = core // 2, core % 2
        own = np.concatenate([np.arange((2 * j + hf) * 128, (2 * j + hf + 1) * 128) for j in range(16)])
        outp[b, own] = res.results[core]["out"]
    return outp
```

```python
import os
import numpy as np
from contextlib import ExitStack
import concourse.bass as bass
import concourse.mybir as mybir
from concourse.bass_utils import run_bass_kernel_spmd

F32 = mybir.dt.float32
BF16 = mybir.dt.bfloat16
I32 = mybir.dt.int32
U32 = mybir.dt.uint32
F32R = mybir.dt.float32r if os.environ.get("NO_F32R") is None else mybir.dt.float32
AF = mybir.ActivationFunctionType
ALU = mybir.AluOpType
AX = mybir.AxisListType

ENGS = ("pe", "act", "dve", "pool", "sp")
EPOCH = 20000
NDMA = 24
D = 1024
SEQ = 4096
C0 = float(np.exp(-0.5))
EPS = 1e-6
GN_EPS = 64e-5


class Sched:
    def __init__(self, nc, stack):
        self.nc = nc
        self.stack = stack
        self.ops = {e: [] for e in ENGS}
        self.cnt = {e: 0 for e in ENGS}
        self.sems = {}
        self.lastw = {}
        self.readers = {}
        self.seen = {e: {} for e in ENGS}
        self.dma_i = 0
        self.dma_cnt = [0] * NDMA

    def sem(self, key):
        if key not in self.sems:
            self.sems[key] = self.stack.enter_context(self.nc.semaphore("s_%s_%s" % key))
        return self.sems[key]

    def _tok(self, eng):
        c = self.cnt[eng]
        return ((eng, c // EPOCH), c % EPOCH + 1)

    def _deps(self, r, w):
        toks = []
        for b in list(r) + list(w):
            t = self.lastw.get(b)
            if t is not None:
                toks.append(t)
        for b in w:
            toks.extend(self.readers.get(b, ()))
        return toks

    def _filter(self, eng, toks):
        need = {}
        for (k, v) in toks:
            if eng == "pe" and k[0] == "pe":
                continue
            if self.seen[eng].get(k, 0) >= v:
                continue
            if need.get(k, 0) < v:
                need[k] = v
        for k, v in need.items():
            self.seen[eng][k] = v
        return [(self.sem(k), v) for k, v in need.items()]

    def _commit(self, tok, r, w):
        for b in w:
            self.lastw[b] = tok
            self.readers[b] = []
        for b in r:
            self.readers.setdefault(b, []).append(tok)

    def op(self, eng, fn, r=(), w=()):
        ex = [k for k in r if k.startswith("@")]
        if ex:
            w = list(w) + ex
            r = [k for k in r if not k.startswith("@")]
        waits = self._filter(eng, self._deps(r, w))
        tok = self._tok(eng)
        self.cnt[eng] += 1
        self.ops[eng].append((fn, waits, (self.sem(tok[0]), 1)))
        self._commit(tok, r, w)
        return tok

    def dma(self, eng, out, in_, r=(), w=()):
        k = self.dma_i % NDMA
        self.dma_i += 1
        toks = self._deps(r, w)
        if self.dma_cnt[k] > 0:
            toks.append((("dma", k), 16 * self.dma_cnt[k]))
        waits = self._filter(eng, toks)
        self.dma_cnt[k] += 1
        tok = (("dma", k), 16 * self.dma_cnt[k])
        fn = lambda e, out=out, in_=in_: e.dma_start(out=out, in_=in_)
        self.ops[eng].append((fn, waits, (self.sem(tok[0]), 16)))
        self._commit(tok, r, w)
        return tok

    def barrier(self):
        toks = []
        for e in ENGS:
            if self.cnt[e] > 0:
                c = self.cnt[e] - 1
                toks.append(((e, c // EPOCH), c % EPOCH + 1))
        for k in range(NDMA):
            if self.dma_cnt[k] > 0:
                toks.append((("dma", k), 16 * self.dma_cnt[k]))
        for e in ENGS:
            waits = [(self.sem(k), v) for (k, v) in toks if self.seen[e].get(k, 0) < v]
            for (k, v) in toks:
                self.seen[e][k] = max(self.seen[e].get(k, 0), v)
            if waits:
                self.ops[e].append((None, waits, None))
        self.lastw.clear()
        self.readers.clear()

    def emit(self):
        nc = self.nc
        ops = self.ops

        def run(e, lst):
            for (fn, waits, inc) in lst:
                for (s, v) in waits:
                    e.wait_ge(s, v)
                if fn is not None:
                    fn(e).then_inc(inc[0], inc[1])

        with nc.Block() as block:
            @block.sync
            def _(e):
                run(e, ops["sp"])

            @block.tensor
            def _(e):
                run(e, ops["pe"])

            @block.scalar
            def _(e):
                run(e, ops["act"])

            @block.vector
            def _(e):
                run(e, ops["dve"])

            @block.gpsimd
            def _(e):
                run(e, ops["pool"])
        self.ops = {e: [] for e in ENGS}


CO = dict(c=0, bsh1=8, bsc1=16, bsh2=24, bsc2=32, nmix=40, nffn=48, dbase=56, abase=60, kk=64, ka=68,
          rk=72, lnw=76, lnb=80)
NCOL = 84
RB = dict(qan=0, kvan=256, qn=384, kn=480, invf=576, bgt1=592)
NRB = 592 + 1024


class K:
    def __init__(self, nblk=32, dbg=False, phases="ABC"):
        self.nblk = nblk
        self.dbg = dbg
        self.phases = phases
        self.nc = bass.Bass("TRN2", target_bir_lowering=False)
        self.build()

    def dram(self, name, shape, dt, kind="ExternalInput"):
        return self.nc.dram_tensor(name, list(shape), dt, kind=kind).ap()

    def sb(self, name, shape, dt, st=None):
        self._uid = getattr(self, "_uid", 0) + 1
        return (st or self.st).enter_context(self.nc.sbuf_tensor("sb%d_%s" % (self._uid, name), list(shape), dt))

    def mm(self, out, lhsT, rhs, r, w, start=True, stop=True):
        self.S.op("pe", lambda e: e.matmul(out, lhsT=lhsT, rhs=rhs, start=start, stop=stop), r=r, w=w)

    def tr(self, out, in_, ident, r, w):
        self.S.op("pe", lambda e: e.transpose(out, in_, ident), r=r, w=w)

    def act(self, out, in_, func, r, w, bias=None, scale=None, accum=None, eng="act"):
        kw = {}
        if bias is not None:
            kw["bias"] = bias
        if scale is not None:
            kw["scale"] = scale
        if accum is not None:
            kw["accum_out"] = accum
        self.S.op("act", lambda e: e.activation(out=out, in_=in_, func=func, **kw), r=r, w=w)

    def tt(self, eng, out, in0, in1, op, r, w):
        self.S.op(eng, lambda e: e.tensor_tensor(out=out, in0=in0, in1=in1, op=op), r=r, w=w)

    def ts(self, eng, out, in0, s1, s2, op0, op1, r, w, accum=None):
        if s2 is None:
            self.S.op(eng, lambda e: e.tensor_single_scalar(out=out, in_=in0, scalar=s1, op=op0), r=r, w=w)
        elif accum is not None:
            self.S.op(eng, lambda e: e.tensor_scalar(out=out, in0=in0, scalar1=s1, scalar2=s2, op0=op0, op1=op1,
                                                     accum_out=accum), r=r, w=w)
        else:
            self.S.op(eng, lambda e: e.tensor_scalar(out=out, in0=in0, scalar1=s1, scalar2=s2, op0=op0, op1=op1),
                      r=r, w=w)

    def stt(self, eng, out, in0, scalar, in1, op0, op1, r, w, accum=None):
        if accum is None:
            self.S.op(eng, lambda e: e.scalar_tensor_tensor(out=out, in0=in0, scalar=scalar, in1=in1, op0=op0,
                                                            op1=op1), r=r, w=w)
        else:
            self.S.op(eng, lambda e: e.scalar_tensor_tensor(out=out, in0=in0, scalar=scalar, in1=in1, op0=op0,
                                                            op1=op1, accum_out=accum), r=r, w=w)

    def cp(self, eng, out, in_, r, w):
        if eng == "act":
            self.S.op("act", lambda e: e.copy(out=out, in_=in_), r=r, w=w)
        else:
            self.S.op(eng, lambda e: e.tensor_copy(out=out, in_=in_), r=r, w=w)

    def recip(self, out, in_, r, w):
        self.S.op("dve", lambda e: e.reciprocal(out=out, in_=in_), r=r, w=w)

    def memset(self, eng, ap, val, w):
        self.S.op(eng, lambda e: e.memset(ap, val), w=w)

    def ld(self, out, in_, w, r=(), eng="sp"):
        self.S.dma(eng, out, in_, r=r, w=w)

    def ps(self, n=1):
        banks = getattr(self, "banks1", (0, 1, 2, 3, 4))
        if not hasattr(self, "bank_last"):
            self.bank_last = {}
            self.bank_cnt = {}
            self.alloc_ctr = 0
        self.alloc_ctr += 1
        b = min(banks, key=lambda x: self.bank_last.get(x, 0))
        self.bank_last[b] = self.alloc_ctr
        c = self.bank_cnt.get((b, n), 0)
        self.bank_cnt[(b, n)] = c + 1
        off = (c % (4 // n)) * n
        return self.pbank[b][:, off * 128:(off + n) * 128], ["@b%d" % b]

    def psb(self, n):
        u = self.psb_ptr
        if u + n > 8:
            u = 0
        self.psb_ptr = u + n
        return self.ptb[:, u * 128:(u + n) * 128], ["@b5"]

    def build(self):
        nc = self.nc
        NB = self.nblk
        T = NB * 128
        din = {}
        for name, shape, dt in [
            ("xf", [SEQ, D], F32), ("xown", [2048, D], F32), ("poso", [128, 16], I32), ("colp", [128, NCOL], F32), ("rowA", [128, 1792], F32),
            ("rowB", [128, NRB], F32), ("rowC", [128, 1024], F32), ("posc", [128, 32], I32),
            ("ident", [128, 128], F32), ("msl", [128, 128], F32), ("mg1", [128, 256], F32),
            ("mg2", [128, 256], F32), ("bones", [128, 128], F32), ("caus", [128, 128], F32),
            ("iota", [128, 128], F32), ("selm", [128, 2], F32),
            ("w_ada", [D, 6 * D], F32), ("w_in", [D, 4256], F32), ("w_uq", [256, 768], F32),
            ("w_ukv", [128, 1024], F32), ("w_o_mla", [512, D], F32), ("wl", [128, 512], F32),
            ("w_gate_up", [128, 512], F32), ("w_o_rwkv", [512, D], F32), ("w_out", [D, D], F32),
            ("w_query", [D, 2048], F32), ("sk1T", [128, 128], F32), ("sk2T", [128, 128], F32),
            ("edT", [128, D, 128], F32), ("eup", [16384, D], F32),
        ]:
            din[name] = self.dram(name, shape, dt)
        self.din = din
        out = self.dram("out", [2048, D], F32, kind="ExternalOutput")
        zscr = self.dram("zscr", [512, SEQ], BF16, kind="Internal")
        x1scr = self.dram("x1scr", [SEQ, D], F32, kind="Internal")
        gscr = self.dram("gscr", [16, 128, 16 * 8 * 128], BF16, kind="Internal")
        self.dbg_out = {}
        if self.dbg:
            for name, shape in [("d_z", [512, SEQ]), ("d_x1", [SEQ, D]), ("d_misc", [128, 4096])]:
                self.dbg_out[name] = self.dram(name, shape, F32 if name != "d_z" else BF16, kind="ExternalOutput")

        with ExitStack() as st0:
            self.st = st0
            self.S = S = Sched(nc, st0)
            self.pbank = [st0.enter_context(nc.psum_tensor("pb%d" % i, [128, 512], F32)) for i in range(8) if i != 5]
            self.pbank.insert(5, None)
            self.ptb = st0.enter_context(nc.psum_tensor("ptb", [128, 1024], BF16))
            self.ps_ptr = 0
            self.psb_ptr = 0
            sb = self.sb
            ident = sb("ident", [128, 128], F32)
            identb = sb("identb", [128, 128], BF16)
            colp = sb("colp", [128, NCOL], F32)
            modc = sb("modc", [128, 32], F32)
            sc1 = sb("sc1", [128, 8], F32)
            sc2 = sb("sc2", [128, 8], F32)
            gtb = sb("gtb", [128, 2048], F32)
            ones = sb("ones", [128, 128], F32)
            self.scr1 = sb("scr1", [128, 2], F32)
            self.ld(ident[:], din["ident"][:, :], w=["ident"])
            self.ld(colp[:], din["colp"][:, :], w=["colp"])
            self.cp("dve", identb[:], ident[:], r=["ident"], w=["identb"])
            self.memset("pool", ones[:], 1.0, w=["ones"])
            self.ident, self.identb, self.colp, self.ones = ident, identb, colp, ones

            with ExitStack() as st:
                self.st = st
                silc = sb("silc", [128, 8], F32)
                self.act(silc[:], colp[:, 0:8], AF.Silu, r=["colp"], w=["silc"])
                wst = [sb("wst%d" % j, [128, 8, 1024], F32) for j in range(2)]
                gi = 0
                mps, mk = self.ps(1)
                for grp, c0 in enumerate([0, 1024, 3072, 4096]):
                    wt = wst[gi % 2]
                    wk = "wst%d" % (gi % 2)
                    gi += 1
                    for k in range(8):
                        self.ld(wt[:, k, :], din["w_ada"][k * 128:(k + 1) * 128, c0:c0 + 1024], w=[wk + "_%d" % k])
                    for j in range(8):
                        for k in range(8):
                            self.mm(mps[:, grp * 8 + j:grp * 8 + j + 1], wt[:, k, j * 128:(j + 1) * 128],
                                    silc[:, k:k + 1], r=[wk + "_%d" % k, "silc"], w=mk, start=(k == 0), stop=(k == 7))
                self.tt("dve", modc[:], mps[:, 0:32], colp[:, 8:40], ALU.add, r=mk + ["colp"], w=["modc"])
                self.stt("dve", sc1[:], modc[:, 8:16], 1.0, colp[:, CO["nmix"]:CO["nmix"] + 8], ALU.add, ALU.mult,
                         r=["modc", "colp"], w=["sc1"])
                self.stt("dve", sc2[:], modc[:, 24:32], 1.0, colp[:, CO["nffn"]:CO["nffn"] + 8], ALU.add, ALU.mult,
                         r=["modc", "colp"], w=["sc2"])
                grow = sb("grow", [1, 2048], F32)
                for grp, c0 in enumerate([2048, 5120]):
                    wt = wst[gi % 2]
                    wk = "wst%d" % (gi % 2)
                    gi += 1
                    for k in range(8):
                        self.ld(wt[:, k, :], din["w_ada"][k * 128:(k + 1) * 128, c0:c0 + 1024], w=[wk + "_%d" % k])
                    for hf in range(2):
                        rp, rk_ = self.ps(4)
                        for k in range(8):
                            self.mm(rp[0:1, :], silc[:, k:k + 1], wt[:, k, hf * 512:(hf + 1) * 512],
                                    r=[wk + "_%d" % k, "silc"], w=rk_, start=(k == 0), stop=(k == 7))
                        self.cp("act", grow[0:1, grp * 1024 + hf * 512: grp * 1024 + (hf + 1) * 512], rp[0:1, :],
                                r=rk_, w=["grow"])
                for q in range(4):
                    bp, bk = self.ps(4)
                    self.mm(bp[:, :], ones[0:1, :], grow[0:1, q * 512:(q + 1) * 512], r=["ones", "grow"], w=bk)
                    self.cp("act", gtb[:, q * 512:(q + 1) * 512], bp[:, :], r=bk, w=["gtb"])
                self.S.barrier()
                self.S.emit()
            self.modc, self.sc1, self.sc2, self.gtb = modc, sc1, sc2, gtb

            if "A" in self.phases:
                with ExitStack() as st:
                    self.st = st
                    self.phaseA(zscr)
                    self.S.barrier()
                    self.S.emit()
            if "B" in self.phases:
                with ExitStack() as st:
                    self.st = st
                    self._phase_st = st
                    self.phaseB(zscr, x1scr)
                    self.S.barrier()
                    self.S.emit()
            if "C" in self.phases:
                with ExitStack() as st:
                    self.st = st
                    self.phaseC(x1scr, gscr, out)
                    self.S.barrier()
                    self.S.emit()
            if self.dbg:
                with ExitStack() as st:
                    self.st = st
                    if "A" in self.phases:
                        self.S.dma("sp", self.dbg_out["d_z"][:, :], zscr[:, :])
                    if "B" in self.phases:
                        self.S.dma("sp", self.dbg_out["d_x1"][:, :], x1scr[:, :])
                    self.S.barrier()
                    self.S.emit()

    def alloc_h(self, nbuf=2):
        sb = self.sb
        self.xt = [sb("xt%d" % j, [128, D], F32) for j in range(nbuf)]
        if nbuf == 1:
            self.xt = [self.xt[0], self.xt[0]]
        self.xn = sb("xn", [128, D], BF16)
        self.junk = self.xn
        self.ss = sb("ss", [128, 2], F32)
        self.hT = [sb("hT%d" % j, [128, 8, 130], BF16) for j in range(nbuf)]
        if nbuf == 1:
            self.hT = [self.hT[0], self.hT[0]]
        self.nbuf = nbuf
        self.memset("pool", self.hT[0][:, :, 0:2], 0.0, w=["hT0"])
        self.epsc = sb("epsc", [128, 2], F32)
        self.memset("pool", self.epsc[:, 0:1], EPS, w=["epsc"])
        self.memset("pool", self.epsc[:, 1:2], GN_EPS, w=["epsc"])

    def make_h(self, i, src, scale, shift, skeys, carry=True, evac_act=False):
        b = i % self.nbuf
        xt, xk = self.xt[b], "xt%d" % b
        hk = "hT%d" % b
        if src is not None:
            self.ld(xt[:], src, w=[xk])
        self.memset("pool", self.ss[:, 0:1], 0.0, w=["ss"])
        self.act(self.junk[:], xt[:], AF.Square, r=[xk], w=["xn", "ss"], accum=self.ss[:, 0:1])
        self.act(self.ss[:, 1:2], self.ss[:, 0:1], AF.Sqrt, r=["ss", "epsc"], w=["ss"], bias=self.epsc[:, 0:1],
                 scale=1.0 / D)
        self.recip(self.ss[:, 1:2], self.ss[:, 1:2], r=["ss"], w=["ss"])
        self.ts("dve", self.xn[:], xt[:], self.ss[:, 1:2], None, ALU.mult, None, r=[xk, "ss"], w=["xn"])
        HS = int(os.environ.get("HSTOP", "99"))
        if HS <= 1:
            return self.hT[b], hk
        for half in range(2):
            ppb, pk = self.psb(4)
            for j in range(4):
                c = half * 4 + j
                self.tr(ppb[:, j * 128:(j + 1) * 128], self.xn[:, c * 128:(c + 1) * 128], self.identb[:],
                        r=["xn", "identb"], w=pk)
            if HS <= 2:
                continue
            for j in range(4):
                c = half * 4 + j
                if os.environ.get("EV", "") == "cp":
                    self.cp("dve", self.hT[b][:, c, 2:130], ppb[:, j * 128:(j + 1) * 128], r=pk, w=[hk])
                elif os.environ.get("EV", "") == "sb":
                    self.ts("dve", self.hT[b][:, c, 2:130], self.xn[:, 0:128],
                            scale[:, c:c + 1], shift[:, c:c + 1], ALU.mult, ALU.add, r=pk + skeys, w=[hk])
                elif evac_act:
                    self.act(self.hT[b][:, c, 2:130], ppb[:, j * 128:(j + 1) * 128], AF.Identity, r=pk + skeys, w=[hk],
                             bias=shift[:, c:c + 1], scale=scale[:, c:c + 1])
                else:
                    self.ts("dve", self.hT[b][:, c, 2:130], ppb[:, j * 128:(j + 1) * 128],
                            scale[:, c:c + 1], shift[:, c:c + 1], ALU.mult, ALU.add, r=pk + skeys, w=[hk])
        if carry and HS > 3:
            self.cp("pool", self.hT[1 - b][:, :, 1:2], self.hT[b][:, :, 129:130], r=[hk], w=["hT%d" % (1 - b)])
        return self.hT[b], hk

    def load_w_bf16(self, dst, dkey, src_rows, ncols, stg, skeys, kchunks):
        for k in range(kchunks):
            self.S.dma("pool", dst[:, k, :], src_rows(k), w=[dkey + "_k%d" % k])
        self.S.op("pool", lambda e: e.memset(self.scr1[0:1, 0:1], 0.0), r=[dkey + "_k%d" % k for k in range(kchunks)], w=[dkey, "scr1"])

    def phaseA(self, zscr):
        sb, din, S = self.sb, self.din, self.S
        NB = self.nblk
        colp = self.colp
        self.alloc_h()
        self.banks1 = (0, 1, 2, 3, 4, 7)
        wa = sb("wa", [128, 8, 1792], BF16)
        wb = sb("wb", [128, 8, 1792], BF16)
        with ExitStack() as stw:
            mu = sb("mu", [128, 1792], F32, stw)
            self.ld(mu[:], din["rowA"][:, :], w=["mu"])
            stg = [sb("stgA%d" % j, [128, 1792], F32, stw) for j in range(2)]
            tmpb = [sb("tmpbA%d" % j, [128, 1792], F32, stw) for j in range(2)]
            for k in range(8):
                s, sk = stg[k % 2], "stgA%d" % (k % 2)
                t, tk = tmpb[k % 2], "tmpbA%d" % (k % 2)
                self.ld(s[:], din["w_in"][k * 128:(k + 1) * 128, 416:2208], w=[sk])
                self.tt("dve", t[:], s[:], mu[:], ALU.mult, r=[sk, "mu"], w=[tk])
                self.cp("act", wb[:, k, :], t[:], r=[tk], w=["wb"])
                self.tt("dve", wa[:, k, :], s[:], t[:], ALU.subtract, r=[sk, tk], w=["wa"])
            S.barrier()
        wl = sb("wl", [128, 512], F32)
        wg = sb("wg", [128, 512], F32)
        msl = sb("msl", [128, 128], F32)
        mg1 = sb("mg1", [128, 256], F32)
        mg2 = sb("mg2", [128, 256], F32)
        bones = sb("bones", [128, 128], F32)
        for t_, n_ in [(wl, "wl"), (wg, "w_gate_up"), (msl, "msl"), (mg1, "mg1"), (mg2, "mg2"), (bones, "bones")]:
            self.ld(t_[:], din[n_][:, :], w=[n_])
        omka = sb("omka", [128, 4], F32)
        self.ts("dve", omka[:], colp[:, CO["ka"]:CO["ka"] + 4], -1.0, 1.0, ALU.mult, ALU.add, r=["colp"], w=["omka"])
        ident, ones = self.ident, self.ones
        identr = sb("identr", [128, 128], F32R)
        self.cp("dve", identr[:], ident[:], r=["ident"], w=["identr"])
        H = sb("H", [128, 4, 64], F32R)
        self.memset("pool", H[:].bitcast(F32), 0.0, w=["H%d_%d" % (g, a) for g in range(4) for a in range(2)])
        rkv = sb("rkv", [128, 12, 128], F32)
        ta = sb("ta", [128, 128], F32)
        sg = sb("sg", [128, 128], F32)
        G = [sb("G_%d" % j, [128, 4, 128], F32) for j in range(2)]
        BON = [sb("BON_%d" % j, [128, 4, 128], F32) for j in range(2)]
        GC = [sb("GC_%d" % j, [128, 4], F32) for j in range(2)]
        PREP_STEPS = int(os.environ.get("PREP_STEPS", "4"))
        sigG = [sb("sig_g%d" % g_, [128, 128], F32) for g_ in range(2)] * 2
        aaG = [sb("aa_g%d" % g_, [128, 128], F32) for g_ in range(2)] * 2
        kkG = [sb("kk_g%d" % g_, [128, 128], F32) for g_ in range(2)] * 2
        sqG = [sb("sq_g%d" % g_, [128, 128], F32) for g_ in range(2)] * 2
        rnG = [sb("rn_g%d" % g_, [128, 128], F32) for g_ in range(2)] * 2
        kapG = [sb("kap_g%d" % g_, [128, 128], F32) for g_ in range(2)] * 2
        tqG = [sb("tq_g%d" % g_, [128, 128], F32) for g_ in range(2)] * 2
        khG = [sb("kh_g%d" % g_, [128, 128], F32) for g_ in range(2)] * 2
        bbG = [sb("bb_g%d" % g_, [128, 128], F32) for g_ in range(2)] * 2
        rkrG = [sb("rkr_g%d" % g_, [128, 128], F32) for g_ in range(2)] * 2
        LsG = [sb("Ls_g%d" % g_, [128, 128], F32) for g_ in range(2)] * 2
        LxG = [sb("Lx_g%d" % g_, [128, 128], F32) for g_ in range(2)] * 2
        EpG = [sb("Ep_g%d" % g_, [128, 128], F32) for g_ in range(2)] * 2
        EnG = [sb("En_g%d" % g_, [128, 128], F32) for g_ in range(2)] * 2
        ExG = [sb("Ex_g%d" % g_, [128, 128], F32) for g_ in range(2)] * 2
        CM1 = [[sb("CM1_%d_%d" % (j, g), [128, 256], F32R) for g in range(4)] for j in range(2)]
        CM2 = [[sb("CM2_%d_%d" % (j, g), [128, 256], F32R) for g in range(4)] for j in range(2)]
        TM = [[sb("TM%d_%d" % (j, g), [128, 4, 128], F32R) for g in range(4)] for j in range(2)]
        WT = [sb("WT%d" % g, [128, 128], F32R) for g in range(4)]
        PPH = [[sb("PP%d_%d" % (j, h), [128, 256], F32R) for j in range(2)] for h in range(8)]
        PH = [[PPH[h][j][:, 0:128] for j in range(2)] for h in range(8)]
        PTH = [[PPH[h][j][:, 128:256] for j in range(2)] for h in range(8)]
        TTH = [sb("TT_%d" % h, [128, 128], F32R) for h in range(8)]
        AkTH = [sb("AkT_%d" % h, [128, 256], F32R) for h in range(8)]
        nBrH = [sb("nBr_%d" % h, [128, 128], F32R) for h in range(8)]
        X0H = [sb("X0_%d" % h, [128, 64], F32R) for h in range(8)]
        U0H = [sb("U0_%d" % h, [128, 64], F32R) for h in range(8)]
        UsbH = [sb("Usb_%d" % h, [128, 64], F32R) for h in range(8)]
        ysb4 = sb("ysb4", [128, 512], F32)
        cen4 = sb("cen4", [128, 512], F32)
        sq4 = sb("sq4", [128, 512], F32)
        rs4 = sb("rs4", [128, 512], F32)
        zt4 = [sb("zt4_%d" % j, [128, 512], BF16) for j in range(2)]

        for i0_ in range(min(2, NB)):
            self.ld(self.xt[i0_][:], din["xf"][i0_ * 128:(i0_ + 1) * 128, :], w=["xt%d" % i0_])

        def prep_gen(i):
            pb = i % 2
            hT, hk = self.make_h(i, None, self.sc1, self.modc[:, 0:8], ["sc1", "modc"])
            if i + 2 < NB:
                self.ld(self.xt[i % 2][:], din["xf"][(i + 2) * 128:(i + 3) * 128, :], w=["xt%d" % (i % 2)])
            STOP = int(os.environ.get("STOP", "99"))
            if STOP <= 1:
                return
            for c in range(14):
                pp, pk = self.ps(1)
                for k in range(8):
                    self.mm(pp[:, :], wa[:, k, c * 128:(c + 1) * 128], hT[:, k, 2:130], r=["wa", hk], w=pk,
                            start=(k == 0), stop=False)
                    self.mm(pp[:, :], wb[:, k, c * 128:(c + 1) * 128], hT[:, k, 1:129], r=["wb", hk], w=pk,
                            start=False, stop=(k == 7))
                if c < 12:
                    self.cp("act", rkv[:, c, :], pp[:, :], r=pk, w=["rkv%d" % c])
                    yield
                elif c == 12:
                    self.act(ta[0:64, :], pp[0:64, :], AF.Tanh, r=pk, w=["ta"])
                    self.cp("act", ta[64:128, :], pp[64:128, :], r=pk, w=["ta"])
                else:
                    self.act(sg[:, :], pp[:, :], AF.Sigmoid, r=pk, w=["sg"])
                    yield
            if STOP <= 2:
                return
            for g in range(4):
                gp, gk = self.ps(1)
                self.mm(gp[:, :], wg[:, g * 128:(g + 1) * 128], sg[:, :], r=["w_gate_up", "sg"], w=gk)
                self.cp("act", G[pb][:, g, :], gp[:, :], r=gk, w=["G%d_%d" % (pb, g)])
                yield
            if STOP <= 3:
                return
            def prep_g(g):
                rK, kK, vK = "rkv%d" % g, "rkv%d" % (4 + g), "rkv%d" % (8 + g)
                r_, k_, v_ = rkv[:, g, :], rkv[:, 4 + g, :], rkv[:, 8 + g, :]
                up, uk = self.ps(1)
                self.mm(up[:, :], wl[0:64, g * 128:(g + 1) * 128], ta[0:64, :], r=["wl", "ta"], w=uk)
                self.act(sigG[g][:], up[:, :], AF.Sigmoid, r=uk + ["colp"], w=["sig%d" % (g % 2)],
                         bias=colp[:, CO["dbase"] + g:CO["dbase"] + g + 1])
                ap_, ak = self.ps(1)
                self.mm(ap_[:, :], wl[64:128, g * 128:(g + 1) * 128], ta[64:128, :], r=["wl", "ta"], w=ak)
                self.act(aaG[g][:], ap_[:, :], AF.Sigmoid, r=ak + ["colp"], w=["aa%d" % (g % 2)],
                         bias=colp[:, CO["abase"] + g:CO["abase"] + g + 1])
                self.ts("dve", kkG[g][:], k_, colp[:, CO["kk"] + g:CO["kk"] + g + 1], None, ALU.mult, None,
                        r=[kK, "colp"], w=["kk%d" % (g % 2)])
                self.tt("pool", sqG[g][:], kkG[g][:], kkG[g][:], ALU.mult, r=["kk%d" % (g % 2)], w=["sq%d" % (g % 2)])
                yield
                sp_, sk_ = self.ps(1)
                self.mm(sp_[:, :], bones[:], sqG[g][:], r=["bones", "sq%d" % (g % 2)], w=sk_)
                self.act(rnG[g][:], sp_[:, :], AF.Sqrt, r=sk_, w=["rn%d" % (g % 2)])
                yield
                self.ts("dve", rnG[g][:], rnG[g][:], 1e-12, None, ALU.max, None, r=["rn%d" % (g % 2)], w=["rn%d" % (g % 2)])
                self.recip(rnG[g][:], rnG[g][:], r=["rn%d" % (g % 2)], w=["rn%d" % (g % 2)])
                self.tt("dve", kapG[g][:], kkG[g][:], rnG[g][:], ALU.mult, r=["kk%d" % (g % 2), "rn%d" % (g % 2)], w=["kap%d" % (g % 2)])
                yield
                self.ts("pool", tqG[g][:], aaG[g][:], colp[:, CO["ka"] + g:CO["ka"] + g + 1], omka[:, g:g + 1], ALU.mult,
                        ALU.add, r=["aa%d" % (g % 2), "colp", "omka"], w=["tq%d" % (g % 2)])
                self.tt("pool", khG[g][:], k_, tqG[g][:], ALU.mult, r=[kK, "tq%d" % (g % 2)], w=["kh%d" % (g % 2)])
                self.tt("dve", bbG[g][:], kapG[g][:], aaG[g][:], ALU.mult, r=["kap%d" % (g % 2), "aa%d" % (g % 2)], w=["bb%d" % (g % 2)])
                self.stt("dve", rkrG[g][:], r_, colp[:, CO["rk"] + g:CO["rk"] + g + 1], khG[g][:], ALU.mult, ALU.mult,
                         r=[rK, "colp", "kh%d" % (g % 2)], w=["rkr%d" % (g % 2)])
                yield
                bp, bk = self.ps(1)
                self.mm(bp[:, :], bones[:], rkrG[g][:], r=["bones", "rkr%d" % (g % 2)], w=bk)
                self.tt("dve", BON[pb][:, g, :], bp[:, :], v_, ALU.mult, r=bk + [vK], w=["BON%d_%d" % (pb, g)])
                yield
                if STOP <= 4:
                    return
                S.op("dve", lambda e: e.tensor_tensor_scan(out=LsG[g][:], data0=ones[:], data1=sigG[g][:], initial=0.0,
                                                          op0=ALU.mult, op1=ALU.add), r=["ones", "sig%d" % (g % 2)], w=["Ls%d" % (g % 2)])
                yield
                self.tt("pool", LxG[g][:], LsG[g][:], sigG[g][:], ALU.subtract, r=["Ls%d" % (g % 2), "sig%d" % (g % 2)], w=["Lx%d" % (g % 2)])
                self.act(EpG[g][:], LsG[g][:], AF.Exp, r=["Ls%d" % (g % 2)], w=["Ep%d" % (g % 2)], scale=-C0)
                self.act(EnG[g][:], LsG[g][:], AF.Exp, r=["Ls%d" % (g % 2)], w=["En%d" % (g % 2)], scale=C0)
                self.act(ExG[g][:], LxG[g][:], AF.Exp, r=["Lx%d" % (g % 2)], w=["Ex%d" % (g % 2)], scale=-C0)
                yield
                c1, c2 = CM1[pb][g], CM2[pb][g]
                c1k, c2k = "CM1_%d_%d" % (pb, g), "CM2_%d_%d" % (pb, g)
                self.tt("dve", c1[:, 0:128], kapG[g][:], ExG[g][:], ALU.mult, r=["kap%d" % (g % 2), "Ex%d" % (g % 2)], w=[c1k])
                self.tt("pool", c1[:, 128:256], r_, EpG[g][:], ALU.mult, r=[rK, "Ep%d" % (g % 2)], w=[c1k])
                self.tt("dve", c2[:, 0:128], bbG[g][:], EnG[g][:], ALU.mult, r=["bb%d" % (g % 2), "En%d" % (g % 2)], w=[c2k])
                self.tt("pool", c2[:, 128:256], khG[g][:], EnG[g][:], ALU.mult, r=["kh%d" % (g % 2), "En%d" % (g % 2)], w=[c2k])
                self.cp("pool", GC[pb][:, g:g + 1], EpG[g][:, 127:128], r=["Ep%d" % (g % 2)], w=["GC%d_%d" % (pb, g)])
                yield
                if STOP <= 5:
                    return
                tm, tmk = TM[pb][g], "TM%d_%d" % (pb, g)
                yield
                tp, tk_ = self.ps(4)
                self.tr(tp[:, 0:128], c1[:, 0:128].bitcast(F32), ident[:], r=[c1k, "ident"], w=tk_)
                self.tr(tp[:, 128:256], c2[:, 0:128].bitcast(F32), ident[:], r=[c2k, "ident"], w=tk_)
                self.tr(tp[:, 256:384], c2[:, 128:256].bitcast(F32), ident[:], r=[c2k, "ident"], w=tk_)
                self.tr(tp[:, 384:512], v_, ident[:], r=[vK, "ident"], w=tk_)
                self.cp("act", tm[:, 0, :], tp[:, 0:128], r=tk_, w=[tmk])
                self.ts("dve", tm[:, 1, :], tp[:, 128:256], -1.0, None, ALU.mult, None, r=tk_, w=[tmk])
                self.cp("act", tm[:, 2:4, :], tp[:, 256:512].rearrange("p (a b) -> p a b", a=2), r=tk_, w=[tmk])
                yield
                yield

            for g0 in (0, 2):
                subs = [prep_g(g0), prep_g(g0 + 1)]
                while subs:
                    for sg_ in list(subs):
                        try:
                            next(sg_)
                        except StopIteration:
                            subs.remove(sg_)
                    yield
            if STOP <= 6:
                return

            yield

        def head_gen(g, a, i):
            pb = i % 2
            STOP = int(os.environ.get("STOP", "99"))
            hx = 2 * g + a
            sx = "_%d" % hx
            c1, c2, tm = CM1[pb][g], CM2[pb][g], TM[pb][g]
            c1k, c2k, tmk = "CM1_%d_%d" % (pb, g), "CM2_%d_%d" % (pb, g), "TM%d_%d" % (pb, g)
            Pq, PTq, TTq, AkTq, nBrq = PH[hx], PTH[hx], TTH[hx], AkTH[hx], nBrH[hx]
            PPq = PPH[hx]
            X0q, U0q, Usbq = X0H[hx], U0H[hx], UsbH[hx]
            yp, yk = self.pbank[6][:, g * 128:(g + 1) * 128], ["@b6"]
            R = slice(64 * a, 64 * a + 64)
            Cc = slice(64 * a, 64 * a + 64)
            Hk = "H%d_%d" % (g, a)
            BC = (lambda ap_: ap_) if a == 0 else (lambda ap_: ap_.bitcast(F32))
            idm, idk = (identr, "identr") if a == 0 else (ident, "ident")
            a_ps, a_k = self.ps(1)
            self.mm(a_ps[:, :], c1[R, 0:128], c2[R, 0:128], r=[c1k, c2k], w=a_k)
            self.tt("dve", Pq[0][:], a_ps[:, :], msl[:], ALU.mult, r=a_k + ["msl"], w=["PP0" + sx])
            gb, gbk = self.ps(2)
            self.mm(gb[:, :], c2[R, 0:128], c1[R, 0:256], r=[c1k, c2k], w=gbk)
            self.tt("dve", PTq[0][:], gb[:, 0:128], mg1[:, 0:128], ALU.mult, r=gbk + ["mg1"], w=["PP0" + sx])
            self.tt("pool" if False else "dve", nBrq[:], gb[:, 128:256], mg1[:, 128:256], ALU.mult,
                    r=gbk + ["mg1"], w=["nBr" + sx])
            gk2, gkk = self.ps(2)
            self.mm(gk2[:, :], c2[R, 128:256], c1[R, 0:256], r=[c1k, c2k], w=gkk)
            self.tt("dve", AkTq[:], gk2[:, :], mg2[:], ALU.mult, r=gkk + ["mg2"], w=["AkT" + sx])
            self.tt("pool", TTq[:], ident[:], PTq[0][:], ALU.subtract, r=["ident", "PP0" + sx], w=["TT" + sx])
            yield
            if STOP <= 7:
                return
            cur = 0
            for j in range(1, 7):
                nxt = 1 - cur
                if j < 6:
                    pq_ps, pq_k = self.ps(2)
                    self.mm(pq_ps[:, 0:128], PTq[cur][:], Pq[cur][:], r=["PP%d" % cur + sx], w=pq_k)
                    self.mm(pq_ps[:, 128:256], Pq[cur][:], PTq[cur][:], r=["PP%d" % cur + sx], w=pq_k)
                    self.cp("act", PPq[nxt][:], pq_ps[:, :], r=pq_k, w=["PP%d" % nxt + sx])
                else:
                    p_ps, p_k = self.ps(1)
                    self.mm(p_ps[:, :], PTq[cur][:], Pq[cur][:], r=["PP%d" % cur + sx], w=p_k)
                    self.cp("act", Pq[nxt][:], p_ps[:, :], r=p_k, w=["PP%d" % nxt + sx])
                yield
                t_ps, t_k = self.ps(1)
                self.mm(t_ps[:, :], Pq[nxt][:], TTq[:], r=["PP%d" % nxt + sx, "TT" + sx], w=t_k)
                self.tt("dve", TTq[:], TTq[:], t_ps[:, :], ALU.add, r=["TT" + sx] + t_k, w=["TT" + sx])
                yield
                cur = nxt
            if STOP <= 8:
                return
            w_ps, w_k = self.ps(1)
            self.mm(w_ps[R, :], BC(tm[:, 0, Cc]), BC(TTq[:]), r=[tmk, "TT" + sx], w=w_k)
            self.cp("act", WT[g][R, :], w_ps[R, :], r=w_k, w=["WT%d_%d" % (g, a)])
            yield
            x_ps, x_k = self.ps(1)
            self.mm(x_ps[:, 0:64], AkTq[:, 0:128], tm[:, 3, Cc], r=["AkT" + sx, tmk], w=x_k)
            self.cp("act", X0q[:], x_ps[:, 0:64], r=x_k, w=["X0" + sx])
            yield
            u0_ps, u0_k = self.ps(1)
            self.mm(u0_ps[:, 0:64], TTq[:], X0q[:], r=["TT" + sx, "X0" + sx], w=u0_k)
            self.cp("act", U0q[:], u0_ps[:, 0:64], r=u0_k, w=["U0" + sx])
            yield
            if STOP <= 9:
                return
            u_ps, u_k = self.ps(1)
            self.mm(u_ps[:, 0:64], WT[g][R, :], H[R, g, :], r=["WT%d_%d" % (g, a), Hk], w=u_k, start=True, stop=False)
            self.mm(u_ps[:, 0:64], identr[:], U0q[:], r=["identr", "U0" + sx], w=u_k, start=False, stop=True)
            self.cp("act", Usbq[:], u_ps[:, 0:64], r=u_k, w=["Usb" + sx])
            yield
            self.mm(yp[R, :], BC(H[R, g, :]), BC(c1[R, 128:256]), r=[Hk, c1k], w=yk, start=True, stop=False)
            self.mm(yp[R, :], BC(Usbq[:]), BC(nBrq[:]), r=["Usb" + sx, "nBr" + sx], w=yk, start=False, stop=False)
            self.mm(yp[R, :], BC(tm[:, 3, Cc]), BC(AkTq[:, 128:256]), r=[tmk, "AkT" + sx], w=yk, start=False, stop=True)
            h_ps, h_k = self.ps(1)
            self.mm(h_ps[R, 0:64], BC(tm[:, 2, Cc]), BC(tm[:, 3, Cc]), r=[tmk], w=h_k, start=True, stop=False)
            self.mm(h_ps[R, 0:64], idm[R, R], BC(H[R, g, :]), r=[idk, Hk], w=h_k, start=False, stop=False)
            self.mm(h_ps[R, 0:64], BC(tm[:, 1, Cc]), BC(Usbq[:]), r=[tmk, "Usb" + sx], w=h_k, start=False, stop=True)
            self.ts("dve", H[R, g, :], h_ps[R, 0:64], GC[pb][R, g:g + 1], None, ALU.mult, None,
                    r=h_k + ["GC%d_%d" % (pb, g)], w=[Hk])
            yield


        def gn(i):
            pb = i % 2
            yp, yk = self.pbank[6][:, 0:512], ["@b6"]
            gk_all = ["G%d_%d" % (pb, g) for g in range(4)]
            bk_all = ["BON%d_%d" % (pb, g) for g in range(4)]
            lnw = colp[:, CO["lnw"]:CO["lnw"] + 4].unsqueeze(2).to_broadcast([128, 4, 128])
            lnb = colp[:, CO["lnb"]:CO["lnb"] + 4].unsqueeze(2).to_broadcast([128, 4, 128])
            v3 = lambda t: t[:].rearrange("p (g t) -> p g t", g=4)
            self.cp("act", ysb4[:], yp[:, :], r=yk, w=["ysb4"])
            m_ps, m_k = self.ps(4)
            self.mm(m_ps[:, :], bones[:], ysb4[:], r=["bones", "ysb4"], w=m_k)
            self.stt("dve", cen4[:], m_ps[:, :], -1.0 / 64, ysb4[:], ALU.mult, ALU.add, r=m_k + ["ysb4"], w=["cen4"])
            self.tt("pool", sq4[:], cen4[:], cen4[:], ALU.mult, r=["cen4"], w=["sq4"])
            v_ps, v_k = self.ps(4)
            self.mm(v_ps[:, :], bones[:], sq4[:], r=["bones", "sq4"], w=v_k)
            self.act(rs4[:], v_ps[:, :], AF.Sqrt, r=v_k + ["epsc"], w=["rs4"], bias=self.epsc[:, 1:2], scale=1.0 / 64)
            self.recip(rs4[:], rs4[:], r=["rs4"], w=["rs4"])
            self.tt("dve", cen4[:], cen4[:], rs4[:], ALU.mult, r=["cen4", "rs4"], w=["cen4"])
            self.tt("pool", v3(cen4), v3(cen4), lnw, ALU.mult, r=["cen4", "colp"], w=["cen4"])
            self.tt("pool", v3(cen4), v3(cen4), lnb, ALU.add, r=["cen4", "colp"], w=["cen4"])
            self.tt("dve", v3(cen4), v3(cen4), BON[pb][:], ALU.add, r=["cen4"] + bk_all, w=["cen4"])
            zb, zk = zt4[i % 2], "zt4_%d" % (i % 2)
            self.tt("dve", v3(zb), v3(cen4), G[pb][:], ALU.mult, r=["cen4"] + gk_all, w=[zk])
            self.ld(zscr[:, i * 128:(i + 1) * 128].rearrange("(g p) t -> p g t", p=128), v3(zb), r=[zk], w=[])

        def drain(gen):
            for _ in gen:
                pass

        drain(prep_gen(0))
        for i in range(NB):
            if int(os.environ.get("STOP", "99")) <= 6:
                if i + 1 < NB:
                    drain(prep_gen(i + 1))
                continue
            heads = [head_gen(g, a, i) for g in range(4) for a in range(2)]
            nxt = prep_gen(i + 1) if i + 1 < NB else None
            while heads or nxt is not None:
                hl = list(heads)
                marks = set(int((k_ + 1) * max(len(hl), 1) / PREP_STEPS) - 1 for k_ in range(PREP_STEPS))
                for hi_, gen in enumerate(hl):
                    try:
                        next(gen)
                    except StopIteration:
                        heads.remove(gen)
                    if nxt is not None and hi_ in marks:
                        try:
                            next(nxt)
                        except StopIteration:
                            nxt = None
                if not hl and nxt is not None:
                    try:
                        next(nxt)
                    except StopIteration:
                        nxt = None
            gn(i)

    def phaseB(self, zscr, x1scr):
        sb, din, S = self.sb, self.din, self.S
        self.banks1 = (0, 1, 2, 3, 4)
        NB = self.nblk
        ident, identb = self.ident, self.identb
        rowB = sb("rowB", [128, 592], F32)
        self.ld(rowB[:], din["rowB"][:, 0:592], w=["rowB"])
        posc = sb("posc", [128, 32], I32)
        posf = sb("posf", [128, 32], F32)
        self.ld(posc[:], din["posc"][:, :], w=["posc"])
        self.cp("dve", posf[:], posc[:], r=["posc"], w=["posf"])
        caus = sb("caus", [128, 128], F32)
        causb = sb("causb", [128, 128], BF16)
        self.ld(caus[:], din["caus"][:, :], w=["caus"])
        self.cp("dve", causb[:], caus[:], r=["caus"], w=["causb"])
        with ExitStack() as stw:
            bg = sb("bg1", [128, 1024], F32, stw)
            self.ld(bg[:], din["rowB"][:, 592:1616], w=["bg1"])
            self.tt("dve", self.gtb[:, 0:1024], self.gtb[:, 0:1024], bg[:], ALU.add, r=["gtb", "bg1"], w=["gtb"])
            S.barrier()
        KnT = sb("KnT", [128, 4, NB * 128], BF16)
        KpT = sb("KpT", [128, NB * 128], BF16)
        Vx = sb("Vx", [128, NB, 8, 65], BF16)
        self.memset("pool", Vx[:, :, :, 64:65], 1.0, w=["Vx"])
        NS = 4
        with ExitStack() as stp1:
            self.st = stp1
            wkv1 = sb("wkv1", [128, 8, 160], BF16)
            wukv1 = sb("wukv1", [128, 1, 1024], BF16)
            self.load_w_bf16(wkv1, "wkv1", lambda k: din["w_in"][k * 128:(k + 1) * 128, 256:416], 160, None, None, 8)
            self.load_w_bf16(wukv1, "wukv1", lambda k: din["w_ukv"][:, :], 1024, None, None, 1)
            eps1 = sb("eps1", [128, 1], F32)
            self.memset("pool", eps1[:], EPS, w=["eps1"])

            def make_slot(sx):
                K = lambda nm: nm + sx
                xt = sb("p1xt" + sx, [128, D], F32)
                xn = sb("p1xn" + sx, [128, D], BF16)
                ss = sb("p1ss" + sx, [128, 2], F32)
                hT = sb("p1hT" + sx, [128, 8, 130], BF16)
                kva = sb("p1kva" + sx, [128, 160], F32)
                kvs = sb("p1kvs" + sx, [128, 1024], F32)
                st2 = sb("p1st2" + sx, [128, 40], F32)
                ckv = sb("p1ckv" + sx, [128, 128], BF16)
                ckvT = sb("p1ckvT" + sx, [128, 128], BF16)
                sqb = sb("p1sqb" + sx, [128, 768], F32)
                knn = sb("p1knn" + sx, [128, 8, 64], BF16)
                kr = sb("p1kr" + sx, [128, 32], F32)
                kr2 = sb("p1kr2" + sx, [128, 128], BF16)
                cs = sb("p1cs" + sx, [128, 32], F32)
                uu = sb("p1uu" + sx, [128, 32], F32)
                ui = sb("p1ui" + sx, [128, 32], I32)
                uf = sb("p1uf" + sx, [128, 32], F32)
                um = sb("p1um" + sx, [128, 32], F32)
                t16 = [sb("p1t16_%d" % j + sx, [128, 8, 16], F32) for j in range(4)]

                def rms_rows(dst_rstd, src_sq_ap, n, rkeys, extra_scale=1.0):
                    self.act(dst_rstd, src_sq_ap, AF.Sqrt, r=rkeys + ["eps1"], w=[K("st2")], bias=eps1[:, 0:1], scale=1.0 / n)
                    self.recip(dst_rstd, dst_rstd, r=[K("st2")], w=[K("st2")])
                    if extra_scale != 1.0:
                        self.ts("dve", dst_rstd, dst_rstd, extra_scale, None, ALU.mult, None, r=[K("st2")], w=[K("st2")])

                def rope(dst, src, nh, rk, wk):
                    cosb = cs[:, 0:16].unsqueeze(1).to_broadcast([128, nh, 16])
                    sinb = cs[:, 16:32].unsqueeze(1).to_broadcast([128, nh, 16])
                    a, b, c, d = [t[:, 0:nh, :] for t in t16]
                    x1_, x2_ = src[:, :, 0:16], src[:, :, 16:32]
                    self.tt("dve", a, x1_, cosb, ALU.mult, r=rk + [K("cs")], w=[K("t16a")])
                    self.tt("pool", b, x2_, sinb, ALU.mult, r=rk + [K("cs")], w=[K("t16b")])
                    self.tt("dve", c, x1_, sinb, ALU.mult, r=rk + [K("cs")], w=[K("t16c")])
                    self.tt("pool", d, x2_, cosb, ALU.mult, r=rk + [K("cs")], w=[K("t16d")])
                    self.tt("dve", dst[:, :, 0:16], a, b, ALU.subtract, r=[K("t16a"), K("t16b")], w=wk)
                    self.tt("dve", dst[:, :, 16:32], c, d, ALU.add, r=[K("t16c"), K("t16d")], w=wk)

                def cossin(pcol):
                    self.ts("dve", uu[:, 0:16], rowB[:, RB["invf"]:RB["invf"] + 16], pcol, None, ALU.mult, None,
                            r=["rowB", "posf"], w=[K("uu")])
                    self.ts("dve", uu[:, 16:32], uu[:, 0:16], 0.0, None, ALU.add, None, r=[K("uu")], w=[K("uu")])
                    self.ts("dve", uu[:, 0:16], uu[:, 0:16], 0.25, None, ALU.add, None, r=[K("uu")], w=[K("uu")])
                    self.cp("dve", ui[:], uu[:], r=[K("uu")], w=[K("ui")])
                    self.cp("dve", uf[:], ui[:], r=[K("ui")], w=[K("uf")])
                    self.tt("dve", uf[:], uu[:], uf[:], ALU.subtract, r=[K("uu"), K("uf")], w=[K("uf")])
                    self.ts("dve", um[:], uf[:], 0.5, None, ALU.is_gt, None, r=[K("uf")], w=[K("um")])
                    self.tt("dve", uf[:], uf[:], um[:], ALU.subtract, r=[K("uf"), K("um")], w=[K("uf")])
                    self.act(cs[:], uf[:], AF.Sin, r=[K("uf")], w=[K("cs")], scale=float(2 * np.pi))


                def gen(i):
                    hk = K("hT0")
                    self.ld(xt[:], din["xf"][i * 128:(i + 1) * 128, :], w=[K("xt0")])
                    self.memset("pool", ss[:, 0:1], 0.0, w=[K("ss")])
                    self.act(xn[:], xt[:], AF.Square, r=[K("xt0")], w=[K("xn"), K("ss")], accum=ss[:, 0:1])
                    yield
                    self.act(ss[:, 1:2], ss[:, 0:1], AF.Sqrt, r=[K("ss"), "eps1"], w=[K("ss")], bias=eps1[:, 0:1], scale=1.0 / D)
                    self.recip(ss[:, 1:2], ss[:, 1:2], r=[K("ss")], w=[K("ss")])
                    self.ts("dve", xn[:], xt[:], ss[:, 1:2], None, ALU.mult, None, r=[K("xt0"), K("ss")], w=[K("xn")])
                    yield
                    for half in range(2):
                        ppb, pk = self.psb(4)
                        for j in range(4):
                            c = half * 4 + j
                            self.tr(ppb[:, j * 128:(j + 1) * 128], xn[:, c * 128:(c + 1) * 128], identb[:], r=[K("xn"), "identb"], w=pk)
                        for j in range(4):
                            c = half * 4 + j
                            self.ts("dve", hT[:, c, 2:130], ppb[:, j * 128:(j + 1) * 128], self.sc1[:, c:c + 1],
                                    self.modc[:, c:c + 1], ALU.mult, ALU.add, r=pk + ["sc1", "modc"], w=[hk])
                        yield
                    cossin(posf[:, i:i + 1])
                    yield
                    kp, kk_ = self.ps(2)
                    for k in range(8):
                        self.mm(kp[:, 0:160], hT[:, k, 2:130], wkv1[:, k, 0:160], r=[hk, "wkv1"], w=kk_, start=(k == 0), stop=(k == 7))
                    self.cp("act", kva[:], kp[:, 0:160], r=kk_, w=[K("kva")])
                    yield
                    self.memset("pool", st2[:, 0:8], 0.0, w=[K("st2")])
                    self.act(sqb[:, 0:128], kva[:, 0:128], AF.Square, r=[K("kva")], w=[K("sqb"), K("st2")], accum=st2[:, 0:1])
                    self.act(sqb[:, 128:160], kva[:, 128:160], AF.Square, r=[K("kva")], w=[K("sqb"), K("st2")], accum=st2[:, 1:2])
                    rms_rows(st2[:, 2:3], st2[:, 0:1], 128, [K("st2")])
                    rms_rows(st2[:, 3:4], st2[:, 1:2], 32, [K("st2")])
                    yield
                    self.stt("dve", ckv[:], kva[:, 0:128], st2[:, 2:3], rowB[:, RB["kvan"]:RB["kvan"] + 128], ALU.mult, ALU.mult,
                             r=[K("kva"), K("st2"), "rowB"], w=[K("ckv")])
                    self.stt("dve", kr[:], kva[:, 128:160], st2[:, 3:4], rowB[:, RB["kn"] + 64:RB["kn"] + 96], ALU.mult,
                             ALU.mult, r=[K("kva"), K("st2"), "rowB"], w=[K("kr")])
                    rope(kr2[:, 0:32].rearrange("p (h d) -> p h d", h=1), kr[:].rearrange("p (h d) -> p h d", h=1), 1,
                         [K("kr")], [K("kr2")])
                    self.cp("pool", kr2[:, 32:128].rearrange("p (c d) -> p c d", c=3), kr2[:, 0:32].unsqueeze(1).to_broadcast([128, 3, 32]), r=[K("kr2")], w=[K("kr2")])
                    yield
                    tpb, tk_ = self.psb(2)
                    self.tr(tpb[:, 0:128], ckv[:], identb[:], r=[K("ckv"), "identb"], w=tk_)
                    self.tr(tpb[:, 128:256], kr2[:], identb[:], r=[K("kr2"), "identb"], w=tk_)
                    self.cp("act", ckvT[:], tpb[:, 0:128], r=tk_, w=[K("ckvT")])
                    self.cp("act", KpT[:, i * 128:(i + 1) * 128], tpb[:, 128:256], r=tk_, w=["KpT"])
                    yield
                    for hf in range(2):
                        vp, vk = self.ps(4)
                        self.mm(vp[:, :], ckvT[:], wukv1[:, 0, hf * 512:(hf + 1) * 512], r=[K("ckvT"), "wukv1"], w=vk)
                        self.cp("act", kvs[:, hf * 512:(hf + 1) * 512], vp[:, :], r=vk, w=[K("kvs")])
                        yield
                    kv3 = kvs[:].rearrange("p (h d) -> p h d", h=8)
                    self.cp("pool", Vx[:, i, :, 0:64], kv3[:, :, 64:128], r=[K("kvs")], w=["Vx"])
                    yield
                    self.tt("dve", sqb[:, 0:512].rearrange("p (h d) -> p h d", h=8), kv3[:, :, 0:64], kv3[:, :, 0:64], ALU.mult,
                            r=[K("kvs")], w=[K("sqb")])
                    S.op("dve", lambda e: e.tensor_reduce(out=st2[:, 8:16], in_=sqb[:, 0:512].rearrange("p (h d) -> p h d", h=8),
                                                          axis=AX.X, op=ALU.add), r=[K("sqb")], w=[K("st2")])
                    rms_rows(st2[:, 16:24], st2[:, 8:16], 64, [K("st2")])
                    yield
                    self.tt("dve", sqb[:, 0:512].rearrange("p (h d) -> p h d", h=8), kv3[:, :, 0:64],
                            st2[:, 16:24].unsqueeze(2).to_broadcast([128, 8, 64]), ALU.mult, r=[K("kvs"), K("st2")], w=[K("sqb")])
                    self.tt("pool", knn[:], sqb[:, 0:512].rearrange("p (h d) -> p h d", h=8),
                            rowB[:, RB["kn"]:RB["kn"] + 64].unsqueeze(1).to_broadcast([128, 8, 64]), ALU.mult,
                            r=[K("sqb"), "rowB"], w=[K("knn")])
                    tpb, tk_ = self.psb(4)
                    for g in range(4):
                        self.tr(tpb[:, g * 128:(g + 1) * 128], knn[:, 2 * g:2 * g + 2, :].rearrange("p h d -> p (h d)"),
                                identb[:], r=[K("knn"), "identb"], w=tk_)
                    self.cp("act", KnT[:, :, i * 128:(i + 1) * 128], tpb[:, 0:512].rearrange("p (g t) -> p g t", g=4), r=tk_,
                            w=["KnT"])

                    yield
                return gen

            slots = [make_slot("_s%d" % j) for j in range(NS)]
            active = []
            nxt_i = 0
            while nxt_i < NB or active:
                while len(active) < NS and nxt_i < NB:
                    active.append(slots[nxt_i % NS](nxt_i))
                    nxt_i += 1
                    break
                for gen in list(active):
                    try:
                        next(gen)
                    except StopIteration:
                        active.remove(gen)
            S.barrier()
        self.st = self._phase_st
        self.alloc_h(1)
        wi = sb("wi", [128, 8, 2464], BF16)
        wuq = sb("wuq", [128, 2, 768], BF16)
        wukv = sb("wukv", [128, 1, 1024], BF16)
        womla = sb("womla", [128, 4, 1024], BF16)
        worw = sb("worw", [128, 4, 1024], BF16)
        wout = sb("wout", [128, 8, 1024], BF16)
        with ExitStack() as stw:
            stg = [sb("stgB%d" % j, [128, 2048], F32, stw) for j in range(2)]
            sk = ["stgB0", "stgB1"]
            self.load_w_bf16(wi[:, :, 0:416], "wi", lambda k: din["w_in"][k * 128:(k + 1) * 128, 0:416], 416, stg, sk, 8)
            self.load_w_bf16(wi[:, :, 416:2464], "wi", lambda k: din["w_in"][k * 128:(k + 1) * 128, 2208:4256], 2048,
                             stg, sk, 8)
            self.load_w_bf16(wuq, "wuq", lambda k: din["w_uq"][k * 128:(k + 1) * 128, :], 768, stg, sk, 2)
            self.load_w_bf16(wukv, "wukv", lambda k: din["w_ukv"][:, :], 1024, stg, sk, 1)
            self.load_w_bf16(womla, "womla", lambda k: din["w_o_mla"][k * 128:(k + 1) * 128, :], 1024, stg, sk, 4)
            self.load_w_bf16(worw, "worw", lambda k: din["w_o_rwkv"][k * 128:(k + 1) * 128, :], 1024, stg, sk, 4)
            self.load_w_bf16(wout, "wout", lambda k: din["w_out"][k * 128:(k + 1) * 128, :], 1024, stg, sk, 8)
            S.barrier()
        kva = sb("kva", [128, 160], F32)
        kvs = sb("kvs", [128, 1024], F32)
        st2 = sb("st2", [128, 40], F32)
        ckv = sb("ckv", [128, 128], BF16)
        ckvT = sb("ckvT", [128, 128], BF16)
        sqb = sb("sqb", [128, 768], F32)
        knn = sb("knn", [128, 8, 64], BF16)
        kr = sb("kr", [128, 32], F32)
        kr2 = sb("kr2", [128, 128], BF16)
        cs = sb("cs", [128, 32], F32)
        uu = sb("uu", [128, 32], F32)
        ui = sb("ui", [128, 32], I32)
        uf = sb("uf", [128, 32], F32)
        um = sb("um", [128, 32], F32)
        t16 = [sb("t16_%d" % j, [128, 8, 16], F32) for j in range(4)]
        qa = sb("qa", [128, 256], F32)
        cq = sb("cq", [128, 256], BF16)
        cqT = sb("cqT", [128, 2, 128], BF16)
        qf = sb("qf", [128, 768], F32)
        qn = sb("qn", [128, 8, 64], BF16)
        qpf = sb("qpf", [128, 8, 32], F32)
        qp = sb("qp", [128, 8, 32], BF16)
        QnB = sb("QnB", [128, 4, 256], BF16)
        QpB = sb("QpB", [128, 4, 256], BF16)
        self.memset("pool", QnB[:], 0.0, w=["QnB"])
        self.memset("pool", QpB[:], 0.0, w=["QpB"])
        PTt = [sb("PTt%d" % j, [128, 512], BF16) for j in range(3)]
        orc = sb("orc", [128, 8], F32)
        OA = sb("OA", [128, 8, 64], BF16)
        OT = sb("OT", [128, 4, 128], BF16)
        ZT = sb("ZT", [128, 4, 128], BF16)
        gsa = sb("gsa", [128, 128], F32)
        gsb = sb("gsb", [128, 128], F32)
        xtmp = sb("xtmp", [128, 512], F32)
        g1 = sb("g1", [128, 128], F32)
        g2 = sb("g2", [128, 128], F32)
        GTt = sb("GTt", [128, 8, 128], BF16)
        ATT_SCALE = 96.0 ** -0.5
        Ops = self.pbank[6:8]

        def rms_rows(dst_rstd, src_sq_ap, n, rkeys, extra_scale=1.0):
            self.act(dst_rstd, src_sq_ap, AF.Sqrt, r=rkeys + ["epsc"], w=["st2"], bias=self.epsc[:, 0:1], scale=1.0 / n)
            self.recip(dst_rstd, dst_rstd, r=["st2"], w=["st2"])
            if extra_scale != 1.0:
                self.ts("dve", dst_rstd, dst_rstd, extra_scale, None, ALU.mult, None, r=["st2"], w=["st2"])

        def rope(dst, src, nh, rk, wk):
            cosb = cs[:, 0:16].unsqueeze(1).to_broadcast([128, nh, 16])
            sinb = cs[:, 16:32].unsqueeze(1).to_broadcast([128, nh, 16])
            a, b, c, d = [t[:, 0:nh, :] for t in t16]
            x1_, x2_ = src[:, :, 0:16], src[:, :, 16:32]
            self.tt("dve", a, x1_, cosb, ALU.mult, r=rk + ["cs"], w=["t16a"])
            self.tt("pool", b, x2_, sinb, ALU.mult, r=rk + ["cs"], w=["t16b"])
            self.tt("dve", c, x1_, sinb, ALU.mult, r=rk + ["cs"], w=["t16c"])
            self.tt("pool", d, x2_, cosb, ALU.mult, r=rk + ["cs"], w=["t16d"])
            self.tt("dve", dst[:, :, 0:16], a, b, ALU.subtract, r=["t16a", "t16b"], w=wk)
            self.tt("dve", dst[:, :, 16:32], c, d, ALU.add, r=["t16c", "t16d"], w=wk)

        def cossin(pcol):
            self.ts("dve", uu[:, 0:16], rowB[:, RB["invf"]:RB["invf"] + 16], pcol, None, ALU.mult, None,
                    r=["rowB", "posf"], w=["uu"])
            self.ts("dve", uu[:, 16:32], uu[:, 0:16], 0.0, None, ALU.add, None, r=["uu"], w=["uu"])
            self.ts("dve", uu[:, 0:16], uu[:, 0:16], 0.25, None, ALU.add, None, r=["uu"], w=["uu"])
            self.cp("dve", ui[:], uu[:], r=["uu"], w=["ui"])
            self.cp("dve", uf[:], ui[:], r=["ui"], w=["uf"])
            self.tt("dve", uf[:], uu[:], uf[:], ALU.subtract, r=["uu", "uf"], w=["uf"])
            self.ts("dve", um[:], uf[:], 0.5, None, ALU.is_gt, None, r=["uf"], w=["um"])
            self.tt("dve", uf[:], uf[:], um[:], ALU.subtract, r=["uf", "um"], w=["uf"])
            self.act(cs[:], uf[:], AF.Sin, r=["uf"], w=["cs"], scale=float(2 * np.pi))

        NBh = NB // 2
        selm = sb("selmB", [128, 2], F32)
        self.ld(selm[:], din["selm"][:, :], w=["selmB"])
        MA = sb("MA", [128, 128], BF16)
        MB = sb("MB", [128, 128], BF16)
        negb = sb("negb", [128, 1], F32)
        self.ts("dve", MA[:], caus[:], selm[:, 0:1], selm[:, 1:2], ALU.mult, ALU.add, r=["caus", "selmB"], w=["MA"])
        self.ts("dve", MB[:], caus[:], selm[:, 1:2], None, ALU.mult, None, r=["caus", "selmB"], w=["MB"])
        self.ts("dve", negb[:], selm[:, 0:1], -10000.0, None, ALU.mult, None, r=["selmB"], w=["negb"])
        poso = sb("poso", [128, 16], I32)
        posfo = sb("posfo", [128, 16], F32)
        self.ld(poso[:], din["poso"][:, :], w=["poso"])
        self.cp("dve", posfo[:], poso[:], r=["poso"], w=["posfo"])
        ZTa = sb("ZTa", [128, 4, 128], BF16)
        ZTb = sb("ZTb", [128, 4, 128], BF16)

        def kb_groups(j):
            groups = []
            lo = list(range(0, 2 * j))
            for q in range(0, len(lo), 2):
                groups.append((lo[q:q + 2], 0))
            groups.append(([2 * j], 1))
            groups.append(([2 * j + 1], 3))
            return groups

        for i in range(NBh):
            hT, hk = self.make_h(i, din["xown"][i * 128:(i + 1) * 128, :], self.sc1, self.modc[:, 0:8],
                                 ["sc1", "modc"], carry=False)
            self.ld(ZTa[:], zscr[:, (2 * i) * 128:(2 * i + 1) * 128].rearrange("(g p) t -> p g t", p=128), w=["ZTa"])
            self.ld(ZTb[:], zscr[:, (2 * i + 1) * 128:(2 * i + 2) * 128].rearrange("(g p) t -> p g t", p=128), w=["ZTb"])
            xt, xk = self.xt[0], "xt0"
            cossin(posfo[:, i:i + 1])
            qp_, qk_ = self.ps(2)
            for k in range(8):
                self.mm(qp_[:, 0:256], hT[:, k, 2:130], wi[:, k, 0:256], r=[hk, "wi"], w=qk_, start=(k == 0), stop=(k == 7))
            self.cp("act", qa[:], qp_[:, 0:256], r=qk_, w=["qa"])
            self.act(sqb[:, 0:256], qa[:], AF.Square, r=["qa"], w=["sqb", "st2"], accum=st2[:, 4:5])
            rms_rows(st2[:, 5:6], st2[:, 4:5], 256, ["st2"])
            self.stt("dve", cq[:], qa[:], st2[:, 5:6], rowB[:, RB["qan"]:RB["qan"] + 256], ALU.mult, ALU.mult,
                     r=["qa", "st2", "rowB"], w=["cq"])
            tpb, tk_ = self.psb(2)
            for c in range(2):
                self.tr(tpb[:, c * 128:(c + 1) * 128], cq[:, c * 128:(c + 1) * 128], identb[:], r=["cq", "identb"], w=tk_)
            self.cp("act", cqT[:], tpb[:, 0:256].rearrange("p (c t) -> p c t", c=2), r=tk_, w=["cqT"])
            for hf in range(2):
                qq, qqk = self.ps(4)
                for c in range(2):
                    self.mm(qq[:, 0:384], cqT[:, c, :], wuq[:, c, hf * 384:(hf + 1) * 384], r=["cqT", "wuq"], w=qqk,
                            start=(c == 0), stop=(c == 1))
                self.cp("act", qf[:, hf * 384:(hf + 1) * 384], qq[:, 0:384], r=qqk, w=["qf"])
            q3 = qf[:].rearrange("p (h d) -> p h d", h=8)
            self.tt("dve", sqb[:].rearrange("p (h d) -> p h d", h=8), q3, q3, ALU.mult, r=["qf"], w=["sqb"])
            s3 = sqb[:].rearrange("p (h d) -> p h d", h=8)
            S.op("dve", lambda e: e.tensor_reduce(out=st2[:, 24:32], in_=s3[:, :, 0:64], axis=AX.X, op=ALU.add),
                 r=["sqb"], w=["st2"])
            S.op("dve", lambda e: e.tensor_reduce(out=st2[:, 32:40], in_=s3[:, :, 64:96], axis=AX.X, op=ALU.add),
                 r=["sqb"], w=["st2"])
            rms_rows(st2[:, 8:16], st2[:, 24:32], 64, ["st2"], ATT_SCALE)
            rms_rows(st2[:, 16:24], st2[:, 32:40], 32, ["st2"], ATT_SCALE)
            self.tt("dve", s3[:, :, 0:64], q3[:, :, 0:64], st2[:, 8:16].unsqueeze(2).to_broadcast([128, 8, 64]),
                    ALU.mult, r=["qf", "st2"], w=["sqb"])
            self.tt("dve", qn[:], s3[:, :, 0:64], rowB[:, RB["qn"]:RB["qn"] + 64].unsqueeze(1).to_broadcast([128, 8, 64]),
                    ALU.mult, r=["sqb", "rowB"], w=["qn"])
            self.tt("dve", s3[:, :, 64:96], q3[:, :, 64:96], st2[:, 16:24].unsqueeze(2).to_broadcast([128, 8, 32]),
                    ALU.mult, r=["qf", "st2"], w=["sqb"])
            self.tt("dve", qpf[:], s3[:, :, 64:96],
                    rowB[:, RB["qn"] + 64:RB["qn"] + 96].unsqueeze(1).to_broadcast([128, 8, 32]), ALU.mult,
                    r=["sqb", "rowB"], w=["qpf"])
            rope(qp[:], qpf[:], 8, ["qpf"], ["qp"])
            tpb, tk_ = self.psb(4)
            for g in range(4):
                self.tr(tpb[:, g * 128:(g + 1) * 128], qn[:, 2 * g:2 * g + 2, :].rearrange("p h d -> p (h d)"),
                        identb[:], r=["qn", "identb"], w=tk_)
            self.cp("act", QnB[0:64, :, 0:128], tpb[0:64, 0:512].rearrange("p (g t) -> p g t", g=4), r=tk_, w=["QnB"])
            self.cp("act", QnB[64:128, :, 128:256], tpb[64:128, 0:512].rearrange("p (g t) -> p g t", g=4), r=tk_, w=["QnB"])
            tpb, tk_ = self.psb(4)
            for g in range(4):
                self.tr(tpb[0:64, g * 128:(g + 1) * 128], qp[:, 2 * g:2 * g + 2, :].rearrange("p h d -> p (h d)"),
                        identb[:], r=["qp", "identb"], w=tk_)
            self.cp("act", QpB[0:32, :, 0:128], tpb[0:32, 0:512].rearrange("p (g t) -> p g t", g=4), r=tk_, w=["QpB"])
            self.cp("act", QpB[32:64, :, 128:256], tpb[32:64, 0:512].rearrange("p (g t) -> p g t", g=4), r=tk_, w=["QpB"])
            last_kb = 2 * i + 1
            work = [(g, kbs, mode) for g in range(4) for (kbs, mode) in kb_groups(i)]
            firsts = [True] * 8

            def stage1(n):
                g, kbs, mode = work[n]
                nb = len(kbs)
                sp_, sk_ = self.ps(4)
                for j, kb in enumerate(kbs):
                    self.mm(sp_[:, j * 256:(j + 1) * 256], KnT[:, g, kb * 128:(kb + 1) * 128], QnB[:, g, :],
                            r=["KnT", "QnB"], w=sk_, start=True, stop=False)
                    self.mm(sp_[:, j * 256:(j + 1) * 256], KpT[:, kb * 128:(kb + 1) * 128], QpB[:, g, :],
                            r=["KpT", "QpB"], w=sk_, start=False, stop=True)
                pt, ptk = PTt[n % 3], "PTt%d" % (n % 3)
                if mode == 2:
                    self.act(pt[:, 0:nb * 256], sp_[:, 0:nb * 256], AF.Exp, r=sk_ + ["negb"], w=[ptk], bias=negb[:, 0:1])
                else:
                    self.act(pt[:, 0:nb * 256], sp_[:, 0:nb * 256], AF.Exp, r=sk_, w=[ptk])
                if mode in (1, 3):
                    mt, mk_ = (MA, "MA") if mode == 1 else (MB, "MB")
                    self.tt("pool", pt[:, 0:256].rearrange("p (a q) -> p a q", a=2), pt[:, 0:256].rearrange("p (a q) -> p a q", a=2),
                            mt[:].unsqueeze(1).to_broadcast([128, 2, 128]), ALU.mult, r=[ptk, mk_], w=[ptk])

            def stage2(n):
                g, kbs, mode = work[n]
                pt, ptk = PTt[n % 3], "PTt%d" % (n % 3)
                for j, kb in enumerate(kbs):
                    for a in range(2):
                        h = 2 * g + a
                        ob = Ops[a]
                        oc = g * 65
                        self.mm(ob[:, oc:oc + 65], pt[:, j * 256 + a * 128:j * 256 + (a + 1) * 128], Vx[:, kb, h, :],
                                r=[ptk, "Vx"], w=["@b%d" % (6 + a)], start=firsts[h], stop=(kb == last_kb))
                        firsts[h] = False

            for n in range(len(work) + 2):
                if n < len(work):
                    stage1(n)
                if n >= 2:
                    stage2(n - 2)
            OA4 = OA[:].rearrange("p (g a) d -> p g a d", a=2)
            for bk in range(2):
                o3 = Ops[bk][:, 0:260].rearrange("p (h d) -> p h d", h=4)
                self.recip(orc[:, bk * 4:(bk + 1) * 4], o3[:, :, 64], r=["@b%d" % (6 + bk)], w=["orc"])
                self.tt("dve", OA4[:, :, bk, :], o3[:, :, 0:64],
                        orc[:, bk * 4:(bk + 1) * 4].unsqueeze(2).to_broadcast([128, 4, 64]), ALU.mult,
                        r=["@b%d" % (6 + bk), "orc"], w=["OA"])
            tpb, tk_ = self.psb(4)
            for g in range(4):
                self.tr(tpb[:, g * 128:(g + 1) * 128], OA[:, 2 * g:2 * g + 2, :].rearrange("p h d -> p (h d)"),
                        identb[:], r=["OA", "identb"], w=tk_)
            self.cp("act", OT[:], tpb[:, 0:512].rearrange("p (g t) -> p g t", g=4), r=tk_, w=["OT"])
            self.ts("dve", ZT[:], ZTa[:], selm[:, 0:1], None, ALU.mult, None, r=["ZTa", "selmB"], w=["ZT"])
            self.stt("dve", ZT[:], ZTb[:], selm[:, 1:2], ZT[:], ALU.mult, ALU.add, r=["ZTb", "selmB", "ZT"], w=["ZT"])
            for m in range(8):
                for (mm_, gs_, gsk) in ((m, gsa, "gsa"), (8 + m, gsb, "gsb")):
                    gp, gk = self.ps(1)
                    for k in range(8):
                        self.mm(gp[:, :], wi[:, k, 416 + mm_ * 128:416 + (mm_ + 1) * 128], hT[:, k, 2:130], r=["wi", hk],
                                w=gk, start=(k == 0), stop=(k == 7))
                    self.act(gs_[:], gp[:, :], AF.Sigmoid, r=gk, w=[gsk])
                p1, k1 = self.ps(1)
                for k in range(4):
                    self.mm(p1[:, :], womla[:, k, m * 128:(m + 1) * 128], OT[:, k, :], r=["womla", "OT"], w=k1,
                            start=(k == 0), stop=(k == 3))
                self.tt("dve", g1[:], p1[:, :], gsa[:], ALU.mult, r=k1 + ["gsa"], w=["g1"])
                p2, k2 = self.ps(1)
                for k in range(4):
                    self.mm(p2[:, :], worw[:, k, m * 128:(m + 1) * 128], ZT[:, k, :], r=["worw", "ZT"], w=k2,
                            start=(k == 0), stop=(k == 3))
                self.tt("dve", g2[:], p2[:, :], gsb[:], ALU.mult, r=k2 + ["gsb"], w=["g2"])
                self.tt("dve", GTt[:, m, :], g1[:], g2[:], ALU.add, r=["g1", "g2"], w=["GTt"])
            for hf in range(2):
                mp, mk = self.ps(4)
                for m in range(8):
                    self.mm(mp[:, :], GTt[:, m, :], wout[:, m, hf * 512:(hf + 1) * 512], r=["GTt", "wout"], w=mk,
                            start=(m == 0), stop=(m == 7))
                self.tt("dve", xtmp[:], mp[:, :], self.gtb[:, hf * 512:(hf + 1) * 512], ALU.mult, r=mk + ["gtb"],
                        w=["xtmp"])
                self.tt("dve", xt[:, hf * 512:(hf + 1) * 512], xt[:, hf * 512:(hf + 1) * 512], xtmp[:], ALU.add,
                        r=[xk, "xtmp"], w=[xk])
            self.ld(x1scr[i * 128:(i + 1) * 128, :], xt[:], r=[xk], w=[])

    def phaseC(self, x1scr, gscr, out):
        sb, din, S = self.sb, self.din, self.S
        NT = int(os.environ.get("NT", "16"))
        TOK = NT * 128
        from_x = "B" not in self.phases
        src = din["xf"] if from_x else x1scr
        ident = self.ident
        selm = sb("selm", [128, 2], F32)
        self.ld(selm[:], din["selm"][:, :], w=["selm"])
        with ExitStack() as stw:
            rowC = sb("rowC", [128, 1024], F32, stw)
            self.ld(rowC[:], din["rowC"][:, :], w=["rowC"])
            self.tt("dve", self.gtb[:, 1024:2048], self.gtb[:, 1024:2048], rowC[:], ALU.add, r=["gtb", "rowC"], w=["gtb"])
            S.barrier()
        H2 = sb("H2", [128, 8, TOK], BF16)
        xa = None

        def load_sel(tt, dst, dkey):
            self.ld(dst[:], src[tt * 128:(tt + 1) * 128, :], w=[dkey])
            if not from_x:
                return
            self.ld(xa[:], src[2048 + tt * 128:2048 + (tt + 1) * 128, :], w=["xa"])
            self.ts("dve", dst[:], dst[:], selm[:, 0:1], None, ALU.mult, None, r=[dkey, "selm"], w=[dkey])
            self.stt("dve", dst[:], xa[:], selm[:, 1:2], dst[:], ALU.mult, ALU.add, r=["xa", "selm", dkey], w=[dkey])

        self.banks1 = (0, 1, 2, 3, 4, 6, 7)
        with ExitStack() as st1:
            self.st = st1
            self.alloc_h(2)
            if from_x:
                xa = sb("xa", [128, D], F32)
            wq = sb("wq", [128, 8, 2048], BF16)
            skT = sb("skT", [128, 2, 128], BF16)
            iota = sb("iota", [128, 128], F32)
            self.ld(iota[:], din["iota"][:, :], w=["iota"])
            with ExitStack() as stw:
                stg = [sb("stgC%d" % j, [128, 2048], F32, stw) for j in range(2)]
                sk = ["stgC0", "stgC1"]
                self.load_w_bf16(wq, "wq", lambda k: din["w_query"][k * 128:(k + 1) * 128, :], 2048, stg, sk, 8)
                self.ld(stg[0][:, 0:128], din["sk1T"][:, :], w=[sk[0]])
                self.ld(stg[1][:, 0:128], din["sk2T"][:, :], w=[sk[1]])
                self.cp("dve", skT[:, 0, :], stg[0][:, 0:128], r=[sk[0]], w=["skT"])
                self.cp("dve", skT[:, 1, :], stg[1][:, 0:128], r=[sk[1]], w=["skT"])
                S.barrier()
            qTs = [sb("qT%d" % j, [128, 16, 128], BF16) for j in range(2)]
            scs = [sb("sc%d" % j, [128, 16, 128], F32) for j in range(2)]
            scrA = sb("scrA", [128, 16, 128], F32)
            scrB = sb("scrB", [128, 8, 256], F32)
            V16 = sb("V16", [128, 16, 16], F32)
            I16u = sb("I16u", [128, 16, 16], U32)
            I16 = sb("I16", [128, 16, 16], F32)
            Asel = sb("Asel", [128, 128], F32)
            Bsel = sb("Bsel", [128, 128], F32)
            Wsel = sb("Wsel", [128, 128], F32)
            ABW = sb("ABW", [128, 3, 128], F32)
            Aoh = [sb("Aoh%d" % j, [128, 128], BF16) for j in range(2)]
            Boh = [sb("Boh%d" % j, [128, 128], BF16) for j in range(2)]
            GT = sb("GT", [128, 128, 128], BF16)
            cand8 = sb("cand8", [128, 8, 256], F32)
            tv8 = sb("tv8", [128, 8, 16], F32)
            posu8 = sb("posu8", [128, 8, 16], U32)
            ju = sb("ju", [128, 2, 8, 16], U32)
            jf = sb("jf", [128, 2, 8, 16], F32)
            E8 = sb("E8", [128, 2048], F32)
            e8 = sb("e8", [128, 8, 16], F32)
            s8 = sb("s8", [128, 16], F32)
            AohB = [sb("AohB%d" % j, [128, 16, 128], BF16) for j in range(2)]
            BohB = [sb("BohB%d" % j, [128, 16, 128], BF16) for j in range(2)]
            def front(tt):
                b = tt % 2
                hT, hk = self.make_h(tt, None, self.sc2, self.modc[:, 16:24], ["sc2", "modc"], carry=False, evac_act=True)
                if tt + 2 < NT:
                    load_sel(tt + 2, self.xt[b], "xt%d" % b)
                self.cp("pool", H2[:, :, tt * 128:(tt + 1) * 128], hT[:, :, 2:130], r=[hk], w=["H2"])
                qT, qTk = qTs[b], "qT%d" % b
                sc, sck = scs[b], "sc%d" % b
                for m in range(16):
                    qp_, qk_ = self.ps(1)
                    for k in range(8):
                        self.mm(qp_[:, :], wq[:, k, m * 128:(m + 1) * 128], hT[:, k, 2:130], r=["wq", hk], w=qk_,
                                start=(k == 0), stop=(k == 7))
                    self.cp("act", qT[:, m, :], qp_[:, :], r=qk_, w=[qTk])
                for m4 in range(4):
                    sp_, sk_ = self.ps(4)
                    for j in range(4):
                        m = m4 * 4 + j
                        self.mm(sp_[:, j * 128:(j + 1) * 128], qT[:, m, :], skT[:, m % 2, :], r=[qTk, "skT"], w=sk_)
                    self.cp("act", sc[:, m4 * 4:(m4 + 1) * 4, :], sp_[:, :].rearrange("p (a b) -> p a b", a=4), r=sk_,
                            w=[sck])

            load_sel(0, self.xt[0], "xt0")
            if NT > 1:
                load_sel(1, self.xt[1], "xt1")
            front(0)
            for tt in range(NT):
                if tt + 1 < NT:
                    front(tt + 1)
                sc, sck = scs[tt % 2], "sc%d" % (tt % 2)

                def top16_multi(items):
                    for (vo, io, sa, sc_, tg, ks) in items:
                        S.op("dve", lambda e, vo=vo, sa=sa: e.max(out=vo[:, 0:8], in_=sa), r=ks, w=["tkv" + tg])
                    for (vo, io, sa, sc_, tg, ks) in items:
                        S.op("dve", lambda e, vo=vo, io=io, sa=sa: e.max_index(out=io[:, 0:8], in_max=vo[:, 0:8], in_values=sa),
                             r=ks + ["tkv" + tg], w=["tki" + tg])
                    for (vo, io, sa, sc_, tg, ks) in items:
                        S.op("dve", lambda e, vo=vo, sa=sa, sc_=sc_: e.match_replace(out=sc_, in_to_replace=vo[:, 0:8],
                                                                                 in_values=sa, imm_value=-1e30),
                             r=ks + ["tkv" + tg], w=["tks" + tg])
                    for (vo, io, sa, sc_, tg, ks) in items:
                        S.op("dve", lambda e, vo=vo, sc_=sc_: e.max(out=vo[:, 8:16], in_=sc_), r=["tks" + tg], w=["tkv" + tg])
                    for (vo, io, sa, sc_, tg, ks) in items:
                        S.op("dve", lambda e, vo=vo, io=io, sc_=sc_: e.max_index(out=io[:, 8:16], in_max=vo[:, 8:16], in_values=sc_),
                             r=["tks" + tg, "tkv" + tg], w=["tki" + tg])

                top16_multi([(V16[:, m, :], I16u[:, m, :], sc[:, m, :], scrA[:, m, :], "a%d" % m, [sck]) for m in range(16)])
                kA_v = ["tkva%d" % m for m in range(16)]
                kA_i = ["tkia%d" % m for m in range(16)]
                self.cp("pool", I16[:], I16u[:], r=kA_i, w=["I16"])
                V4 = V16[:].rearrange("p (h two) k -> p h two k", two=2)
                I4 = I16[:].rearrange("p (h two) k -> p h two k", two=2)
                self.tt("dve", cand8[:].rearrange("p h (a b) -> p h a b", a=16),
                        V4[:, :, 0, :].unsqueeze(3).to_broadcast([128, 8, 16, 16]),
                        V4[:, :, 1, :].unsqueeze(2).to_broadcast([128, 8, 16, 16]), ALU.add, r=kA_v, w=["cand8"])
                top16_multi([(tv8[:, h, :], posu8[:, h, :], cand8[:, h, :], scrB[:, h, :], "b%d" % h, ["cand8"]) for h in range(8)])
                kB_v = ["tkvb%d" % h for h in range(8)]
                kB_i = ["tkib%d" % h for h in range(8)]
                S.op("dve", lambda e: e.tensor_single_scalar(out=ju[:, 0], in_=posu8[:], scalar=4, op=ALU.logical_shift_right),
                     r=kB_i, w=["ju"])
                S.op("dve", lambda e: e.tensor_single_scalar(out=ju[:, 1], in_=posu8[:], scalar=15, op=ALU.bitwise_and),
                     r=kB_i, w=["ju"])
                self.cp("dve", jf[:], ju[:], r=["ju"], w=["jf"])
                io16 = iota[:, 0:16].unsqueeze(1).unsqueeze(1).to_broadcast([128, 8, 16, 16])
                for w_, dst in ((0, Asel), (1, Bsel)):
                    e4 = E8[:].rearrange("p (h k j) -> p h k j", h=8, k=16)
                    self.tt("dve", e4, io16, jf[:, w_].unsqueeze(3).to_broadcast([128, 8, 16, 16]), ALU.is_equal,
                            r=["iota", "jf"], w=["E8"])
                    self.tt("pool" if w_ == 0 else "dve", e4, e4, I4[:, :, w_, :].unsqueeze(2).to_broadcast([128, 8, 16, 16]), ALU.mult,
                            r=["E8", "I16"], w=["E8"])
                    S.op("dve", lambda e, dst=dst, e4=e4: e.tensor_reduce(out=dst[:].rearrange("p (h k) -> p h k", h=8), in_=e4,
                                                                        axis=AX.X, op=ALU.add), r=["E8"], w=[dst.name if False else ("Asel" if dst is Asel else "Bsel")])
                self.tt("dve", e8[:], tv8[:], tv8[:, :, 0:1].to_broadcast([128, 8, 16]), ALU.subtract, r=kB_v, w=["e8"])
                self.act(e8[:], e8[:], AF.Exp, r=["e8"], w=["e8"])
                S.op("dve", lambda e: e.tensor_reduce(out=s8[:, 0:8], in_=e8[:], axis=AX.X, op=ALU.add), r=["e8"], w=["s8"])
                self.recip(s8[:, 8:16], s8[:, 0:8], r=["s8"], w=["s8"])
                self.tt("dve", Wsel[:].rearrange("p (h k) -> p h k", h=8), e8[:], s8[:, 8:16].unsqueeze(2).to_broadcast([128, 8, 16]),
                        ALU.mult, r=["e8", "s8"], w=["Wsel"])
                tp, tk_ = self.ps(4)
                self.tr(tp[:, 0:128], Asel[:], ident[:], r=["Asel", "ident"], w=tk_)
                self.tr(tp[:, 128:256], Bsel[:], ident[:], r=["Bsel", "ident"], w=tk_)
                self.tr(tp[:, 256:384], Wsel[:], ident[:], r=["Wsel", "ident"], w=tk_)
                self.cp("act", ABW[:], tp[:, 0:384].rearrange("p (a b) -> p a b", a=3), r=tk_, w=["ABW"])
                iob = iota[:].unsqueeze(1).to_broadcast([128, 16, 128])
                for tb in range(8):
                    t0 = tb * 16
                    ao, aok = AohB[tb % 2], "AohB%d" % (tb % 2)
                    bo, bok = BohB[tb % 2], "BohB%d" % (tb % 2)
                    self.tt("dve", ao[:], iob, ABW[:, 0, t0:t0 + 16].unsqueeze(2).to_broadcast([128, 16, 128]), ALU.is_equal,
                            r=["iota", "ABW"], w=[aok])
                    self.tt("dve", bo[:], iob, ABW[:, 1, t0:t0 + 16].unsqueeze(2).to_broadcast([128, 16, 128]), ALU.is_equal,
                            r=["iota", "ABW"], w=[bok])
                    self.tt("pool" if tb % 2 == 0 else "dve", bo[:], bo[:],
                            ABW[:, 2, t0:t0 + 16].unsqueeze(2).to_broadcast([128, 16, 128]), ALU.mult, r=[bok, "ABW"], w=[bok])
                    for q4 in range(4):
                        gp, gk = self.ps(4)
                        for j in range(4):
                            tl = q4 * 4 + j
                            self.mm(gp[:, j * 128:(j + 1) * 128], ao[:, tl, :], bo[:, tl, :], r=[aok, bok], w=gk)
                        tg = t0 + q4 * 4
                        self.cp("act", GT[:, :, tg:tg + 4], gp[:, :].rearrange("p (t i) -> p i t", t=4), r=gk, w=["GT"])
                self.ld(gscr[tt, :, :], GT[:].rearrange("p a b -> p (a b)"), r=["GT"], w=["gscr"])
            S.barrier()
        with ExitStack() as st2:
            self.st = st2
            if from_x:
                xa = sb("xa2", [128, D], F32)
            ACC = sb("ACC", [128, NT, D], F32)
            xtCs = [sb("xtC%d" % j, [128, D], F32) for j in range(2)]
            DTbs = [sb("DTb%d" % j, [128, 4, 8, 128], BF16) for j in range(2)]
            UPbs = [sb("UPb%d" % j, [128, 4, D], BF16) for j in range(2)]
            GSl = sb("GSl", [128, NT, 512], BF16)
            Wts = [sb("Wt%d" % j, [128, 4, TOK], BF16) for j in range(2)]
            g_us = [sb("g_u%d" % j, [128, 512], F32) for j in range(2)]
            g_ss = [sb("g_s%d" % j, [128, 512], F32) for j in range(2)]
            gctr = [0]
            eupv = din["eup"].rearrange("(a b) d -> a b d", b=128)
            NCH = (TOK + 511) // 512
            chunks = [(l, tc) for tc in range(NCH) for l in range(4)]
            SQ = float(np.sqrt(0.044715))
            bctr = {"d": 0, "u": 0}

            def bank_of(kind):
                banks = (0, 1, 2, 3) if kind == "d" else (4, 6, 7)
                b = banks[bctr[kind] % len(banks)]
                bctr[kind] += 1
                return self.pbank[b][:, :], ["@b%d" % b]

            def pf_load(g, p):
                if p < 4:
                    self.S.dma("pool", DTbs[g % 2][:, p, :, :], din["edT"][g * 4 + p].rearrange("(k p) i -> p k i", p=128),
                               w=["DTb%d_%d" % (g % 2, p)])
                else:
                    u2 = p - 4
                    self.S.dma("pool", UPbs[g % 2][:, u2, :], eupv[:, g * 4 + u2, :], w=["UPb%d_%d" % (g % 2, u2)])

            def pf_cast(g, p):
                pass

            def load_gsl(g):
                for tt in range(NT):
                    self.ld(GSl[:, tt, :], gscr[tt, :, g * 512:(g + 1) * 512], r=["gscr"], w=["GSl%d" % tt])

            pend = []

            def down_chunk(g, ci):
                l, tc = chunks[ci]
                DTb, dk = DTbs[g % 2], "DTb%d" % (g % 2)
                Wt, wk = Wts[g % 2], "Wt%d" % (g % 2)
                n = min(512, TOK - tc * 512)
                ap_, ak = bank_of("d")
                for k in range(8):
                    self.mm(ap_[:, 0:n], DTb[:, l, k, :], H2[:, k, tc * 512:tc * 512 + n], r=[dk + "_%d" % l, "H2"], w=ak,
                            start=(k == 0), stop=(k == 7))
                gb_ = gctr[0] % 2
                gctr[0] += 1
                g_u, g_s, guk, gsk = g_us[gb_], g_ss[gb_], "g_u%d" % gb_, "g_s%d" % gb_
                self.act(g_u[:, 0:n], ap_[:, 0:n], AF.Square, r=ak, w=[guk], scale=SQ)
                self.stt("dve", g_u[:, 0:n], g_u[:, 0:n], 1.0, ap_[:, 0:n], ALU.add, ALU.mult, r=ak + [guk], w=[guk])
                self.act(g_s[:, 0:n], g_u[:, 0:n], AF.Sigmoid, r=[guk], w=[gsk], scale=1.5957691216057308)
                pend.append((l, tc, n, ap_, ak, g_s, gsk, Wt, wk))

            def down_finish():
                if not pend:
                    return
                l, tc, n, ap_, ak, g_s, gsk, Wt, wk = pend.pop(0)
                gsl = GSl[:, tc * 4:tc * 4 + (n // 128), l * 128:(l + 1) * 128]
                self.tt("pool", g_s[:, 0:n].rearrange("p (a b) -> p a b", b=128), g_s[:, 0:n].rearrange("p (a b) -> p a b", b=128),
                        gsl, ALU.mult, r=[gsk] + ["GSl%d" % t_ for t_ in range(tc * 4, tc * 4 + (n // 128))], w=[gsk])
                self.tt("dve", Wt[:, l, tc * 512:tc * 512 + n], ap_[:, 0:n], g_s[:, 0:n], ALU.mult, r=ak + [gsk], w=[wk])

            def up_tile(g, tt):
                UPb, uk_ = UPbs[g % 2], "UPb%d" % (g % 2)
                Wt, wk = Wts[g % 2], "Wt%d" % (g % 2)
                for hf in range(2):
                    op_, ok_ = bank_of("u")
                    for l in range(4):
                        self.mm(op_[:, :], Wt[:, l, tt * 128:(tt + 1) * 128], UPb[:, l, hf * 512:(hf + 1) * 512],
                                r=[wk, uk_ + "_%d" % l], w=ok_, start=(l == 0), stop=(l == 3))
                    if g == 0:
                        self.cp("dve", ACC[:, tt, hf * 512:(hf + 1) * 512], op_[:, :], r=ok_, w=["ACC%d" % tt])
                    else:
                        self.tt("dve", ACC[:, tt, hf * 512:(hf + 1) * 512], ACC[:, tt, hf * 512:(hf + 1) * 512], op_[:, :],
                                ALU.add, r=ok_ + ["ACC%d" % tt], w=["ACC%d" % tt])

            for p in range(8):
                pf_load(0, p)
                pf_cast(0, p)
            for p in range(4):
                pf_load(1, p)
                pf_cast(1, p)
            load_gsl(0)
            for ci in range(len(chunks)):
                down_chunk(0, ci)
                down_finish()
            for g in range(32):
                if g + 1 < 32:
                    load_gsl(g + 1)
                pieces = []
                if g + 1 < 32:
                    pieces += [(g + 1, p) for p in range(4, 8)]
                if g + 2 < 32:
                    pieces += [(g + 2, p) for p in range(4)]
                nsteps = max(len(chunks), NT, 2 * len(pieces))
                for s_ in range(nsteps):
                    if s_ % 2 == 0 and s_ // 2 < len(pieces):
                        pf_load(*pieces[s_ // 2])
                    if s_ % 2 == 1 and s_ // 2 < len(pieces):
                        pf_cast(*pieces[s_ // 2])
                    if g + 1 < 32 and s_ < len(chunks):
                        down_chunk(g + 1, s_)
                    if s_ < NT:
                        up_tile(g, s_)
                    if len(pend) > 1 or s_ == nsteps - 1:
                        down_finish()
                while pend:
                    down_finish()
            for tt in range(NT):
                b = 0
                xt, xk = xtCs[tt % 2], "xtC%d" % (tt % 2)
                load_sel(tt, xt, xk)
                self.tt("pool", ACC[:, tt, :], ACC[:, tt, :], self.gtb[:, 1024:2048], ALU.mult, r=["ACC%d" % tt, "gtb"],
                        w=["ACC%d" % tt])
                self.tt("dve", xt[:], xt[:], ACC[:, tt, :], ALU.add, r=[xk, "ACC%d" % tt], w=[xk])
                self.ld(out[tt * 128:(tt + 1) * 128, :], xt[:], r=[xk], w=[])


def _col(v):
    v = np.asarray(v, np.float32).reshape(-1)
    return np.ascontiguousarray(v.reshape(-1, 128).T)


def _rep(v):
    v = np.asarray(v, np.float32).reshape(-1)
    return np.ascontiguousarray(np.broadcast_to(v[None, :], (128, v.size)))


def prep(inp, nblk=32):
    f = lambda a: np.ascontiguousarray(np.asarray(a, np.float32))
    x = f(inp["x"])
    c = f(inp["c"])
    pos = np.asarray(inp["positions"], np.int32)
    b_ada = f(inp["b_ada"])[0]
    t = np.arange(128)
    msl = (t[None, :] < t[:, None]).astype(np.float32)
    msu = (t[None, :] > t[:, None]).astype(np.float32)
    mui = (t[None, :] >= t[:, None]).astype(np.float32)
    bones = np.kron(np.eye(2, dtype=np.float32), np.ones((64, 64), np.float32))
    half = 16
    invf = (10000.0 ** (-np.arange(half, dtype=np.float32) / half)).astype(np.float32)
    shared = dict(
        rowA=_rep(inp["mu_shift"]),
        ident=np.eye(128, dtype=np.float32), msl=msl, mg1=np.concatenate([msu, -mui], 1),
        mg2=np.concatenate([msu, mui], 1), bones=bones, caus=mui.copy(),
        iota=_rep(np.arange(128, dtype=np.float32)),
        w_ada=f(inp["w_ada"])[0], w_in=f(inp["w_in"])[0], w_uq=f(inp["w_uq"])[0], w_ukv=f(inp["w_ukv"])[0],
        w_o_mla=f(inp["w_o_mla"])[0],
        wl=np.ascontiguousarray(np.concatenate([f(inp["w_decay_up"])[0], f(inp["w_aaa_up"])[0]], 0)),
        w_gate_up=f(inp["w_gate_up"])[0], w_o_rwkv=f(inp["w_o_rwkv"])[0], w_out=f(inp["w_out"])[0],
        w_query=f(inp["w_query"])[0],
        sk1T=np.ascontiguousarray(f(inp["sub_keys1"])[0].T), sk2T=np.ascontiguousarray(f(inp["sub_keys2"])[0].T),
        edT=np.ascontiguousarray(f(inp["expert_down"])[0].reshape(128, 128, D).transpose(1, 2, 0)),
        eup=f(inp["expert_up"])[0],
        rowC=_rep(b_ada[5 * D:6 * D]),
    )
    maps = []
    for core in range(8):
        b, hf = core // 2, core % 2
        colp = np.concatenate([
            _col(c[b]), _col(b_ada[0:D]), _col(b_ada[D:2 * D]), _col(b_ada[3 * D:4 * D]), _col(b_ada[4 * D:5 * D]),
            _col(inp["norm_mix"]), _col(inp["norm_ffn"]), _col(inp["decay_base"]), _col(inp["aaa_base"]),
            _col(inp["k_k"]), _col(inp["k_a"]), _col(inp["r_k"]), _col(inp["ln_x_w"]), _col(inp["ln_x_b"])], 1)
        rowB = np.concatenate([_rep(inp["q_a_norm"]), _rep(inp["kv_a_norm"]), _rep(inp["q_norm"]), _rep(inp["k_norm"]),
                               _rep(invf / np.float32(2 * np.pi)), _rep(b_ada[2 * D:3 * D])], 1)
        selm = np.zeros((128, 2), np.float32)
        selm[:, hf] = 1.0
        m = dict(shared)
        nh = (nblk // 2) * 128
        xo = np.zeros((2048, D), np.float32)
        own = np.concatenate([np.arange((2 * j + hf) * 128, (2 * j + hf + 1) * 128) for j in range(nblk // 2)])
        xo[:nh] = x[b, own]
        po = np.zeros((2048,), np.int32)
        po[:nh] = pos[b, own]
        m.update(xown=xo, poso=np.ascontiguousarray(po.reshape(16, 128).T), xf=x[b], colp=np.ascontiguousarray(colp), rowB=np.ascontiguousarray(rowB),
                 posc=np.ascontiguousarray(pos[b].reshape(32, 128).T), selm=selm)
        maps.append(m)
    return maps


_CACHE = {}


def kernel(**inputs):
    if "k" not in _CACHE:
        _CACHE["k"] = K()
    kb = _CACHE["k"]
    maps = prep(inputs)
    res = run_bass_kernel_spmd(kb.nc, maps, core_ids=list(range(8)))
    outp = np.zeros((4, SEQ, D), np.float32)
    for core in range(8):
        b, hf = core // 2, core % 2
        own = np.concatenate([np.arange((2 * j + hf) * 128, (2 * j + hf + 1) * 128) for j in range(16)])
        outp[b, own] = res.results[core]["out"]
    return outp
```

```python
import os
import numpy as np
from contextlib import ExitStack
import concourse.bass as bass
import concourse.mybir as mybir
from concourse.bass_utils import run_bass_kernel_spmd

F32 = mybir.dt.float32
BF16 = mybir.dt.bfloat16
I32 = mybir.dt.int32
U32 = mybir.dt.uint32
F32R = mybir.dt.float32r if os.environ.get("NO_F32R") is None else mybir.dt.float32
AF = mybir.ActivationFunctionType
ALU = mybir.AluOpType
AX = mybir.AxisListType

ENGS = ("pe", "act", "dve", "pool", "sp")
EPOCH = 20000
NDMA = 24
D = 1024
SEQ = 4096
C0 = float(np.exp(-0.5))
EPS = 1e-6
GN_EPS = 64e-5


class Sched:
    def __init__(self, nc, stack):
        self.nc = nc
        self.stack = stack
        self.ops = {e: [] for e in ENGS}
        self.cnt = {e: 0 for e in ENGS}
        self.sems = {}
        self.lastw = {}
        self.readers = {}
        self.seen = {e: {} for e in ENGS}
        self.dma_i = 0
        self.dma_cnt = [0] * NDMA

    def sem(self, key):
        if key not in self.sems:
            self.sems[key] = self.stack.enter_context(self.nc.semaphore("s_%s_%s" % key))
        return self.sems[key]

    def _tok(self, eng):
        c = self.cnt[eng]
        return ((eng, c // EPOCH), c % EPOCH + 1)

    def _deps(self, r, w):
        toks = []
        for b in list(r) + list(w):
            t = self.lastw.get(b)
            if t is not None:
                toks.append(t)
        for b in w:
            toks.extend(self.readers.get(b, ()))
        return toks

    def _filter(self, eng, toks):
        need = {}
        for (k, v) in toks:
            if eng == "pe" and k[0] == "pe":
                continue
            if self.seen[eng].get(k, 0) >= v:
                continue
            if need.get(k, 0) < v:
                need[k] = v
        for k, v in need.items():
            self.seen[eng][k] = v
        return [(self.sem(k), v) for k, v in need.items()]

    def _commit(self, tok, r, w):
        for b in w:
            self.lastw[b] = tok
            self.readers[b] = []
        for b in r:
            self.readers.setdefault(b, []).append(tok)

    def op(self, eng, fn, r=(), w=()):
        ex = [k for k in r if k.startswith("@")]
        if ex:
            w = list(w) + ex
            r = [k for k in r if not k.startswith("@")]
        waits = self._filter(eng, self._deps(r, w))
        tok = self._tok(eng)
        self.cnt[eng] += 1
        self.ops[eng].append((fn, waits, (self.sem(tok[0]), 1)))
        self._commit(tok, r, w)
        return tok

    def dma(self, eng, out, in_, r=(), w=()):
        k = self.dma_i % NDMA
        self.dma_i += 1
        toks = self._deps(r, w)
        if self.dma_cnt[k] > 0:
            toks.append((("dma", k), 16 * self.dma_cnt[k]))
        waits = self._filter(eng, toks)
        self.dma_cnt[k] += 1
        tok = (("dma", k), 16 * self.dma_cnt[k])
        fn = lambda e, out=out, in_=in_: e.dma_start(out=out, in_=in_)
        self.ops[eng].append((fn, waits, (self.sem(tok[0]), 16)))
        self._commit(tok, r, w)
        return tok

    def barrier(self):
        toks = []
        for e in ENGS:
            if self.cnt[e] > 0:
                c = self.cnt[e] - 1
                toks.append(((e, c // EPOCH), c % EPOCH + 1))
        for k in range(NDMA):
            if self.dma_cnt[k] > 0:
                toks.append((("dma", k), 16 * self.dma_cnt[k]))
        for e in ENGS:
            waits = [(self.sem(k), v) for (k, v) in toks if self.seen[e].get(k, 0) < v]
            for (k, v) in toks:
                self.seen[e][k] = max(self.seen[e].get(k, 0), v)
            if waits:
                self.ops[e].append((None, waits, None))
        self.lastw.clear()
        self.readers.clear()

    def emit(self):
        nc = self.nc
        ops = self.ops

        def run(e, lst):
            for (fn, waits, inc) in lst:
                for (s, v) in waits:
                    e.wait_ge(s, v)
                if fn is not None:
                    fn(e).then_inc(inc[0], inc[1])

        with nc.Block() as block:
            @block.sync
            def _(e):
                run(e, ops["sp"])

            @block.tensor
            def _(e):
                run(e, ops["pe"])

            @block.scalar
            def _(e):
                run(e, ops["act"])

            @block.vector
            def _(e):
                run(e, ops["dve"])

            @block.gpsimd
            def _(e):
                run(e, ops["pool"])
        self.ops = {e: [] for e in ENGS}


CO = dict(c=0, bsh1=8, bsc1=16, bsh2=24, bsc2=32, nmix=40, nffn=48, dbase=56, abase=60, kk=64, ka=68,
          rk=72, lnw=76, lnb=80)
NCOL = 84
RB = dict(qan=0, kvan=256, qn=384, kn=480, invf=576, bgt1=592)
NRB = 592 + 1024


class K:
    def __init__(self, nblk=32, dbg=False, phases="ABC"):
        self.nblk = nblk
        self.dbg = dbg
        self.phases = phases
        self.nc = bass.Bass("TRN2", target_bir_lowering=False)
        self.build()

    def dram(self, name, shape, dt, kind="ExternalInput"):
        return self.nc.dram_tensor(name, list(shape), dt, kind=kind).ap()

    def sb(self, name, shape, dt, st=None):
        self._uid = getattr(self, "_uid", 0) + 1
        return (st or self.st).enter_context(self.nc.sbuf_tensor("sb%d_%s" % (self._uid, name), list(shape), dt))

    def mm(self, out, lhsT, rhs, r, w, start=True, stop=True):
        self.S.op("pe", lambda e: e.matmul(out, lhsT=lhsT, rhs=rhs, start=start, stop=stop), r=r, w=w)

    def tr(self, out, in_, ident, r, w):
        self.S.op("pe", lambda e: e.transpose(out, in_, ident), r=r, w=w)

    def act(self, out, in_, func, r, w, bias=None, scale=None, accum=None, eng="act"):
        kw = {}
        if bias is not None:
            kw["bias"] = bias
        if scale is not None:
            kw["scale"] = scale
        if accum is not None:
            kw["accum_out"] = accum
        self.S.op("act", lambda e: e.activation(out=out, in_=in_, func=func, **kw), r=r, w=w)

    def tt(self, eng, out, in0, in1, op, r, w):
        self.S.op(eng, lambda e: e.tensor_tensor(out=out, in0=in0, in1=in1, op=op), r=r, w=w)

    def ts(self, eng, out, in0, s1, s2, op0, op1, r, w, accum=None):
        if s2 is None:
            self.S.op(eng, lambda e: e.tensor_single_scalar(out=out, in_=in0, scalar=s1, op=op0), r=r, w=w)
        elif accum is not None:
            self.S.op(eng, lambda e: e.tensor_scalar(out=out, in0=in0, scalar1=s1, scalar2=s2, op0=op0, op1=op1,
                                                     accum_out=accum), r=r, w=w)
        else:
            self.S.op(eng, lambda e: e.tensor_scalar(out=out, in0=in0, scalar1=s1, scalar2=s2, op0=op0, op1=op1),
                      r=r, w=w)

    def stt(self, eng, out, in0, scalar, in1, op0, op1, r, w, accum=None):
        if accum is None:
            self.S.op(eng, lambda e: e.scalar_tensor_tensor(out=out, in0=in0, scalar=scalar, in1=in1, op0=op0,
                                                            op1=op1), r=r, w=w)
        else:
            self.S.op(eng, lambda e: e.scalar_tensor_tensor(out=out, in0=in0, scalar=scalar, in1=in1, op0=op0,
                                                            op1=op1, accum_out=accum), r=r, w=w)

    def cp(self, eng, out, in_, r, w):
        if eng == "act":
            self.S.op("act", lambda e: e.copy(out=out, in_=in_), r=r, w=w)
        else:
            self.S.op(eng, lambda e: e.tensor_copy(out=out, in_=in_), r=r, w=w)

    def recip(self, out, in_, r, w):
        self.S.op("dve", lambda e: e.reciprocal(out=out, in_=in_), r=r, w=w)

    def memset(self, eng, ap, val, w):
        self.S.op(eng, lambda e: e.memset(ap, val), w=w)

    def ld(self, out, in_, w, r=(), eng="sp"):
        self.S.dma(eng, out, in_, r=r, w=w)

    def ps(self, n=1):
        banks = getattr(self, "banks1", (0, 1, 2, 3, 4))
        if not hasattr(self, "bank_last"):
            self.bank_last = {}
            self.bank_cnt = {}
            self.alloc_ctr = 0
        self.alloc_ctr += 1
        b = min(banks, key=lambda x: self.bank_last.get(x, 0))
        self.bank_last[b] = self.alloc_ctr
        c = self.bank_cnt.get((b, n), 0)
        self.bank_cnt[(b, n)] = c + 1
        off = (c % (4 // n)) * n
        return self.pbank[b][:, off * 128:(off + n) * 128], ["@b%d" % b]

    def psb(self, n):
        u = self.psb_ptr
        if u + n > 8:
            u = 0
        self.psb_ptr = u + n
        return self.ptb[:, u * 128:(u + n) * 128], ["@b5"]

    def build(self):
        nc = self.nc
        NB = self.nblk
        T = NB * 128
        din = {}
        for name, shape, dt in [
            ("xf", [SEQ, D], F32), ("xown", [2048, D], F32), ("poso", [128, 16], I32), ("colp", [128, NCOL], F32), ("rowA", [128, 1792], F32),
            ("rowB", [128, NRB], F32), ("rowC", [128, 1024], F32), ("posc", [128, 32], I32),
            ("ident", [128, 128], F32), ("msl", [128, 128], F32), ("mg1", [128, 256], F32),
            ("mg2", [128, 256], F32), ("bones", [128, 128], F32), ("caus", [128, 128], F32),
            ("iota", [128, 128], F32), ("selm", [128, 2], F32),
            ("w_ada", [D, 6 * D], F32), ("w_in", [D, 4256], F32), ("w_uq", [256, 768], F32),
            ("w_ukv", [128, 1024], F32), ("w_o_mla", [512, D], F32), ("wl", [128, 512], F32),
            ("w_gate_up", [128, 512], F32), ("w_o_rwkv", [512, D], F32), ("w_out", [D, D], F32),
            ("w_query", [D, 2048], F32), ("sk1T", [128, 128], F32), ("sk2T", [128, 128], F32),
            ("edT", [128, D, 128], F32), ("eup", [16384, D], F32),
        ]:
            din[name] = self.dram(name, shape, dt)
        self.din = din
        out = self.dram("out", [2048, D], F32, kind="ExternalOutput")
        zscr = self.dram("zscr", [512, SEQ], BF16, kind="Internal")
        x1scr = self.dram("x1scr", [SEQ, D], F32, kind="Internal")
        gscr = self.dram("gscr", [16, 128, 16 * 8 * 128], BF16, kind="Internal")
        self.dbg_out = {}
        if self.dbg:
            for name, shape in [("d_z", [512, SEQ]), ("d_x1", [SEQ, D]), ("d_misc", [128, 4096])]:
                self.dbg_out[name] = self.dram(name, shape, F32 if name != "d_z" else BF16, kind="ExternalOutput")

        with ExitStack() as st0:
            self.st = st0
            self.S = S = Sched(nc, st0)
            self.pbank = [st0.enter_context(nc.psum_tensor("pb%d" % i, [128, 512], F32)) for i in range(8) if i != 5]
            self.pbank.insert(5, None)
            self.ptb = st0.enter_context(nc.psum_tensor("ptb", [128, 1024], BF16))
            self.ps_ptr = 0
            self.psb_ptr = 0
            sb = self.sb
            ident = sb("ident", [128, 128], F32)
            identb = sb("identb", [128, 128], BF16)
            colp = sb("colp", [128, NCOL], F32)
            modc = sb("modc", [128, 32], F32)
            sc1 = sb("sc1", [128, 8], F32)
            sc2 = sb("sc2", [128, 8], F32)
            gtb = sb("gtb", [128, 2048], F32)
            ones = sb("ones", [128, 128], F32)
            self.scr1 = sb("scr1", [128, 2], F32)
            self.ld(ident[:], din["ident"][:, :], w=["ident"])
            self.ld(colp[:], din["colp"][:, :], w=["colp"])
            self.cp("dve", identb[:], ident[:], r=["ident"], w=["identb"])
            self.memset("pool", ones[:], 1.0, w=["ones"])
            self.ident, self.identb, self.colp, self.ones = ident, identb, colp, ones

            with ExitStack() as st:
                self.st = st
                silc = sb("silc", [128, 8], F32)
                self.act(silc[:], colp[:, 0:8], AF.Silu, r=["colp"], w=["silc"])
                wst = [sb("wst%d" % j, [128, 8, 1024], F32) for j in range(2)]
                gi = 0
                mps, mk = self.ps(1)
                for grp, c0 in enumerate([0, 1024, 3072, 4096]):
                    wt = wst[gi % 2]
                    wk = "wst%d" % (gi % 2)
                    gi += 1
                    for k in range(8):
                        self.ld(wt[:, k, :], din["w_ada"][k * 128:(k + 1) * 128, c0:c0 + 1024], w=[wk + "_%d" % k])
                    for j in range(8):
                        for k in range(8):
                            self.mm(mps[:, grp * 8 + j:grp * 8 + j + 1], wt[:, k, j * 128:(j + 1) * 128],
                                    silc[:, k:k + 1], r=[wk + "_%d" % k, "silc"], w=mk, start=(k == 0), stop=(k == 7))
                self.tt("dve", modc[:], mps[:, 0:32], colp[:, 8:40], ALU.add, r=mk + ["colp"], w=["modc"])
                self.stt("dve", sc1[:], modc[:, 8:16], 1.0, colp[:, CO["nmix"]:CO["nmix"] + 8], ALU.add, ALU.mult,
                         r=["modc", "colp"], w=["sc1"])
                self.stt("dve", sc2[:], modc[:, 24:32], 1.0, colp[:, CO["nffn"]:CO["nffn"] + 8], ALU.add, ALU.mult,
                         r=["modc", "colp"], w=["sc2"])
                grow = sb("grow", [1, 2048], F32)
                for grp, c0 in enumerate([2048, 5120]):
                    wt = wst[gi % 2]
                    wk = "wst%d" % (gi % 2)
                    gi += 1
                    for k in range(8):
                        self.ld(wt[:, k, :], din["w_ada"][k * 128:(k + 1) * 128, c0:c0 + 1024], w=[wk + "_%d" % k])
                    for hf in range(2):
                        rp, rk_ = self.ps(4)
                        for k in range(8):
                            self.mm(rp[0:1, :], silc[:, k:k + 1], wt[:, k, hf * 512:(hf + 1) * 512],
                                    r=[wk + "_%d" % k, "silc"], w=rk_, start=(k == 0), stop=(k == 7))
                        self.cp("act", grow[0:1, grp * 1024 + hf * 512: grp * 1024 + (hf + 1) * 512], rp[0:1, :],
                                r=rk_, w=["grow"])
                for q in range(4):
                    bp, bk = self.ps(4)
                    self.mm(bp[:, :], ones[0:1, :], grow[0:1, q * 512:(q + 1) * 512], r=["ones", "grow"], w=bk)
                    self.cp("act", gtb[:, q * 512:(q + 1) * 512], bp[:, :], r=bk, w=["gtb"])
                self.S.barrier()
                self.S.emit()
            self.modc, self.sc1, self.sc2, self.gtb = modc, sc1, sc2, gtb

            if "A" in self.phases:
                with ExitStack() as st:
                    self.st = st
                    self.phaseA(zscr)
                    self.S.barrier()
                    self.S.emit()
            if "B" in self.phases:
                with ExitStack() as st:
                    self.st = st
                    self._phase_st = st
                    self.phaseB(zscr, x1scr)
                    self.S.barrier()
                    self.S.emit()
            if "C" in self.phases:
                with ExitStack() as st:
                    self.st = st
                    self.phaseC(x1scr, gscr, out)
                    self.S.barrier()
                    self.S.emit()
            if self.dbg:
                with ExitStack() as st:
                    self.st = st
                    if "A" in self.phases:
                        self.S.dma("sp", self.dbg_out["d_z"][:, :], zscr[:, :])
                    if "B" in self.phases:
                        self.S.dma("sp", self.dbg_out["d_x1"][:, :], x1scr[:, :])
                    self.S.barrier()
                    self.S.emit()

    def alloc_h(self, nbuf=2):
        sb = self.sb
        self.xt = [sb("xt%d" % j, [128, D], F32) for j in range(nbuf)]
        if nbuf == 1:
            self.xt = [self.xt[0], self.xt[0]]
        self.xn = sb("xn", [128, D], BF16)
        self.junk = self.xn
        self.ss = sb("ss", [128, 2], F32)
        self.hT = [sb("hT%d" % j, [128, 8, 130], BF16) for j in range(nbuf)]
        if nbuf == 1:
            self.hT = [self.hT[0], self.hT[0]]
        self.nbuf = nbuf
        self.memset("pool", self.hT[0][:, :, 0:2], 0.0, w=["hT0"])
        self.epsc = sb("epsc", [128, 2], F32)
        self.memset("pool", self.epsc[:, 0:1], EPS, w=["epsc"])
        self.memset("pool", self.epsc[:, 1:2], GN_EPS, w=["epsc"])

    def make_h(self, i, src, scale, shift, skeys, carry=True, evac_act=False):
        b = i % self.nbuf
        xt, xk = self.xt[b], "xt%d" % b
        hk = "hT%d" % b
        if src is not None:
            self.ld(xt[:], src, w=[xk])
        self.memset("pool", self.ss[:, 0:1], 0.0, w=["ss"])
        self.act(self.junk[:], xt[:], AF.Square, r=[xk], w=["xn", "ss"], accum=self.ss[:, 0:1])
        self.act(self.ss[:, 1:2], self.ss[:, 0:1], AF.Sqrt, r=["ss", "epsc"], w=["ss"], bias=self.epsc[:, 0:1],
                 scale=1.0 / D)
        self.recip(self.ss[:, 1:2], self.ss[:, 1:2], r=["ss"], w=["ss"])
        self.ts("dve", self.xn[:], xt[:], self.ss[:, 1:2], None, ALU.mult, None, r=[xk, "ss"], w=["xn"])
        HS = int(os.environ.get("HSTOP", "99"))
        if HS <= 1:
            return self.hT[b], hk
        for half in range(2):
            ppb, pk = self.psb(4)
            for j in range(4):
                c = half * 4 + j
                self.tr(ppb[:, j * 128:(j + 1) * 128], self.xn[:, c * 128:(c + 1) * 128], self.identb[:],
                        r=["xn", "identb"], w=pk)
            if HS <= 2:
                continue
            for j in range(4):
                c = half * 4 + j
                if os.environ.get("EV", "") == "cp":
                    self.cp("dve", self.hT[b][:, c, 2:130], ppb[:, j * 128:(j + 1) * 128], r=pk, w=[hk])
                elif os.environ.get("EV", "") == "sb":
                    self.ts("dve", self.hT[b][:, c, 2:130], self.xn[:, 0:128],
                            scale[:, c:c + 1], shift[:, c:c + 1], ALU.mult, ALU.add, r=pk + skeys, w=[hk])
                elif evac_act:
                    self.act(self.hT[b][:, c, 2:130], ppb[:, j * 128:(j + 1) * 128], AF.Identity, r=pk + skeys, w=[hk],
                             bias=shift[:, c:c + 1], scale=scale[:, c:c + 1])
                else:
                    self.ts("dve", self.hT[b][:, c, 2:130], ppb[:, j * 128:(j + 1) * 128],
                            scale[:, c:c + 1], shift[:, c:c + 1], ALU.mult, ALU.add, r=pk + skeys, w=[hk])
        if carry and HS > 3:
            self.cp("pool", self.hT[1 - b][:, :, 1:2], self.hT[b][:, :, 129:130], r=[hk], w=["hT%d" % (1 - b)])
        return self.hT[b], hk

    def load_w_bf16(self, dst, dkey, src_rows, ncols, stg, skeys, kchunks):
        for k in range(kchunks):
            self.S.dma("pool", dst[:, k, :], src_rows(k), w=[dkey + "_k%d" % k])
        self.S.op("pool", lambda e: e.memset(self.scr1[0:1, 0:1], 0.0), r=[dkey + "_k%d" % k for k in range(kchunks)], w=[dkey, "scr1"])

    def phaseA(self, zscr):
        sb, din, S = self.sb, self.din, self.S
        NB = self.nblk
        colp = self.colp
        self.alloc_h()
        self.banks1 = (0, 1, 2, 3, 4, 7)
        wa = sb("wa", [128, 8, 1792], BF16)
        wb = sb("wb", [128, 8, 1792], BF16)
        with ExitStack() as stw:
            mu = sb("mu", [128, 1792], F32, stw)
            self.ld(mu[:], din["rowA"][:, :], w=["mu"])
            stg = [sb("stgA%d" % j, [128, 1792], F32, stw) for j in range(2)]
            tmpb = [sb("tmpbA%d" % j, [128, 1792], F32, stw) for j in range(2)]
            for k in range(8):
                s, sk = stg[k % 2], "stgA%d" % (k % 2)
                t, tk = tmpb[k % 2], "tmpbA%d" % (k % 2)
                self.ld(s[:], din["w_in"][k * 128:(k + 1) * 128, 416:2208], w=[sk])
                self.tt("dve", t[:], s[:], mu[:], ALU.mult, r=[sk, "mu"], w=[tk])
                self.cp("act", wb[:, k, :], t[:], r=[tk], w=["wb"])
                self.tt("dve", wa[:, k, :], s[:], t[:], ALU.subtract, r=[sk, tk], w=["wa"])
            S.barrier()
        wl = sb("wl", [128, 512], F32)
        wg = sb("wg", [128, 512], F32)
        msl = sb("msl", [128, 128], F32)
        mg1 = sb("mg1", [128, 256], F32)
        mg2 = sb("mg2", [128, 256], F32)
        bones = sb("bones", [128, 128], F32)
        for t_, n_ in [(wl, "wl"), (wg, "w_gate_up"), (msl, "msl"), (mg1, "mg1"), (mg2, "mg2"), (bones, "bones")]:
            self.ld(t_[:], din[n_][:, :], w=[n_])
        omka = sb("omka", [128, 4], F32)
        self.ts("dve", omka[:], colp[:, CO["ka"]:CO["ka"] + 4], -1.0, 1.0, ALU.mult, ALU.add, r=["colp"], w=["omka"])
        ident, ones = self.ident, self.ones
        identr = sb("identr", [128, 128], F32R)
        self.cp("dve", identr[:], ident[:], r=["ident"], w=["identr"])
        H = sb("H", [128, 4, 64], F32R)
        self.memset("pool", H[:].bitcast(F32), 0.0, w=["H%d_%d" % (g, a) for g in range(4) for a in range(2)])
        rkv = sb("rkv", [128, 12, 128], F32)
        ta = sb("ta", [128, 128], F32)
        sg = sb("sg", [128, 128], F32)
        G = [sb("G_%d" % j, [128, 4, 128], F32) for j in range(2)]
        BON = [sb("BON_%d" % j, [128, 4, 128], F32) for j in range(2)]
        GC = [sb("GC_%d" % j, [128, 4], F32) for j in range(2)]
        PREP_STEPS = int(os.environ.get("PREP_STEPS", "4"))
        sigG = [sb("sig_g%d" % g_, [128, 128], F32) for g_ in range(2)] * 2
        aaG = [sb("aa_g%d" % g_, [128, 128], F32) for g_ in range(2)] * 2
        kkG = [sb("kk_g%d" % g_, [128, 128], F32) for g_ in range(2)] * 2
        sqG = [sb("sq_g%d" % g_, [128, 128], F32) for g_ in range(2)] * 2
        rnG = [sb("rn_g%d" % g_, [128, 128], F32) for g_ in range(2)] * 2
        kapG = [sb("kap_g%d" % g_, [128, 128], F32) for g_ in range(2)] * 2
        tqG = [sb("tq_g%d" % g_, [128, 128], F32) for g_ in range(2)] * 2
        khG = [sb("kh_g%d" % g_, [128, 128], F32) for g_ in range(2)] * 2
        bbG = [sb("bb_g%d" % g_, [128, 128], F32) for g_ in range(2)] * 2
        rkrG = [sb("rkr_g%d" % g_, [128, 128], F32) for g_ in range(2)] * 2
        LsG = [sb("Ls_g%d" % g_, [128, 128], F32) for g_ in range(2)] * 2
        LxG = [sb("Lx_g%d" % g_, [128, 128], F32) for g_ in range(2)] * 2
        EpG = [sb("Ep_g%d" % g_, [128, 128], F32) for g_ in range(2)] * 2
        EnG = [sb("En_g%d" % g_, [128, 128], F32) for g_ in range(2)] * 2
        ExG = [sb("Ex_g%d" % g_, [128, 128], F32) for g_ in range(2)] * 2
        CM1 = [[sb("CM1_%d_%d" % (j, g), [128, 256], F32R) for g in range(4)] for j in range(2)]
        CM2 = [[sb("CM2_%d_%d" % (j, g), [128, 256], F32R) for g in range(4)] for j in range(2)]
        TM = [[sb("TM%d_%d" % (j, g), [128, 4, 128], F32R) for g in range(4)] for j in range(2)]
        WT = [sb("WT%d" % g, [128, 128], F32R) for g in range(4)]
        PPH = [[sb("PP%d_%d" % (j, h), [128, 256], F32R) for j in range(2)] for h in range(8)]
        PH = [[PPH[h][j][:, 0:128] for j in range(2)] for h in range(8)]
        PTH = [[PPH[h][j][:, 128:256] for j in range(2)] for h in range(8)]
        TTH = [sb("TT_%d" % h, [128, 128], F32R) for h in range(8)]
        AkTH = [sb("AkT_%d" % h, [128, 256], F32R) for h in range(8)]
        nBrH = [sb("nBr_%d" % h, [128, 128], F32R) for h in range(8)]
        X0H = [sb("X0_%d" % h, [128, 64], F32R) for h in range(8)]
        U0H = [sb("U0_%d" % h, [128, 64], F32R) for h in range(8)]
        UsbH = [sb("Usb_%d" % h, [128, 64], F32R) for h in range(8)]
        ysb4 = sb("ysb4", [128, 512], F32)
        cen4 = sb("cen4", [128, 512], F32)
        sq4 = sb("sq4", [128, 512], F32)
        rs4 = sb("rs4", [128, 512], F32)
        zt4 = [sb("zt4_%d" % j, [128, 512], BF16) for j in range(2)]

        for i0_ in range(min(2, NB)):
            self.ld(self.xt[i0_][:], din["xf"][i0_ * 128:(i0_ + 1) * 128, :], w=["xt%d" % i0_])

        def prep_gen(i):
            pb = i % 2
            hT, hk = self.make_h(i, None, self.sc1, self.modc[:, 0:8], ["sc1", "modc"])
            if i + 2 < NB:
                self.ld(self.xt[i % 2][:], din["xf"][(i + 2) * 128:(i + 3) * 128, :], w=["xt%d" % (i % 2)])
            STOP = int(os.environ.get("STOP", "99"))
            if STOP <= 1:
                return
            for c in range(14):
                pp, pk = self.ps(1)
                for k in range(8):
                    self.mm(pp[:, :], wa[:, k, c * 128:(c + 1) * 128], hT[:, k, 2:130], r=["wa", hk], w=pk,
                            start=(k == 0), stop=False)
                    self.mm(pp[:, :], wb[:, k, c * 128:(c + 1) * 128], hT[:, k, 1:129], r=["wb", hk], w=pk,
                            start=False, stop=(k == 7))
                if c < 12:
                    self.cp("act", rkv[:, c, :], pp[:, :], r=pk, w=["rkv%d" % c])
                    yield
                elif c == 12:
                    self.act(ta[0:64, :], pp[0:64, :], AF.Tanh, r=pk, w=["ta"])
                    self.cp("act", ta[64:128, :], pp[64:128, :], r=pk, w=["ta"])
                else:
                    self.act(sg[:, :], pp[:, :], AF.Sigmoid, r=pk, w=["sg"])
                    yield
            if STOP <= 2:
                return
            for g in range(4):
                gp, gk = self.ps(1)
                self.mm(gp[:, :], wg[:, g * 128:(g + 1) * 128], sg[:, :], r=["w_gate_up", "sg"], w=gk)
                self.cp("act", G[pb][:, g, :], gp[:, :], r=gk, w=["G%d_%d" % (pb, g)])
                yield
            if STOP <= 3:
                return
            def prep_g(g):
                rK, kK, vK = "rkv%d" % g, "rkv%d" % (4 + g), "rkv%d" % (8 + g)
                r_, k_, v_ = rkv[:, g, :], rkv[:, 4 + g, :], rkv[:, 8 + g, :]
                up, uk = self.ps(1)
                self.mm(up[:, :], wl[0:64, g * 128:(g + 1) * 128], ta[0:64, :], r=["wl", "ta"], w=uk)
                self.act(sigG[g][:], up[:, :], AF.Sigmoid, r=uk + ["colp"], w=["sig%d" % (g % 2)],
                         bias=colp[:, CO["dbase"] + g:CO["dbase"] + g + 1])
                ap_, ak = self.ps(1)
                self.mm(ap_[:, :], wl[64:128, g * 128:(g + 1) * 128], ta[64:128, :], r=["wl", "ta"], w=ak)
                self.act(aaG[g][:], ap_[:, :], AF.Sigmoid, r=ak + ["colp"], w=["aa%d" % (g % 2)],
                         bias=colp[:, CO["abase"] + g:CO["abase"] + g + 1])
                self.ts("dve", kkG[g][:], k_, colp[:, CO["kk"] + g:CO["kk"] + g + 1], None, ALU.mult, None,
                        r=[kK, "colp"], w=["kk%d" % (g % 2)])
                self.tt("pool", sqG[g][:], kkG[g][:], kkG[g][:], ALU.mult, r=["kk%d" % (g % 2)], w=["sq%d" % (g % 2)])
                yield
                sp_, sk_ = self.ps(1)
                self.mm(sp_[:, :], bones[:], sqG[g][:], r=["bones", "sq%d" % (g % 2)], w=sk_)
                self.act(rnG[g][:], sp_[:, :], AF.Sqrt, r=sk_, w=["rn%d" % (g % 2)])
                yield
                self.ts("dve", rnG[g][:], rnG[g][:], 1e-12, None, ALU.max, None, r=["rn%d" % (g % 2)], w=["rn%d" % (g % 2)])
                self.recip(rnG[g][:], rnG[g][:], r=["rn%d" % (g % 2)], w=["rn%d" % (g % 2)])
                self.tt("dve", kapG[g][:], kkG[g][:], rnG[g][:], ALU.mult, r=["kk%d" % (g % 2), "rn%d" % (g % 2)], w=["kap%d" % (g % 2)])
                yield
                self.ts("pool", tqG[g][:], aaG[g][:], colp[:, CO["ka"] + g:CO["ka"] + g + 1], omka[:, g:g + 1], ALU.mult,
                        ALU.add, r=["aa%d" % (g % 2), "colp", "omka"], w=["tq%d" % (g % 2)])
                self.tt("pool", khG[g][:], k_, tqG[g][:], ALU.mult, r=[kK, "tq%d" % (g % 2)], w=["kh%d" % (g % 2)])
                self.tt("dve", bbG[g][:], kapG[g][:], aaG[g][:], ALU.mult, r=["kap%d" % (g % 2), "aa%d" % (g % 2)], w=["bb%d" % (g % 2)])
                self.stt("dve", rkrG[g][:], r_, colp[:, CO["rk"] + g:CO["rk"] + g + 1], khG[g][:], ALU.mult, ALU.mult,
                         r=[rK, "colp", "kh%d" % (g % 2)], w=["rkr%d" % (g % 2)])
                yield
                bp, bk = self.ps(1)
                self.mm(bp[:, :], bones[:], rkrG[g][:], r=["bones", "rkr%d" % (g % 2)], w=bk)
                self.tt("dve", BON[pb][:, g, :], bp[:, :], v_, ALU.mult, r=bk + [vK], w=["BON%d_%d" % (pb, g)])
                yield
                if STOP <= 4:
                    return
                S.op("dve", lambda e: e.tensor_tensor_scan(out=LsG[g][:], data0=ones[:], data1=sigG[g][:], initial=0.0,
                                                          op0=ALU.mult, op1=ALU.add), r=["ones", "sig%d" % (g % 2)], w=["Ls%d" % (g % 2)])
                yield
                self.tt("pool", LxG[g][:], LsG[g][:], sigG[g][:], ALU.subtract, r=["Ls%d" % (g % 2), "sig%d" % (g % 2)], w=["Lx%d" % (g % 2)])
                self.act(EpG[g][:], LsG[g][:], AF.Exp, r=["Ls%d" % (g % 2)], w=["Ep%d" % (g % 2)], scale=-C0)
                self.act(EnG[g][:], LsG[g][:], AF.Exp, r=["Ls%d" % (g % 2)], w=["En%d" % (g % 2)], scale=C0)
                self.act(ExG[g][:], LxG[g][:], AF.Exp, r=["Lx%d" % (g % 2)], w=["Ex%d" % (g % 2)], scale=-C0)
                yield
                c1, c2 = CM1[pb][g], CM2[pb][g]
                c1k, c2k = "CM1_%d_%d" % (pb, g), "CM2_%d_%d" % (pb, g)
                self.tt("dve", c1[:, 0:128], kapG[g][:], ExG[g][:], ALU.mult, r=["kap%d" % (g % 2), "Ex%d" % (g % 2)], w=[c1k])
                self.tt("pool", c1[:, 128:256], r_, EpG[g][:], ALU.mult, r=[rK, "Ep%d" % (g % 2)], w=[c1k])
                self.tt("dve", c2[:, 0:128], bbG[g][:], EnG[g][:], ALU.mult, r=["bb%d" % (g % 2), "En%d" % (g % 2)], w=[c2k])
                self.tt("pool", c2[:, 128:256], khG[g][:], EnG[g][:], ALU.mult, r=["kh%d" % (g % 2), "En%d" % (g % 2)], w=[c2k])
                self.cp("pool", GC[pb][:, g:g + 1], EpG[g][:, 127:128], r=["Ep%d" % (g % 2)], w=["GC%d_%d" % (pb, g)])
                yield
                if STOP <= 5:
                    return
                tm, tmk = TM[pb][g], "TM%d_%d" % (pb, g)
                yield
                tp, tk_ = self.ps(4)
                self.tr(tp[:, 0:128], c1[:, 0:128].bitcast(F32), ident[:], r=[c1k, "ident"], w=tk_)
                self.tr(tp[:, 128:256], c2[:, 0:128].bitcast(F32), ident[:], r=[c2k, "ident"], w=tk_)
                self.tr(tp[:, 256:384], c2[:, 128:256].bitcast(F32), ident[:], r=[c2k, "ident"], w=tk_)
                self.tr(tp[:, 384:512], v_, ident[:], r=[vK, "ident"], w=tk_)
                self.cp("act", tm[:, 0, :], tp[:, 0:128], r=tk_, w=[tmk])
                self.ts("dve", tm[:, 1, :], tp[:, 128:256], -1.0, None, ALU.mult, None, r=tk_, w=[tmk])
                self.cp("act", tm[:, 2:4, :], tp[:, 256:512].rearrange("p (a b) -> p a b", a=2), r=tk_, w=[tmk])
                yield
                yield

            for g0 in (0, 2):
                subs = [prep_g(g0), prep_g(g0 + 1)]
                while subs:
                    for sg_ in list(subs):
                        try:
                            next(sg_)
                        except StopIteration:
                            subs.remove(sg_)
                    yield
            if STOP <= 6:
                return

            yield

        def head_gen(g, a, i):
            pb = i % 2
            STOP = int(os.environ.get("STOP", "99"))
            hx = 2 * g + a
            sx = "_%d" % hx
            c1, c2, tm = CM1[pb][g], CM2[pb][g], TM[pb][g]
            c1k, c2k, tmk = "CM1_%d_%d" % (pb, g), "CM2_%d_%d" % (pb, g), "TM%d_%d" % (pb, g)
            Pq, PTq, TTq, AkTq, nBrq = PH[hx], PTH[hx], TTH[hx], AkTH[hx], nBrH[hx]
            PPq = PPH[hx]
            X0q, U0q, Usbq = X0H[hx], U0H[hx], UsbH[hx]
            yp, yk = self.pbank[6][:, g * 128:(g + 1) * 128], ["@b6"]
            R = slice(64 * a, 64 * a + 64)
            Cc = slice(64 * a, 64 * a + 64)
            Hk = "H%d_%d" % (g, a)
            BC = (lambda ap_: ap_) if a == 0 else (lambda ap_: ap_.bitcast(F32))
            idm, idk = (identr, "identr") if a == 0 else (ident, "ident")
            a_ps, a_k = self.ps(1)
            self.mm(a_ps[:, :], c1[R, 0:128], c2[R, 0:128], r=[c1k, c2k], w=a_k)
            self.tt("dve", Pq[0][:], a_ps[:, :], msl[:], ALU.mult, r=a_k + ["msl"], w=["PP0" + sx])
            gb, gbk = self.ps(2)
            self.mm(gb[:, :], c2[R, 0:128], c1[R, 0:256], r=[c1k, c2k], w=gbk)
            self.tt("dve", PTq[0][:], gb[:, 0:128], mg1[:, 0:128], ALU.mult, r=gbk + ["mg1"], w=["PP0" + sx])
            self.tt("pool" if False else "dve", nBrq[:], gb[:, 128:256], mg1[:, 128:256], ALU.mult,
                    r=gbk + ["mg1"], w=["nBr" + sx])
            gk2, gkk = self.ps(2)
            self.mm(gk2[:, :], c2[R, 128:256], c1[R, 0:256], r=[c1k, c2k], w=gkk)
            self.tt("dve", AkTq[:], gk2[:, :], mg2[:], ALU.mult, r=gkk + ["mg2"], w=["AkT" + sx])
            self.tt("pool", TTq[:], ident[:], PTq[0][:], ALU.subtract, r=["ident", "PP0" + sx], w=["TT" + sx])
            yield
            if STOP <= 7:
                return
            cur = 0
            for j in range(1, 7):
                nxt = 1 - cur
                if j < 6:
                    pq_ps, pq_k = self.ps(2)
                    self.mm(pq_ps[:, 0:128], PTq[cur][:], Pq[cur][:], r=["PP%d" % cur + sx], w=pq_k)
                    self.mm(pq_ps[:, 128:256], Pq[cur][:], PTq[cur][:], r=["PP%d" % cur + sx], w=pq_k)
                    self.cp("act", PPq[nxt][:], pq_ps[:, :], r=pq_k, w=["PP%d" % nxt + sx])
                else:
                    p_ps, p_k = self.ps(1)
                    self.mm(p_ps[:, :], PTq[cur][:], Pq[cur][:], r=["PP%d" % cur + sx], w=p_k)
                    self.cp("act", Pq[nxt][:], p_ps[:, :], r=p_k, w=["PP%d" % nxt + sx])
                yield
                t_ps, t_k = self.ps(1)
                self.mm(t_ps[:, :], Pq[nxt][:], TTq[:], r=["PP%d" % nxt + sx, "TT" + sx], w=t_k)
                self.tt("dve", TTq[:], TTq[:], t_ps[:, :], ALU.add, r=["TT" + sx] + t_k, w=["TT" + sx])
                yield
                cur = nxt
            if STOP <= 8:
                return
            w_ps, w_k = self.ps(1)
            self.mm(w_ps[R, :], BC(tm[:, 0, Cc]), BC(TTq[:]), r=[tmk, "TT" + sx], w=w_k)
            self.cp("act", WT[g][R, :], w_ps[R, :], r=w_k, w=["WT%d_%d" % (g, a)])
            yield
            x_ps, x_k = self.ps(1)
            self.mm(x_ps[:, 0:64], AkTq[:, 0:128], tm[:, 3, Cc], r=["AkT" + sx, tmk], w=x_k)
            self.cp("act", X0q[:], x_ps[:, 0:64], r=x_k, w=["X0" + sx])
            yield
            u0_ps, u0_k = self.ps(1)
            self.mm(u0_ps[:, 0:64], TTq[:], X0q[:], r=["TT" + sx, "X0" + sx], w=u0_k)
            self.cp("act", U0q[:], u0_ps[:, 0:64], r=u0_k, w=["U0" + sx])
            yield
            if STOP <= 9:
                return
            u_ps, u_k = self.ps(1)
            self.mm(u_ps[:, 0:64], WT[g][R, :], H[R, g, :], r=["WT%d_%d" % (g, a), Hk], w=u_k, start=True, stop=False)
            self.mm(u_ps[:, 0:64], identr[:], U0q[:], r=["identr", "U0" + sx], w=u_k, start=False, stop=True)
            self.cp("act", Usbq[:], u_ps[:, 0:64], r=u_k, w=["Usb" + sx])
            yield
            self.mm(yp[R, :], BC(H[R, g, :]), BC(c1[R, 128:256]), r=[Hk, c1k], w=yk, start=True, stop=False)
            self.mm(yp[R, :], BC(Usbq[:]), BC(nBrq[:]), r=["Usb" + sx, "nBr" + sx], w=yk, start=False, stop=False)
            self.mm(yp[R, :], BC(tm[:, 3, Cc]), BC(AkTq[:, 128:256]), r=[tmk, "AkT" + sx], w=yk, start=False, stop=True)
            h_ps, h_k = self.ps(1)
            self.mm(h_ps[R, 0:64], BC(tm[:, 2, Cc]), BC(tm[:, 3, Cc]), r=[tmk], w=h_k, start=True, stop=False)
            self.mm(h_ps[R, 0:64], idm[R, R], BC(H[R, g, :]), r=[idk, Hk], w=h_k, start=False, stop=False)
            self.mm(h_ps[R, 0:64], BC(tm[:, 1, Cc]), BC(Usbq[:]), r=[tmk, "Usb" + sx], w=h_k, start=False, stop=True)
            self.ts("dve", H[R, g, :], h_ps[R, 0:64], GC[pb][R, g:g + 1], None, ALU.mult, None,
                    r=h_k + ["GC%d_%d" % (pb, g)], w=[Hk])
            yield


        def gn(i):
            pb = i % 2
            yp, yk = self.pbank[6][:, 0:512], ["@b6"]
            gk_all = ["G%d_%d" % (pb, g) for g in range(4)]
            bk_all = ["BON%d_%d" % (pb, g) for g in range(4)]
            lnw = colp[:, CO["lnw"]:CO["lnw"] + 4].unsqueeze(2).to_broadcast([128, 4, 128])
            lnb = colp[:, CO["lnb"]:CO["lnb"] + 4].unsqueeze(2).to_broadcast([128, 4, 128])
            v3 = lambda t: t[:].rearrange("p (g t) -> p g t", g=4)
            self.cp("act", ysb4[:], yp[:, :], r=yk, w=["ysb4"])
            m_ps, m_k = self.ps(4)
            self.mm(m_ps[:, :], bones[:], ysb4[:], r=["bones", "ysb4"], w=m_k)
            self.stt("dve", cen4[:], m_ps[:, :], -1.0 / 64, ysb4[:], ALU.mult, ALU.add, r=m_k + ["ysb4"], w=["cen4"])
            self.tt("pool", sq4[:], cen4[:], cen4[:], ALU.mult, r=["cen4"], w=["sq4"])
            v_ps, v_k = self.ps(4)
            self.mm(v_ps[:, :], bones[:], sq4[:], r=["bones", "sq4"], w=v_k)
            self.act(rs4[:], v_ps[:, :], AF.Sqrt, r=v_k + ["epsc"], w=["rs4"], bias=self.epsc[:, 1:2], scale=1.0 / 64)
            self.recip(rs4[:], rs4[:], r=["rs4"], w=["rs4"])
            self.tt("dve", cen4[:], cen4[:], rs4[:], ALU.mult, r=["cen4", "rs4"], w=["cen4"])
            self.tt("pool", v3(cen4), v3(cen4), lnw, ALU.mult, r=["cen4", "colp"], w=["cen4"])
            self.tt("pool", v3(cen4), v3(cen4), lnb, ALU.add, r=["cen4", "colp"], w=["cen4"])
            self.tt("dve", v3(cen4), v3(cen4), BON[pb][:], ALU.add, r=["cen4"] + bk_all, w=["cen4"])
            zb, zk = zt4[i % 2], "zt4_%d" % (i % 2)
            self.tt("dve", v3(zb), v3(cen4), G[pb][:], ALU.mult, r=["cen4"] + gk_all, w=[zk])
            self.ld(zscr[:, i * 128:(i + 1) * 128].rearrange("(g p) t -> p g t", p=128), v3(zb), r=[zk], w=[])

        def drain(gen):
            for _ in gen:
                pass

        drain(prep_gen(0))
        for i in range(NB):
            if int(os.environ.get("STOP", "99")) <= 6:
                if i + 1 < NB:
                    drain(prep_gen(i + 1))
                continue
            heads = [head_gen(g, a, i) for g in range(4) for a in range(2)]
            nxt = prep_gen(i + 1) if i + 1 < NB else None
            while heads or nxt is not None:
                hl = list(heads)
                marks = set(int((k_ + 1) * max(len(hl), 1) / PREP_STEPS) - 1 for k_ in range(PREP_STEPS))
                for hi_, gen in enumerate(hl):
                    try:
                        next(gen)
                    except StopIteration:
                        heads.remove(gen)
                    if nxt is not None and hi_ in marks:
                        try:
                            next(nxt)
                        except StopIteration:
                            nxt = None
                if not hl and nxt is not None:
                    try:
                        next(nxt)
                    except StopIteration:
                        nxt = None
            gn(i)

    def phaseB(self, zscr, x1scr):
        sb, din, S = self.sb, self.din, self.S
        self.banks1 = (0, 1, 2, 3, 4)
        NB = self.nblk
        ident, identb = self.ident, self.identb
        rowB = sb("rowB", [128, 592], F32)
        self.ld(rowB[:], din["rowB"][:, 0:592], w=["rowB"])
        posc = sb("posc", [128, 32], I32)
        posf = sb("posf", [128, 32], F32)
        self.ld(posc[:], din["posc"][:, :], w=["posc"])
        self.cp("dve", posf[:], posc[:], r=["posc"], w=["posf"])
        caus = sb("caus", [128, 128], F32)
        causb = sb("causb", [128, 128], BF16)
        self.ld(caus[:], din["caus"][:, :], w=["caus"])
        self.cp("dve", causb[:], caus[:], r=["caus"], w=["causb"])
        with ExitStack() as stw:
            bg = sb("bg1", [128, 1024], F32, stw)
            self.ld(bg[:], din["rowB"][:, 592:1616], w=["bg1"])
            self.tt("dve", self.gtb[:, 0:1024], self.gtb[:, 0:1024], bg[:], ALU.add, r=["gtb", "bg1"], w=["gtb"])
            S.barrier()
        KnT = sb("KnT", [128, 4, NB * 128], BF16)
        KpT = sb("KpT", [128, NB * 128], BF16)
        Vx = sb("Vx", [128, NB, 8, 65], BF16)
        self.memset("pool", Vx[:, :, :, 64:65], 1.0, w=["Vx"])
        NS = 4
        with ExitStack() as stp1:
            self.st = stp1
            wkv1 = sb("wkv1", [128, 8, 160], BF16)
            wukv1 = sb("wukv1", [128, 1, 1024], BF16)
            self.load_w_bf16(wkv1, "wkv1", lambda k: din["w_in"][k * 128:(k + 1) * 128, 256:416], 160, None, None, 8)
            self.load_w_bf16(wukv1, "wukv1", lambda k: din["w_ukv"][:, :], 1024, None, None, 1)
            eps1 = sb("eps1", [128, 1], F32)
            self.memset("pool", eps1[:], EPS, w=["eps1"])

            def make_slot(sx):
                K = lambda nm: nm + sx
                xt = sb("p1xt" + sx, [128, D], F32)
                xn = sb("p1xn" + sx, [128, D], BF16)
                ss = sb("p1ss" + sx, [128, 2], F32)
                hT = sb("p1hT" + sx, [128, 8, 130], BF16)
                kva = sb("p1kva" + sx, [128, 160], F32)
                kvs = sb("p1kvs" + sx, [128, 1024], F32)
                st2 = sb("p1st2" + sx, [128, 40], F32)
                ckv = sb("p1ckv" + sx, [128, 128], BF16)
                ckvT = sb("p1ckvT" + sx, [128, 128], BF16)
                sqb = sb("p1sqb" + sx, [128, 768], F32)
                knn = sb("p1knn" + sx, [128, 8, 64], BF16)
                kr = sb("p1kr" + sx, [128, 32], F32)
                kr2 = sb("p1kr2" + sx, [128, 128], BF16)
                cs = sb("p1cs" + sx, [128, 32], F32)
                uu = sb("p1uu" + sx, [128, 32], F32)
                ui = sb("p1ui" + sx, [128, 32], I32)
                uf = sb("p1uf" + sx, [128, 32], F32)
                um = sb("p1um" + sx, [128, 32], F32)
                t16 = [sb("p1t16_%d" % j + sx, [128, 8, 16], F32) for j in range(4)]

                def rms_rows(dst_rstd, src_sq_ap, n, rkeys, extra_scale=1.0):
                    self.act(dst_rstd, src_sq_ap, AF.Sqrt, r=rkeys + ["eps1"], w=[K("st2")], bias=eps1[:, 0:1], scale=1.0 / n)
                    self.recip(dst_rstd, dst_rstd, r=[K("st2")], w=[K("st2")])
                    if extra_scale != 1.0:
                        self.ts("dve", dst_rstd, dst_rstd, extra_scale, None, ALU.mult, None, r=[K("st2")], w=[K("st2")])

                def rope(dst, src, nh, rk, wk):
                    cosb = cs[:, 0:16].unsqueeze(1).to_broadcast([128, nh, 16])
                    sinb = cs[:, 16:32].unsqueeze(1).to_broadcast([128, nh, 16])
                    a, b, c, d = [t[:, 0:nh, :] for t in t16]
                    x1_, x2_ = src[:, :, 0:16], src[:, :, 16:32]
                    self.tt("dve", a, x1_, cosb, ALU.mult, r=rk + [K("cs")], w=[K("t16a")])
                    self.tt("pool", b, x2_, sinb, ALU.mult, r=rk + [K("cs")], w=[K("t16b")])
                    self.tt("dve", c, x1_, sinb, ALU.mult, r=rk + [K("cs")], w=[K("t16c")])
                    self.tt("pool", d, x2_, cosb, ALU.mult, r=rk + [K("cs")], w=[K("t16d")])
                    self.tt("dve", dst[:, :, 0:16], a, b, ALU.subtract, r=[K("t16a"), K("t16b")], w=wk)
                    self.tt("dve", dst[:, :, 16:32], c, d, ALU.add, r=[K("t16c"), K("t16d")], w=wk)

                def cossin(pcol):
                    self.ts("dve", uu[:, 0:16], rowB[:, RB["invf"]:RB["invf"] + 16], pcol, None, ALU.mult, None,
                            r=["rowB", "posf"], w=[K("uu")])
                    self.ts("dve", uu[:, 16:32], uu[:, 0:16], 0.0, None, ALU.add, None, r=[K("uu")], w=[K("uu")])
                    self.ts("dve", uu[:, 0:16], uu[:, 0:16], 0.25, None, ALU.add, None, r=[K("uu")], w=[K("uu")])
                    self.cp("dve", ui[:], uu[:], r=[K("uu")], w=[K("ui")])
                    self.cp("dve", uf[:], ui[:], r=[K("ui")], w=[K("uf")])
                    self.tt("dve", uf[:], uu[:], uf[:], ALU.subtract, r=[K("uu"), K("uf")], w=[K("uf")])
                    self.ts("dve", um[:], uf[:], 0.5, None, ALU.is_gt, None, r=[K("uf")], w=[K("um")])
                    self.tt("dve", uf[:], uf[:], um[:], ALU.subtract, r=[K("uf"), K("um")], w=[K("uf")])
                    self.act(cs[:], uf[:], AF.Sin, r=[K("uf")], w=[K("cs")], scale=float(2 * np.pi))


                def gen(i):
                    hk = K("hT0")
                    self.ld(xt[:], din["xf"][i * 128:(i + 1) * 128, :], w=[K("xt0")])
                    self.memset("pool", ss[:, 0:1], 0.0, w=[K("ss")])
                    self.act(xn[:], xt[:], AF.Square, r=[K("xt0")], w=[K("xn"), K("ss")], accum=ss[:, 0:1])
                    yield
                    self.act(ss[:, 1:2], ss[:, 0:1], AF.Sqrt, r=[K("ss"), "eps1"], w=[K("ss")], bias=eps1[:, 0:1], scale=1.0 / D)
                    self.recip(ss[:, 1:2], ss[:, 1:2], r=[K("ss")], w=[K("ss")])
                    self.ts("dve", xn[:], xt[:], ss[:, 1:2], None, ALU.mult, None, r=[K("xt0"), K("ss")], w=[K("xn")])
                    yield
                    for half in range(2):
                        ppb, pk = self.psb(4)
                        for j in range(4):
                            c = half * 4 + j
                            self.tr(ppb[:, j * 128:(j + 1) * 128], xn[:, c * 128:(c + 1) * 128], identb[:], r=[K("xn"), "identb"], w=pk)
                        for j in range(4):
                            c = half * 4 + j
                            if j % 2:
                                self.act(hT[:, c, 2:130], ppb[:, j * 128:(j + 1) * 128], AF.Identity, r=pk + ["sc1", "modc"],
                                         w=[hk], bias=self.modc[:, c:c + 1], scale=self.sc1[:, c:c + 1])
                            else:
                                self.ts("dve", hT[:, c, 2:130], ppb[:, j * 128:(j + 1) * 128], self.sc1[:, c:c + 1],
                                        self.modc[:, c:c + 1], ALU.mult, ALU.add, r=pk + ["sc1", "modc"], w=[hk])
                        yield
                    cossin(posf[:, i:i + 1])
                    yield
                    kp, kk_ = self.ps(2)
                    for k in range(8):
                        self.mm(kp[:, 0:160], hT[:, k, 2:130], wkv1[:, k, 0:160], r=[hk, "wkv1"], w=kk_, start=(k == 0), stop=(k == 7))
                    self.cp("act", kva[:], kp[:, 0:160], r=kk_, w=[K("kva")])
                    yield
                    self.memset("pool", st2[:, 0:8], 0.0, w=[K("st2")])
                    self.act(sqb[:, 0:128], kva[:, 0:128], AF.Square, r=[K("kva")], w=[K("sqb"), K("st2")], accum=st2[:, 0:1])
                    self.act(sqb[:, 128:160], kva[:, 128:160], AF.Square, r=[K("kva")], w=[K("sqb"), K("st2")], accum=st2[:, 1:2])
                    rms_rows(st2[:, 2:3], st2[:, 0:1], 128, [K("st2")])
                    rms_rows(st2[:, 3:4], st2[:, 1:2], 32, [K("st2")])
                    yield
                    self.stt("dve", ckv[:], kva[:, 0:128], st2[:, 2:3], rowB[:, RB["kvan"]:RB["kvan"] + 128], ALU.mult, ALU.mult,
                             r=[K("kva"), K("st2"), "rowB"], w=[K("ckv")])
                    self.stt("dve", kr[:], kva[:, 128:160], st2[:, 3:4], rowB[:, RB["kn"] + 64:RB["kn"] + 96], ALU.mult,
                             ALU.mult, r=[K("kva"), K("st2"), "rowB"], w=[K("kr")])
                    rope(kr2[:, 0:32].rearrange("p (h d) -> p h d", h=1), kr[:].rearrange("p (h d) -> p h d", h=1), 1,
                         [K("kr")], [K("kr2")])
                    self.cp("pool", kr2[:, 32:128].rearrange("p (c d) -> p c d", c=3), kr2[:, 0:32].unsqueeze(1).to_broadcast([128, 3, 32]), r=[K("kr2")], w=[K("kr2")])
                    yield
                    tpb, tk_ = self.psb(2)
                    self.tr(tpb[:, 0:128], ckv[:], identb[:], r=[K("ckv"), "identb"], w=tk_)
                    self.tr(tpb[:, 128:256], kr2[:], identb[:], r=[K("kr2"), "identb"], w=tk_)
                    self.cp("act", ckvT[:], tpb[:, 0:128], r=tk_, w=[K("ckvT")])
                    self.cp("act", KpT[:, i * 128:(i + 1) * 128], tpb[:, 128:256], r=tk_, w=["KpT"])
                    yield
                    for hf in range(2):
                        vp, vk = self.ps(4)
                        self.mm(vp[:, :], ckvT[:], wukv1[:, 0, hf * 512:(hf + 1) * 512], r=[K("ckvT"), "wukv1"], w=vk)
                        self.cp("act", kvs[:, hf * 512:(hf + 1) * 512], vp[:, :], r=vk, w=[K("kvs")])
                        yield
                    kv3 = kvs[:].rearrange("p (h d) -> p h d", h=8)
                    self.cp("pool", Vx[:, i, :, 0:64], kv3[:, :, 64:128], r=[K("kvs")], w=["Vx"])
                    yield
                    self.tt("dve", sqb[:, 0:512].rearrange("p (h d) -> p h d", h=8), kv3[:, :, 0:64], kv3[:, :, 0:64], ALU.mult,
                            r=[K("kvs")], w=[K("sqb")])
                    S.op("dve", lambda e: e.tensor_reduce(out=st2[:, 8:16], in_=sqb[:, 0:512].rearrange("p (h d) -> p h d", h=8),
                                                          axis=AX.X, op=ALU.add), r=[K("sqb")], w=[K("st2")])
                    rms_rows(st2[:, 16:24], st2[:, 8:16], 64, [K("st2")])
                    yield
                    self.tt("dve", sqb[:, 0:512].rearrange("p (h d) -> p h d", h=8), kv3[:, :, 0:64],
                            st2[:, 16:24].unsqueeze(2).to_broadcast([128, 8, 64]), ALU.mult, r=[K("kvs"), K("st2")], w=[K("sqb")])
                    self.tt("pool", knn[:], sqb[:, 0:512].rearrange("p (h d) -> p h d", h=8),
                            rowB[:, RB["kn"]:RB["kn"] + 64].unsqueeze(1).to_broadcast([128, 8, 64]), ALU.mult,
                            r=[K("sqb"), "rowB"], w=[K("knn")])
                    tpb, tk_ = self.psb(4)
                    for g in range(4):
                        self.tr(tpb[:, g * 128:(g + 1) * 128], knn[:, 2 * g:2 * g + 2, :].rearrange("p h d -> p (h d)"),
                                identb[:], r=[K("knn"), "identb"], w=tk_)
                    self.cp("act", KnT[:, :, i * 128:(i + 1) * 128], tpb[:, 0:512].rearrange("p (g t) -> p g t", g=4), r=tk_,
                            w=["KnT"])

                    yield
                return gen

            slots = [make_slot("_s%d" % j) for j in range(NS)]
            active = []
            nxt_i = 0
            while nxt_i < NB or active:
                while len(active) < NS and nxt_i < NB:
                    active.append(slots[nxt_i % NS](nxt_i))
                    nxt_i += 1
                    break
                for gen in list(active):
                    try:
                        next(gen)
                    except StopIteration:
                        active.remove(gen)
            S.barrier()
        self.st = self._phase_st
        self.alloc_h(1)
        wi = sb("wi", [128, 8, 2464], BF16)
        wuq = sb("wuq", [128, 2, 768], BF16)
        wukv = sb("wukv", [128, 1, 1024], BF16)
        womla = sb("womla", [128, 4, 1024], BF16)
        worw = sb("worw", [128, 4, 1024], BF16)
        wout = sb("wout", [128, 8, 1024], BF16)
        with ExitStack() as stw:
            stg = [sb("stgB%d" % j, [128, 2048], F32, stw) for j in range(2)]
            sk = ["stgB0", "stgB1"]
            self.load_w_bf16(wi[:, :, 0:416], "wi", lambda k: din["w_in"][k * 128:(k + 1) * 128, 0:416], 416, stg, sk, 8)
            self.load_w_bf16(wi[:, :, 416:2464], "wi", lambda k: din["w_in"][k * 128:(k + 1) * 128, 2208:4256], 2048,
                             stg, sk, 8)
            self.load_w_bf16(wuq, "wuq", lambda k: din["w_uq"][k * 128:(k + 1) * 128, :], 768, stg, sk, 2)
            self.load_w_bf16(wukv, "wukv", lambda k: din["w_ukv"][:, :], 1024, stg, sk, 1)
            self.load_w_bf16(womla, "womla", lambda k: din["w_o_mla"][k * 128:(k + 1) * 128, :], 1024, stg, sk, 4)
            self.load_w_bf16(worw, "worw", lambda k: din["w_o_rwkv"][k * 128:(k + 1) * 128, :], 1024, stg, sk, 4)
            self.load_w_bf16(wout, "wout", lambda k: din["w_out"][k * 128:(k + 1) * 128, :], 1024, stg, sk, 8)
            S.barrier()
        kva = sb("kva", [128, 160], F32)
        kvs = sb("kvs", [128, 1024], F32)
        st2 = sb("st2", [128, 40], F32)
        ckv = sb("ckv", [128, 128], BF16)
        ckvT = sb("ckvT", [128, 128], BF16)
        sqb = sb("sqb", [128, 768], F32)
        knn = sb("knn", [128, 8, 64], BF16)
        kr = sb("kr", [128, 32], F32)
        kr2 = sb("kr2", [128, 128], BF16)
        cs = sb("cs", [128, 32], F32)
        uu = sb("uu", [128, 32], F32)
        ui = sb("ui", [128, 32], I32)
        uf = sb("uf", [128, 32], F32)
        um = sb("um", [128, 32], F32)
        t16 = [sb("t16_%d" % j, [128, 8, 16], F32) for j in range(4)]
        qa = sb("qa", [128, 256], F32)
        cq = sb("cq", [128, 256], BF16)
        cqT = sb("cqT", [128, 2, 128], BF16)
        qf = sb("qf", [128, 768], F32)
        qn = sb("qn", [128, 8, 64], BF16)
        qpf = sb("qpf", [128, 8, 32], F32)
        qp = sb("qp", [128, 8, 32], BF16)
        QnB = sb("QnB", [128, 4, 256], BF16)
        QpB = sb("QpB", [128, 4, 256], BF16)
        self.memset("pool", QnB[:], 0.0, w=["QnB"])
        self.memset("pool", QpB[:], 0.0, w=["QpB"])
        PTt = [sb("PTt%d" % j, [128, 512], BF16) for j in range(3)]
        orc = sb("orc", [128, 8], F32)
        OA = sb("OA", [128, 8, 64], BF16)
        OT = sb("OT", [128, 4, 128], BF16)
        ZT = sb("ZT", [128, 4, 128], BF16)
        gsa = sb("gsa", [128, 128], F32)
        gsb = sb("gsb", [128, 128], F32)
        xtmp = sb("xtmp", [128, 512], F32)
        g1 = sb("g1", [128, 128], F32)
        g2 = sb("g2", [128, 128], F32)
        GTt = sb("GTt", [128, 8, 128], BF16)
        ATT_SCALE = 96.0 ** -0.5
        Ops = self.pbank[6:8]

        def rms_rows(dst_rstd, src_sq_ap, n, rkeys, extra_scale=1.0):
            self.act(dst_rstd, src_sq_ap, AF.Sqrt, r=rkeys + ["epsc"], w=["st2"], bias=self.epsc[:, 0:1], scale=1.0 / n)
            self.recip(dst_rstd, dst_rstd, r=["st2"], w=["st2"])
            if extra_scale != 1.0:
                self.ts("dve", dst_rstd, dst_rstd, extra_scale, None, ALU.mult, None, r=["st2"], w=["st2"])

        def rope(dst, src, nh, rk, wk):
            cosb = cs[:, 0:16].unsqueeze(1).to_broadcast([128, nh, 16])
            sinb = cs[:, 16:32].unsqueeze(1).to_broadcast([128, nh, 16])
            a, b, c, d = [t[:, 0:nh, :] for t in t16]
            x1_, x2_ = src[:, :, 0:16], src[:, :, 16:32]
            self.tt("dve", a, x1_, cosb, ALU.mult, r=rk + ["cs"], w=["t16a"])
            self.tt("pool", b, x2_, sinb, ALU.mult, r=rk + ["cs"], w=["t16b"])
            self.tt("dve", c, x1_, sinb, ALU.mult, r=rk + ["cs"], w=["t16c"])
            self.tt("pool", d, x2_, cosb, ALU.mult, r=rk + ["cs"], w=["t16d"])
            self.tt("dve", dst[:, :, 0:16], a, b, ALU.subtract, r=["t16a", "t16b"], w=wk)
            self.tt("dve", dst[:, :, 16:32], c, d, ALU.add, r=["t16c", "t16d"], w=wk)

        def cossin(pcol):
            self.ts("dve", uu[:, 0:16], rowB[:, RB["invf"]:RB["invf"] + 16], pcol, None, ALU.mult, None,
                    r=["rowB", "posf"], w=["uu"])
            self.ts("dve", uu[:, 16:32], uu[:, 0:16], 0.0, None, ALU.add, None, r=["uu"], w=["uu"])
            self.ts("dve", uu[:, 0:16], uu[:, 0:16], 0.25, None, ALU.add, None, r=["uu"], w=["uu"])
            self.cp("dve", ui[:], uu[:], r=["uu"], w=["ui"])
            self.cp("dve", uf[:], ui[:], r=["ui"], w=["uf"])
            self.tt("dve", uf[:], uu[:], uf[:], ALU.subtract, r=["uu", "uf"], w=["uf"])
            self.ts("dve", um[:], uf[:], 0.5, None, ALU.is_gt, None, r=["uf"], w=["um"])
            self.tt("dve", uf[:], uf[:], um[:], ALU.subtract, r=["uf", "um"], w=["uf"])
            self.act(cs[:], uf[:], AF.Sin, r=["uf"], w=["cs"], scale=float(2 * np.pi))

        NBh = NB // 2
        selm = sb("selmB", [128, 2], F32)
        self.ld(selm[:], din["selm"][:, :], w=["selmB"])
        MA = sb("MA", [128, 128], BF16)
        MB = sb("MB", [128, 128], BF16)
        negb = sb("negb", [128, 1], F32)
        self.ts("dve", MA[:], caus[:], selm[:, 0:1], selm[:, 1:2], ALU.mult, ALU.add, r=["caus", "selmB"], w=["MA"])
        self.ts("dve", MB[:], caus[:], selm[:, 1:2], None, ALU.mult, None, r=["caus", "selmB"], w=["MB"])
        self.ts("dve", negb[:], selm[:, 0:1], -10000.0, None, ALU.mult, None, r=["selmB"], w=["negb"])
        poso = sb("poso", [128, 16], I32)
        posfo = sb("posfo", [128, 16], F32)
        self.ld(poso[:], din["poso"][:, :], w=["poso"])
        self.cp("dve", posfo[:], poso[:], r=["poso"], w=["posfo"])
        ZTa = sb("ZTa", [128, 4, 128], BF16)
        ZTb = sb("ZTb", [128, 4, 128], BF16)

        def kb_groups(j):
            groups = []
            lo = list(range(0, 2 * j))
            for q in range(0, len(lo), 2):
                groups.append((lo[q:q + 2], 0))
            groups.append(([2 * j], 1))
            groups.append(([2 * j + 1], 3))
            return groups

        for i in range(NBh):
            hT, hk = self.make_h(i, din["xown"][i * 128:(i + 1) * 128, :], self.sc1, self.modc[:, 0:8],
                                 ["sc1", "modc"], carry=False, evac_act=True)
            self.ld(ZTa[:], zscr[:, (2 * i) * 128:(2 * i + 1) * 128].rearrange("(g p) t -> p g t", p=128), w=["ZTa"])
            self.ld(ZTb[:], zscr[:, (2 * i + 1) * 128:(2 * i + 2) * 128].rearrange("(g p) t -> p g t", p=128), w=["ZTb"])
            xt, xk = self.xt[0], "xt0"
            cossin(posfo[:, i:i + 1])
            qp_, qk_ = self.ps(2)
            for k in range(8):
                self.mm(qp_[:, 0:256], hT[:, k, 2:130], wi[:, k, 0:256], r=[hk, "wi"], w=qk_, start=(k == 0), stop=(k == 7))
            self.cp("act", qa[:], qp_[:, 0:256], r=qk_, w=["qa"])
            self.act(sqb[:, 0:256], qa[:], AF.Square, r=["qa"], w=["sqb", "st2"], accum=st2[:, 4:5])
            rms_rows(st2[:, 5:6], st2[:, 4:5], 256, ["st2"])
            self.stt("dve", cq[:], qa[:], st2[:, 5:6], rowB[:, RB["qan"]:RB["qan"] + 256], ALU.mult, ALU.mult,
                     r=["qa", "st2", "rowB"], w=["cq"])
            tpb, tk_ = self.psb(2)
            for c in range(2):
                self.tr(tpb[:, c * 128:(c + 1) * 128], cq[:, c * 128:(c + 1) * 128], identb[:], r=["cq", "identb"], w=tk_)
            self.cp("act", cqT[:], tpb[:, 0:256].rearrange("p (c t) -> p c t", c=2), r=tk_, w=["cqT"])
            for hf in range(2):
                qq, qqk = self.ps(4)
                for c in range(2):
                    self.mm(qq[:, 0:384], cqT[:, c, :], wuq[:, c, hf * 384:(hf + 1) * 384], r=["cqT", "wuq"], w=qqk,
                            start=(c == 0), stop=(c == 1))
                self.cp("act", qf[:, hf * 384:(hf + 1) * 384], qq[:, 0:384], r=qqk, w=["qf"])
            q3 = qf[:].rearrange("p (h d) -> p h d", h=8)
            self.tt("dve", sqb[:].rearrange("p (h d) -> p h d", h=8), q3, q3, ALU.mult, r=["qf"], w=["sqb"])
            s3 = sqb[:].rearrange("p (h d) -> p h d", h=8)
            S.op("dve", lambda e: e.tensor_reduce(out=st2[:, 24:32], in_=s3[:, :, 0:64], axis=AX.X, op=ALU.add),
                 r=["sqb"], w=["st2"])
            S.op("dve", lambda e: e.tensor_reduce(out=st2[:, 32:40], in_=s3[:, :, 64:96], axis=AX.X, op=ALU.add),
                 r=["sqb"], w=["st2"])
            rms_rows(st2[:, 8:16], st2[:, 24:32], 64, ["st2"], ATT_SCALE)
            rms_rows(st2[:, 16:24], st2[:, 32:40], 32, ["st2"], ATT_SCALE)
            self.tt("dve", s3[:, :, 0:64], q3[:, :, 0:64], st2[:, 8:16].unsqueeze(2).to_broadcast([128, 8, 64]),
                    ALU.mult, r=["qf", "st2"], w=["sqb"])
            self.tt("dve", qn[:], s3[:, :, 0:64], rowB[:, RB["qn"]:RB["qn"] + 64].unsqueeze(1).to_broadcast([128, 8, 64]),
                    ALU.mult, r=["sqb", "rowB"], w=["qn"])
            self.tt("dve", s3[:, :, 64:96], q3[:, :, 64:96], st2[:, 16:24].unsqueeze(2).to_broadcast([128, 8, 32]),
                    ALU.mult, r=["qf", "st2"], w=["sqb"])
            self.tt("dve", qpf[:], s3[:, :, 64:96],
                    rowB[:, RB["qn"] + 64:RB["qn"] + 96].unsqueeze(1).to_broadcast([128, 8, 32]), ALU.mult,
                    r=["sqb", "rowB"], w=["qpf"])
            rope(qp[:], qpf[:], 8, ["qpf"], ["qp"])
            tpb, tk_ = self.psb(4)
            for g in range(4):
                self.tr(tpb[:, g * 128:(g + 1) * 128], qn[:, 2 * g:2 * g + 2, :].rearrange("p h d -> p (h d)"),
                        identb[:], r=["qn", "identb"], w=tk_)
            self.cp("act", QnB[0:64, :, 0:128], tpb[0:64, 0:512].rearrange("p (g t) -> p g t", g=4), r=tk_, w=["QnB"])
            self.cp("act", QnB[64:128, :, 128:256], tpb[64:128, 0:512].rearrange("p (g t) -> p g t", g=4), r=tk_, w=["QnB"])
            tpb, tk_ = self.psb(4)
            for g in range(4):
                self.tr(tpb[0:64, g * 128:(g + 1) * 128], qp[:, 2 * g:2 * g + 2, :].rearrange("p h d -> p (h d)"),
                        identb[:], r=["qp", "identb"], w=tk_)
            self.cp("act", QpB[0:32, :, 0:128], tpb[0:32, 0:512].rearrange("p (g t) -> p g t", g=4), r=tk_, w=["QpB"])
            self.cp("act", QpB[32:64, :, 128:256], tpb[32:64, 0:512].rearrange("p (g t) -> p g t", g=4), r=tk_, w=["QpB"])
            last_kb = 2 * i + 1
            work = [(g, kbs, mode) for g in range(4) for (kbs, mode) in kb_groups(i)]
            firsts = [True] * 8

            def stage1(n):
                g, kbs, mode = work[n]
                nb = len(kbs)
                sp_, sk_ = self.ps(4)
                for j, kb in enumerate(kbs):
                    self.mm(sp_[:, j * 256:(j + 1) * 256], KnT[:, g, kb * 128:(kb + 1) * 128], QnB[:, g, :],
                            r=["KnT", "QnB"], w=sk_, start=True, stop=False)
                    self.mm(sp_[:, j * 256:(j + 1) * 256], KpT[:, kb * 128:(kb + 1) * 128], QpB[:, g, :],
                            r=["KpT", "QpB"], w=sk_, start=False, stop=True)
                pt, ptk = PTt[n % 3], "PTt%d" % (n % 3)
                if mode == 2:
                    self.act(pt[:, 0:nb * 256], sp_[:, 0:nb * 256], AF.Exp, r=sk_ + ["negb"], w=[ptk], bias=negb[:, 0:1])
                else:
                    self.act(pt[:, 0:nb * 256], sp_[:, 0:nb * 256], AF.Exp, r=sk_, w=[ptk])
                if mode in (1, 3):
                    mt, mk_ = (MA, "MA") if mode == 1 else (MB, "MB")
                    self.tt("pool", pt[:, 0:256].rearrange("p (a q) -> p a q", a=2), pt[:, 0:256].rearrange("p (a q) -> p a q", a=2),
                            mt[:].unsqueeze(1).to_broadcast([128, 2, 128]), ALU.mult, r=[ptk, mk_], w=[ptk])

            def stage2(n):
                g, kbs, mode = work[n]
                pt, ptk = PTt[n % 3], "PTt%d" % (n % 3)
                for j, kb in enumerate(kbs):
                    for a in range(2):
                        h = 2 * g + a
                        ob = Ops[a]
                        oc = g * 65
                        self.mm(ob[:, oc:oc + 65], pt[:, j * 256 + a * 128:j * 256 + (a + 1) * 128], Vx[:, kb, h, :],
                                r=[ptk, "Vx"], w=["@b%d" % (6 + a)], start=firsts[h], stop=(kb == last_kb))
                        firsts[h] = False

            for n in range(len(work) + 2):
                if n < len(work):
                    stage1(n)
                if n >= 2:
                    stage2(n - 2)
            OA4 = OA[:].rearrange("p (g a) d -> p g a d", a=2)
            for bk in range(2):
                o3 = Ops[bk][:, 0:260].rearrange("p (h d) -> p h d", h=4)
                self.recip(orc[:, bk * 4:(bk + 1) * 4], o3[:, :, 64], r=["@b%d" % (6 + bk)], w=["orc"])
                self.tt("dve", OA4[:, :, bk, :], o3[:, :, 0:64],
                        orc[:, bk * 4:(bk + 1) * 4].unsqueeze(2).to_broadcast([128, 4, 64]), ALU.mult,
                        r=["@b%d" % (6 + bk), "orc"], w=["OA"])
            tpb, tk_ = self.psb(4)
            for g in range(4):
                self.tr(tpb[:, g * 128:(g + 1) * 128], OA[:, 2 * g:2 * g + 2, :].rearrange("p h d -> p (h d)"),
                        identb[:], r=["OA", "identb"], w=tk_)
            self.cp("act", OT[:], tpb[:, 0:512].rearrange("p (g t) -> p g t", g=4), r=tk_, w=["OT"])
            self.ts("dve", ZT[:], ZTa[:], selm[:, 0:1], None, ALU.mult, None, r=["ZTa", "selmB"], w=["ZT"])
            self.stt("dve", ZT[:], ZTb[:], selm[:, 1:2], ZT[:], ALU.mult, ALU.add, r=["ZTb", "selmB", "ZT"], w=["ZT"])
            for m in range(8):
                for (mm_, gs_, gsk) in ((m, gsa, "gsa"), (8 + m, gsb, "gsb")):
                    gp, gk = self.ps(1)
                    for k in range(8):
                        self.mm(gp[:, :], wi[:, k, 416 + mm_ * 128:416 + (mm_ + 1) * 128], hT[:, k, 2:130], r=["wi", hk],
                                w=gk, start=(k == 0), stop=(k == 7))
                    self.act(gs_[:], gp[:, :], AF.Sigmoid, r=gk, w=[gsk])
                p1, k1 = self.ps(1)
                for k in range(4):
                    self.mm(p1[:, :], womla[:, k, m * 128:(m + 1) * 128], OT[:, k, :], r=["womla", "OT"], w=k1,
                            start=(k == 0), stop=(k == 3))
                self.tt("dve", g1[:], p1[:, :], gsa[:], ALU.mult, r=k1 + ["gsa"], w=["g1"])
                p2, k2 = self.ps(1)
                for k in range(4):
                    self.mm(p2[:, :], worw[:, k, m * 128:(m + 1) * 128], ZT[:, k, :], r=["worw", "ZT"], w=k2,
                            start=(k == 0), stop=(k == 3))
                self.tt("dve", g2[:], p2[:, :], gsb[:], ALU.mult, r=k2 + ["gsb"], w=["g2"])
                self.tt("dve", GTt[:, m, :], g1[:], g2[:], ALU.add, r=["g1", "g2"], w=["GTt"])
            for hf in range(2):
                mp, mk = self.ps(4)
                for m in range(8):
                    self.mm(mp[:, :], GTt[:, m, :], wout[:, m, hf * 512:(hf + 1) * 512], r=["GTt", "wout"], w=mk,
                            start=(m == 0), stop=(m == 7))
                self.tt("dve", xtmp[:], mp[:, :], self.gtb[:, hf * 512:(hf + 1) * 512], ALU.mult, r=mk + ["gtb"],
                        w=["xtmp"])
                self.tt("dve", xt[:, hf * 512:(hf + 1) * 512], xt[:, hf * 512:(hf + 1) * 512], xtmp[:], ALU.add,
                        r=[xk, "xtmp"], w=[xk])
            self.ld(x1scr[i * 128:(i + 1) * 128, :], xt[:], r=[xk], w=[])

    def phaseC(self, x1scr, gscr, out):
        sb, din, S = self.sb, self.din, self.S
        NT = int(os.environ.get("NT", "16"))
        TOK = NT * 128
        from_x = "B" not in self.phases
        src = din["xf"] if from_x else x1scr
        ident = self.ident
        selm = sb("selm", [128, 2], F32)
        self.ld(selm[:], din["selm"][:, :], w=["selm"])
        with ExitStack() as stw:
            rowC = sb("rowC", [128, 1024], F32, stw)
            self.ld(rowC[:], din["rowC"][:, :], w=["rowC"])
            self.tt("dve", self.gtb[:, 1024:2048], self.gtb[:, 1024:2048], rowC[:], ALU.add, r=["gtb", "rowC"], w=["gtb"])
            S.barrier()
        H2 = sb("H2", [128, 8, TOK], BF16)
        xa = None

        def load_sel(tt, dst, dkey):
            self.ld(dst[:], src[tt * 128:(tt + 1) * 128, :], w=[dkey])
            if not from_x:
                return
            self.ld(xa[:], src[2048 + tt * 128:2048 + (tt + 1) * 128, :], w=["xa"])
            self.ts("dve", dst[:], dst[:], selm[:, 0:1], None, ALU.mult, None, r=[dkey, "selm"], w=[dkey])
            self.stt("dve", dst[:], xa[:], selm[:, 1:2], dst[:], ALU.mult, ALU.add, r=["xa", "selm", dkey], w=[dkey])

        self.banks1 = (0, 1, 2, 3, 4, 6, 7)
        with ExitStack() as st1:
            self.st = st1
            self.alloc_h(2)
            if from_x:
                xa = sb("xa", [128, D], F32)
            wq = sb("wq", [128, 8, 2048], BF16)
            skT = sb("skT", [128, 2, 128], BF16)
            iota = sb("iota", [128, 128], F32)
            self.ld(iota[:], din["iota"][:, :], w=["iota"])
            with ExitStack() as stw:
                stg = [sb("stgC%d" % j, [128, 2048], F32, stw) for j in range(2)]
                sk = ["stgC0", "stgC1"]
                self.load_w_bf16(wq, "wq", lambda k: din["w_query"][k * 128:(k + 1) * 128, :], 2048, stg, sk, 8)
                self.ld(stg[0][:, 0:128], din["sk1T"][:, :], w=[sk[0]])
                self.ld(stg[1][:, 0:128], din["sk2T"][:, :], w=[sk[1]])
                self.cp("dve", skT[:, 0, :], stg[0][:, 0:128], r=[sk[0]], w=["skT"])
                self.cp("dve", skT[:, 1, :], stg[1][:, 0:128], r=[sk[1]], w=["skT"])
                S.barrier()
            qTs = [sb("qT%d" % j, [128, 16, 128], BF16) for j in range(2)]
            scs = [sb("sc%d" % j, [128, 16, 128], F32) for j in range(2)]
            scrA = sb("scrA", [128, 16, 128], F32)
            scrB = sb("scrB", [128, 8, 256], F32)
            V16 = sb("V16", [128, 16, 16], F32)
            I16u = sb("I16u", [128, 16, 16], U32)
            I16 = sb("I16", [128, 16, 16], F32)
            Asel = sb("Asel", [128, 128], F32)
            Bsel = sb("Bsel", [128, 128], F32)
            Wsel = sb("Wsel", [128, 128], F32)
            ABW = sb("ABW", [128, 3, 128], F32)
            Aoh = [sb("Aoh%d" % j, [128, 128], BF16) for j in range(2)]
            Boh = [sb("Boh%d" % j, [128, 128], BF16) for j in range(2)]
            GT = sb("GT", [128, 128, 128], BF16)
            cand8 = sb("cand8", [128, 8, 256], F32)
            tv8 = sb("tv8", [128, 8, 16], F32)
            posu8 = sb("posu8", [128, 8, 16], U32)
            ju = sb("ju", [128, 2, 8, 16], U32)
            jf = sb("jf", [128, 2, 8, 16], F32)
            E8 = sb("E8", [128, 2048], F32)
            e8 = sb("e8", [128, 8, 16], F32)
            s8 = sb("s8", [128, 16], F32)
            AohB = [sb("AohB%d" % j, [128, 16, 128], BF16) for j in range(2)]
            BohB = [sb("BohB%d" % j, [128, 16, 128], BF16) for j in range(2)]
            def front(tt):
                b = tt % 2
                hT, hk = self.make_h(tt, None, self.sc2, self.modc[:, 16:24], ["sc2", "modc"], carry=False, evac_act=True)
                if tt + 2 < NT:
                    load_sel(tt + 2, self.xt[b], "xt%d" % b)
                self.cp("pool", H2[:, :, tt * 128:(tt + 1) * 128], hT[:, :, 2:130], r=[hk], w=["H2"])
                qT, qTk = qTs[b], "qT%d" % b
                sc, sck = scs[b], "sc%d" % b
                for m in range(16):
                    qp_, qk_ = self.ps(1)
                    for k in range(8):
                        self.mm(qp_[:, :], wq[:, k, m * 128:(m + 1) * 128], hT[:, k, 2:130], r=["wq", hk], w=qk_,
                                start=(k == 0), stop=(k == 7))
                    self.cp("act", qT[:, m, :], qp_[:, :], r=qk_, w=[qTk])
                for m4 in range(4):
                    sp_, sk_ = self.ps(4)
                    for j in range(4):
                        m = m4 * 4 + j
                        self.mm(sp_[:, j * 128:(j + 1) * 128], qT[:, m, :], skT[:, m % 2, :], r=[qTk, "skT"], w=sk_)
                    self.cp("act", sc[:, m4 * 4:(m4 + 1) * 4, :], sp_[:, :].rearrange("p (a b) -> p a b", a=4), r=sk_,
                            w=[sck])

            load_sel(0, self.xt[0], "xt0")
            if NT > 1:
                load_sel(1, self.xt[1], "xt1")
            front(0)
            for tt in range(NT):
                if tt + 1 < NT:
                    front(tt + 1)
                sc, sck = scs[tt % 2], "sc%d" % (tt % 2)

                def top16_multi(items):
                    for (vo, io, sa, sc_, tg, ks) in items:
                        S.op("dve", lambda e, vo=vo, sa=sa: e.max(out=vo[:, 0:8], in_=sa), r=ks, w=["tkv" + tg])
                    for (vo, io, sa, sc_, tg, ks) in items:
                        S.op("dve", lambda e, vo=vo, io=io, sa=sa: e.max_index(out=io[:, 0:8], in_max=vo[:, 0:8], in_values=sa),
                             r=ks + ["tkv" + tg], w=["tki" + tg])
                    for (vo, io, sa, sc_, tg, ks) in items:
                        S.op("dve", lambda e, vo=vo, sa=sa, sc_=sc_: e.match_replace(out=sc_, in_to_replace=vo[:, 0:8],
                                                                                 in_values=sa, imm_value=-1e30),
                             r=ks + ["tkv" + tg], w=["tks" + tg])
                    for (vo, io, sa, sc_, tg, ks) in items:
                        S.op("dve", lambda e, vo=vo, sc_=sc_: e.max(out=vo[:, 8:16], in_=sc_), r=["tks" + tg], w=["tkv" + tg])
                    for (vo, io, sa, sc_, tg, ks) in items:
                        S.op("dve", lambda e, vo=vo, io=io, sc_=sc_: e.max_index(out=io[:, 8:16], in_max=vo[:, 8:16], in_values=sc_),
                             r=["tks" + tg, "tkv" + tg], w=["tki" + tg])

                top16_multi([(V16[:, m, :], I16u[:, m, :], sc[:, m, :], scrA[:, m, :], "a%d" % m, [sck]) for m in range(16)])
                kA_v = ["tkva%d" % m for m in range(16)]
                kA_i = ["tkia%d" % m for m in range(16)]
                self.cp("pool", I16[:], I16u[:], r=kA_i, w=["I16"])
                V4 = V16[:].rearrange("p (h two) k -> p h two k", two=2)
                I4 = I16[:].rearrange("p (h two) k -> p h two k", two=2)
                self.tt("dve", cand8[:].rearrange("p h (a b) -> p h a b", a=16),
                        V4[:, :, 0, :].unsqueeze(3).to_broadcast([128, 8, 16, 16]),
                        V4[:, :, 1, :].unsqueeze(2).to_broadcast([128, 8, 16, 16]), ALU.add, r=kA_v, w=["cand8"])
                top16_multi([(tv8[:, h, :], posu8[:, h, :], cand8[:, h, :], scrB[:, h, :], "b%d" % h, ["cand8"]) for h in range(8)])
                kB_v = ["tkvb%d" % h for h in range(8)]
                kB_i = ["tkib%d" % h for h in range(8)]
                S.op("dve", lambda e: e.tensor_single_scalar(out=ju[:, 0], in_=posu8[:], scalar=4, op=ALU.logical_shift_right),
                     r=kB_i, w=["ju"])
                S.op("dve", lambda e: e.tensor_single_scalar(out=ju[:, 1], in_=posu8[:], scalar=15, op=ALU.bitwise_and),
                     r=kB_i, w=["ju"])
                self.cp("dve", jf[:], ju[:], r=["ju"], w=["jf"])
                io16 = iota[:, 0:16].unsqueeze(1).unsqueeze(1).to_broadcast([128, 8, 16, 16])
                for w_, dst in ((0, Asel), (1, Bsel)):
                    e4 = E8[:].rearrange("p (h k j) -> p h k j", h=8, k=16)
                    self.tt("dve", e4, io16, jf[:, w_].unsqueeze(3).to_broadcast([128, 8, 16, 16]), ALU.is_equal,
                            r=["iota", "jf"], w=["E8"])
                    self.tt("pool" if w_ == 0 else "dve", e4, e4, I4[:, :, w_, :].unsqueeze(2).to_broadcast([128, 8, 16, 16]), ALU.mult,
                            r=["E8", "I16"], w=["E8"])
                    S.op("dve", lambda e, dst=dst, e4=e4: e.tensor_reduce(out=dst[:].rearrange("p (h k) -> p h k", h=8), in_=e4,
                                                                        axis=AX.X, op=ALU.add), r=["E8"], w=[dst.name if False else ("Asel" if dst is Asel else "Bsel")])
                self.tt("dve", e8[:], tv8[:], tv8[:, :, 0:1].to_broadcast([128, 8, 16]), ALU.subtract, r=kB_v, w=["e8"])
                self.act(e8[:], e8[:], AF.Exp, r=["e8"], w=["e8"])
                S.op("dve", lambda e: e.tensor_reduce(out=s8[:, 0:8], in_=e8[:], axis=AX.X, op=ALU.add), r=["e8"], w=["s8"])
                self.recip(s8[:, 8:16], s8[:, 0:8], r=["s8"], w=["s8"])
                self.tt("dve", Wsel[:].rearrange("p (h k) -> p h k", h=8), e8[:], s8[:, 8:16].unsqueeze(2).to_broadcast([128, 8, 16]),
                        ALU.mult, r=["e8", "s8"], w=["Wsel"])
                tp, tk_ = self.ps(4)
                self.tr(tp[:, 0:128], Asel[:], ident[:], r=["Asel", "ident"], w=tk_)
                self.tr(tp[:, 128:256], Bsel[:], ident[:], r=["Bsel", "ident"], w=tk_)
                self.tr(tp[:, 256:384], Wsel[:], ident[:], r=["Wsel", "ident"], w=tk_)
                self.cp("act", ABW[:], tp[:, 0:384].rearrange("p (a b) -> p a b", a=3), r=tk_, w=["ABW"])
                iob = iota[:].unsqueeze(1).to_broadcast([128, 16, 128])
                for tb in range(8):
                    t0 = tb * 16
                    ao, aok = AohB[tb % 2], "AohB%d" % (tb % 2)
                    bo, bok = BohB[tb % 2], "BohB%d" % (tb % 2)
                    self.tt("dve", ao[:], iob, ABW[:, 0, t0:t0 + 16].unsqueeze(2).to_broadcast([128, 16, 128]), ALU.is_equal,
                            r=["iota", "ABW"], w=[aok])
                    self.tt("dve", bo[:], iob, ABW[:, 1, t0:t0 + 16].unsqueeze(2).to_broadcast([128, 16, 128]), ALU.is_equal,
                            r=["iota", "ABW"], w=[bok])
                    self.tt("pool" if tb % 2 == 0 else "dve", bo[:], bo[:],
                            ABW[:, 2, t0:t0 + 16].unsqueeze(2).to_broadcast([128, 16, 128]), ALU.mult, r=[bok, "ABW"], w=[bok])
                    for q4 in range(4):
                        gp, gk = self.ps(4)
                        for j in range(4):
                            tl = q4 * 4 + j
                            self.mm(gp[:, j * 128:(j + 1) * 128], ao[:, tl, :], bo[:, tl, :], r=[aok, bok], w=gk)
                        tg = t0 + q4 * 4
                        self.cp("act", GT[:, :, tg:tg + 4], gp[:, :].rearrange("p (t i) -> p i t", t=4), r=gk, w=["GT"])
                self.ld(gscr[tt, :, :], GT[:].rearrange("p a b -> p (a b)"), r=["GT"], w=["gscr"])
            S.barrier()
        with ExitStack() as st2:
            self.st = st2
            if from_x:
                xa = sb("xa2", [128, D], F32)
            ACC = sb("ACC", [128, NT, D], F32)
            xtCs = [sb("xtC%d" % j, [128, D], F32) for j in range(2)]
            DTbs = [sb("DTb%d" % j, [128, 4, 8, 128], BF16) for j in range(2)]
            UPbs = [sb("UPb%d" % j, [128, 4, D], BF16) for j in range(2)]
            GSl = sb("GSl", [128, NT, 512], BF16)
            Wts = [sb("Wt%d" % j, [128, 4, TOK], BF16) for j in range(2)]
            g_us = [sb("g_u%d" % j, [128, 512], F32) for j in range(2)]
            g_ss = [sb("g_s%d" % j, [128, 512], F32) for j in range(2)]
            gctr = [0]
            eupv = din["eup"].rearrange("(a b) d -> a b d", b=128)
            NCH = (TOK + 511) // 512
            chunks = [(l, tc) for tc in range(NCH) for l in range(4)]
            SQ = float(np.sqrt(0.044715))
            bctr = {"d": 0, "u": 0}

            def bank_of(kind):
                banks = (0, 1, 2, 3) if kind == "d" else (4, 6, 7)
                b = banks[bctr[kind] % len(banks)]
                bctr[kind] += 1
                return self.pbank[b][:, :], ["@b%d" % b]

            def pf_load(g, p):
                if p < 4:
                    self.S.dma("pool", DTbs[g % 2][:, p, :, :], din["edT"][g * 4 + p].rearrange("(k p) i -> p k i", p=128),
                               w=["DTb%d_%d" % (g % 2, p)])
                else:
                    u2 = p - 4
                    self.S.dma("pool", UPbs[g % 2][:, u2, :], eupv[:, g * 4 + u2, :], w=["UPb%d_%d" % (g % 2, u2)])

            def pf_cast(g, p):
                pass

            def load_gsl(g):
                for tt in range(NT):
                    self.ld(GSl[:, tt, :], gscr[tt, :, g * 512:(g + 1) * 512], r=["gscr"], w=["GSl%d" % tt])

            pend = []

            def down_chunk(g, ci):
                l, tc = chunks[ci]
                DTb, dk = DTbs[g % 2], "DTb%d" % (g % 2)
                Wt, wk = Wts[g % 2], "Wt%d" % (g % 2)
                n = min(512, TOK - tc * 512)
                ap_, ak = bank_of("d")
                for k in range(8):
                    self.mm(ap_[:, 0:n], DTb[:, l, k, :], H2[:, k, tc * 512:tc * 512 + n], r=[dk + "_%d" % l, "H2"], w=ak,
                            start=(k == 0), stop=(k == 7))
                gb_ = gctr[0] % 2
                gctr[0] += 1
                g_u, g_s, guk, gsk = g_us[gb_], g_ss[gb_], "g_u%d" % gb_, "g_s%d" % gb_
                self.act(g_u[:, 0:n], ap_[:, 0:n], AF.Square, r=ak, w=[guk], scale=SQ)
                self.stt("dve", g_u[:, 0:n], g_u[:, 0:n], 1.0, ap_[:, 0:n], ALU.add, ALU.mult, r=ak + [guk], w=[guk])
                self.act(g_s[:, 0:n], g_u[:, 0:n], AF.Sigmoid, r=[guk], w=[gsk], scale=1.5957691216057308)
                pend.append((l, tc, n, ap_, ak, g_s, gsk, Wt, wk))

            def down_finish():
                if not pend:
                    return
                l, tc, n, ap_, ak, g_s, gsk, Wt, wk = pend.pop(0)
                gsl = GSl[:, tc * 4:tc * 4 + (n // 128), l * 128:(l + 1) * 128]
                self.tt("pool", g_s[:, 0:n].rearrange("p (a b) -> p a b", b=128), g_s[:, 0:n].rearrange("p (a b) -> p a b", b=128),
                        gsl, ALU.mult, r=[gsk] + ["GSl%d" % t_ for t_ in range(tc * 4, tc * 4 + (n // 128))], w=[gsk])
                self.tt("dve", Wt[:, l, tc * 512:tc * 512 + n], ap_[:, 0:n], g_s[:, 0:n], ALU.mult, r=ak + [gsk], w=[wk])

            def up_tile(g, tt):
                UPb, uk_ = UPbs[g % 2], "UPb%d" % (g % 2)
                Wt, wk = Wts[g % 2], "Wt%d" % (g % 2)
                for hf in range(2):
                    op_, ok_ = bank_of("u")
                    for l in range(4):
                        self.mm(op_[:, :], Wt[:, l, tt * 128:(tt + 1) * 128], UPb[:, l, hf * 512:(hf + 1) * 512],
                                r=[wk, uk_ + "_%d" % l], w=ok_, start=(l == 0), stop=(l == 3))
                    if g == 0:
                        self.cp("dve", ACC[:, tt, hf * 512:(hf + 1) * 512], op_[:, :], r=ok_, w=["ACC%d" % tt])
                    else:
                        self.tt("dve", ACC[:, tt, hf * 512:(hf + 1) * 512], ACC[:, tt, hf * 512:(hf + 1) * 512], op_[:, :],
                                ALU.add, r=ok_ + ["ACC%d" % tt], w=["ACC%d" % tt])

            for p in range(8):
                pf_load(0, p)
                pf_cast(0, p)
            for p in range(4):
                pf_load(1, p)
                pf_cast(1, p)
            load_gsl(0)
            for ci in range(len(chunks)):
                down_chunk(0, ci)
                down_finish()
            for g in range(32):
                if g + 1 < 32:
                    load_gsl(g + 1)
                pieces = []
                if g + 1 < 32:
                    pieces += [(g + 1, p) for p in range(4, 8)]
                if g + 2 < 32:
                    pieces += [(g + 2, p) for p in range(4)]
                nsteps = max(len(chunks), NT, 2 * len(pieces))
                for s_ in range(nsteps):
                    if s_ % 2 == 0 and s_ // 2 < len(pieces):
                        pf_load(*pieces[s_ // 2])
                    if s_ % 2 == 1 and s_ // 2 < len(pieces):
                        pf_cast(*pieces[s_ // 2])
                    if g + 1 < 32 and s_ < len(chunks):
                        down_chunk(g + 1, s_)
                    if s_ < NT:
                        up_tile(g, s_)
                    if len(pend) > 1 or s_ == nsteps - 1:
                        down_finish()
                while pend:
                    down_finish()
            for tt in range(NT):
                b = 0
                xt, xk = xtCs[tt % 2], "xtC%d" % (tt % 2)
                load_sel(tt, xt, xk)
                self.tt("pool", ACC[:, tt, :], ACC[:, tt, :], self.gtb[:, 1024:2048], ALU.mult, r=["ACC%d" % tt, "gtb"],
                        w=["ACC%d" % tt])
                self.tt("dve", xt[:], xt[:], ACC[:, tt, :], ALU.add, r=[xk, "ACC%d" % tt], w=[xk])
                self.ld(out[tt * 128:(tt + 1) * 128, :], xt[:], r=[xk], w=[])


def _col(v):
    v = np.asarray(v, np.float32).reshape(-1)
    return np.ascontiguousarray(v.reshape(-1, 128).T)


def _rep(v):
    v = np.asarray(v, np.float32).reshape(-1)
    return np.ascontiguousarray(np.broadcast_to(v[None, :], (128, v.size)))


def prep(inp, nblk=32):
    f = lambda a: np.ascontiguousarray(np.asarray(a, np.float32))
    x = f(inp["x"])
    c = f(inp["c"])
    pos = np.asarray(inp["positions"], np.int32)
    b_ada = f(inp["b_ada"])[0]
    t = np.arange(128)
    msl = (t[None, :] < t[:, None]).astype(np.float32)
    msu = (t[None, :] > t[:, None]).astype(np.float32)
    mui = (t[None, :] >= t[:, None]).astype(np.float32)
    bones = np.kron(np.eye(2, dtype=np.float32), np.ones((64, 64), np.float32))
    half = 16
    invf = (10000.0 ** (-np.arange(half, dtype=np.float32) / half)).astype(np.float32)
    shared = dict(
        rowA=_rep(inp["mu_shift"]),
        ident=np.eye(128, dtype=np.float32), msl=msl, mg1=np.concatenate([msu, -mui], 1),
        mg2=np.concatenate([msu, mui], 1), bones=bones, caus=mui.copy(),
        iota=_rep(np.arange(128, dtype=np.float32)),
        w_ada=f(inp["w_ada"])[0], w_in=f(inp["w_in"])[0], w_uq=f(inp["w_uq"])[0], w_ukv=f(inp["w_ukv"])[0],
        w_o_mla=f(inp["w_o_mla"])[0],
        wl=np.ascontiguousarray(np.concatenate([f(inp["w_decay_up"])[0], f(inp["w_aaa_up"])[0]], 0)),
        w_gate_up=f(inp["w_gate_up"])[0], w_o_rwkv=f(inp["w_o_rwkv"])[0], w_out=f(inp["w_out"])[0],
        w_query=f(inp["w_query"])[0],
        sk1T=np.ascontiguousarray(f(inp["sub_keys1"])[0].T), sk2T=np.ascontiguousarray(f(inp["sub_keys2"])[0].T),
        edT=np.ascontiguousarray(f(inp["expert_down"])[0].reshape(128, 128, D).transpose(1, 2, 0)),
        eup=f(inp["expert_up"])[0],
        rowC=_rep(b_ada[5 * D:6 * D]),
    )
    maps = []
    for core in range(8):
        b, hf = core // 2, core % 2
        colp = np.concatenate([
            _col(c[b]), _col(b_ada[0:D]), _col(b_ada[D:2 * D]), _col(b_ada[3 * D:4 * D]), _col(b_ada[4 * D:5 * D]),
            _col(inp["norm_mix"]), _col(inp["norm_ffn"]), _col(inp["decay_base"]), _col(inp["aaa_base"]),
            _col(inp["k_k"]), _col(inp["k_a"]), _col(inp["r_k"]), _col(inp["ln_x_w"]), _col(inp["ln_x_b"])], 1)
        rowB = np.concatenate([_rep(inp["q_a_norm"]), _rep(inp["kv_a_norm"]), _rep(inp["q_norm"]), _rep(inp["k_norm"]),
                               _rep(invf / np.float32(2 * np.pi)), _rep(b_ada[2 * D:3 * D])], 1)
        selm = np.zeros((128, 2), np.float32)
        selm[:, hf] = 1.0
        m = dict(shared)
        nh = (nblk // 2) * 128
        xo = np.zeros((2048, D), np.float32)
        own = np.concatenate([np.arange((2 * j + hf) * 128, (2 * j + hf + 1) * 128) for j in range(nblk // 2)])
        xo[:nh] = x[b, own]
        po = np.zeros((2048,), np.int32)
        po[:nh] = pos[b, own]
        m.update(xown=xo, poso=np.ascontiguousarray(po.reshape(16, 128).T), xf=x[b], colp=np.ascontiguousarray(colp), rowB=np.ascontiguousarray(rowB),
                 posc=np.ascontiguousarray(pos[b].reshape(32, 128).T), selm=selm)
        maps.append(m)
    return maps


_CACHE = {}


def kernel(**inputs):
    if "k" not in _CACHE:
        _CACHE["k"] = K()
    kb = _CACHE["k"]
    maps = prep(inputs)
    res = run_bass_kernel_spmd(kb.nc, maps, core_ids=list(range(8)))
    outp = np.zeros((4, SEQ, D), np.float32)
    for core in range(8):
        b, hf = core // 2, core % 2
        own = np.concatenate([np.arange((2 * j + hf) * 128, (2 * j + hf + 1) * 128) for j in range(16)])
        outp[b, own] = res.results[core]["out"]
    return outp
```

```python
import os
import numpy as np
from contextlib import ExitStack
import concourse.bass as bass
import concourse.mybir as mybir
from concourse.bass_utils import run_bass_kernel_spmd

F32 = mybir.dt.float32
BF16 = mybir.dt.bfloat16
I32 = mybir.dt.int32
U32 = mybir.dt.uint32
F32R = mybir.dt.float32r if os.environ.get("NO_F32R") is None else mybir.dt.float32
AF = mybir.ActivationFunctionType
ALU = mybir.AluOpType
AX = mybir.AxisListType

ENGS = ("pe", "act", "dve", "pool", "sp")
EPOCH = 20000
NDMA = 48
D = 1024
SEQ = 4096
C0 = float(np.exp(-0.5))
EPS = 1e-6
GN_EPS = 64e-5


class Sched:
    def __init__(self, nc, stack):
        self.nc = nc
        self.stack = stack
        self.ops = {e: [] for e in ENGS}
        self.cnt = {e: 0 for e in ENGS}
        self.sems = {}
        self.lastw = {}
        self.readers = {}
        self.seen = {e: {} for e in ENGS}
        self.dma_i = 0
        self.dma_cnt = [0] * NDMA

    def sem(self, key):
        if key not in self.sems:
            self.sems[key] = self.stack.enter_context(self.nc.semaphore("s_%s_%s" % key))
        return self.sems[key]

    def _tok(self, eng):
        c = self.cnt[eng]
        return ((eng, c // EPOCH), c % EPOCH + 1)

    def _deps(self, r, w):
        toks = []
        for b in list(r) + list(w):
            t = self.lastw.get(b)
            if t is not None:
                toks.append(t)
        for b in w:
            toks.extend(self.readers.get(b, ()))
        return toks

    def _filter(self, eng, toks):
        need = {}
        for (k, v) in toks:
            if eng == "pe" and k[0] == "pe":
                continue
            if self.seen[eng].get(k, 0) >= v:
                continue
            if need.get(k, 0) < v:
                need[k] = v
        for k, v in need.items():
            self.seen[eng][k] = v
        return [(self.sem(k), v) for k, v in need.items()]

    def _commit(self, tok, r, w):
        for b in w:
            self.lastw[b] = tok
            self.readers[b] = []
        for b in r:
            self.readers.setdefault(b, []).append(tok)

    def op(self, eng, fn, r=(), w=()):
        ex = [k for k in r if k.startswith("@")]
        if ex:
            w = list(w) + ex
            r = [k for k in r if not k.startswith("@")]
        waits = self._filter(eng, self._deps(r, w))
        tok = self._tok(eng)
        self.cnt[eng] += 1
        self.ops[eng].append((fn, waits, (self.sem(tok[0]), 1)))
        self._commit(tok, r, w)
        return tok

    def dma(self, eng, out, in_, r=(), w=()):
        k = self.dma_i % NDMA
        self.dma_i += 1
        toks = self._deps(r, w)
        if self.dma_cnt[k] > 0:
            toks.append((("dma", k), 16 * self.dma_cnt[k]))
        waits = self._filter(eng, toks)
        self.dma_cnt[k] += 1
        tok = (("dma", k), 16 * self.dma_cnt[k])
        fn = lambda e, out=out, in_=in_: e.dma_start(out=out, in_=in_)
        self.ops[eng].append((fn, waits, (self.sem(tok[0]), 16)))
        self._commit(tok, r, w)
        return tok

    def barrier(self):
        toks = []
        for e in ENGS:
            if self.cnt[e] > 0:
                c = self.cnt[e] - 1
                toks.append(((e, c // EPOCH), c % EPOCH + 1))
        for k in range(NDMA):
            if self.dma_cnt[k] > 0:
                toks.append((("dma", k), 16 * self.dma_cnt[k]))
        for e in ENGS:
            waits = [(self.sem(k), v) for (k, v) in toks if self.seen[e].get(k, 0) < v]
            for (k, v) in toks:
                self.seen[e][k] = max(self.seen[e].get(k, 0), v)
            if waits:
                self.ops[e].append((None, waits, None))
        self.lastw.clear()
        self.readers.clear()

    def emit(self):
        nc = self.nc
        ops = self.ops

        def run(e, lst):
            for (fn, waits, inc) in lst:
                for (s, v) in waits:
                    e.wait_ge(s, v)
                if fn is not None:
                    fn(e).then_inc(inc[0], inc[1])

        with nc.Block() as block:
            @block.sync
            def _(e):
                run(e, ops["sp"])

            @block.tensor
            def _(e):
                run(e, ops["pe"])

            @block.scalar
            def _(e):
                run(e, ops["act"])

            @block.vector
            def _(e):
                run(e, ops["dve"])

            @block.gpsimd
            def _(e):
                run(e, ops["pool"])
        self.ops = {e: [] for e in ENGS}


CO = dict(c=0, bsh1=8, bsc1=16, bsh2=24, bsc2=32, nmix=40, nffn=48, dbase=56, abase=60, kk=64, ka=68,
          rk=72, lnw=76, lnb=80)
NCOL = 84
RB = dict(qan=0, kvan=256, qn=384, kn=480, invf=576, bgt1=592)
NRB = 592 + 1024


class K:
    def __init__(self, nblk=32, dbg=False, phases="ABC"):
        self.nblk = nblk
        self.dbg = dbg
        self.phases = phases
        self.nc = bass.Bass("TRN2", target_bir_lowering=False)
        self.build()

    def dram(self, name, shape, dt, kind="ExternalInput"):
        return self.nc.dram_tensor(name, list(shape), dt, kind=kind).ap()

    def sb(self, name, shape, dt, st=None):
        self._uid = getattr(self, "_uid", 0) + 1
        return (st or self.st).enter_context(self.nc.sbuf_tensor("sb%d_%s" % (self._uid, name), list(shape), dt))

    def mm(self, out, lhsT, rhs, r, w, start=True, stop=True):
        self.S.op("pe", lambda e: e.matmul(out, lhsT=lhsT, rhs=rhs, start=start, stop=stop), r=r, w=w)

    def tr(self, out, in_, ident, r, w):
        self.S.op("pe", lambda e: e.transpose(out, in_, ident), r=r, w=w)

    def act(self, out, in_, func, r, w, bias=None, scale=None, accum=None, eng="act"):
        kw = {}
        if bias is not None:
            kw["bias"] = bias
        if scale is not None:
            kw["scale"] = scale
        if accum is not None:
            kw["accum_out"] = accum
        self.S.op("act", lambda e: e.activation(out=out, in_=in_, func=func, **kw), r=r, w=w)

    def tt(self, eng, out, in0, in1, op, r, w):
        self.S.op(eng, lambda e: e.tensor_tensor(out=out, in0=in0, in1=in1, op=op), r=r, w=w)

    def ts(self, eng, out, in0, s1, s2, op0, op1, r, w, accum=None):
        if s2 is None:
            self.S.op(eng, lambda e: e.tensor_single_scalar(out=out, in_=in0, scalar=s1, op=op0), r=r, w=w)
        elif accum is not None:
            self.S.op(eng, lambda e: e.tensor_scalar(out=out, in0=in0, scalar1=s1, scalar2=s2, op0=op0, op1=op1,
                                                     accum_out=accum), r=r, w=w)
        else:
            self.S.op(eng, lambda e: e.tensor_scalar(out=out, in0=in0, scalar1=s1, scalar2=s2, op0=op0, op1=op1),
                      r=r, w=w)

    def stt(self, eng, out, in0, scalar, in1, op0, op1, r, w, accum=None):
        if accum is None:
            self.S.op(eng, lambda e: e.scalar_tensor_tensor(out=out, in0=in0, scalar=scalar, in1=in1, op0=op0,
                                                            op1=op1), r=r, w=w)
        else:
            self.S.op(eng, lambda e: e.scalar_tensor_tensor(out=out, in0=in0, scalar=scalar, in1=in1, op0=op0,
                                                            op1=op1, accum_out=accum), r=r, w=w)

    def cp(self, eng, out, in_, r, w):
        if eng == "act":
            self.S.op("act", lambda e: e.copy(out=out, in_=in_), r=r, w=w)
        else:
            self.S.op(eng, lambda e: e.tensor_copy(out=out, in_=in_), r=r, w=w)

    def recip(self, out, in_, r, w):
        self.S.op("dve", lambda e: e.reciprocal(out=out, in_=in_), r=r, w=w)

    def memset(self, eng, ap, val, w):
        self.S.op(eng, lambda e: e.memset(ap, val), w=w)

    def ld(self, out, in_, w, r=(), eng="sp"):
        self.S.dma(eng, out, in_, r=r, w=w)

    def ps(self, n=1):
        banks = getattr(self, "banks1", (0, 1, 2, 3, 4))
        if not hasattr(self, "bank_last"):
            self.bank_last = {}
            self.bank_cnt = {}
            self.alloc_ctr = 0
        self.alloc_ctr += 1
        b = min(banks, key=lambda x: self.bank_last.get(x, 0))
        self.bank_last[b] = self.alloc_ctr
        c = self.bank_cnt.get((b, n), 0)
        self.bank_cnt[(b, n)] = c + 1
        off = (c % (4 // n)) * n
        return self.pbank[b][:, off * 128:(off + n) * 128], ["@b%d" % b]

    def psb(self, n):
        u = self.psb_ptr
        if u + n > 8:
            u = 0
        self.psb_ptr = u + n
        return self.ptb[:, u * 128:(u + n) * 128], ["@b5"]

    def build(self):
        nc = self.nc
        NB = self.nblk
        T = NB * 128
        din = {}
        for name, shape, dt in [
            ("xf", [SEQ, D], F32), ("xown", [2048, D], F32), ("poso", [128, 16], I32), ("colp", [128, NCOL], F32), ("rowA", [128, 1792], F32),
            ("rowB", [128, NRB], F32), ("rowC", [128, 1024], F32), ("posc", [128, 32], I32),
            ("ident", [128, 128], F32), ("msl", [128, 128], F32), ("mg1", [128, 256], F32),
            ("mg2", [128, 256], F32), ("bones", [128, 128], F32), ("caus", [128, 128], F32),
            ("iota", [128, 128], F32), ("selm", [128, 2], F32),
            ("w_ada", [D, 6 * D], F32), ("w_in", [D, 4256], F32), ("w_uq", [256, 768], F32),
            ("w_ukv", [128, 1024], F32), ("w_o_mla", [512, D], F32), ("wl", [128, 512], F32),
            ("w_gate_up", [128, 512], F32), ("w_o_rwkv", [512, D], F32), ("w_out", [D, D], F32),
            ("w_query", [D, 2048], F32), ("sk1T", [128, 128], F32), ("sk2T", [128, 128], F32),
            ("edT", [128, D, 128], F32), ("eup", [16384, D], F32),
        ]:
            din[name] = self.dram(name, shape, dt)
        self.din = din
        out = self.dram("out", [2048, D], F32, kind="ExternalOutput")
        zscr = self.dram("zscr", [512, SEQ], BF16, kind="Internal")
        x1scr = self.dram("x1scr", [SEQ, D], F32, kind="Internal")
        gscr = self.dram("gscr", [16, 128, 16 * 8 * 128], BF16, kind="Internal")
        self.dbg_out = {}
        if self.dbg:
            for name, shape in [("d_z", [512, SEQ]), ("d_x1", [SEQ, D]), ("d_misc", [128, 4096])]:
                self.dbg_out[name] = self.dram(name, shape, F32 if name != "d_z" else BF16, kind="ExternalOutput")

        with ExitStack() as st0:
            self.st = st0
            self.S = S = Sched(nc, st0)
            self.pbank = [st0.enter_context(nc.psum_tensor("pb%d" % i, [128, 512], F32)) for i in range(8) if i != 5]
            self.pbank.insert(5, None)
            self.ptb = st0.enter_context(nc.psum_tensor("ptb", [128, 1024], BF16))
            self.ps_ptr = 0
            self.psb_ptr = 0
            sb = self.sb
            ident = sb("ident", [128, 128], F32)
            identb = sb("identb", [128, 128], BF16)
            colp = sb("colp", [128, NCOL], F32)
            modc = sb("modc", [128, 32], F32)
            sc1 = sb("sc1", [128, 8], F32)
            sc2 = sb("sc2", [128, 8], F32)
            gtb = sb("gtb", [128, 2048], F32)
            ones = sb("ones", [128, 128], F32)
            self.scr1 = sb("scr1", [128, 2], F32)
            self.ld(ident[:], din["ident"][:, :], w=["ident"])
            self.ld(colp[:], din["colp"][:, :], w=["colp"])
            self.cp("dve", identb[:], ident[:], r=["ident"], w=["identb"])
            self.memset("pool", ones[:], 1.0, w=["ones"])
            self.ident, self.identb, self.colp, self.ones = ident, identb, colp, ones

            with ExitStack() as st:
                self.st = st
                silc = sb("silc", [128, 8], F32)
                self.act(silc[:], colp[:, 0:8], AF.Silu, r=["colp"], w=["silc"])
                wst = [sb("wst%d" % j, [128, 8, 1024], F32) for j in range(2)]
                gi = 0
                mps, mk = self.ps(1)
                for grp, c0 in enumerate([0, 1024, 3072, 4096]):
                    wt = wst[gi % 2]
                    wk = "wst%d" % (gi % 2)
                    gi += 1
                    for k in range(8):
                        self.ld(wt[:, k, :], din["w_ada"][k * 128:(k + 1) * 128, c0:c0 + 1024], w=[wk + "_%d" % k])
                    for j in range(8):
                        for k in range(8):
                            self.mm(mps[:, grp * 8 + j:grp * 8 + j + 1], wt[:, k, j * 128:(j + 1) * 128],
                                    silc[:, k:k + 1], r=[wk + "_%d" % k, "silc"], w=mk, start=(k == 0), stop=(k == 7))
                self.tt("dve", modc[:], mps[:, 0:32], colp[:, 8:40], ALU.add, r=mk + ["colp"], w=["modc"])
                self.stt("dve", sc1[:], modc[:, 8:16], 1.0, colp[:, CO["nmix"]:CO["nmix"] + 8], ALU.add, ALU.mult,
                         r=["modc", "colp"], w=["sc1"])
                self.stt("dve", sc2[:], modc[:, 24:32], 1.0, colp[:, CO["nffn"]:CO["nffn"] + 8], ALU.add, ALU.mult,
                         r=["modc", "colp"], w=["sc2"])
                grow = sb("grow", [1, 2048], F32)
                for grp, c0 in enumerate([2048, 5120]):
                    wt = wst[gi % 2]
                    wk = "wst%d" % (gi % 2)
                    gi += 1
                    for k in range(8):
                        self.ld(wt[:, k, :], din["w_ada"][k * 128:(k + 1) * 128, c0:c0 + 1024], w=[wk + "_%d" % k])
                    for hf in range(2):
                        rp, rk_ = self.ps(4)
                        for k in range(8):
                            self.mm(rp[0:1, :], silc[:, k:k + 1], wt[:, k, hf * 512:(hf + 1) * 512],
                                    r=[wk + "_%d" % k, "silc"], w=rk_, start=(k == 0), stop=(k == 7))
                        self.cp("act", grow[0:1, grp * 1024 + hf * 512: grp * 1024 + (hf + 1) * 512], rp[0:1, :],
                                r=rk_, w=["grow"])
                for q in range(4):
                    bp, bk = self.ps(4)
                    self.mm(bp[:, :], ones[0:1, :], grow[0:1, q * 512:(q + 1) * 512], r=["ones", "grow"], w=bk)
                    self.cp("act", gtb[:, q * 512:(q + 1) * 512], bp[:, :], r=bk, w=["gtb"])
                self.S.barrier()
                self.S.emit()
            self.modc, self.sc1, self.sc2, self.gtb = modc, sc1, sc2, gtb

            if "A" in self.phases:
                with ExitStack() as st:
                    self.st = st
                    self.phaseA(zscr)
                    self.S.barrier()
                    self.S.emit()
            if "B" in self.phases:
                with ExitStack() as st:
                    self.st = st
                    self._phase_st = st
                    self.phaseB(zscr, x1scr)
                    self.S.barrier()
                    self.S.emit()
            if "C" in self.phases:
                with ExitStack() as st:
                    self.st = st
                    self.phaseC(x1scr, gscr, out)
                    self.S.barrier()
                    self.S.emit()
            if self.dbg:
                with ExitStack() as st:
                    self.st = st
                    if "A" in self.phases:
                        self.S.dma("sp", self.dbg_out["d_z"][:, :], zscr[:, :])
                    if "B" in self.phases:
                        self.S.dma("sp", self.dbg_out["d_x1"][:, :], x1scr[:, :])
                    self.S.barrier()
                    self.S.emit()

    def alloc_h(self, nbuf=2):
        sb = self.sb
        self.xt = [sb("xt%d" % j, [128, D], F32) for j in range(nbuf)]
        if nbuf == 1:
            self.xt = [self.xt[0], self.xt[0]]
        self.xn = sb("xn", [128, D], BF16)
        self.junk = self.xn
        self.ss = sb("ss", [128, 2], F32)
        self.hT = [sb("hT%d" % j, [128, 8, 130], BF16) for j in range(nbuf)]
        if nbuf == 1:
            self.hT = [self.hT[0], self.hT[0]]
        self.nbuf = nbuf
        self.memset("pool", self.hT[0][:, :, 0:2], 0.0, w=["hT0"])
        self.epsc = sb("epsc", [128, 2], F32)
        self.memset("pool", self.epsc[:, 0:1], EPS, w=["epsc"])
        self.memset("pool", self.epsc[:, 1:2], GN_EPS, w=["epsc"])

    def make_h(self, i, src, scale, shift, skeys, carry=True, evac_act=False):
        b = i % self.nbuf
        xt, xk = self.xt[b], "xt%d" % b
        hk = "hT%d" % b
        if src is not None:
            self.ld(xt[:], src, w=[xk])
        self.memset("pool", self.ss[:, 0:1], 0.0, w=["ss"])
        self.act(self.junk[:], xt[:], AF.Square, r=[xk], w=["xn", "ss"], accum=self.ss[:, 0:1])
        self.act(self.ss[:, 1:2], self.ss[:, 0:1], AF.Sqrt, r=["ss", "epsc"], w=["ss"], bias=self.epsc[:, 0:1],
                 scale=1.0 / D)
        self.recip(self.ss[:, 1:2], self.ss[:, 1:2], r=["ss"], w=["ss"])
        self.ts("dve", self.xn[:], xt[:], self.ss[:, 1:2], None, ALU.mult, None, r=[xk, "ss"], w=["xn"])
        HS = int(os.environ.get("HSTOP", "99"))
        if HS <= 1:
            return self.hT[b], hk
        for half in range(2):
            ppb, pk = self.psb(4)
            for j in range(4):
                c = half * 4 + j
                self.tr(ppb[:, j * 128:(j + 1) * 128], self.xn[:, c * 128:(c + 1) * 128], self.identb[:],
                        r=["xn", "identb"], w=pk)
            if HS <= 2:
                continue
            for j in range(4):
                c = half * 4 + j
                if os.environ.get("EV", "") == "cp":
                    self.cp("dve", self.hT[b][:, c, 2:130], ppb[:, j * 128:(j + 1) * 128], r=pk, w=[hk])
                elif os.environ.get("EV", "") == "sb":
                    self.ts("dve", self.hT[b][:, c, 2:130], self.xn[:, 0:128],
                            scale[:, c:c + 1], shift[:, c:c + 1], ALU.mult, ALU.add, r=pk + skeys, w=[hk])
                elif evac_act:
                    self.act(self.hT[b][:, c, 2:130], ppb[:, j * 128:(j + 1) * 128], AF.Identity, r=pk + skeys, w=[hk],
                             bias=shift[:, c:c + 1], scale=scale[:, c:c + 1])
                else:
                    self.ts("dve", self.hT[b][:, c, 2:130], ppb[:, j * 128:(j + 1) * 128],
                            scale[:, c:c + 1], shift[:, c:c + 1], ALU.mult, ALU.add, r=pk + skeys, w=[hk])
        if carry and HS > 3:
            self.cp("pool", self.hT[1 - b][:, :, 1:2], self.hT[b][:, :, 129:130], r=[hk], w=["hT%d" % (1 - b)])
        return self.hT[b], hk

    def load_w_bf16(self, dst, dkey, src_rows, ncols, stg, skeys, kchunks):
        for k in range(kchunks):
            self.S.dma("pool", dst[:, k, :], src_rows(k), w=[dkey + "_k%d" % k])
        self.S.op("pool", lambda e: e.memset(self.scr1[0:1, 0:1], 0.0), r=[dkey + "_k%d" % k for k in range(kchunks)], w=[dkey, "scr1"])

    def phaseA(self, zscr):
        sb, din, S = self.sb, self.din, self.S
        NB = self.nblk
        colp = self.colp
        self.alloc_h()
        self.banks1 = (0, 1, 2, 3, 4, 7)
        wa = sb("wa", [128, 8, 1792], BF16)
        wb = sb("wb", [128, 8, 1792], BF16)
        with ExitStack() as stw:
            mu = sb("mu", [128, 1792], F32, stw)
            self.ld(mu[:], din["rowA"][:, :], w=["mu"])
            stg = [sb("stgA%d" % j, [128, 1792], F32, stw) for j in range(2)]
            tmpb = [sb("tmpbA%d" % j, [128, 1792], F32, stw) for j in range(2)]
            for k in range(8):
                s, sk = stg[k % 2], "stgA%d" % (k % 2)
                t, tk = tmpb[k % 2], "tmpbA%d" % (k % 2)
                self.ld(s[:], din["w_in"][k * 128:(k + 1) * 128, 416:2208], w=[sk])
                self.tt("dve", t[:], s[:], mu[:], ALU.mult, r=[sk, "mu"], w=[tk])
                self.cp("act", wb[:, k, :], t[:], r=[tk], w=["wb"])
                self.tt("dve", wa[:, k, :], s[:], t[:], ALU.subtract, r=[sk, tk], w=["wa"])
            S.barrier()
        wl = sb("wl", [128, 512], F32)
        wg = sb("wg", [128, 512], F32)
        msl = sb("msl", [128, 128], F32)
        mg1 = sb("mg1", [128, 256], F32)
        mg2 = sb("mg2", [128, 256], F32)
        bones = sb("bones", [128, 128], F32)
        for t_, n_ in [(wl, "wl"), (wg, "w_gate_up"), (msl, "msl"), (mg1, "mg1"), (mg2, "mg2"), (bones, "bones")]:
            self.ld(t_[:], din[n_][:, :], w=[n_])
        omka = sb("omka", [128, 4], F32)
        self.ts("dve", omka[:], colp[:, CO["ka"]:CO["ka"] + 4], -1.0, 1.0, ALU.mult, ALU.add, r=["colp"], w=["omka"])
        ident, ones = self.ident, self.ones
        identr = sb("identr", [128, 128], F32R)
        self.cp("dve", identr[:], ident[:], r=["ident"], w=["identr"])
        H = sb("H", [128, 4, 64], F32R)
        self.memset("pool", H[:].bitcast(F32), 0.0, w=["H%d_%d" % (g, a) for g in range(4) for a in range(2)])
        rkv = sb("rkv", [128, 12, 128], F32)
        ta = sb("ta", [128, 128], F32)
        sg = sb("sg", [128, 128], F32)
        G = [sb("G_%d" % j, [128, 4, 128], F32) for j in range(2)]
        BON = [sb("BON_%d" % j, [128, 4, 128], F32) for j in range(2)]
        GC = [sb("GC_%d" % j, [128, 4], F32) for j in range(2)]
        PREP_STEPS = int(os.environ.get("PREP_STEPS", "4"))
        sigG = [sb("sig_g%d" % g_, [128, 128], F32) for g_ in range(2)] * 2
        aaG = [sb("aa_g%d" % g_, [128, 128], F32) for g_ in range(2)] * 2
        kkG = [sb("kk_g%d" % g_, [128, 128], F32) for g_ in range(2)] * 2
        sqG = [sb("sq_g%d" % g_, [128, 128], F32) for g_ in range(2)] * 2
        rnG = [sb("rn_g%d" % g_, [128, 128], F32) for g_ in range(2)] * 2
        kapG = [sb("kap_g%d" % g_, [128, 128], F32) for g_ in range(2)] * 2
        tqG = [sb("tq_g%d" % g_, [128, 128], F32) for g_ in range(2)] * 2
        khG = [sb("kh_g%d" % g_, [128, 128], F32) for g_ in range(2)] * 2
        bbG = [sb("bb_g%d" % g_, [128, 128], F32) for g_ in range(2)] * 2
        rkrG = [sb("rkr_g%d" % g_, [128, 128], F32) for g_ in range(2)] * 2
        LsG = [sb("Ls_g%d" % g_, [128, 128], F32) for g_ in range(2)] * 2
        LxG = [sb("Lx_g%d" % g_, [128, 128], F32) for g_ in range(2)] * 2
        EpG = [sb("Ep_g%d" % g_, [128, 128], F32) for g_ in range(2)] * 2
        EnG = [sb("En_g%d" % g_, [128, 128], F32) for g_ in range(2)] * 2
        ExG = [sb("Ex_g%d" % g_, [128, 128], F32) for g_ in range(2)] * 2
        CM1 = [[sb("CM1_%d_%d" % (j, g), [128, 256], F32R) for g in range(4)] for j in range(2)]
        CM2 = [[sb("CM2_%d_%d" % (j, g), [128, 256], F32R) for g in range(4)] for j in range(2)]
        TM = [[sb("TM%d_%d" % (j, g), [128, 4, 128], F32R) for g in range(4)] for j in range(2)]
        WT = [sb("WT%d" % g, [128, 128], F32R) for g in range(4)]
        PPH = [[sb("PP%d_%d" % (j, h), [128, 256], F32R) for j in range(2)] for h in range(8)]
        PH = [[PPH[h][j][:, 0:128] for j in range(2)] for h in range(8)]
        PTH = [[PPH[h][j][:, 128:256] for j in range(2)] for h in range(8)]
        TTH = [sb("TT_%d" % h, [128, 128], F32R) for h in range(8)]
        AkTH = [sb("AkT_%d" % h, [128, 256], F32R) for h in range(8)]
        nBrH = [sb("nBr_%d" % h, [128, 128], F32R) for h in range(8)]
        X0H = [sb("X0_%d" % h, [128, 64], F32R) for h in range(8)]
        U0H = [sb("U0_%d" % h, [128, 64], F32R) for h in range(8)]
        UsbH = [sb("Usb_%d" % h, [128, 64], F32R) for h in range(8)]
        ysb4 = sb("ysb4", [128, 512], F32)
        cen4 = sb("cen4", [128, 512], F32)
        sq4 = sb("sq4", [128, 512], F32)
        rs4 = sb("rs4", [128, 512], F32)
        zt4 = [sb("zt4_%d" % j, [128, 512], BF16) for j in range(2)]

        for i0_ in range(min(2, NB)):
            self.ld(self.xt[i0_][:], din["xf"][i0_ * 128:(i0_ + 1) * 128, :], w=["xt%d" % i0_])

        def prep_gen(i):
            pb = i % 2
            hT, hk = self.make_h(i, None, self.sc1, self.modc[:, 0:8], ["sc1", "modc"])
            if i + 2 < NB:
                self.ld(self.xt[i % 2][:], din["xf"][(i + 2) * 128:(i + 3) * 128, :], w=["xt%d" % (i % 2)])
            STOP = int(os.environ.get("STOP", "99"))
            if STOP <= 1:
                return
            for c in range(14):
                pp, pk = self.ps(1)
                for k in range(8):
                    self.mm(pp[:, :], wa[:, k, c * 128:(c + 1) * 128], hT[:, k, 2:130], r=["wa", hk], w=pk,
                            start=(k == 0), stop=False)
                    self.mm(pp[:, :], wb[:, k, c * 128:(c + 1) * 128], hT[:, k, 1:129], r=["wb", hk], w=pk,
                            start=False, stop=(k == 7))
                if c < 12:
                    self.cp("act", rkv[:, c, :], pp[:, :], r=pk, w=["rkv%d" % c])
                    yield
                elif c == 12:
                    self.act(ta[0:64, :], pp[0:64, :], AF.Tanh, r=pk, w=["ta"])
                    self.cp("act", ta[64:128, :], pp[64:128, :], r=pk, w=["ta"])
                else:
                    self.act(sg[:, :], pp[:, :], AF.Sigmoid, r=pk, w=["sg"])
                    yield
            if STOP <= 2:
                return
            for g in range(4):
                gp, gk = self.ps(1)
                self.mm(gp[:, :], wg[:, g * 128:(g + 1) * 128], sg[:, :], r=["w_gate_up", "sg"], w=gk)
                self.cp("act", G[pb][:, g, :], gp[:, :], r=gk, w=["G%d_%d" % (pb, g)])
                yield
            if STOP <= 3:
                return
            def prep_g(g):
                rK, kK, vK = "rkv%d" % g, "rkv%d" % (4 + g), "rkv%d" % (8 + g)
                r_, k_, v_ = rkv[:, g, :], rkv[:, 4 + g, :], rkv[:, 8 + g, :]
                up, uk = self.ps(1)
                self.mm(up[:, :], wl[0:64, g * 128:(g + 1) * 128], ta[0:64, :], r=["wl", "ta"], w=uk)
                self.act(sigG[g][:], up[:, :], AF.Sigmoid, r=uk + ["colp"], w=["sig%d" % (g % 2)],
                         bias=colp[:, CO["dbase"] + g:CO["dbase"] + g + 1])
                ap_, ak = self.ps(1)
                self.mm(ap_[:, :], wl[64:128, g * 128:(g + 1) * 128], ta[64:128, :], r=["wl", "ta"], w=ak)
                self.act(aaG[g][:], ap_[:, :], AF.Sigmoid, r=ak + ["colp"], w=["aa%d" % (g % 2)],
                         bias=colp[:, CO["abase"] + g:CO["abase"] + g + 1])
                self.ts("dve", kkG[g][:], k_, colp[:, CO["kk"] + g:CO["kk"] + g + 1], None, ALU.mult, None,
                        r=[kK, "colp"], w=["kk%d" % (g % 2)])
                self.tt("pool", sqG[g][:], kkG[g][:], kkG[g][:], ALU.mult, r=["kk%d" % (g % 2)], w=["sq%d" % (g % 2)])
                yield
                sp_, sk_ = self.ps(1)
                self.mm(sp_[:, :], bones[:], sqG[g][:], r=["bones", "sq%d" % (g % 2)], w=sk_)
                self.act(rnG[g][:], sp_[:, :], AF.Sqrt, r=sk_, w=["rn%d" % (g % 2)])
                yield
                self.ts("dve", rnG[g][:], rnG[g][:], 1e-12, None, ALU.max, None, r=["rn%d" % (g % 2)], w=["rn%d" % (g % 2)])
                self.recip(rnG[g][:], rnG[g][:], r=["rn%d" % (g % 2)], w=["rn%d" % (g % 2)])
                self.tt("dve", kapG[g][:], kkG[g][:], rnG[g][:], ALU.mult, r=["kk%d" % (g % 2), "rn%d" % (g % 2)], w=["kap%d" % (g % 2)])
                yield
                self.ts("pool", tqG[g][:], aaG[g][:], colp[:, CO["ka"] + g:CO["ka"] + g + 1], omka[:, g:g + 1], ALU.mult,
                        ALU.add, r=["aa%d" % (g % 2), "colp", "omka"], w=["tq%d" % (g % 2)])
                self.tt("pool", khG[g][:], k_, tqG[g][:], ALU.mult, r=[kK, "tq%d" % (g % 2)], w=["kh%d" % (g % 2)])
                self.tt("dve", bbG[g][:], kapG[g][:], aaG[g][:], ALU.mult, r=["kap%d" % (g % 2), "aa%d" % (g % 2)], w=["bb%d" % (g % 2)])
                self.stt("dve", rkrG[g][:], r_, colp[:, CO["rk"] + g:CO["rk"] + g + 1], khG[g][:], ALU.mult, ALU.mult,
                         r=[rK, "colp", "kh%d" % (g % 2)], w=["rkr%d" % (g % 2)])
                yield
                bp, bk = self.ps(1)
                self.mm(bp[:, :], bones[:], rkrG[g][:], r=["bones", "rkr%d" % (g % 2)], w=bk)
                self.tt("dve", BON[pb][:, g, :], bp[:, :], v_, ALU.mult, r=bk + [vK], w=["BON%d_%d" % (pb, g)])
                yield
                if STOP <= 4:
                    return
                S.op("dve", lambda e: e.tensor_tensor_scan(out=LsG[g][:], data0=ones[:], data1=sigG[g][:], initial=0.0,
                                                          op0=ALU.mult, op1=ALU.add), r=["ones", "sig%d" % (g % 2)], w=["Ls%d" % (g % 2)])
                yield
                self.tt("pool", LxG[g][:], LsG[g][:], sigG[g][:], ALU.subtract, r=["Ls%d" % (g % 2), "sig%d" % (g % 2)], w=["Lx%d" % (g % 2)])
                self.act(EpG[g][:], LsG[g][:], AF.Exp, r=["Ls%d" % (g % 2)], w=["Ep%d" % (g % 2)], scale=-C0)
                self.act(EnG[g][:], LsG[g][:], AF.Exp, r=["Ls%d" % (g % 2)], w=["En%d" % (g % 2)], scale=C0)
                self.act(ExG[g][:], LxG[g][:], AF.Exp, r=["Lx%d" % (g % 2)], w=["Ex%d" % (g % 2)], scale=-C0)
                yield
                c1, c2 = CM1[pb][g], CM2[pb][g]
                c1k, c2k = "CM1_%d_%d" % (pb, g), "CM2_%d_%d" % (pb, g)
                self.tt("dve", c1[:, 0:128], kapG[g][:], ExG[g][:], ALU.mult, r=["kap%d" % (g % 2), "Ex%d" % (g % 2)], w=[c1k])
                self.tt("pool", c1[:, 128:256], r_, EpG[g][:], ALU.mult, r=[rK, "Ep%d" % (g % 2)], w=[c1k])
                self.tt("dve", c2[:, 0:128], bbG[g][:], EnG[g][:], ALU.mult, r=["bb%d" % (g % 2), "En%d" % (g % 2)], w=[c2k])
                self.tt("pool", c2[:, 128:256], khG[g][:], EnG[g][:], ALU.mult, r=["kh%d" % (g % 2), "En%d" % (g % 2)], w=[c2k])
                self.cp("pool", GC[pb][:, g:g + 1], EpG[g][:, 127:128], r=["Ep%d" % (g % 2)], w=["GC%d_%d" % (pb, g)])
                yield
                if STOP <= 5:
                    return
                tm, tmk = TM[pb][g], "TM%d_%d" % (pb, g)
                yield
                tp, tk_ = self.ps(4)
                self.tr(tp[:, 0:128], c1[:, 0:128].bitcast(F32), ident[:], r=[c1k, "ident"], w=tk_)
                self.tr(tp[:, 128:256], c2[:, 0:128].bitcast(F32), ident[:], r=[c2k, "ident"], w=tk_)
                self.tr(tp[:, 256:384], c2[:, 128:256].bitcast(F32), ident[:], r=[c2k, "ident"], w=tk_)
                self.tr(tp[:, 384:512], v_, ident[:], r=[vK, "ident"], w=tk_)
                self.cp("act", tm[:, 0, :], tp[:, 0:128], r=tk_, w=[tmk])
                self.ts("dve", tm[:, 1, :], tp[:, 128:256], -1.0, None, ALU.mult, None, r=tk_, w=[tmk])
                self.cp("act", tm[:, 2:4, :], tp[:, 256:512].rearrange("p (a b) -> p a b", a=2), r=tk_, w=[tmk])
                yield
                yield

            for g0 in (0, 2):
                subs = [prep_g(g0), prep_g(g0 + 1)]
                while subs:
                    for sg_ in list(subs):
                        try:
                            next(sg_)
                        except StopIteration:
                            subs.remove(sg_)
                    yield
            if STOP <= 6:
                return

            yield

        def head_gen(g, a, i):
            pb = i % 2
            STOP = int(os.environ.get("STOP", "99"))
            hx = 2 * g + a
            sx = "_%d" % hx
            c1, c2, tm = CM1[pb][g], CM2[pb][g], TM[pb][g]
            c1k, c2k, tmk = "CM1_%d_%d" % (pb, g), "CM2_%d_%d" % (pb, g), "TM%d_%d" % (pb, g)
            Pq, PTq, TTq, AkTq, nBrq = PH[hx], PTH[hx], TTH[hx], AkTH[hx], nBrH[hx]
            PPq = PPH[hx]
            X0q, U0q, Usbq = X0H[hx], U0H[hx], UsbH[hx]
            yp, yk = self.pbank[6][:, g * 128:(g + 1) * 128], ["@b6"]
            R = slice(64 * a, 64 * a + 64)
            Cc = slice(64 * a, 64 * a + 64)
            Hk = "H%d_%d" % (g, a)
            BC = (lambda ap_: ap_) if a == 0 else (lambda ap_: ap_.bitcast(F32))
            idm, idk = (identr, "identr") if a == 0 else (ident, "ident")
            a_ps, a_k = self.ps(1)
            self.mm(a_ps[:, :], c1[R, 0:128], c2[R, 0:128], r=[c1k, c2k], w=a_k)
            self.tt("dve", Pq[0][:], a_ps[:, :], msl[:], ALU.mult, r=a_k + ["msl"], w=["PP0" + sx])
            gb, gbk = self.ps(2)
            self.mm(gb[:, :], c2[R, 0:128], c1[R, 0:256], r=[c1k, c2k], w=gbk)
            self.tt("dve", PTq[0][:], gb[:, 0:128], mg1[:, 0:128], ALU.mult, r=gbk + ["mg1"], w=["PP0" + sx])
            self.tt("pool" if False else "dve", nBrq[:], gb[:, 128:256], mg1[:, 128:256], ALU.mult,
                    r=gbk + ["mg1"], w=["nBr" + sx])
            gk2, gkk = self.ps(2)
            self.mm(gk2[:, :], c2[R, 128:256], c1[R, 0:256], r=[c1k, c2k], w=gkk)
            self.tt("dve", AkTq[:], gk2[:, :], mg2[:], ALU.mult, r=gkk + ["mg2"], w=["AkT" + sx])
            self.tt("pool", TTq[:], ident[:], PTq[0][:], ALU.subtract, r=["ident", "PP0" + sx], w=["TT" + sx])
            yield
            if STOP <= 7:
                return
            cur = 0
            for j in range(1, 7):
                nxt = 1 - cur
                if j < 6:
                    pq_ps, pq_k = self.ps(2)
                    self.mm(pq_ps[:, 0:128], PTq[cur][:], Pq[cur][:], r=["PP%d" % cur + sx], w=pq_k)
                    self.mm(pq_ps[:, 128:256], Pq[cur][:], PTq[cur][:], r=["PP%d" % cur + sx], w=pq_k)
                    self.cp("act", PPq[nxt][:], pq_ps[:, :], r=pq_k, w=["PP%d" % nxt + sx])
                else:
                    p_ps, p_k = self.ps(1)
                    self.mm(p_ps[:, :], PTq[cur][:], Pq[cur][:], r=["PP%d" % cur + sx], w=p_k)
                    self.cp("act", Pq[nxt][:], p_ps[:, :], r=p_k, w=["PP%d" % nxt + sx])
                yield
                t_ps, t_k = self.ps(1)
                self.mm(t_ps[:, :], Pq[nxt][:], TTq[:], r=["PP%d" % nxt + sx, "TT" + sx], w=t_k)
                self.tt("dve", TTq[:], TTq[:], t_ps[:, :], ALU.add, r=["TT" + sx] + t_k, w=["TT" + sx])
                yield
                cur = nxt
            if STOP <= 8:
                return
            w_ps, w_k = self.ps(1)
            self.mm(w_ps[R, :], BC(tm[:, 0, Cc]), BC(TTq[:]), r=[tmk, "TT" + sx], w=w_k)
            self.cp("act", WT[g][R, :], w_ps[R, :], r=w_k, w=["WT%d_%d" % (g, a)])
            yield
            x_ps, x_k = self.ps(1)
            self.mm(x_ps[:, 0:64], AkTq[:, 0:128], tm[:, 3, Cc], r=["AkT" + sx, tmk], w=x_k)
            self.cp("act", X0q[:], x_ps[:, 0:64], r=x_k, w=["X0" + sx])
            yield
            u0_ps, u0_k = self.ps(1)
            self.mm(u0_ps[:, 0:64], TTq[:], X0q[:], r=["TT" + sx, "X0" + sx], w=u0_k)
            self.cp("act", U0q[:], u0_ps[:, 0:64], r=u0_k, w=["U0" + sx])
            yield
            if STOP <= 9:
                return
            u_ps, u_k = self.ps(1)
            self.mm(u_ps[:, 0:64], WT[g][R, :], H[R, g, :], r=["WT%d_%d" % (g, a), Hk], w=u_k, start=True, stop=False)
            self.mm(u_ps[:, 0:64], identr[:], U0q[:], r=["identr", "U0" + sx], w=u_k, start=False, stop=True)
            self.cp("act", Usbq[:], u_ps[:, 0:64], r=u_k, w=["Usb" + sx])
            yield
            self.mm(yp[R, :], BC(H[R, g, :]), BC(c1[R, 128:256]), r=[Hk, c1k], w=yk, start=True, stop=False)
            self.mm(yp[R, :], BC(Usbq[:]), BC(nBrq[:]), r=["Usb" + sx, "nBr" + sx], w=yk, start=False, stop=False)
            self.mm(yp[R, :], BC(tm[:, 3, Cc]), BC(AkTq[:, 128:256]), r=[tmk, "AkT" + sx], w=yk, start=False, stop=True)
            h_ps, h_k = self.ps(1)
            self.mm(h_ps[R, 0:64], BC(tm[:, 2, Cc]), BC(tm[:, 3, Cc]), r=[tmk], w=h_k, start=True, stop=False)
            self.mm(h_ps[R, 0:64], idm[R, R], BC(H[R, g, :]), r=[idk, Hk], w=h_k, start=False, stop=False)
            self.mm(h_ps[R, 0:64], BC(tm[:, 1, Cc]), BC(Usbq[:]), r=[tmk, "Usb" + sx], w=h_k, start=False, stop=True)
            self.ts("dve", H[R, g, :], h_ps[R, 0:64], GC[pb][R, g:g + 1], None, ALU.mult, None,
                    r=h_k + ["GC%d_%d" % (pb, g)], w=[Hk])
            yield


        def gn(i):
            pb = i % 2
            yp, yk = self.pbank[6][:, 0:512], ["@b6"]
            gk_all = ["G%d_%d" % (pb, g) for g in range(4)]
            bk_all = ["BON%d_%d" % (pb, g) for g in range(4)]
            lnw = colp[:, CO["lnw"]:CO["lnw"] + 4].unsqueeze(2).to_broadcast([128, 4, 128])
            lnb = colp[:, CO["lnb"]:CO["lnb"] + 4].unsqueeze(2).to_broadcast([128, 4, 128])
            v3 = lambda t: t[:].rearrange("p (g t) -> p g t", g=4)
            self.cp("act", ysb4[:], yp[:, :], r=yk, w=["ysb4"])
            m_ps, m_k = self.ps(4)
            self.mm(m_ps[:, :], bones[:], ysb4[:], r=["bones", "ysb4"], w=m_k)
            self.stt("dve", cen4[:], m_ps[:, :], -1.0 / 64, ysb4[:], ALU.mult, ALU.add, r=m_k + ["ysb4"], w=["cen4"])
            self.tt("pool", sq4[:], cen4[:], cen4[:], ALU.mult, r=["cen4"], w=["sq4"])
            v_ps, v_k = self.ps(4)
            self.mm(v_ps[:, :], bones[:], sq4[:], r=["bones", "sq4"], w=v_k)
            self.act(rs4[:], v_ps[:, :], AF.Sqrt, r=v_k + ["epsc"], w=["rs4"], bias=self.epsc[:, 1:2], scale=1.0 / 64)
            self.recip(rs4[:], rs4[:], r=["rs4"], w=["rs4"])
            self.tt("dve", cen4[:], cen4[:], rs4[:], ALU.mult, r=["cen4", "rs4"], w=["cen4"])
            self.tt("pool", v3(cen4), v3(cen4), lnw, ALU.mult, r=["cen4", "colp"], w=["cen4"])
            self.tt("pool", v3(cen4), v3(cen4), lnb, ALU.add, r=["cen4", "colp"], w=["cen4"])
            self.tt("dve", v3(cen4), v3(cen4), BON[pb][:], ALU.add, r=["cen4"] + bk_all, w=["cen4"])
            zb, zk = zt4[i % 2], "zt4_%d" % (i % 2)
            self.tt("dve", v3(zb), v3(cen4), G[pb][:], ALU.mult, r=["cen4"] + gk_all, w=[zk])
            self.ld(zscr[:, i * 128:(i + 1) * 128].rearrange("(g p) t -> p g t", p=128), v3(zb), r=[zk], w=[])

        def drain(gen):
            for _ in gen:
                pass

        drain(prep_gen(0))
        for i in range(NB):
            if int(os.environ.get("STOP", "99")) <= 6:
                if i + 1 < NB:
                    drain(prep_gen(i + 1))
                continue
            heads = [head_gen(g, a, i) for g in range(4) for a in range(2)]
            nxt = prep_gen(i + 1) if i + 1 < NB else None
            while heads or nxt is not None:
                hl = list(heads)
                marks = set(int((k_ + 1) * max(len(hl), 1) / PREP_STEPS) - 1 for k_ in range(PREP_STEPS))
                for hi_, gen in enumerate(hl):
                    try:
                        next(gen)
                    except StopIteration:
                        heads.remove(gen)
                    if nxt is not None and hi_ in marks:
                        try:
                            next(nxt)
                        except StopIteration:
                            nxt = None
                if not hl and nxt is not None:
                    try:
                        next(nxt)
                    except StopIteration:
                        nxt = None
            gn(i)

    def phaseB(self, zscr, x1scr):
        sb, din, S = self.sb, self.din, self.S
        self.banks1 = (0, 1, 2, 3, 4)
        NB = self.nblk
        ident, identb = self.ident, self.identb
        rowB = sb("rowB", [128, 592], F32)
        self.ld(rowB[:], din["rowB"][:, 0:592], w=["rowB"])
        posc = sb("posc", [128, 32], I32)
        posf = sb("posf", [128, 32], F32)
        self.ld(posc[:], din["posc"][:, :], w=["posc"])
        self.cp("dve", posf[:], posc[:], r=["posc"], w=["posf"])
        caus = sb("caus", [128, 128], F32)
        causb = sb("causb", [128, 128], BF16)
        self.ld(caus[:], din["caus"][:, :], w=["caus"])
        self.cp("dve", causb[:], caus[:], r=["caus"], w=["causb"])
        with ExitStack() as stw:
            bg = sb("bg1", [128, 1024], F32, stw)
            self.ld(bg[:], din["rowB"][:, 592:1616], w=["bg1"])
            self.tt("dve", self.gtb[:, 0:1024], self.gtb[:, 0:1024], bg[:], ALU.add, r=["gtb", "bg1"], w=["gtb"])
            S.barrier()
        KnT = sb("KnT", [128, 4, NB * 128], BF16)
        KpT = sb("KpT", [128, NB * 128], BF16)
        Vx = sb("Vx", [128, NB, 8, 65], BF16)
        self.memset("pool", Vx[:, :, :, 64:65], 1.0, w=["Vx"])
        NS = 4
        with ExitStack() as stp1:
            self.st = stp1
            wkv1 = sb("wkv1", [128, 8, 160], BF16)
            wukv1 = sb("wukv1", [128, 1, 1024], BF16)
            self.load_w_bf16(wkv1, "wkv1", lambda k: din["w_in"][k * 128:(k + 1) * 128, 256:416], 160, None, None, 8)
            self.load_w_bf16(wukv1, "wukv1", lambda k: din["w_ukv"][:, :], 1024, None, None, 1)
            eps1 = sb("eps1", [128, 1], F32)
            self.memset("pool", eps1[:], EPS, w=["eps1"])

            def make_slot(sx):
                K = lambda nm: nm + sx
                xt = sb("p1xt" + sx, [128, D], F32)
                xn = sb("p1xn" + sx, [128, D], BF16)
                ss = sb("p1ss" + sx, [128, 2], F32)
                hT = sb("p1hT" + sx, [128, 8, 130], BF16)
                kva = sb("p1kva" + sx, [128, 160], F32)
                kvs = sb("p1kvs" + sx, [128, 1024], F32)
                st2 = sb("p1st2" + sx, [128, 40], F32)
                ckv = sb("p1ckv" + sx, [128, 128], BF16)
                ckvT = sb("p1ckvT" + sx, [128, 128], BF16)
                sqb = sb("p1sqb" + sx, [128, 768], F32)
                knn = sb("p1knn" + sx, [128, 8, 64], BF16)
                kr = sb("p1kr" + sx, [128, 32], F32)
                kr2 = sb("p1kr2" + sx, [128, 128], BF16)
                cs = sb("p1cs" + sx, [128, 32], F32)
                uu = sb("p1uu" + sx, [128, 32], F32)
                ui = sb("p1ui" + sx, [128, 32], I32)
                uf = sb("p1uf" + sx, [128, 32], F32)
                um = sb("p1um" + sx, [128, 32], F32)
                t16 = [sb("p1t16_%d" % j + sx, [128, 8, 16], F32) for j in range(4)]

                def rms_rows(dst_rstd, src_sq_ap, n, rkeys, extra_scale=1.0):
                    self.act(dst_rstd, src_sq_ap, AF.Sqrt, r=rkeys + ["eps1"], w=[K("st2")], bias=eps1[:, 0:1], scale=1.0 / n)
                    self.recip(dst_rstd, dst_rstd, r=[K("st2")], w=[K("st2")])
                    if extra_scale != 1.0:
                        self.ts("dve", dst_rstd, dst_rstd, extra_scale, None, ALU.mult, None, r=[K("st2")], w=[K("st2")])

                def rope(dst, src, nh, rk, wk):
                    cosb = cs[:, 0:16].unsqueeze(1).to_broadcast([128, nh, 16])
                    sinb = cs[:, 16:32].unsqueeze(1).to_broadcast([128, nh, 16])
                    a, b, c, d = [t[:, 0:nh, :] for t in t16]
                    x1_, x2_ = src[:, :, 0:16], src[:, :, 16:32]
                    self.tt("dve", a, x1_, cosb, ALU.mult, r=rk + [K("cs")], w=[K("t16a")])
                    self.tt("pool", b, x2_, sinb, ALU.mult, r=rk + [K("cs")], w=[K("t16b")])
                    self.tt("dve", c, x1_, sinb, ALU.mult, r=rk + [K("cs")], w=[K("t16c")])
                    self.tt("pool", d, x2_, cosb, ALU.mult, r=rk + [K("cs")], w=[K("t16d")])
                    self.tt("dve", dst[:, :, 0:16], a, b, ALU.subtract, r=[K("t16a"), K("t16b")], w=wk)
                    self.tt("dve", dst[:, :, 16:32], c, d, ALU.add, r=[K("t16c"), K("t16d")], w=wk)

                def cossin(pcol):
                    self.ts("dve", uu[:, 0:16], rowB[:, RB["invf"]:RB["invf"] + 16], pcol, None, ALU.mult, None,
                            r=["rowB", "posf"], w=[K("uu")])
                    self.ts("dve", uu[:, 16:32], uu[:, 0:16], 0.0, None, ALU.add, None, r=[K("uu")], w=[K("uu")])
                    self.ts("dve", uu[:, 0:16], uu[:, 0:16], 0.25, None, ALU.add, None, r=[K("uu")], w=[K("uu")])
                    self.cp("dve", ui[:], uu[:], r=[K("uu")], w=[K("ui")])
                    self.cp("dve", uf[:], ui[:], r=[K("ui")], w=[K("uf")])
                    self.tt("dve", uf[:], uu[:], uf[:], ALU.subtract, r=[K("uu"), K("uf")], w=[K("uf")])
                    self.ts("dve", um[:], uf[:], 0.5, None, ALU.is_gt, None, r=[K("uf")], w=[K("um")])
                    self.tt("dve", uf[:], uf[:], um[:], ALU.subtract, r=[K("uf"), K("um")], w=[K("uf")])
                    self.act(cs[:], uf[:], AF.Sin, r=[K("uf")], w=[K("cs")], scale=float(2 * np.pi))


                def gen(i):
                    hk = K("hT0")
                    self.ld(xt[:], din["xf"][i * 128:(i + 1) * 128, :], w=[K("xt0")])
                    self.memset("pool", ss[:, 0:1], 0.0, w=[K("ss")])
                    self.act(xn[:], xt[:], AF.Square, r=[K("xt0")], w=[K("xn"), K("ss")], accum=ss[:, 0:1])
                    yield
                    self.act(ss[:, 1:2], ss[:, 0:1], AF.Sqrt, r=[K("ss"), "eps1"], w=[K("ss")], bias=eps1[:, 0:1], scale=1.0 / D)
                    self.recip(ss[:, 1:2], ss[:, 1:2], r=[K("ss")], w=[K("ss")])
                    self.ts("dve", xn[:], xt[:], ss[:, 1:2], None, ALU.mult, None, r=[K("xt0"), K("ss")], w=[K("xn")])
                    yield
                    for half in range(2):
                        ppb, pk = self.psb(4)
                        for j in range(4):
                            c = half * 4 + j
                            self.tr(ppb[:, j * 128:(j + 1) * 128], xn[:, c * 128:(c + 1) * 128], identb[:], r=[K("xn"), "identb"], w=pk)
                        for j in range(4):
                            c = half * 4 + j
                            self.ts("dve", hT[:, c, 2:130], ppb[:, j * 128:(j + 1) * 128], self.sc1[:, c:c + 1],
                                    self.modc[:, c:c + 1], ALU.mult, ALU.add, r=pk + ["sc1", "modc"], w=[hk])
                        yield
                    cossin(posf[:, i:i + 1])
                    yield
                    kp, kk_ = self.ps(2)
                    for k in range(8):
                        self.mm(kp[:, 0:160], hT[:, k, 2:130], wkv1[:, k, 0:160], r=[hk, "wkv1"], w=kk_, start=(k == 0), stop=(k == 7))
                    self.cp("act", kva[:], kp[:, 0:160], r=kk_, w=[K("kva")])
                    yield
                    self.memset("pool", st2[:, 0:8], 0.0, w=[K("st2")])
                    self.act(sqb[:, 0:128], kva[:, 0:128], AF.Square, r=[K("kva")], w=[K("sqb"), K("st2")], accum=st2[:, 0:1])
                    self.act(sqb[:, 128:160], kva[:, 128:160], AF.Square, r=[K("kva")], w=[K("sqb"), K("st2")], accum=st2[:, 1:2])
                    rms_rows(st2[:, 2:3], st2[:, 0:1], 128, [K("st2")])
                    rms_rows(st2[:, 3:4], st2[:, 1:2], 32, [K("st2")])
                    yield
                    self.stt("dve", ckv[:], kva[:, 0:128], st2[:, 2:3], rowB[:, RB["kvan"]:RB["kvan"] + 128], ALU.mult, ALU.mult,
                             r=[K("kva"), K("st2"), "rowB"], w=[K("ckv")])
                    self.stt("dve", kr[:], kva[:, 128:160], st2[:, 3:4], rowB[:, RB["kn"] + 64:RB["kn"] + 96], ALU.mult,
                             ALU.mult, r=[K("kva"), K("st2"), "rowB"], w=[K("kr")])
                    rope(kr2[:, 0:32].rearrange("p (h d) -> p h d", h=1), kr[:].rearrange("p (h d) -> p h d", h=1), 1,
                         [K("kr")], [K("kr2")])
                    self.cp("pool", kr2[:, 32:128].rearrange("p (c d) -> p c d", c=3), kr2[:, 0:32].unsqueeze(1).to_broadcast([128, 3, 32]), r=[K("kr2")], w=[K("kr2")])
                    yield
                    tpb, tk_ = self.psb(2)
                    self.tr(tpb[:, 0:128], ckv[:], identb[:], r=[K("ckv"), "identb"], w=tk_)
                    self.tr(tpb[:, 128:256], kr2[:], identb[:], r=[K("kr2"), "identb"], w=tk_)
                    self.cp("act", ckvT[:], tpb[:, 0:128], r=tk_, w=[K("ckvT")])
                    self.cp("act", KpT[:, i * 128:(i + 1) * 128], tpb[:, 128:256], r=tk_, w=["KpT"])
                    yield
                    for hf in range(2):
                        vp, vk = self.ps(4)
                        self.mm(vp[:, :], ckvT[:], wukv1[:, 0, hf * 512:(hf + 1) * 512], r=[K("ckvT"), "wukv1"], w=vk)
                        self.cp("act", kvs[:, hf * 512:(hf + 1) * 512], vp[:, :], r=vk, w=[K("kvs")])
                        yield
                    kv3 = kvs[:].rearrange("p (h d) -> p h d", h=8)
                    self.cp("pool", Vx[:, i, :, 0:64], kv3[:, :, 64:128], r=[K("kvs")], w=["Vx"])
                    yield
                    self.tt("dve", sqb[:, 0:512].rearrange("p (h d) -> p h d", h=8), kv3[:, :, 0:64], kv3[:, :, 0:64], ALU.mult,
                            r=[K("kvs")], w=[K("sqb")])
                    S.op("dve", lambda e: e.tensor_reduce(out=st2[:, 8:16], in_=sqb[:, 0:512].rearrange("p (h d) -> p h d", h=8),
                                                          axis=AX.X, op=ALU.add), r=[K("sqb")], w=[K("st2")])
                    rms_rows(st2[:, 16:24], st2[:, 8:16], 64, [K("st2")])
                    yield
                    self.tt("dve", sqb[:, 0:512].rearrange("p (h d) -> p h d", h=8), kv3[:, :, 0:64],
                            st2[:, 16:24].unsqueeze(2).to_broadcast([128, 8, 64]), ALU.mult, r=[K("kvs"), K("st2")], w=[K("sqb")])
                    self.tt("pool", knn[:], sqb[:, 0:512].rearrange("p (h d) -> p h d", h=8),
                            rowB[:, RB["kn"]:RB["kn"] + 64].unsqueeze(1).to_broadcast([128, 8, 64]), ALU.mult,
                            r=[K("sqb"), "rowB"], w=[K("knn")])
                    tpb, tk_ = self.psb(4)
                    for g in range(4):
                        self.tr(tpb[:, g * 128:(g + 1) * 128], knn[:, 2 * g:2 * g + 2, :].rearrange("p h d -> p (h d)"),
                                identb[:], r=[K("knn"), "identb"], w=tk_)
                    self.cp("act", KnT[:, :, i * 128:(i + 1) * 128], tpb[:, 0:512].rearrange("p (g t) -> p g t", g=4), r=tk_,
                            w=["KnT"])

                    yield
                return gen

            slots = [make_slot("_s%d" % j) for j in range(NS)]
            active = []
            nxt_i = 0
            while nxt_i < NB or active:
                while len(active) < NS and nxt_i < NB:
                    active.append(slots[nxt_i % NS](nxt_i))
                    nxt_i += 1
                    break
                for gen in list(active):
                    try:
                        next(gen)
                    except StopIteration:
                        active.remove(gen)
            S.barrier()
        self.st = self._phase_st
        self.alloc_h(1)
        wi = sb("wi", [128, 8, 2464], BF16)
        wuq = sb("wuq", [128, 2, 768], BF16)
        wukv = sb("wukv", [128, 1, 1024], BF16)
        womla = sb("womla", [128, 4, 1024], BF16)
        worw = sb("worw", [128, 4, 1024], BF16)
        wout = sb("wout", [128, 8, 1024], BF16)
        with ExitStack() as stw:
            stg = [sb("stgB%d" % j, [128, 2048], F32, stw) for j in range(2)]
            sk = ["stgB0", "stgB1"]
            self.load_w_bf16(wi[:, :, 0:416], "wi", lambda k: din["w_in"][k * 128:(k + 1) * 128, 0:416], 416, stg, sk, 8)
            self.load_w_bf16(wi[:, :, 416:2464], "wi", lambda k: din["w_in"][k * 128:(k + 1) * 128, 2208:4256], 2048,
                             stg, sk, 8)
            self.load_w_bf16(wuq, "wuq", lambda k: din["w_uq"][k * 128:(k + 1) * 128, :], 768, stg, sk, 2)
            self.load_w_bf16(wukv, "wukv", lambda k: din["w_ukv"][:, :], 1024, stg, sk, 1)
            self.load_w_bf16(womla, "womla", lambda k: din["w_o_mla"][k * 128:(k + 1) * 128, :], 1024, stg, sk, 4)
            self.load_w_bf16(worw, "worw", lambda k: din["w_o_rwkv"][k * 128:(k + 1) * 128, :], 1024, stg, sk, 4)
            self.load_w_bf16(wout, "wout", lambda k: din["w_out"][k * 128:(k + 1) * 128, :], 1024, stg, sk, 8)
            S.barrier()
        kva = sb("kva", [128, 160], F32)
        kvs = sb("kvs", [128, 1024], F32)
        st2 = sb("st2", [128, 40], F32)
        ckv = sb("ckv", [128, 128], BF16)
        ckvT = sb("ckvT", [128, 128], BF16)
        sqb = sb("sqb", [128, 768], F32)
        knn = sb("knn", [128, 8, 64], BF16)
        kr = sb("kr", [128, 32], F32)
        kr2 = sb("kr2", [128, 128], BF16)
        cs = sb("cs", [128, 32], F32)
        uu = sb("uu", [128, 32], F32)
        ui = sb("ui", [128, 32], I32)
        uf = sb("uf", [128, 32], F32)
        um = sb("um", [128, 32], F32)
        t16 = [sb("t16_%d" % j, [128, 8, 16], F32) for j in range(4)]
        qa = sb("qa", [128, 256], F32)
        cq = sb("cq", [128, 256], BF16)
        cqT = sb("cqT", [128, 2, 128], BF16)
        qf = sb("qf", [128, 768], F32)
        qn = sb("qn", [128, 8, 64], BF16)
        qpf = sb("qpf", [128, 8, 32], F32)
        qp = sb("qp", [128, 8, 32], BF16)
        QnB = sb("QnB", [128, 4, 256], BF16)
        QpB = sb("QpB", [128, 4, 256], BF16)
        self.memset("pool", QnB[:], 0.0, w=["QnB"])
        self.memset("pool", QpB[:], 0.0, w=["QpB"])
        PTt = [sb("PTt%d" % j, [128, 512], BF16) for j in range(3)]
        orc = sb("orc", [128, 8], F32)
        OA = sb("OA", [128, 8, 64], BF16)
        OT = sb("OT", [128, 4, 128], BF16)
        ZT = sb("ZT", [128, 4, 128], BF16)
        gsa = sb("gsa", [128, 128], F32)
        gsb = sb("gsb", [128, 128], F32)
        xtmp = sb("xtmp", [128, 512], F32)
        g1 = sb("g1", [128, 128], F32)
        g2 = sb("g2", [128, 128], F32)
        GTt = sb("GTt", [128, 8, 128], BF16)
        ATT_SCALE = 96.0 ** -0.5
        Ops = self.pbank[6:8]

        def rms_rows(dst_rstd, src_sq_ap, n, rkeys, extra_scale=1.0):
            self.act(dst_rstd, src_sq_ap, AF.Sqrt, r=rkeys + ["epsc"], w=["st2"], bias=self.epsc[:, 0:1], scale=1.0 / n)
            self.recip(dst_rstd, dst_rstd, r=["st2"], w=["st2"])
            if extra_scale != 1.0:
                self.ts("dve", dst_rstd, dst_rstd, extra_scale, None, ALU.mult, None, r=["st2"], w=["st2"])

        def rope(dst, src, nh, rk, wk):
            cosb = cs[:, 0:16].unsqueeze(1).to_broadcast([128, nh, 16])
            sinb = cs[:, 16:32].unsqueeze(1).to_broadcast([128, nh, 16])
            a, b, c, d = [t[:, 0:nh, :] for t in t16]
            x1_, x2_ = src[:, :, 0:16], src[:, :, 16:32]
            self.tt("dve", a, x1_, cosb, ALU.mult, r=rk + ["cs"], w=["t16a"])
            self.tt("pool", b, x2_, sinb, ALU.mult, r=rk + ["cs"], w=["t16b"])
            self.tt("dve", c, x1_, sinb, ALU.mult, r=rk + ["cs"], w=["t16c"])
            self.tt("pool", d, x2_, cosb, ALU.mult, r=rk + ["cs"], w=["t16d"])
            self.tt("dve", dst[:, :, 0:16], a, b, ALU.subtract, r=["t16a", "t16b"], w=wk)
            self.tt("dve", dst[:, :, 16:32], c, d, ALU.add, r=["t16c", "t16d"], w=wk)

        def cossin(pcol):
            self.ts("dve", uu[:, 0:16], rowB[:, RB["invf"]:RB["invf"] + 16], pcol, None, ALU.mult, None,
                    r=["rowB", "posf"], w=["uu"])
            self.ts("dve", uu[:, 16:32], uu[:, 0:16], 0.0, None, ALU.add, None, r=["uu"], w=["uu"])
            self.ts("dve", uu[:, 0:16], uu[:, 0:16], 0.25, None, ALU.add, None, r=["uu"], w=["uu"])
            self.cp("dve", ui[:], uu[:], r=["uu"], w=["ui"])
            self.cp("dve", uf[:], ui[:], r=["ui"], w=["uf"])
            self.tt("dve", uf[:], uu[:], uf[:], ALU.subtract, r=["uu", "uf"], w=["uf"])
            self.ts("dve", um[:], uf[:], 0.5, None, ALU.is_gt, None, r=["uf"], w=["um"])
            self.tt("dve", uf[:], uf[:], um[:], ALU.subtract, r=["uf", "um"], w=["uf"])
            self.act(cs[:], uf[:], AF.Sin, r=["uf"], w=["cs"], scale=float(2 * np.pi))

        NBh = NB // 2
        selm = sb("selmB", [128, 2], F32)
        self.ld(selm[:], din["selm"][:, :], w=["selmB"])
        MA = sb("MA", [128, 128], BF16)
        MB = sb("MB", [128, 128], BF16)
        negb = sb("negb", [128, 1], F32)
        self.ts("dve", MA[:], caus[:], selm[:, 0:1], selm[:, 1:2], ALU.mult, ALU.add, r=["caus", "selmB"], w=["MA"])
        self.ts("dve", MB[:], caus[:], selm[:, 1:2], None, ALU.mult, None, r=["caus", "selmB"], w=["MB"])
        self.ts("dve", negb[:], selm[:, 0:1], -10000.0, None, ALU.mult, None, r=["selmB"], w=["negb"])
        poso = sb("poso", [128, 16], I32)
        posfo = sb("posfo", [128, 16], F32)
        self.ld(poso[:], din["poso"][:, :], w=["poso"])
        self.cp("dve", posfo[:], poso[:], r=["poso"], w=["posfo"])
        ZTa = sb("ZTa", [128, 4, 128], BF16)
        ZTb = sb("ZTb", [128, 4, 128], BF16)

        def kb_groups(j):
            groups = []
            lo = list(range(0, 2 * j))
            for q in range(0, len(lo), 2):
                groups.append((lo[q:q + 2], 0))
            groups.append(([2 * j], 1))
            groups.append(([2 * j + 1], 3))
            return groups

        for i in range(NBh):
            hT, hk = self.make_h(i, din["xown"][i * 128:(i + 1) * 128, :], self.sc1, self.modc[:, 0:8],
                                 ["sc1", "modc"], carry=False)
            self.ld(ZTa[:], zscr[:, (2 * i) * 128:(2 * i + 1) * 128].rearrange("(g p) t -> p g t", p=128), w=["ZTa"])
            self.ld(ZTb[:], zscr[:, (2 * i + 1) * 128:(2 * i + 2) * 128].rearrange("(g p) t -> p g t", p=128), w=["ZTb"])
            xt, xk = self.xt[0], "xt0"
            cossin(posfo[:, i:i + 1])
            qp_, qk_ = self.ps(2)
            for k in range(8):
                self.mm(qp_[:, 0:256], hT[:, k, 2:130], wi[:, k, 0:256], r=[hk, "wi"], w=qk_, start=(k == 0), stop=(k == 7))
            self.cp("act", qa[:], qp_[:, 0:256], r=qk_, w=["qa"])
            self.act(sqb[:, 0:256], qa[:], AF.Square, r=["qa"], w=["sqb", "st2"], accum=st2[:, 4:5])
            rms_rows(st2[:, 5:6], st2[:, 4:5], 256, ["st2"])
            self.stt("dve", cq[:], qa[:], st2[:, 5:6], rowB[:, RB["qan"]:RB["qan"] + 256], ALU.mult, ALU.mult,
                     r=["qa", "st2", "rowB"], w=["cq"])
            tpb, tk_ = self.psb(2)
            for c in range(2):
                self.tr(tpb[:, c * 128:(c + 1) * 128], cq[:, c * 128:(c + 1) * 128], identb[:], r=["cq", "identb"], w=tk_)
            self.cp("act", cqT[:], tpb[:, 0:256].rearrange("p (c t) -> p c t", c=2), r=tk_, w=["cqT"])
            for hf in range(2):
                qq, qqk = self.ps(4)
                for c in range(2):
                    self.mm(qq[:, 0:384], cqT[:, c, :], wuq[:, c, hf * 384:(hf + 1) * 384], r=["cqT", "wuq"], w=qqk,
                            start=(c == 0), stop=(c == 1))
                self.cp("act", qf[:, hf * 384:(hf + 1) * 384], qq[:, 0:384], r=qqk, w=["qf"])
            q3 = qf[:].rearrange("p (h d) -> p h d", h=8)
            self.tt("dve", sqb[:].rearrange("p (h d) -> p h d", h=8), q3, q3, ALU.mult, r=["qf"], w=["sqb"])
            s3 = sqb[:].rearrange("p (h d) -> p h d", h=8)
            S.op("dve", lambda e: e.tensor_reduce(out=st2[:, 24:32], in_=s3[:, :, 0:64], axis=AX.X, op=ALU.add),
                 r=["sqb"], w=["st2"])
            S.op("dve", lambda e: e.tensor_reduce(out=st2[:, 32:40], in_=s3[:, :, 64:96], axis=AX.X, op=ALU.add),
                 r=["sqb"], w=["st2"])
            rms_rows(st2[:, 8:16], st2[:, 24:32], 64, ["st2"], ATT_SCALE)
            rms_rows(st2[:, 16:24], st2[:, 32:40], 32, ["st2"], ATT_SCALE)
            self.tt("dve", s3[:, :, 0:64], q3[:, :, 0:64], st2[:, 8:16].unsqueeze(2).to_broadcast([128, 8, 64]),
                    ALU.mult, r=["qf", "st2"], w=["sqb"])
            self.tt("dve", qn[:], s3[:, :, 0:64], rowB[:, RB["qn"]:RB["qn"] + 64].unsqueeze(1).to_broadcast([128, 8, 64]),
                    ALU.mult, r=["sqb", "rowB"], w=["qn"])
            self.tt("dve", s3[:, :, 64:96], q3[:, :, 64:96], st2[:, 16:24].unsqueeze(2).to_broadcast([128, 8, 32]),
                    ALU.mult, r=["qf", "st2"], w=["sqb"])
            self.tt("dve", qpf[:], s3[:, :, 64:96],
                    rowB[:, RB["qn"] + 64:RB["qn"] + 96].unsqueeze(1).to_broadcast([128, 8, 32]), ALU.mult,
                    r=["sqb", "rowB"], w=["qpf"])
            rope(qp[:], qpf[:], 8, ["qpf"], ["qp"])
            tpb, tk_ = self.psb(4)
            for g in range(4):
                self.tr(tpb[:, g * 128:(g + 1) * 128], qn[:, 2 * g:2 * g + 2, :].rearrange("p h d -> p (h d)"),
                        identb[:], r=["qn", "identb"], w=tk_)
            self.cp("act", QnB[0:64, :, 0:128], tpb[0:64, 0:512].rearrange("p (g t) -> p g t", g=4), r=tk_, w=["QnB"])
            self.cp("act", QnB[64:128, :, 128:256], tpb[64:128, 0:512].rearrange("p (g t) -> p g t", g=4), r=tk_, w=["QnB"])
            tpb, tk_ = self.psb(4)
            for g in range(4):
                self.tr(tpb[0:64, g * 128:(g + 1) * 128], qp[:, 2 * g:2 * g + 2, :].rearrange("p h d -> p (h d)"),
                        identb[:], r=["qp", "identb"], w=tk_)
            self.cp("act", QpB[0:32, :, 0:128], tpb[0:32, 0:512].rearrange("p (g t) -> p g t", g=4), r=tk_, w=["QpB"])
            self.cp("act", QpB[32:64, :, 128:256], tpb[32:64, 0:512].rearrange("p (g t) -> p g t", g=4), r=tk_, w=["QpB"])
            last_kb = 2 * i + 1
            work = [(g, kbs, mode) for g in range(4) for (kbs, mode) in kb_groups(i)]
            firsts = [True] * 8

            def stage1(n):
                g, kbs, mode = work[n]
                nb = len(kbs)
                sp_, sk_ = self.ps(4)
                for j, kb in enumerate(kbs):
                    self.mm(sp_[:, j * 256:(j + 1) * 256], KnT[:, g, kb * 128:(kb + 1) * 128], QnB[:, g, :],
                            r=["KnT", "QnB"], w=sk_, start=True, stop=False)
                    self.mm(sp_[:, j * 256:(j + 1) * 256], KpT[:, kb * 128:(kb + 1) * 128], QpB[:, g, :],
                            r=["KpT", "QpB"], w=sk_, start=False, stop=True)
                pt, ptk = PTt[n % 3], "PTt%d" % (n % 3)
                if mode == 2:
                    self.act(pt[:, 0:nb * 256], sp_[:, 0:nb * 256], AF.Exp, r=sk_ + ["negb"], w=[ptk], bias=negb[:, 0:1])
                else:
                    self.act(pt[:, 0:nb * 256], sp_[:, 0:nb * 256], AF.Exp, r=sk_, w=[ptk])
                if mode in (1, 3):
                    mt, mk_ = (MA, "MA") if mode == 1 else (MB, "MB")
                    self.tt("pool", pt[:, 0:256].rearrange("p (a q) -> p a q", a=2), pt[:, 0:256].rearrange("p (a q) -> p a q", a=2),
                            mt[:].unsqueeze(1).to_broadcast([128, 2, 128]), ALU.mult, r=[ptk, mk_], w=[ptk])

            def stage2(n):
                g, kbs, mode = work[n]
                pt, ptk = PTt[n % 3], "PTt%d" % (n % 3)
                for j, kb in enumerate(kbs):
                    for a in range(2):
                        h = 2 * g + a
                        ob = Ops[a]
                        oc = g * 65
                        self.mm(ob[:, oc:oc + 65], pt[:, j * 256 + a * 128:j * 256 + (a + 1) * 128], Vx[:, kb, h, :],
                                r=[ptk, "Vx"], w=["@b%d" % (6 + a)], start=firsts[h], stop=(kb == last_kb))
                        firsts[h] = False

            for n in range(len(work) + 2):
                if n < len(work):
                    stage1(n)
                if n >= 2:
                    stage2(n - 2)
            OA4 = OA[:].rearrange("p (g a) d -> p g a d", a=2)
            for bk in range(2):
                o3 = Ops[bk][:, 0:260].rearrange("p (h d) -> p h d", h=4)
                self.recip(orc[:, bk * 4:(bk + 1) * 4], o3[:, :, 64], r=["@b%d" % (6 + bk)], w=["orc"])
                self.tt("dve", OA4[:, :, bk, :], o3[:, :, 0:64],
                        orc[:, bk * 4:(bk + 1) * 4].unsqueeze(2).to_broadcast([128, 4, 64]), ALU.mult,
                        r=["@b%d" % (6 + bk), "orc"], w=["OA"])
            tpb, tk_ = self.psb(4)
            for g in range(4):
                self.tr(tpb[:, g * 128:(g + 1) * 128], OA[:, 2 * g:2 * g + 2, :].rearrange("p h d -> p (h d)"),
                        identb[:], r=["OA", "identb"], w=tk_)
            self.cp("act", OT[:], tpb[:, 0:512].rearrange("p (g t) -> p g t", g=4), r=tk_, w=["OT"])
            self.ts("dve", ZT[:], ZTa[:], selm[:, 0:1], None, ALU.mult, None, r=["ZTa", "selmB"], w=["ZT"])
            self.stt("dve", ZT[:], ZTb[:], selm[:, 1:2], ZT[:], ALU.mult, ALU.add, r=["ZTb", "selmB", "ZT"], w=["ZT"])
            for m in range(8):
                for (mm_, gs_, gsk) in ((m, gsa, "gsa"), (8 + m, gsb, "gsb")):
                    gp, gk = self.ps(1)
                    for k in range(8):
                        self.mm(gp[:, :], wi[:, k, 416 + mm_ * 128:416 + (mm_ + 1) * 128], hT[:, k, 2:130], r=["wi", hk],
                                w=gk, start=(k == 0), stop=(k == 7))
                    self.act(gs_[:], gp[:, :], AF.Sigmoid, r=gk, w=[gsk])
                p1, k1 = self.ps(1)
                for k in range(4):
                    self.mm(p1[:, :], womla[:, k, m * 128:(m + 1) * 128], OT[:, k, :], r=["womla", "OT"], w=k1,
                            start=(k == 0), stop=(k == 3))
                self.tt("dve", g1[:], p1[:, :], gsa[:], ALU.mult, r=k1 + ["gsa"], w=["g1"])
                p2, k2 = self.ps(1)
                for k in range(4):
                    self.mm(p2[:, :], worw[:, k, m * 128:(m + 1) * 128], ZT[:, k, :], r=["worw", "ZT"], w=k2,
                            start=(k == 0), stop=(k == 3))
                self.tt("dve", g2[:], p2[:, :], gsb[:], ALU.mult, r=k2 + ["gsb"], w=["g2"])
                self.tt("dve", GTt[:, m, :], g1[:], g2[:], ALU.add, r=["g1", "g2"], w=["GTt"])
            for hf in range(2):
                mp, mk = self.ps(4)
                for m in range(8):
                    self.mm(mp[:, :], GTt[:, m, :], wout[:, m, hf * 512:(hf + 1) * 512], r=["GTt", "wout"], w=mk,
                            start=(m == 0), stop=(m == 7))
                self.tt("dve", xtmp[:], mp[:, :], self.gtb[:, hf * 512:(hf + 1) * 512], ALU.mult, r=mk + ["gtb"],
                        w=["xtmp"])
                self.tt("dve", xt[:, hf * 512:(hf + 1) * 512], xt[:, hf * 512:(hf + 1) * 512], xtmp[:], ALU.add,
                        r=[xk, "xtmp"], w=[xk])
            self.ld(x1scr[i * 128:(i + 1) * 128, :], xt[:], r=[xk], w=[])

    def phaseC(self, x1scr, gscr, out):
        sb, din, S = self.sb, self.din, self.S
        NT = int(os.environ.get("NT", "16"))
        TOK = NT * 128
        from_x = "B" not in self.phases
        src = din["xf"] if from_x else x1scr
        ident = self.ident
        selm = sb("selm", [128, 2], F32)
        self.ld(selm[:], din["selm"][:, :], w=["selm"])
        with ExitStack() as stw:
            rowC = sb("rowC", [128, 1024], F32, stw)
            self.ld(rowC[:], din["rowC"][:, :], w=["rowC"])
            self.tt("dve", self.gtb[:, 1024:2048], self.gtb[:, 1024:2048], rowC[:], ALU.add, r=["gtb", "rowC"], w=["gtb"])
            S.barrier()
        H2 = sb("H2", [128, 8, TOK], BF16)
        xa = None

        def load_sel(tt, dst, dkey):
            self.ld(dst[:], src[tt * 128:(tt + 1) * 128, :], w=[dkey])
            if not from_x:
                return
            self.ld(xa[:], src[2048 + tt * 128:2048 + (tt + 1) * 128, :], w=["xa"])
            self.ts("dve", dst[:], dst[:], selm[:, 0:1], None, ALU.mult, None, r=[dkey, "selm"], w=[dkey])
            self.stt("dve", dst[:], xa[:], selm[:, 1:2], dst[:], ALU.mult, ALU.add, r=["xa", "selm", dkey], w=[dkey])

        self.banks1 = (0, 1, 2, 3, 4, 6, 7)
        with ExitStack() as st1:
            self.st = st1
            self.alloc_h(2)
            if from_x:
                xa = sb("xa", [128, D], F32)
            wq = sb("wq", [128, 8, 2048], BF16)
            skT = sb("skT", [128, 2, 128], BF16)
            iota = sb("iota", [128, 128], F32)
            self.ld(iota[:], din["iota"][:, :], w=["iota"])
            with ExitStack() as stw:
                stg = [sb("stgC%d" % j, [128, 2048], F32, stw) for j in range(2)]
                sk = ["stgC0", "stgC1"]
                self.load_w_bf16(wq, "wq", lambda k: din["w_query"][k * 128:(k + 1) * 128, :], 2048, stg, sk, 8)
                self.ld(stg[0][:, 0:128], din["sk1T"][:, :], w=[sk[0]])
                self.ld(stg[1][:, 0:128], din["sk2T"][:, :], w=[sk[1]])
                self.cp("dve", skT[:, 0, :], stg[0][:, 0:128], r=[sk[0]], w=["skT"])
                self.cp("dve", skT[:, 1, :], stg[1][:, 0:128], r=[sk[1]], w=["skT"])
                S.barrier()
            qTs = [sb("qT%d" % j, [128, 16, 128], BF16) for j in range(2)]
            scs = [sb("sc%d" % j, [128, 16, 128], F32) for j in range(2)]
            scrA = sb("scrA", [128, 16, 128], F32)
            scrB = sb("scrB", [128, 8, 256], F32)
            V16 = sb("V16", [128, 16, 16], F32)
            I16u = sb("I16u", [128, 16, 16], U32)
            I16 = sb("I16", [128, 16, 16], F32)
            Asel = sb("Asel", [128, 128], F32)
            Bsel = sb("Bsel", [128, 128], F32)
            Wsel = sb("Wsel", [128, 128], F32)
            ABW = sb("ABW", [128, 3, 128], F32)
            Aoh = [sb("Aoh%d" % j, [128, 128], BF16) for j in range(2)]
            Boh = [sb("Boh%d" % j, [128, 128], BF16) for j in range(2)]
            GT = sb("GT", [128, 128, 128], BF16)
            cand8 = sb("cand8", [128, 8, 256], F32)
            tv8 = sb("tv8", [128, 8, 16], F32)
            posu8 = sb("posu8", [128, 8, 16], U32)
            ju = sb("ju", [128, 2, 8, 16], U32)
            jf = sb("jf", [128, 2, 8, 16], F32)
            E8 = sb("E8", [128, 2048], F32)
            e8 = sb("e8", [128, 8, 16], F32)
            s8 = sb("s8", [128, 16], F32)
            AohB = [sb("AohB%d" % j, [128, 16, 128], BF16) for j in range(2)]
            BohB = [sb("BohB%d" % j, [128, 16, 128], BF16) for j in range(2)]
            def front(tt):
                b = tt % 2
                hT, hk = self.make_h(tt, None, self.sc2, self.modc[:, 16:24], ["sc2", "modc"], carry=False, evac_act=True)
                if tt + 2 < NT:
                    load_sel(tt + 2, self.xt[b], "xt%d" % b)
                self.cp("pool", H2[:, :, tt * 128:(tt + 1) * 128], hT[:, :, 2:130], r=[hk], w=["H2"])
                qT, qTk = qTs[b], "qT%d" % b
                sc, sck = scs[b], "sc%d" % b
                for m in range(16):
                    qp_, qk_ = self.ps(1)
                    for k in range(8):
                        self.mm(qp_[:, :], wq[:, k, m * 128:(m + 1) * 128], hT[:, k, 2:130], r=["wq", hk], w=qk_,
                                start=(k == 0), stop=(k == 7))
                    self.cp("act", qT[:, m, :], qp_[:, :], r=qk_, w=[qTk])
                for m4 in range(4):
                    sp_, sk_ = self.ps(4)
                    for j in range(4):
                        m = m4 * 4 + j
                        self.mm(sp_[:, j * 128:(j + 1) * 128], qT[:, m, :], skT[:, m % 2, :], r=[qTk, "skT"], w=sk_)
                    self.cp("act", sc[:, m4 * 4:(m4 + 1) * 4, :], sp_[:, :].rearrange("p (a b) -> p a b", a=4), r=sk_,
                            w=[sck])

            load_sel(0, self.xt[0], "xt0")
            if NT > 1:
                load_sel(1, self.xt[1], "xt1")
            front(0)
            for tt in range(NT):
                if tt + 1 < NT:
                    front(tt + 1)
                sc, sck = scs[tt % 2], "sc%d" % (tt % 2)

                def top16_multi(items):
                    for (vo, io, sa, sc_, tg, ks) in items:
                        S.op("dve", lambda e, vo=vo, sa=sa: e.max(out=vo[:, 0:8], in_=sa), r=ks, w=["tkv" + tg])
                    for (vo, io, sa, sc_, tg, ks) in items:
                        S.op("dve", lambda e, vo=vo, io=io, sa=sa: e.max_index(out=io[:, 0:8], in_max=vo[:, 0:8], in_values=sa),
                             r=ks + ["tkv" + tg], w=["tki" + tg])
                    for (vo, io, sa, sc_, tg, ks) in items:
                        S.op("dve", lambda e, vo=vo, sa=sa, sc_=sc_: e.match_replace(out=sc_, in_to_replace=vo[:, 0:8],
                                                                                 in_values=sa, imm_value=-1e30),
                             r=ks + ["tkv" + tg], w=["tks" + tg])
                    for (vo, io, sa, sc_, tg, ks) in items:
                        S.op("dve", lambda e, vo=vo, sc_=sc_: e.max(out=vo[:, 8:16], in_=sc_), r=["tks" + tg], w=["tkv" + tg])
                    for (vo, io, sa, sc_, tg, ks) in items:
                        S.op("dve", lambda e, vo=vo, io=io, sc_=sc_: e.max_index(out=io[:, 8:16], in_max=vo[:, 8:16], in_values=sc_),
                             r=["tks" + tg, "tkv" + tg], w=["tki" + tg])

                top16_multi([(V16[:, m, :], I16u[:, m, :], sc[:, m, :], scrA[:, m, :], "a%d" % m, [sck]) for m in range(16)])
                kA_v = ["tkva%d" % m for m in range(16)]
                kA_i = ["tkia%d" % m for m in range(16)]
                self.cp("pool", I16[:], I16u[:], r=kA_i, w=["I16"])
                V4 = V16[:].rearrange("p (h two) k -> p h two k", two=2)
                I4 = I16[:].rearrange("p (h two) k -> p h two k", two=2)
                self.tt("dve", cand8[:].rearrange("p h (a b) -> p h a b", a=16),
                        V4[:, :, 0, :].unsqueeze(3).to_broadcast([128, 8, 16, 16]),
                        V4[:, :, 1, :].unsqueeze(2).to_broadcast([128, 8, 16, 16]), ALU.add, r=kA_v, w=["cand8"])
                top16_multi([(tv8[:, h, :], posu8[:, h, :], cand8[:, h, :], scrB[:, h, :], "b%d" % h, ["cand8"]) for h in range(8)])
                kB_v = ["tkvb%d" % h for h in range(8)]
                kB_i = ["tkib%d" % h for h in range(8)]
                S.op("dve", lambda e: e.tensor_single_scalar(out=ju[:, 0], in_=posu8[:], scalar=4, op=ALU.logical_shift_right),
                     r=kB_i, w=["ju"])
                S.op("dve", lambda e: e.tensor_single_scalar(out=ju[:, 1], in_=posu8[:], scalar=15, op=ALU.bitwise_and),
                     r=kB_i, w=["ju"])
                self.cp("dve", jf[:], ju[:], r=["ju"], w=["jf"])
                io16 = iota[:, 0:16].unsqueeze(1).unsqueeze(1).to_broadcast([128, 8, 16, 16])
                for w_, dst in ((0, Asel), (1, Bsel)):
                    e4 = E8[:].rearrange("p (h k j) -> p h k j", h=8, k=16)
                    self.tt("dve", e4, io16, jf[:, w_].unsqueeze(3).to_broadcast([128, 8, 16, 16]), ALU.is_equal,
                            r=["iota", "jf"], w=["E8"])
                    self.tt("pool" if w_ == 0 else "dve", e4, e4, I4[:, :, w_, :].unsqueeze(2).to_broadcast([128, 8, 16, 16]), ALU.mult,
                            r=["E8", "I16"], w=["E8"])
                    S.op("dve", lambda e, dst=dst, e4=e4: e.tensor_reduce(out=dst[:].rearrange("p (h k) -> p h k", h=8), in_=e4,
                                                                        axis=AX.X, op=ALU.add), r=["E8"], w=[dst.name if False else ("Asel" if dst is Asel else "Bsel")])
                self.tt("dve", e8[:], tv8[:], tv8[:, :, 0:1].to_broadcast([128, 8, 16]), ALU.subtract, r=kB_v, w=["e8"])
                self.act(e8[:], e8[:], AF.Exp, r=["e8"], w=["e8"])
                S.op("dve", lambda e: e.tensor_reduce(out=s8[:, 0:8], in_=e8[:], axis=AX.X, op=ALU.add), r=["e8"], w=["s8"])
                self.recip(s8[:, 8:16], s8[:, 0:8], r=["s8"], w=["s8"])
                self.tt("dve", Wsel[:].rearrange("p (h k) -> p h k", h=8), e8[:], s8[:, 8:16].unsqueeze(2).to_broadcast([128, 8, 16]),
                        ALU.mult, r=["e8", "s8"], w=["Wsel"])
                tp, tk_ = self.ps(4)
                self.tr(tp[:, 0:128], Asel[:], ident[:], r=["Asel", "ident"], w=tk_)
                self.tr(tp[:, 128:256], Bsel[:], ident[:], r=["Bsel", "ident"], w=tk_)
                self.tr(tp[:, 256:384], Wsel[:], ident[:], r=["Wsel", "ident"], w=tk_)
                self.cp("act", ABW[:], tp[:, 0:384].rearrange("p (a b) -> p a b", a=3), r=tk_, w=["ABW"])
                iob = iota[:].unsqueeze(1).to_broadcast([128, 16, 128])
                for tb in range(8):
                    t0 = tb * 16
                    ao, aok = AohB[tb % 2], "AohB%d" % (tb % 2)
                    bo, bok = BohB[tb % 2], "BohB%d" % (tb % 2)
                    self.tt("dve", ao[:], iob, ABW[:, 0, t0:t0 + 16].unsqueeze(2).to_broadcast([128, 16, 128]), ALU.is_equal,
                            r=["iota", "ABW"], w=[aok])
                    self.tt("dve", bo[:], iob, ABW[:, 1, t0:t0 + 16].unsqueeze(2).to_broadcast([128, 16, 128]), ALU.is_equal,
                            r=["iota", "ABW"], w=[bok])
                    self.tt("pool" if tb % 2 == 0 else "dve", bo[:], bo[:],
                            ABW[:, 2, t0:t0 + 16].unsqueeze(2).to_broadcast([128, 16, 128]), ALU.mult, r=[bok, "ABW"], w=[bok])
                    for q4 in range(4):
                        gp, gk = self.ps(4)
                        for j in range(4):
                            tl = q4 * 4 + j
                            self.mm(gp[:, j * 128:(j + 1) * 128], ao[:, tl, :], bo[:, tl, :], r=[aok, bok], w=gk)
                        tg = t0 + q4 * 4
                        self.cp("act", GT[:, :, tg:tg + 4], gp[:, :].rearrange("p (t i) -> p i t", t=4), r=gk, w=["GT"])
                self.ld(gscr[tt, :, :], GT[:].rearrange("p a b -> p (a b)"), r=["GT"], w=["gscr"])
            S.barrier()
        with ExitStack() as st2:
            self.st = st2
            if from_x:
                xa = sb("xa2", [128, D], F32)
            ACC = sb("ACC", [128, NT, D], F32)
            xtCs = [sb("xtC%d" % j, [128, D], F32) for j in range(2)]
            DTbs = [sb("DTb%d" % j, [128, 4, 8, 128], BF16) for j in range(2)]
            UPbs = [sb("UPb%d" % j, [128, 4, D], BF16) for j in range(2)]
            GSl = sb("GSl", [128, NT, 512], BF16)
            Wts = [sb("Wt%d" % j, [128, 4, TOK], BF16) for j in range(2)]
            g_us = [sb("g_u%d" % j, [128, 512], F32) for j in range(2)]
            g_ss = [sb("g_s%d" % j, [128, 512], F32) for j in range(2)]
            gctr = [0]
            eupv = din["eup"].rearrange("(a b) d -> a b d", b=128)
            NCH = (TOK + 511) // 512
            chunks = [(l, tc) for tc in range(NCH) for l in range(4)]
            SQ = float(np.sqrt(0.044715))
            bctr = {"d": 0, "u": 0}

            def bank_of(kind):
                banks = (0, 1, 2, 3) if kind == "d" else (4, 6, 7)
                b = banks[bctr[kind] % len(banks)]
                bctr[kind] += 1
                return self.pbank[b][:, :], ["@b%d" % b]

            def pf_load(g, p):
                if p < 4:
                    self.S.dma("pool", DTbs[g % 2][:, p, :, :], din["edT"][g * 4 + p].rearrange("(k p) i -> p k i", p=128),
                               w=["DTb%d_%d" % (g % 2, p)])
                else:
                    u2 = p - 4
                    self.S.dma("pool", UPbs[g % 2][:, u2, :], eupv[:, g * 4 + u2, :], w=["UPb%d_%d" % (g % 2, u2)])

            def pf_cast(g, p):
                pass

            def load_gsl(g):
                for tt in range(NT):
                    self.ld(GSl[:, tt, :], gscr[tt, :, g * 512:(g + 1) * 512], r=["gscr"], w=["GSl%d" % tt])

            pend = []

            def down_chunk(g, ci):
                l, tc = chunks[ci]
                DTb, dk = DTbs[g % 2], "DTb%d" % (g % 2)
                Wt, wk = Wts[g % 2], "Wt%d" % (g % 2)
                n = min(512, TOK - tc * 512)
                ap_, ak = bank_of("d")
                for k in range(8):
                    self.mm(ap_[:, 0:n], DTb[:, l, k, :], H2[:, k, tc * 512:tc * 512 + n], r=[dk + "_%d" % l, "H2"], w=ak,
                            start=(k == 0), stop=(k == 7))
                gb_ = gctr[0] % 2
                gctr[0] += 1
                g_u, g_s, guk, gsk = g_us[gb_], g_ss[gb_], "g_u%d" % gb_, "g_s%d" % gb_
                self.act(g_u[:, 0:n], ap_[:, 0:n], AF.Square, r=ak, w=[guk], scale=SQ)
                self.stt("dve", g_u[:, 0:n], g_u[:, 0:n], 1.0, ap_[:, 0:n], ALU.add, ALU.mult, r=ak + [guk], w=[guk])
                self.act(g_s[:, 0:n], g_u[:, 0:n], AF.Sigmoid, r=[guk], w=[gsk], scale=1.5957691216057308)
                pend.append((l, tc, n, ap_, ak, g_s, gsk, Wt, wk))

            def down_finish():
                if not pend:
                    return
                l, tc, n, ap_, ak, g_s, gsk, Wt, wk = pend.pop(0)
                gsl = GSl[:, tc * 4:tc * 4 + (n // 128), l * 128:(l + 1) * 128]
                self.tt("pool", g_s[:, 0:n].rearrange("p (a b) -> p a b", b=128), g_s[:, 0:n].rearrange("p (a b) -> p a b", b=128),
                        gsl, ALU.mult, r=[gsk] + ["GSl%d" % t_ for t_ in range(tc * 4, tc * 4 + (n // 128))], w=[gsk])
                self.tt("dve", Wt[:, l, tc * 512:tc * 512 + n], ap_[:, 0:n], g_s[:, 0:n], ALU.mult, r=ak + [gsk], w=[wk])

            def up_tile(g, tt):
                UPb, uk_ = UPbs[g % 2], "UPb%d" % (g % 2)
                Wt, wk = Wts[g % 2], "Wt%d" % (g % 2)
                for hf in range(2):
                    op_, ok_ = bank_of("u")
                    for l in range(4):
                        self.mm(op_[:, :], Wt[:, l, tt * 128:(tt + 1) * 128], UPb[:, l, hf * 512:(hf + 1) * 512],
                                r=[wk, uk_ + "_%d" % l], w=ok_, start=(l == 0), stop=(l == 3))
                    if g == 0:
                        self.cp("dve", ACC[:, tt, hf * 512:(hf + 1) * 512], op_[:, :], r=ok_, w=["ACC%d" % tt])
                    else:
                        self.tt("dve", ACC[:, tt, hf * 512:(hf + 1) * 512], ACC[:, tt, hf * 512:(hf + 1) * 512], op_[:, :],
                                ALU.add, r=ok_ + ["ACC%d" % tt], w=["ACC%d" % tt])

            for p in range(8):
                pf_load(0, p)
                pf_cast(0, p)
            for p in range(4):
                pf_load(1, p)
                pf_cast(1, p)
            load_gsl(0)
            for ci in range(len(chunks)):
                down_chunk(0, ci)
                down_finish()
            for g in range(32):
                if g + 1 < 32:
                    load_gsl(g + 1)
                pieces = []
                if g + 1 < 32:
                    pieces += [(g + 1, p) for p in range(4, 8)]
                if g + 2 < 32:
                    pieces += [(g + 2, p) for p in range(4)]
                nsteps = max(len(chunks), NT, 2 * len(pieces))
                for s_ in range(nsteps):
                    if s_ % 2 == 0 and s_ // 2 < len(pieces):
                        pf_load(*pieces[s_ // 2])
                    if s_ % 2 == 1 and s_ // 2 < len(pieces):
                        pf_cast(*pieces[s_ // 2])
                    if g + 1 < 32 and s_ < len(chunks):
                        down_chunk(g + 1, s_)
                    if s_ < NT:
                        up_tile(g, s_)
                    if len(pend) > 1 or s_ == nsteps - 1:
                        down_finish()
                while pend:
                    down_finish()
            for tt in range(NT):
                b = 0
                xt, xk = xtCs[tt % 2], "xtC%d" % (tt % 2)
                load_sel(tt, xt, xk)
                self.tt("pool", ACC[:, tt, :], ACC[:, tt, :], self.gtb[:, 1024:2048], ALU.mult, r=["ACC%d" % tt, "gtb"],
                        w=["ACC%d" % tt])
                self.tt("dve", xt[:], xt[:], ACC[:, tt, :], ALU.add, r=[xk, "ACC%d" % tt], w=[xk])
                self.ld(out[tt * 128:(tt + 1) * 128, :], xt[:], r=[xk], w=[])


def _col(v):
    v = np.asarray(v, np.float32).reshape(-1)
    return np.ascontiguousarray(v.reshape(-1, 128).T)


def _rep(v):
    v = np.asarray(v, np.float32).reshape(-1)
    return np.ascontiguousarray(np.broadcast_to(v[None, :], (128, v.size)))


def prep(inp, nblk=32):
    f = lambda a: np.ascontiguousarray(np.asarray(a, np.float32))
    x = f(inp["x"])
    c = f(inp["c"])
    pos = np.asarray(inp["positions"], np.int32)
    b_ada = f(inp["b_ada"])[0]
    t = np.arange(128)
    msl = (t[None, :] < t[:, None]).astype(np.float32)
    msu = (t[None, :] > t[:, None]).astype(np.float32)
    mui = (t[None, :] >= t[:, None]).astype(np.float32)
    bones = np.kron(np.eye(2, dtype=np.float32), np.ones((64, 64), np.float32))
    half = 16
    invf = (10000.0 ** (-np.arange(half, dtype=np.float32) / half)).astype(np.float32)
    shared = dict(
        rowA=_rep(inp["mu_shift"]),
        ident=np.eye(128, dtype=np.float32), msl=msl, mg1=np.concatenate([msu, -mui], 1),
        mg2=np.concatenate([msu, mui], 1), bones=bones, caus=mui.copy(),
        iota=_rep(np.arange(128, dtype=np.float32)),
        w_ada=f(inp["w_ada"])[0], w_in=f(inp["w_in"])[0], w_uq=f(inp["w_uq"])[0], w_ukv=f(inp["w_ukv"])[0],
        w_o_mla=f(inp["w_o_mla"])[0],
        wl=np.ascontiguousarray(np.concatenate([f(inp["w_decay_up"])[0], f(inp["w_aaa_up"])[0]], 0)),
        w_gate_up=f(inp["w_gate_up"])[0], w_o_rwkv=f(inp["w_o_rwkv"])[0], w_out=f(inp["w_out"])[0],
        w_query=f(inp["w_query"])[0],
        sk1T=np.ascontiguousarray(f(inp["sub_keys1"])[0].T), sk2T=np.ascontiguousarray(f(inp["sub_keys2"])[0].T),
        edT=np.ascontiguousarray(f(inp["expert_down"])[0].reshape(128, 128, D).transpose(1, 2, 0)),
        eup=f(inp["expert_up"])[0],
        rowC=_rep(b_ada[5 * D:6 * D]),
    )
    maps = []
    for core in range(8):
        b, hf = core // 2, core % 2
        colp = np.concatenate([
            _col(c[b]), _col(b_ada[0:D]), _col(b_ada[D:2 * D]), _col(b_ada[3 * D:4 * D]), _col(b_ada[4 * D:5 * D]),
            _col(inp["norm_mix"]), _col(inp["norm_ffn"]), _col(inp["decay_base"]), _col(inp["aaa_base"]),
            _col(inp["k_k"]), _col(inp["k_a"]), _col(inp["r_k"]), _col(inp["ln_x_w"]), _col(inp["ln_x_b"])], 1)
        rowB = np.concatenate([_rep(inp["q_a_norm"]), _rep(inp["kv_a_norm"]), _rep(inp["q_norm"]), _rep(inp["k_norm"]),
                               _rep(invf / np.float32(2 * np.pi)), _rep(b_ada[2 * D:3 * D])], 1)
        selm = np.zeros((128, 2), np.float32)
        selm[:, hf] = 1.0
        m = dict(shared)
        nh = (nblk // 2) * 128
        xo = np.zeros((2048, D), np.float32)
        own = np.concatenate([np.arange((2 * j + hf) * 128, (2 * j + hf + 1) * 128) for j in range(nblk // 2)])
        xo[:nh] = x[b, own]
        po = np.zeros((2048,), np.int32)
        po[:nh] = pos[b, own]
        m.update(xown=xo, poso=np.ascontiguousarray(po.reshape(16, 128).T), xf=x[b], colp=np.ascontiguousarray(colp), rowB=np.ascontiguousarray(rowB),
                 posc=np.ascontiguousarray(pos[b].reshape(32, 128).T), selm=selm)
        maps.append(m)
    return maps


_CACHE = {}


def kernel(**inputs):
    if "k" not in _CACHE:
        _CACHE["k"] = K()
    kb = _CACHE["k"]
    maps = prep(inputs)
    res = run_bass_kernel_spmd(kb.nc, maps, core_ids=list(range(8)))
    outp = np.zeros((4, SEQ, D), np.float32)
    for core in range(8):
        b, hf = core // 2, core % 2
        own = np.concatenate([np.arange((2 * j + hf) * 128, (2 * j + hf + 1) * 128) for j in range(16)])
        outp[b, own] = res.results[core]["out"]
    return outp
```
